# Optimizing a Trainium2 kernel written in Bass

```python
import math
import jax, jax.numpy as jnp
from jax import lax
import numpy as np

D_MODEL = 1024
BATCH = 16
SEQ = 2048
DEPTH = 2

GRID_W = 64
CTX_LEN = 256
ROPE_THETA = 10000.0
NORM_EPS = 1e-6
SUBLN_EPS = 1e-5
LN_EPS = 1e-5
Q_BLOCK = 128

A_GROUPS = 8
A_WIDTH = D_MODEL // 2
A_GROUP_DIM = A_WIDTH // A_GROUPS
A_CHUNK = 128

B_HEADS = 4
B_HEAD_DIM = 64
B_V_DIM = 2 * B_HEAD_DIM
B_WIDTH = B_HEADS * B_V_DIM
B_QK = B_HEADS * 2 * B_HEAD_DIM

EVEN_IN = 3 * A_WIDTH + 2 * B_QK + 2 * B_WIDTH
EVEN_MIX = A_WIDTH + B_WIDTH
EVEN_SPLITS = (A_WIDTH, 2 * A_WIDTH, 3 * A_WIDTH, 3 * A_WIDTH + B_QK,
               3 * A_WIDTH + 2 * B_QK, 3 * A_WIDTH + 2 * B_QK + B_WIDTH)

C_HEADS = 8
C_NOPE = 128
C_ROPE = 64
C_V = 128
C_Q_RANK = 256
C_KV_RANK = 128
C_WIDTH = C_HEADS * C_V
ODD_IN = C_Q_RANK + C_KV_RANK + C_ROPE + C_WIDTH
ODD_SPLITS = (C_Q_RANK, C_Q_RANK + C_KV_RANK, C_Q_RANK + C_KV_RANK + C_ROPE)

N_EVEN = (DEPTH + 1) // 2
N_ODD = DEPTH // 2

kernel_name = 'hybrid_gmlp_diffattn_mla_prefix_dit'

F32 = jnp.float32


def rms_norm(x, w, eps=NORM_EPS):
    xf = x.astype(F32)
    y = xf * lax.rsqrt(jnp.mean(xf * xf, axis=-1, keepdims=True) + eps)
    return (y * w.astype(F32)).astype(x.dtype)


def layer_norm(x, w, b, eps=LN_EPS):
    xf = x.astype(F32)
    mu = jnp.mean(xf, axis=-1, keepdims=True)
    xc = xf - mu
    y = xc * lax.rsqrt(jnp.mean(xc * xc, axis=-1, keepdims=True) + eps)
    return (y * w.astype(F32) + b.astype(F32)).astype(x.dtype)


def adaln_terms(cond, w_ada, b_ada):
    m = jax.nn.silu(cond) @ w_ada + b_ada
    return jnp.split(m, 3, axis=-1)


def axial_rope_tables(seq, dim):
    rows = seq // GRID_W
    row = jnp.repeat(jnp.arange(rows), GRID_W).astype(F32)
    col = jnp.tile(jnp.arange(GRID_W), rows).astype(F32)
    half = dim // 2
    inv = ROPE_THETA ** (-jnp.arange(0, half, 2, dtype=F32) / half)
    ang_r = row[:, None] * inv[None, :]
    ang_c = col[:, None] * inv[None, :]
    ang = jnp.concatenate([ang_r, ang_r, ang_c, ang_c], axis=-1)
    return jnp.cos(ang), jnp.sin(ang)


def apply_rope(x, cos, sin):
    extra = x.ndim - 3
    shp = (1, cos.shape[0]) + (1,) * extra + (cos.shape[1],)
    c = cos.reshape(shp)
    s = sin.reshape(shp)
    seg = x.reshape(x.shape[:-1] + (2, 2, x.shape[-1] // 4))
    rot = jnp.concatenate([-seg[..., 1:, :], seg[..., :1, :]], axis=-2).reshape(x.shape)
    return (x.astype(F32) * c + rot.astype(F32) * s).astype(x.dtype)


def sweep_query_blocks(fn, q):
    b, s = q.shape[0], q.shape[1]
    nb = s // Q_BLOCK
    qb = jnp.moveaxis(q.reshape((b, nb, Q_BLOCK) + q.shape[2:]), 1, 0)
    ob = lax.map(fn, qb)
    return jnp.moveaxis(ob, 0, 1).reshape((b, s) + ob.shape[3:])


def spatial_gating(u, v, w_s, b_s, ln_w, ln_b):
    bsz, length, _ = v.shape
    vn = layer_norm(v, ln_w, ln_b)
    vc = vn.reshape(bsz, length // A_CHUNK, A_CHUNK, A_GROUPS, A_GROUP_DIM)
    mixed = jnp.einsum('gpq,bnqgd->bnpgd', w_s, vc) + b_s.T[None, None, :, :, None]
    return u * mixed.reshape(bsz, length, A_WIDTH)


def diff_attn_core(q, k, v, lam):
    s = jnp.einsum('bqhmd,bkhmd->bhmqk', q, k).astype(F32) * (B_HEAD_DIM ** -0.5)
    p = jax.nn.softmax(s, axis=-1)
    a = p[:, :, 0] - lam * p[:, :, 1]
    return jnp.einsum('bhqk,bkhd->bqhd', a.astype(v.dtype), v)


def mla_core(q, k, v):
    s = jnp.einsum('bqhd,bkhd->bhqk', q, k).astype(F32) * ((C_NOPE + C_ROPE) ** -0.5)
    p = jax.nn.softmax(s, axis=-1)
    return jnp.einsum('bhqk,bkhd->bqhd', p.astype(v.dtype), v)


def even_mixer(xl, xc, need_ctx, li, w_in, w_s, b_s, ln_w, ln_b, lq1, lk1, lq2, lk2, subln_w, w_out):
    bsz, seq, _ = xl.shape
    clen = xc.shape[1]
    cos, sin = axial_rope_tables(seq, B_HEAD_DIM)
    lam_init = 0.8 - 0.6 * math.exp(-0.3 * li)
    lam = (jnp.exp(jnp.sum(lq1.astype(F32) * lk1.astype(F32)))
           - jnp.exp(jnp.sum(lq2.astype(F32) * lk2.astype(F32))) + lam_init)

    au_l, av_l, az_l, bq_l, bk_l, bv_l, bz_l = jnp.split(xl @ w_in, EVEN_SPLITS, axis=-1)
    au_c, av_c, az_c, bq_c, bk_c, bv_c, bz_c = jnp.split(xc @ w_in, EVEN_SPLITS, axis=-1)

    def a_branch(u, v, z):
        return spatial_gating(jax.nn.gelu(u, approximate=False), jax.nn.gelu(v, approximate=False),
                              w_s, b_s, ln_w, ln_b) * jax.nn.silu(z)

    def b_post(o, z, length):
        o = rms_norm(o, subln_w, SUBLN_EPS) * (1.0 - lam_init)
        return o.reshape(bsz, length, B_WIDTH) * jax.nn.silu(z)

    qk_shape_l = (bsz, seq, B_HEADS, 2, B_HEAD_DIM)
    qk_shape_c = (bsz, clen, B_HEADS, 2, B_HEAD_DIM)
    q_l = apply_rope(bq_l.reshape(qk_shape_l), cos, sin)
    k_l = apply_rope(bk_l.reshape(qk_shape_l), cos, sin)
    v_l = bv_l.reshape(bsz, seq, B_HEADS, B_V_DIM)
    k_c = bk_c.reshape(qk_shape_c)
    v_c = bv_c.reshape(bsz, clen, B_HEADS, B_V_DIM)
    k_all = jnp.concatenate([k_c, k_l], axis=1)
    v_all = jnp.concatenate([v_c, v_l], axis=1)

    ob_l = sweep_query_blocks(lambda qb: diff_attn_core(qb, k_all, v_all, lam), q_l)
    out_l = jnp.concatenate([a_branch(au_l, av_l, az_l), b_post(ob_l, bz_l, seq)], axis=-1) @ w_out

    out_c = None
    if need_ctx:
        q_c = bq_c.reshape(qk_shape_c)
        ob_c = diff_attn_core(q_c, k_c, v_c, lam)
        out_c = jnp.concatenate([a_branch(au_c, av_c, az_c), b_post(ob_c, bz_c, clen)], axis=-1) @ w_out
    return out_l, out_c


def odd_mixer(xl, xc, need_ctx, w_in, q_norm_w, wq_b, kv_norm_w, wkv_b, w_out):
    bsz, seq, _ = xl.shape
    clen = xc.shape[1]
    cos, sin = axial_rope_tables(seq, C_ROPE)

    cq_l, ckv_l, kr_l, z_l = jnp.split(xl @ w_in, ODD_SPLITS, axis=-1)
    cq_c, ckv_c, kr_c, z_c = jnp.split(xc @ w_in, ODD_SPLITS, axis=-1)

    def queries(cq, length):
        q = (rms_norm(cq, q_norm_w) @ wq_b).reshape(bsz, length, C_HEADS, C_NOPE + C_ROPE)
        return q[..., :C_NOPE], q[..., C_NOPE:]

    def keys_values(ckv, k_rope, length):
        kv = (rms_norm(ckv, kv_norm_w) @ wkv_b).reshape(bsz, length, C_HEADS, C_NOPE + C_V)
        k_rope_h = jnp.broadcast_to(k_rope[:, :, None, :], (bsz, length, C_HEADS, C_ROPE))
        return jnp.concatenate([kv[..., :C_NOPE], k_rope_h], axis=-1), kv[..., C_NOPE:]

    qn_l, qr_l = queries(cq_l, seq)
    q_l = jnp.concatenate([qn_l, apply_rope(qr_l, cos, sin)], axis=-1)
    kr_l_rot = apply_rope(kr_l[:, :, None, :], cos, sin)[:, :, 0, :]
    k_l, v_l = keys_values(ckv_l, kr_l_rot, seq)
    k_c, v_c = keys_values(ckv_c, kr_c, clen)
    k_all = jnp.concatenate([k_c, k_l], axis=1)
    v_all = jnp.concatenate([v_c, v_l], axis=1)

    o_l = sweep_query_blocks(lambda qb: mla_core(qb, k_all, v_all), q_l)
    out_l = (o_l.reshape(bsz, seq, C_WIDTH) * jax.nn.silu(z_l)) @ w_out

    out_c = None
    if need_ctx:
        qn_c, qr_c = queries(cq_c, clen)
        q_c = jnp.concatenate([qn_c, qr_c], axis=-1)
        o_c = mla_core(q_c, k_c, v_c)
        out_c = (o_c.reshape(bsz, clen, C_WIDTH) * jax.nn.silu(z_c)) @ w_out
    return out_l, out_c


def setup_inputs(seed: int = 0) -> dict:
    key = jax.random.key(seed)
    ks = jax.random.split(key, 32)
    nrm = jax.random.normal
    D = D_MODEL
    return {
        'x': nrm(ks[0], (BATCH, SEQ, D), F32),
        'c': nrm(ks[1], (BATCH, D), F32),
        'ctx': nrm(ks[2], (BATCH, CTX_LEN, D), F32),
        'c_ctx': nrm(ks[3], (D,), F32),
        'norm_w': 1.0 + 0.02 * nrm(ks[4], (DEPTH, D), F32),
        'ada_w': 0.5 * D ** -0.5 * nrm(ks[5], (DEPTH, D, 3 * D), F32),
        'ada_b': 0.01 * nrm(ks[6], (DEPTH, 3 * D), F32),
        'even_w_in': D ** -0.5 * nrm(ks[7], (N_EVEN, D, EVEN_IN), F32),
        'a_ws': A_CHUNK ** -0.5 * nrm(ks[8], (N_EVEN, A_GROUPS, A_CHUNK, A_CHUNK), F32),
        'a_bs': 1.0 + 0.02 * nrm(ks[9], (N_EVEN, A_GROUPS, A_CHUNK), F32),
        'a_ln_w': 1.0 + 0.02 * nrm(ks[10], (N_EVEN, A_WIDTH), F32),
        'a_ln_b': 0.01 * nrm(ks[11], (N_EVEN, A_WIDTH), F32),
        'b_lq1': 0.1 * nrm(ks[12], (N_EVEN, B_HEAD_DIM), F32),
        'b_lk1': 0.1 * nrm(ks[13], (N_EVEN, B_HEAD_DIM), F32),
        'b_lq2': 0.1 * nrm(ks[14], (N_EVEN, B_HEAD_DIM), F32),
        'b_lk2': 0.1 * nrm(ks[15], (N_EVEN, B_HEAD_DIM), F32),
        'b_subln_w': 1.0 + 0.02 * nrm(ks[16], (N_EVEN, B_V_DIM), F32),
        'even_w_out': EVEN_MIX ** -0.5 * nrm(ks[17], (N_EVEN, EVEN_MIX, D), F32),
        'odd_w_in': D ** -0.5 * nrm(ks[18], (N_ODD, D, ODD_IN), F32),
        'c_q_norm_w': 1.0 + 0.02 * nrm(ks[19], (N_ODD, C_Q_RANK), F32),
        'c_wq_b': C_Q_RANK ** -0.5 * nrm(ks[20], (N_ODD, C_Q_RANK, C_HEADS * (C_NOPE + C_ROPE)), F32),
        'c_kv_norm_w': 1.0 + 0.02 * nrm(ks[21], (N_ODD, C_KV_RANK), F32),
        'c_wkv_b': C_KV_RANK ** -0.5 * nrm(ks[22], (N_ODD, C_KV_RANK, C_HEADS * (C_NOPE + C_V)), F32),
        'odd_w_out': C_WIDTH ** -0.5 * nrm(ks[23], (N_ODD, C_WIDTH, D), F32),
        'final_w': 1.0 + 0.02 * nrm(ks[24], (D,), F32),
    }


def reference(x, c, ctx, c_ctx, norm_w, ada_w, ada_b, even_w_in, a_ws, a_bs, a_ln_w, a_ln_b,
              b_lq1, b_lk1, b_lq2, b_lk2, b_subln_w, even_w_out, odd_w_in, c_q_norm_w, c_wq_b,
              c_kv_norm_w, c_wkv_b, odd_w_out, final_w):
    h_lat, h_ctx = x, ctx
    for li in range(DEPTH):
        need_ctx = li < DEPTH - 1
        shift_l, scale_l, gate_l = adaln_terms(c, ada_w[li], ada_b[li])
        shift_c, scale_c, gate_c = adaln_terms(c_ctx, ada_w[li], ada_b[li])
        xl = rms_norm(h_lat, norm_w[li]) * (1.0 + scale_l[:, None, :]) + shift_l[:, None, :]
        xc = rms_norm(h_ctx, norm_w[li]) * (1.0 + scale_c) + shift_c
        if li % 2 == 0:
            e = li // 2
            out_l, out_c = even_mixer(xl, xc, need_ctx, li, even_w_in[e], a_ws[e], a_bs[e],
                                      a_ln_w[e], a_ln_b[e], b_lq1[e], b_lk1[e], b_lq2[e], b_lk2[e],
                                      b_subln_w[e], even_w_out[e])
        else:
            o = li // 2
            out_l, out_c = odd_mixer(xl, xc, need_ctx, odd_w_in[o], c_q_norm_w[o], c_wq_b[o],
                                     c_kv_norm_w[o], c_wkv_b[o], odd_w_out[o])
        h_lat = h_lat + gate_l[:, None, :] * out_l
        if need_ctx:
            h_ctx = h_ctx + gate_c * out_c
    return rms_norm(h_lat, final_w)
```

```python
import math
from contextlib import ExitStack
import numpy as np
import concourse.bass as bass
import concourse.mybir as mybir
from concourse.bass_utils import run_bass_kernel_spmd

F32 = mybir.dt.float32
BF16 = mybir.dt.bfloat16
AF = mybir.ActivationFunctionType
ALU = mybir.AluOpType

EPOCH = 3800
NT = 18
D = 1024

LNW, LNB, SUBLN, QN, KVN, FINAL = 0, 512, 1024, 1152, 1408, 1536
LQ1, LK1, LQ2, LK2 = 2560, 2624, 2688, 2752
ABS, NW, ADAB, COND, COS, SIN, IDENT = 2816, 2824, 2840, 2888, 2912, 4064, 5216
NS = 5344


class Buf:
    __slots__ = ("name", "lw", "rd", "dsem", "dval", "excl")

    def __init__(self, name, excl=False):
        self.name = name
        self.excl = excl
        self.lw = None
        self.rd = {}
        self.dsem = None
        self.dval = 0


class Sched:
    def __init__(self, nc, stack):
        self.nc = nc
        self.stack = stack
        self.names = ["pe", "act", "dve", "pool", "sp"]
        self.streams = {n: [] for n in self.names}
        self.sems = {n: [self._sem(f"s_{n}0")] for n in self.names}
        self.cnt = {n: 0 for n in self.names}
        self.seen = {n: {} for n in self.names}
        self.dbufs = []
        self.free_dsems = []

    def _sem(self, name):
        return self.stack.enter_context(self.nc.semaphore(name))

    def _cur(self, eng):
        if self.cnt[eng] >= EPOCH:
            self.sems[eng].append(self._sem(f"s_{eng}{len(self.sems[eng])}"))
            self.cnt[eng] = 0
        return self.sems[eng][-1]

    def _need(self, eng, waits, dep):
        if dep is None:
            return
        sem, val = dep
        if self.seen[eng].get(sem, 0) >= val:
            return
        if waits.get(sem, 0) < val:
            waits[sem] = val

    def op(self, eng, fn, R=(), W=()):
        if any(b.excl for b in R):
            W = list(W) + [b for b in R if b.excl and b not in W]
            R = [b for b in R if not b.excl]
        waits = {}
        own = set(id(s) for s in self.sems[eng])
        for b in R:
            self._need(eng, waits, b.lw)
        for b in W:
            if b.lw is not None and id(b.lw[0]) not in own:
                self._need(eng, waits, b.lw)
            for r in b.rd.items():
                if id(r[0]) in own:
                    continue
                self._need(eng, waits, r)
        sem = self._cur(eng)
        self.cnt[eng] += 1
        val = self.cnt[eng]
        for s, v in waits.items():
            self.seen[eng][s] = v
        self.streams[eng].append((list(waits.items()), fn, sem, 1))
        for b in W:
            b.lw = (sem, val)
            b.rd = {}
        for b in R:
            b.rd[sem] = val

    def dma(self, q, fn, R=(), W=()):
        waits = {}
        for b in R:
            self._need(q, waits, b.lw)
        tgt = W[0]
        if tgt.dsem is None:
            if self.free_dsems:
                tgt.dsem, tgt.dval = self.free_dsems.pop()
            else:
                tgt.dsem = self._sem(f"d{len(self.dbufs)}_{tgt.name}".replace("/", "_"))
            self.dbufs.append(tgt)
        for b in W:
            if b.lw is not None and b.lw[0] is not tgt.dsem:
                self._need(q, waits, b.lw)
            for r in b.rd.items():
                self._need(q, waits, r)
        for s, v in waits.items():
            self.seen[q][s] = v
        tgt.dval += 16
        self.streams[q].append((list(waits.items()), fn, tgt.dsem, 16))
        for b in W:
            b.lw = (tgt.dsem, tgt.dval)
            b.rd = {}
        for b in R:
            b.rd[tgt.dsem] = tgt.dval

    def barrier(self):
        deps = []
        for n in self.names:
            if self.cnt[n] > 0:
                deps.append((self.sems[n][-1], self.cnt[n]))
            for s in self.sems[n][:-1]:
                deps.append((s, EPOCH))
        for b in self.dbufs:
            deps.append((b.dsem, b.dval))
        for n in self.names:
            waits = {}
            for d in deps:
                self._need(n, waits, d)
            for s, v in waits.items():
                self.seen[n][s] = v
            if waits:
                self.streams[n].append((list(waits.items()), None, None, 0))
        for b in self.dbufs:
            if b.dval < 3000:
                self.free_dsems.append((b.dsem, b.dval))
            b.dsem = None
        self.dbufs = []

    def emit(self):
        with self.nc.Block() as block:
            def mk(name):
                def body(e):
                    for waits, fn, sem, inc in self.streams[name]:
                        for s, v in waits:
                            e.wait_ge(s, v)
                        if fn is not None:
                            fn(e).then_inc(sem, inc)
                return body
            block.tensor(mk("pe"))
            block.scalar(mk("act"))
            block.vector(mk("dve"))
            block.gpsimd(mk("pool"))
            block.sync(mk("sp"))


class Region:
    def __init__(self, big, lo, hi):
        self.big, self.lo, self.hi, self.cur = big, lo, hi, lo

    def f32(self, words, name="t"):
        a = self.cur
        self.cur += (words + 7) // 8 * 8
        assert self.cur <= self.hi, f"SBUF region overflow at {name}: {self.cur} > {self.hi}"
        return self.big[:, a:a + words], Buf(name)

    def bf(self, elems, name="t"):
        ap, b = self.f32((elems + 1) // 2, name)
        return ap.bitcast(BF16), b

    def sub(self):
        return Region(self.big, self.cur, self.hi)


class _Stop(Exception):
    pass


def build_program(dbg=False, stop=99):
    nc = bass.Bass("TRN2", target_bir_lowering=False)

    def din(name, shape):
        return nc.dram_tensor(name, shape, F32, kind="ExternalInput").ap()

    xin = din("xin", [2, NT * 128, D])
    small_d = din("small", [128, NS])
    ada_w = din("ada_w", [2, D, 3 * D])
    w_in0 = din("w_in0", [D, 3584])
    w_out0 = din("w_out0", [D, D])
    w_in1 = din("w_in1", [D, 1472])
    wq_b = din("wq_b", [256, 1536])
    wkv_b = din("wkv_b", [128, 2048])
    w_out1 = din("w_out1", [D, D])
    a_wsT = din("a_wsT", [128, 8, 128])
    out_d = nc.dram_tensor("out", [2, 2048, D], F32, kind="ExternalOutput").ap()
    scr = nc.dram_tensor("gate_scr", [6, D], F32, kind="Internal").ap()
    if dbg:
        dbg_h = nc.dram_tensor("dbg_h", [2, NT * 128, D], F32, kind="ExternalOutput").ap()

    with ExitStack() as st:
        S = Sched(nc, st)
        BIGW = 53000
        big = st.enter_context(nc.sbuf_tensor("big", [128, BIGW], F32))[:, :]
        PS = []
        for i in range(7):
            PS.append((st.enter_context(nc.psum_tensor(f"ps{i}", [128, 512], F32))[:, :], Buf(f"ps{i}", excl=True)))
        PT = st.enter_context(nc.psum_tensor("pt", [128, 1024], BF16))[:, :]
        b_PT = Buf("pt", excl=True)
        PTb = [b_PT, b_PT]

        def mm(out, lhsT, rhs, start, stop, R, W, skip=False):
            S.op("pe", lambda e: e.matmul(out, lhsT=lhsT, rhs=rhs, start=start, stop=stop, skip_group_check=skip), R=R, W=W)

        def tr(out, in_, ident, R, W):
            S.op("pe", lambda e: e.transpose(out, in_, ident), R=R, W=W)

        def act(out, in_, func, R, W, scale=1.0, bias=0.0):
            S.op("act", lambda e: e.activation(out=out, in_=in_, func=func, bias=bias, scale=scale), R=R, W=W)

        def tt(eng, out, in0, in1, op, R, W):
            S.op(eng, lambda e: e.tensor_tensor(out=out, in0=in0, in1=in1, op=op), R=R, W=W)

        def ts(eng, out, in0, s1, s2, op0, op1, R, W):
            if s2 is None:
                S.op(eng, lambda e: e.tensor_scalar(out=out, in0=in0, scalar1=s1, scalar2=None, op0=op0), R=R, W=W)
            else:
                S.op(eng, lambda e: e.tensor_scalar(out=out, in0=in0, scalar1=s1, scalar2=s2, op0=op0, op1=op1), R=R, W=W)

        def stt(eng, out, in0, scalar, in1, op0, op1, R, W):
            S.op(eng, lambda e: e.scalar_tensor_tensor(out=out, in0=in0, scalar=scalar, in1=in1, op0=op0, op1=op1), R=R, W=W)

        def cp(eng, out, in_, R, W):
            if eng == "act":
                S.op("act", lambda e: e.copy(out=out, in_=in_), R=R, W=W)
            else:
                S.op(eng, lambda e: e.tensor_copy(out=out, in_=in_), R=R, W=W)

        def recip(out, in_, R, W):
            S.op("dve", lambda e: e.reciprocal(out=out, in_=in_), R=R, W=W)

        def dma(q, out, in_, R, W):
            S.dma(q, lambda e: e.dma_start(out=out, in_=in_), R=R, W=W)

        top = Region(big, 0, BIGW)
        small, b_small = top.f32(NS, "small")
        h_all, _ = top.f32(NT * D, "h")
        h = h_all.rearrange("p (t f) -> p t f", t=NT)
        b_h = [Buf(f"h{t}") for t in range(NT)]
        G_l, b_Gl = top.f32(D, "G_l")
        G_c, b_Gc = top.f32(D, "G_c")
        identb, b_identb = top.bf(128, "identb")
        wsT_f, b_wsT = top.bf(8 * 128, "wsT")
        wsT = wsT_f.rearrange("p (g q) -> p g q", g=8)
        biasT, b_biasT = top.f32(512, "biasT")
        mod_f, b_mod = top.f32(2 * 72, "mod")
        mod = mod_f.rearrange("p (l f r) -> p l f r", l=2, r=3)
        acol_f, b_acol = top.f32(2 * 3 * 8, "acol")
        acol = acol_f.rearrange("p (l r f) -> p l r f", l=2, r=3)
        scT_f, b_scT = top.bf(24, "scT")
        scT = scT_f.rearrange("p (k r) -> p k r", r=3)
        lamt, b_lam = top.f32(8, "lam")
        subln8, b_subln8 = top.f32(128, "subln8")
        stats, b_stats = top.f32(32, "stats")
        identf = small[:, IDENT:IDENT + 128]
        phase0 = top.cur

        dma("sp", small, small_d[:, :], R=[], W=[b_small])
        dma("pool", identb, small_d[:, IDENT:IDENT + 128], R=[], W=[b_identb])
        dma("pool", wsT, a_wsT[:, :, :], R=[], W=[b_wsT])

        act(scT, small[:, COND:COND + 24].rearrange("p (k r) -> p k r", r=3), AF.Silu, R=[b_small], W=[b_scT])
        setup = Region(big, phase0, BIGW)
        wada = []
        for i in range(2):
            ap, b = setup.bf(8 * 512, f"wada{i}")
            wada.append((ap.rearrange("p (k n) -> p k n", k=8), b))
        pcs = 0
        for l in range(2):
            pm, b_pm = PS[l]
            for j in range(6):
                wt, b_wt = wada[pcs % 2]
                pcs += 1
                dma("pool", wt, ada_w[l, :, j * 512:(j + 1) * 512].rearrange("(k p) n -> p k n", p=128), R=[], W=[b_wt])
                for fl in range(4):
                    fc = j * 4 + fl
                    for kc in range(8):
                        mm(pm[:, fc * 3:fc * 3 + 3], wt[:, kc, fl * 128:(fl + 1) * 128], scT[:, kc, :],
                           kc == 0, kc == 7, R=[b_wt, b_scT], W=[b_pm])
            tt("dve", mod[:, l, :, :], pm[:, 0:72].rearrange("p (f r) -> p f r", r=3),
               small[:, ADAB + l * 24:ADAB + (l + 1) * 24].unsqueeze(2).to_broadcast([128, 24, 3]), ALU.add,
               R=[b_pm, b_small], W=[b_mod])
            for r in range(3):
                stt("dve", acol[:, l, r, :], mod[:, l, 8:16, r], 1.0, small[:, NW + l * 8:NW + (l + 1) * 8],
                    ALU.add, ALU.mult, R=[b_mod, b_small], W=[b_acol])
        lt, b_lt = setup.f32(128, "lamtmp")
        tt("dve", lt[:, 0:64], small[:, LQ1:LQ1 + 64], small[:, LK1:LK1 + 64], ALU.mult, R=[b_small], W=[b_lt])
        tt("dve", lt[:, 64:128], small[:, LQ2:LQ2 + 64], small[:, LK2:LK2 + 64], ALU.mult, R=[b_small], W=[b_lt])
        S.op("dve", lambda e: e.reduce_sum(out=lamt[:, 0:2], in_=lt.rearrange("p (a b) -> p a b", a=2),
                                           axis=mybir.AxisListType.X), R=[b_lt], W=[b_lam])
        act(lamt[:, 2:4], lamt[:, 0:2], AF.Exp, R=[b_lam], W=[b_lam])
        tt("dve", lamt[:, 4:5], lamt[:, 3:4], lamt[:, 2:3], ALU.subtract, R=[b_lam], W=[b_lam])
        ts("dve", lamt[:, 5:6], lamt[:, 4:5], -0.2, None, ALU.add, None, R=[b_lam], W=[b_lam])
        neglam = lamt[:, 5:6]
        ts("dve", subln8, small[:, SUBLN:SUBLN + 128], 0.8, None, ALU.mult, None, R=[b_small], W=[b_subln8])
        cp("dve", biasT.rearrange("p (g d) -> p g d", g=8), small[:, ABS:ABS + 8].unsqueeze(2).to_broadcast([128, 8, 64]),
           R=[b_small], W=[b_biasT])

        b_scr = Buf("gate_scr")
        for l in range(2):
            for r in range(3):
                S.dma("sp", (lambda l, r: lambda e: e.dma_start(
                    out=scr[l * 3 + r, :].rearrange("(f p) -> p f", p=128), in_=mod[:, l, 16:24, r],
                    allow_slow_non_contiguous=True))(l, r), R=[b_mod], W=[b_scr])

        def build_G(reg, G, b_G, l, r):
            dma("sp", G, scr[l * 3 + r:l * 3 + r + 1, :].to_broadcast([128, D]), R=[b_scr], W=[b_G])

        def rstd_of(reg_stats, src, b_src, n, eps, name="rs"):
            st_ap, b_st = reg_stats
            nch = (n + 511) // 512
            w = n // nch
            for c in range(nch):
                S.op("dve", (lambda c: lambda e: e.bn_stats(out=st_ap[:, c * 6:(c + 1) * 6], in_=src[:, c * w:(c + 1) * w]))(c),
                     R=[b_src], W=[b_st])
            S.op("dve", lambda e: e.bn_aggr(out=st_ap[:, 12:14], in_=st_ap[:, 0:6 * nch].rearrange("p (c s) -> p c s", s=6)),
                 R=[b_st], W=[b_st])
            stt("dve", st_ap[:, 14:15], st_ap[:, 12:13], st_ap[:, 12:13], st_ap[:, 13:14], ALU.mult, ALU.add, R=[b_st], W=[b_st])
            act(st_ap[:, 15:16], st_ap[:, 14:15], AF.Ln, R=[b_st], W=[b_st], bias=eps)
            act(st_ap[:, 15:16], st_ap[:, 15:16], AF.Exp, R=[b_st], W=[b_st], scale=-0.5)
            return st_ap[:, 15:16], st_ap[:, 12:13], b_st

        def ckpt(k, bi):
            if stop == k:
                S.barrier()
                if dbg:
                    b_dbg = Buf("dbgstop")
                    for t in range(NT):
                        dma("sp", dbg_h[bi, t * 128:(t + 1) * 128, :], h[:, t, :], R=[b_h[t]], W=[b_dbg])
                    S.barrier()
                raise _Stop()

        def make_xlT(src, b_src, l, r, xn, b_xn, st_pair, xlT_dst, b_xlT, alt):
            rstd, _, b_st = rstd_of(st_pair, src, b_src, 1024, 1e-6)
            act(xn, src, AF.Identity, R=[b_src, b_st], W=[b_xn], scale=rstd)
            ckpt(0.57, 0)
            for half in range(2):
                bp = PTb[half]
                if half == 1:
                    ckpt(0.596, 0)
                for q in range(4):
                    fc = half * 4 + q
                    tr(PT[:, fc * 128:(fc + 1) * 128], xn[:, fc * 128:(fc + 1) * 128], identb, R=[b_xn, b_identb], W=[bp])
                if half == 1:
                    ckpt(0.597, 0)
                ckpt(0.58, 0)
                for q in range(4):
                    fc = half * 4 + q
                    sc = acol[:, l, r, fc:fc + 1]
                    bi = mod[:, l, fc, r:r + 1]
                    if q == 1:
                        ckpt(0.59, 0)
                    if q == 2:
                        ckpt(0.595, 0)
                    if (fc + alt) % 2 == 0:
                        ts("dve", xlT_dst[:, fc, :], PT[:, fc * 128:(fc + 1) * 128], sc, bi, ALU.mult, ALU.add,
                           R=[bp, b_acol, b_mod], W=[b_xlT])
                    else:
                        act(xlT_dst[:, fc, :], PT[:, fc * 128:(fc + 1) * 128], AF.Identity, R=[bp, b_acol, b_mod], W=[b_xlT],
                            scale=sc, bias=bi)

        def proj(ps, b_ps, xlT_tile, b_xlT, W, b_W, c0, n, nk=8):
            for kc in range(nk):
                mm(ps[:, 0:n], xlT_tile[:, kc, :], W[:, kc, c0:c0 + n], kc == 0, kc == nk - 1, R=[b_xlT, b_W], W=[b_ps])

        def rope(reg_t, src, b_src, ng, t, dst, b_dst, eng2="pool"):
            (tc_, b_tc), (ts_, b_ts) = reg_t
            n = ng * 64
            cosb = small[:, COS + t * 64:COS + (t + 1) * 64]
            sinb = small[:, SIN + t * 64:SIN + (t + 1) * 64]
            sv = src.rearrange("p (g s t d) -> p g s t d", g=ng, s=2, t=2)
            tv = ts_[:, 0:n].rearrange("p (g s t d) -> p g s t d", g=ng, s=2, t=2)
            sn = sinb.rearrange("p (s t d) -> p s t d", s=2, t=2)
            tt("dve", tc_[:, 0:n].rearrange("p (g d) -> p g d", g=ng), src.rearrange("p (g d) -> p g d", g=ng),
               cosb.unsqueeze(1).to_broadcast([128, ng, 64]), ALU.mult, R=[b_src, b_small], W=[b_tc])
            for hf in range(2):
                tt("dve", tv[:, :, :, hf, :], sv[:, :, :, 1 - hf, :],
                   sn[:, :, hf, :].unsqueeze(1).to_broadcast([128, ng, 2, 16]), ALU.mult, R=[b_src, b_small], W=[b_ts])
            tt(eng2, dst, tc_[:, 0:n], ts_[:, 0:n], ALU.add, R=[b_tc, b_ts], W=[b_dst])

        def load_w(q, dst, b_dst, src2d, c0, n):
            dma(q, dst, src2d[:, c0:c0 + n].rearrange("(k p) n -> p k n", p=128), R=[], W=[b_dst])

        S.barrier()

        def ckpt(k, bi):
            if stop == k:
                S.barrier()
                if dbg:
                    b_dbg = Buf("dbgstop")
                    for t in range(NT):
                        dma("sp", dbg_h[bi, t * 128:(t + 1) * 128, :], h[:, t, :], R=[b_h[t]], W=[b_dbg])
                    S.barrier()
                raise _Stop()

        try:
          ckpt(0, 0)
          for bi in range(2):
              L0 = Region(big, phase0, BIGW)
              wKV_f, b_wKV = Region(big, BIGW - 4096, BIGW).bf(8 * 1024, "wKV")
              wKV = wKV_f.rearrange("p (k n) -> p k n", k=8)
              wQZ_f, b_wQZ = L0.bf(8 * 1024, "wQZ")
              wQZ = wQZ_f.rearrange("p (k n) -> p k n", k=8)
              woB_f, b_woB = L0.bf(4 * 1024, "woB")
              woB = woB_f.rearrange("p (k n) -> p k n", k=4)
              build_G(L0.sub(), G_l, b_Gl, 0, bi)
              build_G(L0.sub(), G_c, b_Gc, 0, 2)
              ckpt(0.5, bi)

              A = Region(big, L0.cur, BIGW - 4096)
              wA_f, b_wA = A.bf(8 * 1536, "wA")
              wA = wA_f.rearrange("p (k n) -> p k n", k=8)
              woA_f, b_woA = A.bf(4 * 1024, "woA")
              woA = woA_f.rearrange("p (k n) -> p k n", k=4)
              load_w("pool", wA, b_wA, w_in0, 0, 1536)
              dma("pool", woA, w_out0[0:512, :].rearrange("(k p) n -> p k n", p=128), R=[], W=[b_woA])
              xs = [A.f32(1024, f"xs{i}") for i in range(2)]
              xn, b_xn = A.bf(1024, "xn")
              xlTs = []
              for i in range(2):
                  ap, b = A.bf(8 * 128, f"xlT{i}")
                  xlTs.append((ap.rearrange("p (k n) -> p k n", k=8), b))
              stp = A.f32(16, "st")
              stp2 = A.f32(16, "st2")
              gu, b_gu = A.f32(512, "gu")
              gv, b_gv = A.f32(512, "gv")
              sz, b_sz = A.f32(512, "sz")
              xh, b_xh = A.f32(512, "xh")
              t1, b_t1 = xh, b_xh
              vn, b_vn = A.bf(512, "vn")
              mixa, b_mixa = A.bf(512, "mixa")
              mixT_f, b_mixT = A.bf(4 * 128, "mixT")
              mixT = mixT_f.rearrange("p (k n) -> p k n", k=4)
              tmpo, b_tmpo = A.f32(512, "tmpo")
              load_w("pool", wKV, b_wKV, w_in0, 2048, 1024)
              for t in range(NT):
                  r = 2 if t < 2 else bi
                  G, b_G = (G_c, b_Gc) if t < 2 else (G_l, b_Gl)
                  x_t, b_x = xs[t % 2]
                  xlT, b_xlT = xlTs[t % 2]
                  dma("sp", x_t, xin[bi, t * 128:(t + 1) * 128, :], R=[], W=[b_x])
                  ckpt(0.55, bi)
                  make_xlT(x_t, b_x, 0, r, xn, b_xn, stp, xlT, b_xlT, t)
                  ckpt(0.6, bi)
                  for i in range(3):
                      proj(PS[i][0], PS[i][1], xlT, b_xlT, wA, b_wA, i * 512, 512)
                  ckpt(0.65, bi)
                  act(gu, PS[0][0], AF.Gelu, R=[PS[0][1]], W=[b_gu])
                  act(gv, PS[1][0], AF.Gelu, R=[PS[1][1]], W=[b_gv])
                  act(sz, PS[2][0], AF.Silu, R=[PS[2][1]], W=[b_sz])
                  st2, b_st2 = stp2
                  S.op("dve", (lambda st2, gv: lambda e: e.bn_stats(out=st2[:, 0:6], in_=gv))(st2, gv), R=[b_gv], W=[b_st2])
                  S.op("dve", (lambda st2: lambda e: e.bn_aggr(out=st2[:, 12:14], in_=st2[:, 0:6]))(st2), R=[b_st2], W=[b_st2])
                  act(st2[:, 15:16], st2[:, 13:14], AF.Ln, R=[b_st2], W=[b_st2], bias=1e-5)
                  act(st2[:, 15:16], st2[:, 15:16], AF.Exp, R=[b_st2], W=[b_st2], scale=-0.5)
                  ts("dve", xh, gv, st2[:, 12:13], st2[:, 15:16], ALU.subtract, ALU.mult, R=[b_gv, b_st2], W=[b_xh])
                  tt("pool", xh, xh, small[:, LNW:LNW + 512], ALU.mult, R=[b_xh, b_small], W=[b_xh])
                  tt("pool", vn, xh, small[:, LNB:LNB + 512], ALU.add, R=[b_xh, b_small], W=[b_vn])
                  tt("pool", gu, gu, sz, ALU.mult, R=[b_gu, b_sz], W=[b_gu])
                  psg, b_psg = PS[3]
                  for g in range(8):
                      mm(psg[:, g * 64:(g + 1) * 64], wsT[:, g, :], vn[:, g * 64:(g + 1) * 64], True, True,
                         R=[b_wsT, b_vn], W=[b_psg])
                  tt("dve", t1, psg, biasT, ALU.add, R=[b_psg, b_biasT], W=[b_t1])
                  tt("dve", mixa, t1, gu, ALU.mult, R=[b_t1, b_gu], W=[b_mixa])
                  for q in range(4):
                      tr(PT[:, q * 128:(q + 1) * 128], mixa[:, q * 128:(q + 1) * 128], identb, R=[b_mixa, b_identb], W=[PTb[0]])
                  cp("act", mixT, PT[:, 0:512].rearrange("p (k n) -> p k n", k=4), R=[PTb[0]], W=[b_mixT])
                  for n in range(2):
                      po, b_po = PS[4 + n]
                      for kc in range(4):
                          mm(po, mixT[:, kc, :], woA[:, kc, n * 512:(n + 1) * 512], kc == 0, kc == 3, R=[b_mixT, b_woA], W=[b_po])
                      tt("dve", tmpo, po, G[:, n * 512:(n + 1) * 512], ALU.mult, R=[b_po, b_G], W=[b_tmpo])
                      tt("pool", h[:, t, n * 512:(n + 1) * 512], tmpo, x_t[:, n * 512:(n + 1) * 512], ALU.add,
                         R=[b_tmpo, b_x], W=[b_h[t]])
                  ckpt(0.7, bi)
              S.barrier()
              ckpt(1, bi)

              KV = L0.sub()
              KT_f, b_KT = KV.bf(4 * 2304, "KT")
              KT = KT_f.rearrange("p (h n) -> p h n", h=4)
              V_f, b_V = KV.bf(NT * 4 * 132, "V")
              V = V_f.rearrange("p (t h d) -> p t h d", t=NT, h=4)
              K2 = Region(big, KV.cur, BIGW - 4096)
              xs = [K2.f32(1024, f"xs{i}") for i in range(2)]
              xn, b_xn = K2.bf(1024, "xn")
              xlTs = []
              for i in range(2):
                  ap, b = K2.bf(8 * 128, f"xlT{i}")
                  xlTs.append((ap.rearrange("p (k n) -> p k n", k=8), b))
              stp = K2.f32(16, "st")
              rt = (K2.f32(512, "tc"), K2.f32(512, "ts"))
              kr, b_kr = K2.bf(512, "kr")
              load_w("pool", wQZ[:, :, 0:512], b_wQZ, w_in0, 1536, 512)
              load_w("pool", wQZ[:, :, 512:1024], b_wQZ, w_in0, 3072, 512)
              dma("pool", woB, w_out0[512:1024, :].rearrange("(k p) n -> p k n", p=128), R=[], W=[b_woB])
              S.op("dve", (lambda V: lambda e: e.memset(V[:, :, :, 128:129], 1.0))(V), R=[], W=[b_V])
              for t in range(NT):
                  r = 2 if t < 2 else bi
                  x_t, b_x = xs[t % 2]
                  xlT, b_xlT = xlTs[t % 2]
                  dma("sp", x_t, xin[bi, t * 128:(t + 1) * 128, :], R=[], W=[b_x])
                  make_xlT(x_t, b_x, 0, r, xn, b_xn, stp, xlT, b_xlT, t)
                  proj(PS[0][0], PS[0][1], xlT, b_xlT, wKV, b_wKV, 0, 512)
                  proj(PS[1][0], PS[1][1], xlT, b_xlT, wKV, b_wKV, 512, 512)
                  rope(rt, PS[0][0], PS[0][1], 8, t, kr, b_kr)
                  for hh in range(4):
                      tr(PT[:, hh * 128:(hh + 1) * 128], kr[:, hh * 128:(hh + 1) * 128], identb, R=[b_kr, b_identb], W=[PTb[0]])
                  cp("act", KT[:, :, t * 128:(t + 1) * 128], PT[:, 0:512].rearrange("p (h n) -> p h n", h=4), R=[PTb[0]], W=[b_KT])
                  cp("act", V[:, t, :, 0:128], PS[1][0].rearrange("p (h d) -> p h d", h=4), R=[PS[1][1]], W=[b_V])
              S.barrier()
              ckpt(2, bi)

              Bp = KV.sub()
              xs = [Bp.f32(1024, f"xs{i}") for i in range(2)]
              xn, b_xn = Bp.bf(1024, "xn")
              xlT_f, b_xlT = Bp.bf(8 * 256, "xlT")
              xlT = xlT_f.rearrange("p (k n) -> p k n", k=8)
              stp = Bp.f32(16, "st")
              stp3 = Bp.f32(16, "st3")
              rt = (Bp.f32(512, "tc"), Bp.f32(512, "ts"))
              qr, b_qr = Bp.bf(512, "qr")
              QT_f, b_QT = Bp.bf(4 * 256, "QT")
              QT = QT_f.rearrange("p (h n) -> p h n", h=4)
              NPT = 3
              pts = [Bp.bf(512, f"pT{i}") for i in range(NPT)]
              oh, b_oh = Bp.f32(128, "oh")
              rr, b_rr = Bp.f32(8, "rr")
              omix_f, b_omix = Bp.f32(2 * 512, "omix")
              omix = omix_f.rearrange("p (j f) -> p j f", j=2)
              sbz, b_sbz = Bp.f32(512, "sbz")
              mixb_f, b_mixb = Bp.bf(2 * 512, "mixb")
              mixb = mixb_f.rearrange("p (j f) -> p j f", j=2)
              mixT_f, b_mixT = Bp.bf(4 * 256, "mixT")
              mixT = mixT_f.rearrange("p (k n) -> p k n", k=4)
              tmpo, b_tmpo = Bp.f32(512, "tmpo")
              ptc = 0
              for s in range(9):
                  tiles = [2 * s, 2 * s + 1]
                  r = 2 if s == 0 else bi
                  G, b_G = (G_c, b_Gc) if s == 0 else (G_l, b_Gl)
                  nch = 2 if s == 0 else NT
                  for j, t in enumerate(tiles):
                      x_t, b_x = xs[j]
                      dma("sp", x_t, xin[bi, t * 128:(t + 1) * 128, :], R=[], W=[b_x])
                      make_xlT(x_t, b_x, 0, r, xn, b_xn, stp, xlT[:, :, j * 128:(j + 1) * 128], b_xlT, j)
                  for j, t in enumerate(tiles):
                      proj(PS[5][0], PS[5][1], xlT[:, :, j * 128:(j + 1) * 128], b_xlT, wQZ, b_wQZ, 0, 512)
                      rope(rt, PS[5][0], PS[5][1], 8, t, qr, b_qr)
                      for hh in range(4):
                          tr(PT[:, hh * 128:(hh + 1) * 128], qr[:, hh * 128:(hh + 1) * 128], identb, R=[b_qr, b_identb], W=[PTb[0]])
                      cp("act", QT[:, :, j * 128:(j + 1) * 128], PT[:, 0:512].rearrange("p (h n) -> p h n", h=4), R=[PTb[0]], W=[b_QT])
                  for hh in range(4):
                      for m in range(2):
                          po, b_po = PS[3 + m]
                          for cpi in range(nch // 2):
                              psc, b_psc = PS[cpi % 3]
                              for cc in range(2):
                                  c = 2 * cpi + cc
                                  mm(psc[:, cc * 256:(cc + 1) * 256], KT[m * 64:(m + 1) * 64, hh, c * 128:(c + 1) * 128],
                                     QT[m * 64:(m + 1) * 64, hh, :], True, True, R=[b_KT, b_QT], W=[b_psc])
                              pT, b_pT = pts[ptc % NPT]
                              ptc += 1
                              act(pT, psc, AF.Exp, R=[b_psc], W=[b_pT], scale=0.125)
                              for cc in range(2):
                                  c = 2 * cpi + cc
                                  for j in range(2):
                                      mm(po[:, j * 132:j * 132 + 129], pT[:, cc * 256 + j * 128:cc * 256 + (j + 1) * 128],
                                         V[:, c, hh, 0:129], c == 0 and j == 0, c == nch - 1, R=[b_pT, b_V], W=[b_po], skip=True)
                      p0, b_p0 = PS[3]
                      p1, b_p1 = PS[4]
                      for j in range(2):
                          recip(rr[:, 0:1], p0[:, j * 132 + 128:j * 132 + 129], R=[b_p0], W=[b_rr])
                          recip(rr[:, 1:2], p1[:, j * 132 + 128:j * 132 + 129], R=[b_p1], W=[b_rr])
                          tt("dve", rr[:, 2:3], rr[:, 1:2], neglam, ALU.mult, R=[b_rr, b_lam], W=[b_rr])
                          ts("dve", oh, p0[:, j * 132:j * 132 + 128], rr[:, 0:1], None, ALU.mult, None, R=[b_p0, b_rr], W=[b_oh])
                          stt("dve", oh, p1[:, j * 132:j * 132 + 128], rr[:, 2:3], oh, ALU.mult, ALU.add, R=[b_p1, b_rr, b_oh], W=[b_oh])
                          rstd, _, b_st3 = rstd_of(stp3, oh, b_oh, 128, 1e-5)
                          stt("dve", omix[:, j, hh * 128:(hh + 1) * 128], oh, rstd, subln8, ALU.mult, ALU.mult,
                              R=[b_oh, b_st3, b_subln8], W=[b_omix])
                  for j, t in enumerate(tiles):
                      proj(PS[5][0], PS[5][1], xlT[:, :, j * 128:(j + 1) * 128], b_xlT, wQZ, b_wQZ, 512, 512)
                      act(sbz, PS[5][0], AF.Silu, R=[PS[5][1]], W=[b_sbz])
                      tt("dve", mixb[:, j, :], omix[:, j, :], sbz, ALU.mult, R=[b_omix, b_sbz], W=[b_mixb])
                      for q in range(4):
                          tr(PT[:, 512 + q * 128:512 + (q + 1) * 128], mixb[:, j, q * 128:(q + 1) * 128], identb,
                             R=[b_mixb, b_identb], W=[PTb[1]])
                      cp("act", mixT[:, :, j * 128:(j + 1) * 128], PT[:, 512:1024].rearrange("p (k n) -> p k n", k=4),
                         R=[PTb[1]], W=[b_mixT])
                  for j, t in enumerate(tiles):
                      for n in range(2):
                          po, b_po = PS[5 + n]
                          for kc in range(4):
                              mm(po, mixT[:, kc, j * 128:(j + 1) * 128], woB[:, kc, n * 512:(n + 1) * 512], kc == 0, kc == 3,
                                 R=[b_mixT, b_woB], W=[b_po])
                          tt("dve", tmpo, po, G[:, n * 512:(n + 1) * 512], ALU.mult, R=[b_po, b_G], W=[b_tmpo])
                          tt("pool", h[:, t, n * 512:(n + 1) * 512], tmpo, h[:, t, n * 512:(n + 1) * 512], ALU.add,
                             R=[b_tmpo, b_h[t]], W=[b_h[t]])
              S.barrier()
              ckpt(3, bi)
              if dbg:
                  b_dbg = Buf(f"dbg{bi}")
                  for t in range(NT):
                      dma("sp", dbg_h[bi, t * 128:(t + 1) * 128, :], h[:, t, :], R=[b_h[t]], W=[b_dbg])
                  S.barrier()

              L1 = Region(big, phase0, BIGW)
              build_G(L1.sub(), G_l, b_Gl, 1, bi)
              w1_f, b_w1 = L1.bf(8 * 1472, "w_in1")
              w1 = w1_f.rearrange("p (k n) -> p k n", k=8)
              wqn_f, b_wqn = L1.bf(2 * 8 * 128, "wqn")
              wqn = wqn_f.rearrange("p (k h d) -> p k h d", k=2, h=8)
              wqr_f, b_wqr = L1.bf(2 * 8 * 64, "wqr")
              wqr = wqr_f.rearrange("p (k h d) -> p k h d", k=2, h=8)
              wkv_f, b_wkv = L1.bf(2048, "wkv")
              wkv = wkv_f.rearrange("p (h d) -> p h d", h=8)
              WkT_f, b_WkT = L1.bf(8 * 128, "WkT")
              WkT = WkT_f.rearrange("p (h c) -> p h c", h=8)
              wo1_f, b_wo1 = L1.bf(8 * 1024, "wo1")
              wo1 = wo1_f.rearrange("p (k n) -> p k n", k=8)
              ckvT, b_ckvT = L1.bf(2304, "ckvT")
              krT, b_krT = L1.bf(2304, "krT")
              VS_f, b_VS = L1.bf(NT * 132, "VS")
              VS = VS_f.rearrange("p (t d) -> p t d", t=NT)
              load_w("pool", w1, b_w1, w_in1, 0, 1472)
              wq3 = wq_b.rearrange("(k p) (h d) -> p k h d", p=128, h=8)
              for kc in range(2):
                  dma("pool", wqn[:, kc, :, :], wq3[:, kc, :, 0:128], R=[], W=[b_wqn])
                  dma("pool", wqr[:, kc, :, :], wq3[:, kc, :, 128:192], R=[], W=[b_wqr])
              dma("pool", wkv, wkv_b.rearrange("p (h d) -> p h d", h=8), R=[], W=[b_wkv])
              dma("pool", wo1, w_out1[:, :].rearrange("(k p) n -> p k n", p=128), R=[], W=[b_wo1])
              for hh in range(8):
                  tr(PT[:, hh * 128:(hh + 1) * 128], wkv[:, hh, 0:128], identb, R=[b_wkv, b_identb], W=[PTb[hh // 4]])
              cp("dve", WkT[:, 0:4, :], PT[:, 0:512].rearrange("p (h c) -> p h c", h=4), R=[PTb[0]], W=[b_WkT])
              cp("dve", WkT[:, 4:8, :], PT[:, 512:1024].rearrange("p (h c) -> p h c", h=4), R=[PTb[1]], W=[b_WkT])

              K1 = L1.sub()
              xn, b_xn = K1.bf(1024, "xn")
              xlTs = []
              for i in range(2):
                  ap, b = K1.bf(8 * 128, f"xlT{i}")
                  xlTs.append((ap.rearrange("p (k n) -> p k n", k=8), b))
              stp = K1.f32(16, "st")
              stp3 = K1.f32(16, "st3")
              rt = (K1.f32(64, "tc"), K1.f32(64, "ts"))
              krd, b_krd = K1.bf(128, "krd")
              S.op("dve", (lambda VS: lambda e: e.memset(VS[:, :, 128:129], 1.0))(VS), R=[], W=[b_VS])
              for t in range(NT):
                  r = 2 if t < 2 else bi
                  xlT, b_xlT = xlTs[t % 2]
                  make_xlT(h[:, t, :], b_h[t], 1, r, xn, b_xn, stp, xlT, b_xlT, t)
                  pk, b_pk = PS[t % 2]
                  proj(pk, b_pk, xlT, b_xlT, w1, b_w1, 256, 192)
                  rstd, _, b_st3 = rstd_of(stp3, pk[:, 0:128], b_pk, 128, 1e-6)
                  stt("dve", VS[:, t, 0:128], pk[:, 0:128], rstd, small[:, KVN:KVN + 128], ALU.mult, ALU.mult,
                      R=[b_pk, b_st3, b_small], W=[b_VS])
                  rope(rt, pk[:, 128:192], b_pk, 1, t, krd[:, 0:64], b_krd, eng2="dve")
                  cp("dve", krd[:, 64:128], krd[:, 0:64], R=[b_krd], W=[b_krd])
                  tr(PT[:, 0:128], VS[:, t, 0:128], identb, R=[b_VS, b_identb], W=[PTb[0]])
                  tr(PT[:, 128:256], krd, identb, R=[b_krd, b_identb], W=[PTb[0]])
                  cp("act", ckvT[:, t * 128:(t + 1) * 128], PT[:, 0:128], R=[PTb[0]], W=[b_ckvT])
                  cp("act", krT[:, t * 128:(t + 1) * 128], PT[:, 128:256], R=[PTb[0]], W=[b_krT])
              S.barrier()
              ckpt(4, bi)

              Q1 = L1.sub()
              xn, b_xn = Q1.bf(1024, "xn")
              xlT_f, b_xlT = Q1.bf(8 * 256, "xlT/mixT")
              xlT = xlT_f.rearrange("p (k n) -> p k n", k=8)
              mixT, b_mixT = xlT, b_xlT
              stp = Q1.f32(16, "st")
              stp3 = Q1.f32(16, "st3")
              cqn, b_cqn = Q1.bf(256, "cqn")
              cqnT_f, b_cqnT = Q1.bf(2 * 256, "cqnT")
              cqnT = cqnT_f.rearrange("p (k n) -> p k n", k=2)
              qnTs = [Q1.bf(256, f"qnT{i}") for i in range(2)]
              qpT_f, b_qpT = Q1.bf(8 * 256, "qpT/onT")
              qpT = qpT_f.rearrange("p (h n) -> p h n", h=8)
              onT, b_onT = qpT, b_qpT
              rt = (Q1.f32(512, "tc"), Q1.f32(512, "ts"))
              qr, b_qr = Q1.bf(512, "qr")
              qrT_f, b_qrT = Q1.bf(4 * 256, "qrT")
              qrT = qrT_f.rearrange("p (g n) -> p g n", g=4)
              pts = [Q1.bf(512, f"pT{i}") for i in range(NPT)]
              rr, b_rr = Q1.f32(8, "rr")
              on_f, b_on = Q1.bf(2 * 1024, "on")
              on = on_f.rearrange("p (j f) -> p j f", j=2)
              szz, b_szz = Q1.f32(512, "sz")
              mix_f, b_mix = Q1.bf(2 * 1024, "mix")
              mix = mix_f.rearrange("p (j f) -> p j f", j=2)
              tmpo, b_tmpo = Q1.f32(512, "tmpo")
              ybuf = [(h[:, i, :], Buf(f"y{i}")) for i in range(2)]
              b_outd = [Buf(f"outd{bi}{i}") for i in range(2)]
              sc1 = 1.0 / math.sqrt(192.0)
              ptc = 0
              yc = 0
              for s in range(1, 9):
                  tiles = [2 * s, 2 * s + 1]
                  for j, t in enumerate(tiles):
                      make_xlT(h[:, t, :], b_h[t], 1, bi, xn, b_xn, stp, xlT[:, :, j * 128:(j + 1) * 128], b_xlT, j)
                  for j, t in enumerate(tiles):
                      pq, b_pq = PS[5]
                      proj(pq, b_pq, xlT[:, :, j * 128:(j + 1) * 128], b_xlT, w1, b_w1, 0, 256)
                      rstd, _, b_st3 = rstd_of(stp3, pq[:, 0:256], b_pq, 256, 1e-6)
                      stt("dve", cqn, pq[:, 0:256], rstd, small[:, QN:QN + 256], ALU.mult, ALU.mult,
                          R=[b_pq, b_st3, b_small], W=[b_cqn])
                      for kc in range(2):
                          tr(PT[:, kc * 128:(kc + 1) * 128], cqn[:, kc * 128:(kc + 1) * 128], identb, R=[b_cqn, b_identb], W=[PTb[0]])
                      cp("act", cqnT[:, :, j * 128:(j + 1) * 128], PT[:, 0:256].rearrange("p (k n) -> p k n", k=2),
                         R=[PTb[0]], W=[b_cqnT])
                  for j, t in enumerate(tiles):
                      pq, b_pq = PS[5]
                      for kc in range(2):
                          mm(pq, cqnT[:, kc, j * 128:(j + 1) * 128], wqr[:, kc, :, :], kc == 0, kc == 1, R=[b_cqnT, b_wqr], W=[b_pq])
                      rope(rt, pq, b_pq, 8, t, qr, b_qr)
                      for g in range(4):
                          tr(PT[:, 512 + g * 128:512 + (g + 1) * 128], qr[:, g * 128:(g + 1) * 128], identb,
                             R=[b_qr, b_identb], W=[PTb[1]])
                      cp("act", qrT[:, :, j * 128:(j + 1) * 128], PT[:, 512:1024].rearrange("p (g n) -> p g n", g=4),
                         R=[PTb[1]], W=[b_qrT])
                  for hh in range(8):
                      pq, b_pq = PS[5 + hh % 2]
                      qnT, b_qnT = qnTs[hh % 2]
                      for kc in range(2):
                          mm(pq[:, 0:256], wqn[:, kc, hh, :], cqnT[:, kc, :], kc == 0, kc == 1, R=[b_wqn, b_cqnT], W=[b_pq])
                      cp("dve", qnT, pq[:, 0:256], R=[b_pq], W=[b_qnT])
                      mm(pq[:, 256:512], WkT[:, hh, :], qnT, True, True, R=[b_WkT, b_qnT], W=[b_pq])
                      cp("dve", qpT[:, hh, :], pq[:, 256:512], R=[b_pq], W=[b_qpT])
                  for hh in range(8):
                      po, b_po = PS[3 + hh % 2]
                      hp = hh % 2
                      for cpi in range(NT // 2):
                          psc, b_psc = PS[cpi % 3]
                          for cc in range(2):
                              c = 2 * cpi + cc
                              mm(psc[:, cc * 256:(cc + 1) * 256], ckvT[:, c * 128:(c + 1) * 128], qpT[:, hh, :], True, False,
                                 R=[b_ckvT, b_qpT], W=[b_psc])
                              mm(psc[:, cc * 256:(cc + 1) * 256], krT[hp * 64:(hp + 1) * 64, c * 128:(c + 1) * 128],
                                 qrT[hp * 64:(hp + 1) * 64, hh // 2, :], False, True, R=[b_krT, b_qrT], W=[b_psc])
                          pT, b_pT = pts[ptc % NPT]
                          ptc += 1
                          act(pT, psc, AF.Exp, R=[b_psc], W=[b_pT], scale=sc1)
                          for cc in range(2):
                              c = 2 * cpi + cc
                              for j in range(2):
                                  mm(po[:, j * 132:j * 132 + 129], pT[:, cc * 256 + j * 128:cc * 256 + (j + 1) * 128],
                                     VS[:, c, 0:129], c == 0 and j == 0, c == NT - 1, R=[b_pT, b_VS], W=[b_po], skip=True)
                      for j in range(2):
                          recip(rr[:, j:j + 1], po[:, j * 132 + 128:j * 132 + 129], R=[b_po], W=[b_rr])
                          ts("dve", on[:, j, hh * 128:(hh + 1) * 128], po[:, j * 132:j * 132 + 128], rr[:, j:j + 1], None,
                             ALU.mult, None, R=[b_po, b_rr], W=[b_on])
                  for j, t in enumerate(tiles):
                      for hh in range(8):
                          tr(PT[:, hh * 128:(hh + 1) * 128], on[:, j, hh * 128:(hh + 1) * 128], identb, R=[b_on, b_identb],
                             W=[PTb[hh // 4]])
                      cp("act", onT[:, 0:4, j * 128:(j + 1) * 128], PT[:, 0:512].rearrange("p (h n) -> p h n", h=4),
                         R=[PTb[0]], W=[b_onT])
                      cp("dve", onT[:, 4:8, j * 128:(j + 1) * 128], PT[:, 512:1024].rearrange("p (h n) -> p h n", h=4),
                         R=[PTb[1]], W=[b_onT])
                  for j, t in enumerate(tiles):
                      for n in range(2):
                          pe_, b_pe = PS[5]
                          pz, b_pz = PS[6]
                          for q in range(4):
                              hh = n * 4 + q
                              mm(pe_[:, q * 128:(q + 1) * 128], onT[:, hh, j * 128:(j + 1) * 128], wkv[:, hh, 128:256], True, True,
                                 R=[b_onT, b_wkv], W=[b_pe])
                          proj(pz, b_pz, xlT[:, :, j * 128:(j + 1) * 128], b_xlT, w1, b_w1, 448 + n * 512, 512)
                          act(szz, pz, AF.Silu, R=[b_pz], W=[b_szz])
                          tt("dve", mix[:, j, n * 512:(n + 1) * 512], pe_, szz, ALU.mult, R=[b_pe, b_szz], W=[b_mix])
                  for j, t in enumerate(tiles):
                      for kc in range(8):
                          tr(PT[:, kc * 128:(kc + 1) * 128], mix[:, j, kc * 128:(kc + 1) * 128], identb, R=[b_mix, b_identb],
                             W=[PTb[kc // 4]])
                      cp("act", mixT[:, 0:4, j * 128:(j + 1) * 128], PT[:, 0:512].rearrange("p (k n) -> p k n", k=4),
                         R=[PTb[0]], W=[b_mixT])
                      cp("dve", mixT[:, 4:8, j * 128:(j + 1) * 128], PT[:, 512:1024].rearrange("p (k n) -> p k n", k=4),
                         R=[PTb[1]], W=[b_mixT])
                  for j, t in enumerate(tiles):
                      for n in range(2):
                          po, b_po = PS[5 + n]
                          for kc in range(8):
                              mm(po, mixT[:, kc, j * 128:(j + 1) * 128], wo1[:, kc, n * 512:(n + 1) * 512], kc == 0, kc == 7,
                                 R=[b_mixT, b_wo1], W=[b_po])
                          tt("dve", tmpo, po, G_l[:, n * 512:(n + 1) * 512], ALU.mult, R=[b_po, b_Gl], W=[b_tmpo])
                          tt("pool", h[:, t, n * 512:(n + 1) * 512], tmpo, h[:, t, n * 512:(n + 1) * 512], ALU.add,
                             R=[b_tmpo, b_h[t]], W=[b_h[t]])
                      y, b_y = ybuf[yc % 2]
                      rstd, _, b_stf = rstd_of(stp, h[:, t, :], b_h[t], 1024, 1e-6)
                      stt("dve", y, h[:, t, :], rstd, small[:, FINAL:FINAL + 1024], ALU.mult, ALU.mult,
                          R=[b_h[t], b_stf, b_small], W=[b_y])
                      dma("sp", out_d[bi, (t - 2) * 128:(t - 1) * 128, :], y, R=[b_y], W=[b_outd[yc % 2]])
                      yc += 1
              S.barrier()
        except _Stop:
            pass
        S.emit()
        build_program.last_S = S
    return nc


_NC_CACHE = {}


def _rope_tables():
    seq, gw, dim = 2048, 64, 64
    rows = seq // gw
    row = np.repeat(np.arange(rows), gw).astype(np.float32)
    col = np.tile(np.arange(gw), rows).astype(np.float32)
    half = dim // 2
    inv = (np.float32(10000.0) ** (-np.arange(0, half, 2, dtype=np.float32) / np.float32(half))).astype(np.float32)
    ang_r = row[:, None] * inv[None, :]
    ang_c = col[:, None] * inv[None, :]
    ang = np.concatenate([ang_r, ang_r, ang_c, ang_c], axis=-1).astype(np.float32)
    cos = np.cos(ang).astype(np.float32)
    sin = np.sin(ang).astype(np.float32)
    sgn = np.tile(np.concatenate([-np.ones(16), np.ones(16)]), 2).astype(np.float32)
    cos_all = np.concatenate([np.ones((256, 64), np.float32), cos], 0)
    sin_all = np.concatenate([np.zeros((256, 64), np.float32), sin * sgn[None, :]], 0)
    cos_t = cos_all.reshape(NT, 128, 64).transpose(1, 0, 2).reshape(128, NT * 64)
    sin_t = sin_all.reshape(NT, 128, 64).transpose(1, 0, 2).reshape(128, NT * 64)
    return cos_t, sin_t


def _pack_small(core, c, c_ctx, norm_w, ada_b, a_bs, a_ln_w, a_ln_b, lq1, lk1, lq2, lk2, subln, qn, kvn, final_w):
    sm = np.zeros((128, NS), np.float32)
    rep = lambda v: np.broadcast_to(np.asarray(v, np.float32)[None, :], (128, v.shape[0]))
    sm[:, LNW:LNW + 512] = rep(a_ln_w[0])
    sm[:, LNB:LNB + 512] = rep(a_ln_b[0])
    sm[:, SUBLN:SUBLN + 128] = rep(subln[0])
    sm[:, QN:QN + 256] = rep(qn[0])
    sm[:, KVN:KVN + 128] = rep(kvn[0])
    sm[:, FINAL:FINAL + 1024] = rep(final_w)
    sm[:, LQ1:LQ1 + 64] = rep(lq1[0])
    sm[:, LK1:LK1 + 64] = rep(lk1[0])
    sm[:, LQ2:LQ2 + 64] = rep(lq2[0])
    sm[:, LK2:LK2 + 64] = rep(lk2[0])
    sm[:, ABS:ABS + 8] = a_bs[0].T
    sm[:, NW:NW + 16] = norm_w.reshape(2, 8, 128).transpose(2, 0, 1).reshape(128, 16)
    sm[:, ADAB:ADAB + 48] = ada_b.reshape(2, 24, 128).transpose(2, 0, 1).reshape(128, 48)
    cond = np.stack([c[2 * core], c[2 * core + 1], c_ctx], 0)
    sm[:, COND:COND + 24] = cond.reshape(3, 8, 128).transpose(2, 1, 0).reshape(128, 24)
    cos_t, sin_t = _rope_tables()
    sm[:, COS:COS + NT * 64] = cos_t
    sm[:, SIN:SIN + NT * 64] = sin_t
    sm[:, IDENT:IDENT + 128] = np.eye(128, dtype=np.float32)
    return sm


def kernel(x, c, ctx, c_ctx, norm_w, ada_w, ada_b, even_w_in, a_ws, a_bs, a_ln_w, a_ln_b,
           b_lq1, b_lk1, b_lq2, b_lk2, b_subln_w, even_w_out, odd_w_in, c_q_norm_w, c_wq_b,
           c_kv_norm_w, c_wkv_b, odd_w_out, final_w, _dbg=False, _stop=99, _ncores=8):
    f = lambda a: np.ascontiguousarray(np.asarray(a, dtype=np.float32))
    x, c, ctx, c_ctx = f(x), f(c), f(ctx), f(c_ctx)
    key = (bool(_dbg), _stop)
    if key not in _NC_CACHE:
        _NC_CACHE[key] = build_program(dbg=bool(_dbg), stop=_stop)
    nc = _NC_CACHE[key]
    shared = {
        "ada_w": f(ada_w), "w_in0": f(even_w_in)[0], "w_out0": f(even_w_out)[0], "w_in1": f(odd_w_in)[0],
        "wq_b": f(c_wq_b)[0], "wkv_b": f(c_wkv_b)[0], "w_out1": f(odd_w_out)[0],
        "a_wsT": np.ascontiguousarray(f(a_ws)[0].transpose(2, 0, 1)),
    }
    in_maps = []
    for core in range(_ncores):
        xin = np.concatenate([ctx[2 * core:2 * core + 2], x[2 * core:2 * core + 2]], axis=1)
        sm = _pack_small(core, c, c_ctx, f(norm_w), f(ada_b), f(a_bs), f(a_ln_w), f(a_ln_b), f(b_lq1), f(b_lk1),
                         f(b_lq2), f(b_lk2), f(b_subln_w), f(c_q_norm_w), f(c_kv_norm_w), f(final_w))
        m = {"xin": np.ascontiguousarray(xin), "small": sm}
        m.update(shared)
        in_maps.append(m)
    res = run_bass_kernel_spmd(nc, in_maps, core_ids=list(range(_ncores)))
    out = np.concatenate([r["out"] for r in res.results], axis=0).astype(np.float32)
    if _dbg:
        return out, np.concatenate([r["dbg_h"] for r in res.results], axis=0)
    return out
```

```python
import math
from contextlib import ExitStack
import numpy as np
import concourse.bass as bass
import concourse.mybir as mybir
from concourse.bass_utils import run_bass_kernel_spmd

F32 = mybir.dt.float32
BF16 = mybir.dt.bfloat16
AF = mybir.ActivationFunctionType
ALU = mybir.AluOpType

import os
XLT_DVE_ONLY = os.environ.get('XLT_DVE_ONLY', '0') == '1'
EPOCH = 3800
NT = 18
D = 1024

LNW, LNB, SUBLN, QN, KVN, FINAL = 0, 512, 1024, 1152, 1408, 1536
LQ1, LK1, LQ2, LK2 = 2560, 2624, 2688, 2752
ABS, NW, ADAB, COND, COS, SIN, IDENT = 2816, 2824, 2840, 2888, 2912, 4064, 5216
NS = 5344


class Buf:
    __slots__ = ("name", "lw", "rd", "dsem", "dval", "excl")

    def __init__(self, name, excl=False):
        self.name = name
        self.excl = excl
        self.lw = None
        self.rd = {}
        self.dsem = None
        self.dval = 0


class Sched:
    def __init__(self, nc, stack):
        self.nc = nc
        self.stack = stack
        self.names = ["pe", "act", "dve", "pool", "sp"]
        self.streams = {n: [] for n in self.names}
        self.sems = {n: [self._sem(f"s_{n}0")] for n in self.names}
        self.cnt = {n: 0 for n in self.names}
        self.seen = {n: {} for n in self.names}
        self.dbufs = []
        self.free_dsems = []

    def _sem(self, name):
        return self.stack.enter_context(self.nc.semaphore(name))

    def _cur(self, eng):
        if self.cnt[eng] >= EPOCH:
            self.sems[eng].append(self._sem(f"s_{eng}{len(self.sems[eng])}"))
            self.cnt[eng] = 0
        return self.sems[eng][-1]

    def _need(self, eng, waits, dep):
        if dep is None:
            return
        sem, val = dep
        if self.seen[eng].get(sem, 0) >= val:
            return
        if waits.get(sem, 0) < val:
            waits[sem] = val

    def op(self, eng, fn, R=(), W=()):
        if any(b.excl for b in R):
            W = list(W) + [b for b in R if b.excl and b not in W]
            R = [b for b in R if not b.excl]
        waits = {}
        own = set(id(s) for s in self.sems[eng])
        for b in R:
            self._need(eng, waits, b.lw)
        for b in W:
            if b.lw is not None and id(b.lw[0]) not in own:
                self._need(eng, waits, b.lw)
            for r in b.rd.items():
                if id(r[0]) in own:
                    continue
                self._need(eng, waits, r)
        sem = self._cur(eng)
        self.cnt[eng] += 1
        val = self.cnt[eng]
        for s, v in waits.items():
            self.seen[eng][s] = v
        self.streams[eng].append((list(waits.items()), fn, sem, 1))
        for b in W:
            b.lw = (sem, val)
            b.rd = {}
        for b in R:
            b.rd[sem] = val

    def dma(self, q, fn, R=(), W=()):
        waits = {}
        for b in R:
            self._need(q, waits, b.lw)
        tgt = W[0]
        if tgt.dsem is None:
            if self.free_dsems:
                tgt.dsem, tgt.dval = self.free_dsems.pop()
            else:
                tgt.dsem = self._sem(f"d{len(self.dbufs)}_{tgt.name}".replace("/", "_"))
            self.dbufs.append(tgt)
        for b in W:
            if b.lw is not None and b.lw[0] is not tgt.dsem:
                self._need(q, waits, b.lw)
            for r in b.rd.items():
                self._need(q, waits, r)
        for s, v in waits.items():
            self.seen[q][s] = v
        tgt.dval += 16
        self.streams[q].append((list(waits.items()), fn, tgt.dsem, 16))
        for b in W:
            b.lw = (tgt.dsem, tgt.dval)
            b.rd = {}
        for b in R:
            b.rd[tgt.dsem] = tgt.dval

    def barrier(self):
        deps = []
        for n in self.names:
            if self.cnt[n] > 0:
                deps.append((self.sems[n][-1], self.cnt[n]))
            for s in self.sems[n][:-1]:
                deps.append((s, EPOCH))
        for b in self.dbufs:
            deps.append((b.dsem, b.dval))
        for n in self.names:
            waits = {}
            for d in deps:
                self._need(n, waits, d)
            for s, v in waits.items():
                self.seen[n][s] = v
            if waits:
                self.streams[n].append((list(waits.items()), None, None, 0))
        for b in self.dbufs:
            if b.dval < 3000:
                self.free_dsems.append((b.dsem, b.dval))
            b.dsem = None
        self.dbufs = []

    def emit(self):
        with self.nc.Block() as block:
            def mk(name):
                def body(e):
                    for waits, fn, sem, inc in self.streams[name]:
                        for s, v in waits:
                            e.wait_ge(s, v)
                        if fn is not None:
                            fn(e).then_inc(sem, inc)
                return body
            block.tensor(mk("pe"))
            block.scalar(mk("act"))
            block.vector(mk("dve"))
            block.gpsimd(mk("pool"))
            block.sync(mk("sp"))


class Region:
    def __init__(self, big, lo, hi):
        self.big, self.lo, self.hi, self.cur = big, lo, hi, lo

    def f32(self, words, name="t"):
        a = self.cur
        self.cur += (words + 7) // 8 * 8
        assert self.cur <= self.hi, f"SBUF region overflow at {name}: {self.cur} > {self.hi}"
        return self.big[:, a:a + words], Buf(name)

    def bf(self, elems, name="t"):
        ap, b = self.f32((elems + 1) // 2, name)
        return ap.bitcast(BF16), b

    def sub(self):
        return Region(self.big, self.cur, self.hi)


class _Stop(Exception):
    pass


def build_program(dbg=False, stop=99):
    nc = bass.Bass("TRN2", target_bir_lowering=False)

    def din(name, shape):
        return nc.dram_tensor(name, shape, F32, kind="ExternalInput").ap()

    xin = din("xin", [2, NT * 128, D])
    small_d = din("small", [128, NS])
    ada_w = din("ada_w", [2, D, 3 * D])
    w_in0 = din("w_in0", [D, 3584])
    w_out0 = din("w_out0", [D, D])
    w_in1 = din("w_in1", [D, 1472])
    wq_b = din("wq_b", [256, 1536])
    wkv_b = din("wkv_b", [128, 2048])
    w_out1 = din("w_out1", [D, D])
    a_wsT = din("a_wsT", [128, 8, 128])
    out_d = nc.dram_tensor("out", [2, 2048, D], F32, kind="ExternalOutput").ap()
    adab_g = din("adab_g", [2, D])
    if dbg:
        dbg_h = nc.dram_tensor("dbg_h", [2, NT * 128, D], F32, kind="ExternalOutput").ap()

    with ExitStack() as st:
        S = Sched(nc, st)
        BIGW = 53200
        big = st.enter_context(nc.sbuf_tensor("big", [128, BIGW], F32))[:, :]
        PS = []
        for i in range(7):
            PS.append((st.enter_context(nc.psum_tensor(f"ps{i}", [128, 512], F32))[:, :], Buf(f"ps{i}", excl=True)))
        PT = st.enter_context(nc.psum_tensor("pt", [128, 1024], BF16))[:, :]
        b_PT = Buf("pt", excl=True)
        PTb = [b_PT, b_PT]

        def mm(out, lhsT, rhs, start, stop, R, W, skip=False):
            S.op("pe", lambda e: e.matmul(out, lhsT=lhsT, rhs=rhs, start=start, stop=stop, skip_group_check=skip), R=R, W=W)

        def tr(out, in_, ident, R, W):
            S.op("pe", lambda e: e.transpose(out, in_, ident), R=R, W=W)

        def act(out, in_, func, R, W, scale=1.0, bias=0.0):
            S.op("act", lambda e: e.activation(out=out, in_=in_, func=func, bias=bias, scale=scale), R=R, W=W)

        def tt(eng, out, in0, in1, op, R, W):
            S.op(eng, lambda e: e.tensor_tensor(out=out, in0=in0, in1=in1, op=op), R=R, W=W)

        def ts(eng, out, in0, s1, s2, op0, op1, R, W):
            if s2 is None:
                S.op(eng, lambda e: e.tensor_scalar(out=out, in0=in0, scalar1=s1, scalar2=None, op0=op0), R=R, W=W)
            else:
                S.op(eng, lambda e: e.tensor_scalar(out=out, in0=in0, scalar1=s1, scalar2=s2, op0=op0, op1=op1), R=R, W=W)

        def stt(eng, out, in0, scalar, in1, op0, op1, R, W):
            S.op(eng, lambda e: e.scalar_tensor_tensor(out=out, in0=in0, scalar=scalar, in1=in1, op0=op0, op1=op1), R=R, W=W)

        def cp(eng, out, in_, R, W):
            if eng == "act":
                S.op("act", lambda e: e.copy(out=out, in_=in_), R=R, W=W)
            else:
                S.op(eng, lambda e: e.tensor_copy(out=out, in_=in_), R=R, W=W)

        def recip(out, in_, R, W):
            S.op("dve", lambda e: e.reciprocal(out=out, in_=in_), R=R, W=W)

        def dma(q, out, in_, R, W):
            S.dma(q, lambda e: e.dma_start(out=out, in_=in_), R=R, W=W)

        top = Region(big, 0, BIGW)
        small, b_small = top.f32(NS, "small")
        h_all, _ = top.f32(NT * D, "h")
        h = h_all.rearrange("p (t f) -> p t f", t=NT)
        b_h = [Buf(f"h{t}") for t in range(NT)]
        G_l, b_Gl = top.f32(D, "G_l")
        G_c, b_Gc = top.f32(D, "G_c")
        identb, b_identb = top.bf(128, "identb")
        wsT_f, b_wsT = top.bf(8 * 128, "wsT")
        wsT = wsT_f.rearrange("p (g q) -> p g q", g=8)
        biasT, b_biasT = top.f32(512, "biasT")
        mod_f, b_mod = top.f32(2 * 72, "mod")
        mod = mod_f.rearrange("p (l f r) -> p l f r", l=2, r=3)
        acol_f, b_acol = top.f32(2 * 3 * 8, "acol")
        acol = acol_f.rearrange("p (l r f) -> p l r f", l=2, r=3)
        scT_f, b_scT = top.bf(24, "scT")
        scT = scT_f.rearrange("p (k r) -> p k r", r=3)
        lamt, b_lam = top.f32(8, "lam")
        subln8, b_subln8 = top.f32(128, "subln8")
        stats, b_stats = top.f32(32, "stats")
        condrep_f, b_condrep = top.bf(8 * 128, "condrep")
        condrep = condrep_f.rearrange("p (k n) -> p k n", k=8)
        brow, b_brow = top.bf(1024, "brow")
        ones1, b_ones1 = top.bf(128, "ones1")
        wKV_f, b_wKV = Region(big, BIGW - 4096, BIGW).bf(8 * 1024, "wKV")
        wKV = wKV_f.rearrange("p (k n) -> p k n", k=8)
        phase0 = top.cur

        dma("sp", small, small_d[:, :], R=[], W=[b_small])
        dma("pool", identb, small_d[:, IDENT:IDENT + 128], R=[], W=[b_identb])
        dma("pool", wsT, a_wsT[:, :, :], R=[], W=[b_wsT])
        S.op("dve", lambda e: e.memset(ones1, 1.0), R=[], W=[b_ones1])

        act(scT, small[:, COND:COND + 24].rearrange("p (k r) -> p k r", r=3), AF.Silu, R=[b_small], W=[b_scT])
        setup = Region(big, phase0, BIGW)
        wada = []
        for i in range(2):
            ap, b = setup.bf(8 * 512, f"wada{i}")
            wada.append((ap.rearrange("p (k n) -> p k n", k=8), b))
        pcs = 0
        for l in range(2):
            pm, b_pm = PS[l]
            for j in range(6):
                wt, b_wt = wada[pcs % 2]
                pcs += 1
                dma("pool", wt, ada_w[l, :, j * 512:(j + 1) * 512].rearrange("(k p) n -> p k n", p=128), R=[], W=[b_wt])
                for fl in range(4):
                    fc = j * 4 + fl
                    for kc in range(8):
                        mm(pm[:, fc * 3:fc * 3 + 3], wt[:, kc, fl * 128:(fl + 1) * 128], scT[:, kc, :],
                           kc == 0, kc == 7, R=[b_wt, b_scT], W=[b_pm])
            tt("dve", mod[:, l, :, :], pm[:, 0:72].rearrange("p (f r) -> p f r", r=3),
               small[:, ADAB + l * 24:ADAB + (l + 1) * 24].unsqueeze(2).to_broadcast([128, 24, 3]), ALU.add,
               R=[b_pm, b_small], W=[b_mod])
            for r in range(3):
                stt("dve", acol[:, l, r, :], mod[:, l, 8:16, r], 1.0, small[:, NW + l * 8:NW + (l + 1) * 8],
                    ALU.add, ALU.mult, R=[b_mod, b_small], W=[b_acol])
        lt, b_lt = setup.f32(128, "lamtmp")
        tt("dve", lt[:, 0:64], small[:, LQ1:LQ1 + 64], small[:, LK1:LK1 + 64], ALU.mult, R=[b_small], W=[b_lt])
        tt("dve", lt[:, 64:128], small[:, LQ2:LQ2 + 64], small[:, LK2:LK2 + 64], ALU.mult, R=[b_small], W=[b_lt])
        S.op("dve", lambda e: e.reduce_sum(out=lamt[:, 0:2], in_=lt.rearrange("p (a b) -> p a b", a=2),
                                           axis=mybir.AxisListType.X), R=[b_lt], W=[b_lam])
        act(lamt[:, 2:4], lamt[:, 0:2], AF.Exp, R=[b_lam], W=[b_lam])
        tt("dve", lamt[:, 4:5], lamt[:, 3:4], lamt[:, 2:3], ALU.subtract, R=[b_lam], W=[b_lam])
        ts("dve", lamt[:, 5:6], lamt[:, 4:5], -0.2, None, ALU.add, None, R=[b_lam], W=[b_lam])
        neglam = lamt[:, 5:6]
        ts("dve", subln8, small[:, SUBLN:SUBLN + 128], 0.8, None, ALU.mult, None, R=[b_small], W=[b_subln8])
        cp("dve", biasT.rearrange("p (g d) -> p g d", g=8), small[:, ABS:ABS + 8].unsqueeze(2).to_broadcast([128, 8, 64]),
           R=[b_small], W=[b_biasT])

        def build_G(l, conds):
            load_w("pool", wKV, b_wKV, ada_w[l], 2048, 1024)
            dma("pool", brow[0:1, :], adab_g[l:l + 1, :], R=[], W=[b_brow])
            pg, b_pg = PS[6]
            for (r, G, b_G) in conds:
                cp("dve", condrep, scT[:, :, r:r + 1].to_broadcast([128, 8, 128]), R=[b_scT], W=[b_condrep])
                for n in range(2):
                    for kc in range(8):
                        mm(pg, condrep[:, kc, :], wKV[:, kc, n * 512:(n + 1) * 512], kc == 0, False, R=[b_condrep, b_wKV], W=[b_pg])
                    mm(pg, ones1[0:1, :], brow[0:1, n * 512:(n + 1) * 512], False, True, R=[b_ones1, b_brow], W=[b_pg])
                    cp("dve", G[:, n * 512:(n + 1) * 512], pg, R=[b_pg], W=[b_G])

        def rstd_of(reg_stats, src, b_src, n, eps, name="rs"):
            st_ap, b_st = reg_stats
            nch = (n + 511) // 512
            w = n // nch
            for c in range(nch):
                S.op("dve", (lambda c: lambda e: e.bn_stats(out=st_ap[:, c * 6:(c + 1) * 6], in_=src[:, c * w:(c + 1) * w]))(c),
                     R=[b_src], W=[b_st])
            S.op("dve", lambda e: e.bn_aggr(out=st_ap[:, 12:14], in_=st_ap[:, 0:6 * nch].rearrange("p (c s) -> p c s", s=6)),
                 R=[b_st], W=[b_st])
            stt("dve", st_ap[:, 14:15], st_ap[:, 12:13], st_ap[:, 12:13], st_ap[:, 13:14], ALU.mult, ALU.add, R=[b_st], W=[b_st])
            act(st_ap[:, 15:16], st_ap[:, 14:15], AF.Ln, R=[b_st], W=[b_st], bias=eps)
            act(st_ap[:, 15:16], st_ap[:, 15:16], AF.Exp, R=[b_st], W=[b_st], scale=-0.5)
            return st_ap[:, 15:16], st_ap[:, 12:13], b_st

        def ckpt(k, bi):
            if stop == k:
                S.barrier()
                if dbg:
                    b_dbg = Buf("dbgstop")
                    for t in range(NT):
                        dma("sp", dbg_h[bi, t * 128:(t + 1) * 128, :], h[:, t, :], R=[b_h[t]], W=[b_dbg])
                    S.barrier()
                raise _Stop()

        def make_xlT(src, b_src, l, r, xn, b_xn, st_pair, xlT_dst, b_xlT, alt):
            rstd, _, b_st = rstd_of(st_pair, src, b_src, 1024, 1e-6)
            act(xn, src, AF.Identity, R=[b_src, b_st], W=[b_xn], scale=rstd)
            ckpt(0.57, 0)
            for half in range(2):
                bp = PTb[half]
                if half == 1:
                    ckpt(0.596, 0)
                for q in range(4):
                    fc = half * 4 + q
                    tr(PT[:, fc * 128:(fc + 1) * 128], xn[:, fc * 128:(fc + 1) * 128], identb, R=[b_xn, b_identb], W=[bp])
                if half == 1:
                    ckpt(0.597, 0)
                ckpt(0.58, 0)
                for q in range(4):
                    fc = half * 4 + q
                    sc = acol[:, l, r, fc:fc + 1]
                    bi = mod[:, l, fc, r:r + 1]
                    if q == 1:
                        ckpt(0.59, 0)
                    if q == 2:
                        ckpt(0.595, 0)
                    if (fc + alt) % 2 == 0 or XLT_DVE_ONLY:
                        ts("dve", xlT_dst[:, fc, :], PT[:, fc * 128:(fc + 1) * 128], sc, bi, ALU.mult, ALU.add,
                           R=[bp, b_acol, b_mod], W=[b_xlT])
                    else:
                        act(xlT_dst[:, fc, :], PT[:, fc * 128:(fc + 1) * 128], AF.Identity, R=[bp, b_acol, b_mod], W=[b_xlT],
                            scale=sc, bias=bi)

        def proj(ps, b_ps, xlT_tile, b_xlT, W, b_W, c0, n, nk=8):
            for kc in range(nk):
                mm(ps[:, 0:n], xlT_tile[:, kc, :], W[:, kc, c0:c0 + n], kc == 0, kc == nk - 1, R=[b_xlT, b_W], W=[b_ps])

        def rope(reg_t, src, b_src, ng, t, dst, b_dst, eng2="pool"):
            (tc_, b_tc), (ts_, b_ts) = reg_t
            n = ng * 64
            cosb = small[:, COS + t * 64:COS + (t + 1) * 64]
            sinb = small[:, SIN + t * 64:SIN + (t + 1) * 64]
            sv = src.rearrange("p (g s t d) -> p g s t d", g=ng, s=2, t=2)
            tv = ts_[:, 0:n].rearrange("p (g s t d) -> p g s t d", g=ng, s=2, t=2)
            sn = sinb.rearrange("p (s t d) -> p s t d", s=2, t=2)
            tt("dve", tc_[:, 0:n].rearrange("p (g d) -> p g d", g=ng), src.rearrange("p (g d) -> p g d", g=ng),
               cosb.unsqueeze(1).to_broadcast([128, ng, 64]), ALU.mult, R=[b_src, b_small], W=[b_tc])
            for hf in range(2):
                tt("dve", tv[:, :, :, hf, :], sv[:, :, :, 1 - hf, :],
                   sn[:, :, hf, :].unsqueeze(1).to_broadcast([128, ng, 2, 16]), ALU.mult, R=[b_src, b_small], W=[b_ts])
            tt(eng2, dst, tc_[:, 0:n], ts_[:, 0:n], ALU.add, R=[b_tc, b_ts], W=[b_dst])

        def load_w(q, dst, b_dst, src2d, c0, n):
            dma(q, dst, src2d[:, c0:c0 + n].rearrange("(k p) n -> p k n", p=128), R=[], W=[b_dst])

        S.barrier()

        def ckpt(k, bi):
            if stop == k:
                S.barrier()
                if dbg:
                    b_dbg = Buf("dbgstop")
                    for t in range(NT):
                        dma("sp", dbg_h[bi, t * 128:(t + 1) * 128, :], h[:, t, :], R=[b_h[t]], W=[b_dbg])
                    S.barrier()
                raise _Stop()

        try:
          ckpt(0, 0)
          for bi in range(2):
              L0 = Region(big, phase0, BIGW)
              wQZ_f, b_wQZ = L0.bf(8 * 1024, "wQZ")
              wQZ = wQZ_f.rearrange("p (k n) -> p k n", k=8)
              woB_f, b_woB = L0.bf(4 * 1024, "woB")
              woB = woB_f.rearrange("p (k n) -> p k n", k=4)
              build_G(0, [(bi, G_l, b_Gl), (2, G_c, b_Gc)])
              ckpt(0.5, bi)

              A = Region(big, L0.cur, BIGW - 4096)
              wA_f, b_wA = A.bf(8 * 1536, "wA")
              wA = wA_f.rearrange("p (k n) -> p k n", k=8)
              woA_f, b_woA = A.bf(4 * 1024, "woA")
              woA = woA_f.rearrange("p (k n) -> p k n", k=4)
              load_w("pool", wA, b_wA, w_in0, 0, 1536)
              dma("pool", woA, w_out0[0:512, :].rearrange("(k p) n -> p k n", p=128), R=[], W=[b_woA])
              xs = [A.f32(1024, f"xs{i}") for i in range(2)]
              xn, b_xn = A.bf(1024, "xn")
              xlTs = []
              for i in range(2):
                  ap, b = A.bf(8 * 128, f"xlT{i}")
                  xlTs.append((ap.rearrange("p (k n) -> p k n", k=8), b))
              stp = A.f32(16, "st")
              stp2 = A.f32(16, "st2")
              gu, b_gu = A.f32(512, "gu")
              gv, b_gv = A.f32(512, "gv")
              sz, b_sz = A.f32(512, "sz")
              xh, b_xh = A.f32(512, "xh")
              t1, b_t1 = xh, b_xh
              vn, b_vn = A.bf(512, "vn")
              mixa, b_mixa = A.bf(512, "mixa")
              mixT_f, b_mixT = vn, b_vn
              mixT = mixT_f.rearrange("p (k n) -> p k n", k=4)
              tmpo, b_tmpo = xh, b_xh
              load_w("pool", wKV, b_wKV, w_in0, 2048, 1024)
              for t in range(NT):
                  r = 2 if t < 2 else bi
                  G, b_G = (G_c, b_Gc) if t < 2 else (G_l, b_Gl)
                  x_t, b_x = xs[t % 2]
                  xlT, b_xlT = xlTs[t % 2]
                  dma("sp", x_t, xin[bi, t * 128:(t + 1) * 128, :], R=[], W=[b_x])
                  ckpt(0.55, bi)
                  make_xlT(x_t, b_x, 0, r, xn, b_xn, stp, xlT, b_xlT, t)
                  ckpt(0.6, bi)
                  for i in range(3):
                      proj(PS[i][0], PS[i][1], xlT, b_xlT, wA, b_wA, i * 512, 512)
                  ckpt(0.65, bi)
                  act(gu, PS[0][0], AF.Gelu, R=[PS[0][1]], W=[b_gu])
                  act(gv, PS[1][0], AF.Gelu, R=[PS[1][1]], W=[b_gv])
                  act(sz, PS[2][0], AF.Silu, R=[PS[2][1]], W=[b_sz])
                  st2, b_st2 = stp2
                  S.op("dve", (lambda st2, gv: lambda e: e.bn_stats(out=st2[:, 0:6], in_=gv))(st2, gv), R=[b_gv], W=[b_st2])
                  S.op("dve", (lambda st2: lambda e: e.bn_aggr(out=st2[:, 12:14], in_=st2[:, 0:6]))(st2), R=[b_st2], W=[b_st2])
                  act(st2[:, 15:16], st2[:, 13:14], AF.Ln, R=[b_st2], W=[b_st2], bias=1e-5)
                  act(st2[:, 15:16], st2[:, 15:16], AF.Exp, R=[b_st2], W=[b_st2], scale=-0.5)
                  ts("dve", xh, gv, st2[:, 12:13], st2[:, 15:16], ALU.subtract, ALU.mult, R=[b_gv, b_st2], W=[b_xh])
                  tt("pool", xh, xh, small[:, LNW:LNW + 512], ALU.mult, R=[b_xh, b_small], W=[b_xh])
                  tt("pool", vn, xh, small[:, LNB:LNB + 512], ALU.add, R=[b_xh, b_small], W=[b_vn])
                  tt("pool", gu, gu, sz, ALU.mult, R=[b_gu, b_sz], W=[b_gu])
                  psg, b_psg = PS[3]
                  for g in range(8):
                      mm(psg[:, g * 64:(g + 1) * 64], wsT[:, g, :], vn[:, g * 64:(g + 1) * 64], True, True,
                         R=[b_wsT, b_vn], W=[b_psg])
                  tt("dve", t1, psg, biasT, ALU.add, R=[b_psg, b_biasT], W=[b_t1])
                  tt("dve", mixa, t1, gu, ALU.mult, R=[b_t1, b_gu], W=[b_mixa])
                  for q in range(4):
                      tr(PT[:, q * 128:(q + 1) * 128], mixa[:, q * 128:(q + 1) * 128], identb, R=[b_mixa, b_identb], W=[PTb[0]])
                  cp("act", mixT, PT[:, 0:512].rearrange("p (k n) -> p k n", k=4), R=[PTb[0]], W=[b_mixT])
                  for n in range(2):
                      po, b_po = PS[4 + n]
                      for kc in range(4):
                          mm(po, mixT[:, kc, :], woA[:, kc, n * 512:(n + 1) * 512], kc == 0, kc == 3, R=[b_mixT, b_woA], W=[b_po])
                      tt("dve", tmpo, po, G[:, n * 512:(n + 1) * 512], ALU.mult, R=[b_po, b_G], W=[b_tmpo])
                      tt("pool", h[:, t, n * 512:(n + 1) * 512], tmpo, x_t[:, n * 512:(n + 1) * 512], ALU.add,
                         R=[b_tmpo, b_x], W=[b_h[t]])
                  ckpt(0.7, bi)
              S.barrier()
              ckpt(1, bi)

              KV = L0.sub()
              KT_f, b_KT = KV.bf(4 * 2304, "KT")
              KT = KT_f.rearrange("p (h n) -> p h n", h=4)
              V_f, b_V = KV.bf(NT * 4 * 132, "V")
              V = V_f.rearrange("p (t h d) -> p t h d", t=NT, h=4)
              K2 = Region(big, KV.cur, BIGW - 4096)
              xs = [K2.f32(1024, f"xs{i}") for i in range(2)]
              xn, b_xn = K2.bf(1024, "xn")
              xlTs = []
              for i in range(2):
                  ap, b = K2.bf(8 * 128, f"xlT{i}")
                  xlTs.append((ap.rearrange("p (k n) -> p k n", k=8), b))
              stp = K2.f32(16, "st")
              rt = (K2.f32(512, "tc"), K2.f32(512, "ts"))
              kr, b_kr = K2.bf(512, "kr")
              load_w("pool", wQZ[:, :, 0:512], b_wQZ, w_in0, 1536, 512)
              load_w("pool", wQZ[:, :, 512:1024], b_wQZ, w_in0, 3072, 512)
              dma("pool", woB, w_out0[512:1024, :].rearrange("(k p) n -> p k n", p=128), R=[], W=[b_woB])
              S.op("dve", (lambda V: lambda e: e.memset(V[:, :, :, 128:129], 1.0))(V), R=[], W=[b_V])
              for t in range(NT):
                  r = 2 if t < 2 else bi
                  x_t, b_x = xs[t % 2]
                  xlT, b_xlT = xlTs[t % 2]
                  dma("sp", x_t, xin[bi, t * 128:(t + 1) * 128, :], R=[], W=[b_x])
                  make_xlT(x_t, b_x, 0, r, xn, b_xn, stp, xlT, b_xlT, t)
                  proj(PS[0][0], PS[0][1], xlT, b_xlT, wKV, b_wKV, 0, 512)
                  proj(PS[1][0], PS[1][1], xlT, b_xlT, wKV, b_wKV, 512, 512)
                  rope(rt, PS[0][0], PS[0][1], 8, t, kr, b_kr)
                  for hh in range(4):
                      tr(PT[:, hh * 128:(hh + 1) * 128], kr[:, hh * 128:(hh + 1) * 128], identb, R=[b_kr, b_identb], W=[PTb[0]])
                  cp("act", KT[:, :, t * 128:(t + 1) * 128], PT[:, 0:512].rearrange("p (h n) -> p h n", h=4), R=[PTb[0]], W=[b_KT])
                  cp("act", V[:, t, :, 0:128], PS[1][0].rearrange("p (h d) -> p h d", h=4), R=[PS[1][1]], W=[b_V])
              S.barrier()
              ckpt(2, bi)

              Bp = KV.sub()
              xs = [Bp.f32(1024, f"xs{i}") for i in range(2)]
              xn, b_xn = Bp.bf(1024, "xn")
              xlT_f, b_xlT = Bp.bf(8 * 256, "xlT")
              xlT = xlT_f.rearrange("p (k n) -> p k n", k=8)
              stp = Bp.f32(16, "st")
              stp3 = Bp.f32(16, "st3")
              rt = (Bp.f32(512, "tc"), Bp.f32(512, "ts"))
              qr, b_qr = Bp.bf(512, "qr")
              QT_f, b_QT = Bp.bf(4 * 256, "QT")
              QT = QT_f.rearrange("p (h n) -> p h n", h=4)
              NPT = 3
              pts = [Bp.bf(512, f"pT{i}") for i in range(NPT)]
              oh, b_oh = Bp.f32(128, "oh")
              rr, b_rr = Bp.f32(8, "rr")
              omix_f, b_omix = Bp.f32(2 * 512, "omix")
              omix = omix_f.rearrange("p (j f) -> p j f", j=2)
              sbz, b_sbz = Bp.f32(512, "sbz")
              mixb_f, b_mixb = Bp.bf(2 * 512, "mixb")
              mixb = mixb_f.rearrange("p (j f) -> p j f", j=2)
              mixT_f, b_mixT = Bp.bf(4 * 256, "mixT")
              mixT = mixT_f.rearrange("p (k n) -> p k n", k=4)
              tmpo, b_tmpo = sbz, b_sbz
              ptc = 0
              for s in range(9):
                  tiles = [2 * s, 2 * s + 1]
                  r = 2 if s == 0 else bi
                  G, b_G = (G_c, b_Gc) if s == 0 else (G_l, b_Gl)
                  nch = 2 if s == 0 else NT
                  for j, t in enumerate(tiles):
                      x_t, b_x = xs[j]
                      dma("sp", x_t, xin[bi, t * 128:(t + 1) * 128, :], R=[], W=[b_x])
                      make_xlT(x_t, b_x, 0, r, xn, b_xn, stp, xlT[:, :, j * 128:(j + 1) * 128], b_xlT, j)
                  for j, t in enumerate(tiles):
                      proj(PS[5][0], PS[5][1], xlT[:, :, j * 128:(j + 1) * 128], b_xlT, wQZ, b_wQZ, 0, 512)
                      rope(rt, PS[5][0], PS[5][1], 8, t, qr, b_qr)
                      for hh in range(4):
                          tr(PT[:, hh * 128:(hh + 1) * 128], qr[:, hh * 128:(hh + 1) * 128], identb, R=[b_qr, b_identb], W=[PTb[0]])
                      cp("act", QT[:, :, j * 128:(j + 1) * 128], PT[:, 0:512].rearrange("p (h n) -> p h n", h=4), R=[PTb[0]], W=[b_QT])
                  items = [(hh, m, cpi) for hh in range(4) for m in range(2) for cpi in range(nch // 2)]

                  def qk0(it, idx):
                      hh, m, cpi = it
                      psc, b_psc = PS[idx % 3]
                      for cc in range(2):
                          c = 2 * cpi + cc
                          mm(psc[:, cc * 256:(cc + 1) * 256], KT[m * 64:(m + 1) * 64, hh, c * 128:(c + 1) * 128],
                             QT[m * 64:(m + 1) * 64, hh, :], True, True, R=[b_KT, b_QT], W=[b_psc])

                  def pv0(it, idx, pT, b_pT):
                      hh, m, cpi = it
                      po, b_po = PS[3 + m]
                      for cc in range(2):
                          c = 2 * cpi + cc
                          for j in range(2):
                              mm(po[:, j * 132:j * 132 + 129], pT[:, cc * 256 + j * 128:cc * 256 + (j + 1) * 128],
                                 V[:, c, hh, 0:129], c == 0 and j == 0, c == nch - 1, R=[b_pT, b_V], W=[b_po], skip=True)

                  qk0(items[0], 0)
                  for idx, it in enumerate(items):
                      hh, m, cpi = it
                      psc, b_psc = PS[idx % 3]
                      pT, b_pT = pts[ptc % NPT]
                      ptc += 1
                      act(pT, psc, AF.Exp, R=[b_psc], W=[b_pT], scale=0.125)
                      if idx + 1 < len(items):
                          qk0(items[idx + 1], idx + 1)
                      pv0(it, idx, pT, b_pT)
                      if not (m == 1 and cpi == nch // 2 - 1):
                          continue
                      p0, b_p0 = PS[3]
                      p1, b_p1 = PS[4]
                      for j in range(2):
                          recip(rr[:, 0:1], p0[:, j * 132 + 128:j * 132 + 129], R=[b_p0], W=[b_rr])
                          recip(rr[:, 1:2], p1[:, j * 132 + 128:j * 132 + 129], R=[b_p1], W=[b_rr])
                          tt("dve", rr[:, 2:3], rr[:, 1:2], neglam, ALU.mult, R=[b_rr, b_lam], W=[b_rr])
                          ts("dve", oh, p0[:, j * 132:j * 132 + 128], rr[:, 0:1], None, ALU.mult, None, R=[b_p0, b_rr], W=[b_oh])
                          stt("dve", oh, p1[:, j * 132:j * 132 + 128], rr[:, 2:3], oh, ALU.mult, ALU.add, R=[b_p1, b_rr, b_oh], W=[b_oh])
                          rstd, _, b_st3 = rstd_of(stp3, oh, b_oh, 128, 1e-5)
                          stt("dve", omix[:, j, hh * 128:(hh + 1) * 128], oh, rstd, subln8, ALU.mult, ALU.mult,
                              R=[b_oh, b_st3, b_subln8], W=[b_omix])
                  for j, t in enumerate(tiles):
                      proj(PS[5][0], PS[5][1], xlT[:, :, j * 128:(j + 1) * 128], b_xlT, wQZ, b_wQZ, 512, 512)
                      act(sbz, PS[5][0], AF.Silu, R=[PS[5][1]], W=[b_sbz])
                      tt("dve", mixb[:, j, :], omix[:, j, :], sbz, ALU.mult, R=[b_omix, b_sbz], W=[b_mixb])
                      for q in range(4):
                          tr(PT[:, 512 + q * 128:512 + (q + 1) * 128], mixb[:, j, q * 128:(q + 1) * 128], identb,
                             R=[b_mixb, b_identb], W=[PTb[1]])
                      cp("act", mixT[:, :, j * 128:(j + 1) * 128], PT[:, 512:1024].rearrange("p (k n) -> p k n", k=4),
                         R=[PTb[1]], W=[b_mixT])
                  for j, t in enumerate(tiles):
                      for n in range(2):
                          po, b_po = PS[5 + n]
                          for kc in range(4):
                              mm(po, mixT[:, kc, j * 128:(j + 1) * 128], woB[:, kc, n * 512:(n + 1) * 512], kc == 0, kc == 3,
                                 R=[b_mixT, b_woB], W=[b_po])
                          tt("dve", tmpo, po, G[:, n * 512:(n + 1) * 512], ALU.mult, R=[b_po, b_G], W=[b_tmpo])
                          tt("pool", h[:, t, n * 512:(n + 1) * 512], tmpo, h[:, t, n * 512:(n + 1) * 512], ALU.add,
                             R=[b_tmpo, b_h[t]], W=[b_h[t]])
              S.barrier()
              ckpt(3, bi)
              if dbg:
                  b_dbg = Buf(f"dbg{bi}")
                  for t in range(NT):
                      dma("sp", dbg_h[bi, t * 128:(t + 1) * 128, :], h[:, t, :], R=[b_h[t]], W=[b_dbg])
                  S.barrier()

              L1 = Region(big, phase0, BIGW)
              build_G(1, [(bi, G_l, b_Gl)])
              w1_f, b_w1 = L1.bf(8 * 1472, "w_in1")
              w1 = w1_f.rearrange("p (k n) -> p k n", k=8)
              wqn_f, b_wqn = L1.bf(2 * 8 * 128, "wqn")
              wqn = wqn_f.rearrange("p (k h d) -> p k h d", k=2, h=8)
              wqr_f, b_wqr = L1.bf(2 * 8 * 64, "wqr")
              wqr = wqr_f.rearrange("p (k h d) -> p k h d", k=2, h=8)
              wkv_f, b_wkv = L1.bf(2048, "wkv")
              wkv = wkv_f.rearrange("p (h d) -> p h d", h=8)
              WkT_f, b_WkT = L1.bf(8 * 128, "WkT")
              WkT = WkT_f.rearrange("p (h c) -> p h c", h=8)
              wo1_f, b_wo1 = L1.bf(8 * 1024, "wo1")
              wo1 = wo1_f.rearrange("p (k n) -> p k n", k=8)
              ckvT, b_ckvT = L1.bf(2304, "ckvT")
              krT, b_krT = L1.bf(2304, "krT")
              VS_f, b_VS = L1.bf(NT * 132, "VS")
              VS = VS_f.rearrange("p (t d) -> p t d", t=NT)
              load_w("pool", w1, b_w1, w_in1, 0, 1472)
              wq3 = wq_b.rearrange("(k p) (h d) -> p k h d", p=128, h=8)
              for kc in range(2):
                  dma("pool", wqn[:, kc, :, :], wq3[:, kc, :, 0:128], R=[], W=[b_wqn])
                  dma("pool", wqr[:, kc, :, :], wq3[:, kc, :, 128:192], R=[], W=[b_wqr])
              dma("pool", wkv, wkv_b.rearrange("p (h d) -> p h d", h=8), R=[], W=[b_wkv])
              dma("pool", wo1, w_out1[:, :].rearrange("(k p) n -> p k n", p=128), R=[], W=[b_wo1])
              for hh in range(8):
                  tr(PT[:, hh * 128:(hh + 1) * 128], wkv[:, hh, 0:128], identb, R=[b_wkv, b_identb], W=[PTb[hh // 4]])
              cp("dve", WkT[:, 0:4, :], PT[:, 0:512].rearrange("p (h c) -> p h c", h=4), R=[PTb[0]], W=[b_WkT])
              cp("dve", WkT[:, 4:8, :], PT[:, 512:1024].rearrange("p (h c) -> p h c", h=4), R=[PTb[1]], W=[b_WkT])

              ckpt(3.5, bi)
              S.barrier()
              K1 = L1.sub()
              xn, b_xn = K1.bf(1024, "xn")
              xlTs = []
              for i in range(2):
                  ap, b = K1.bf(8 * 128, f"xlT{i}")
                  xlTs.append((ap.rearrange("p (k n) -> p k n", k=8), b))
              stp = K1.f32(16, "st")
              stp3 = K1.f32(16, "st3")
              rt = (K1.f32(64, "tc"), K1.f32(64, "ts"))
              krd, b_krd = K1.bf(128, "krd")
              S.op("dve", (lambda VS: lambda e: e.memset(VS[:, :, 128:129], 1.0))(VS), R=[], W=[b_VS])
              for t in range(NT):
                  r = 2 if t < 2 else bi
                  xlT, b_xlT = xlTs[t % 2]
                  make_xlT(h[:, t, :], b_h[t], 1, r, xn, b_xn, stp, xlT, b_xlT, t)
                  pk, b_pk = PS[t % 2]
                  if os.environ.get("NOPROJ", "0") == "1":
                      continue
                  if os.environ.get("USE_WO1", "0") == "1":
                      proj(pk, b_pk, xlT, b_xlT, wo1, b_wo1, 256, 192)
                  else:
                      proj(pk, b_pk, xlT, b_xlT, w1, b_w1, 256, int(os.environ.get("KVN_N", "192")))
                  import os as _os
                  _v = int(_os.environ.get("KV1V", "9"))
                  if _v < 1:
                      continue
                  rstd, _, b_st3 = rstd_of(stp3, pk[:, 0:128], b_pk, 128, 1e-6)
                  stt("dve", VS[:, t, 0:128], pk[:, 0:128], rstd, small[:, KVN:KVN + 128], ALU.mult, ALU.mult,
                      R=[b_pk, b_st3, b_small], W=[b_VS])
                  if _v < 2:
                      continue
                  rope(rt, pk[:, 128:192], b_pk, 1, t, krd[:, 0:64], b_krd, eng2="dve")
                  cp("dve", krd[:, 64:128], krd[:, 0:64], R=[b_krd], W=[b_krd])
                  if _v < 3:
                      continue
                  tr(PT[:, 0:128], VS[:, t, 0:128], identb, R=[b_VS, b_identb], W=[PTb[0]])
                  tr(PT[:, 128:256], krd, identb, R=[b_krd, b_identb], W=[PTb[0]])
                  cp("act", ckvT[:, t * 128:(t + 1) * 128], PT[:, 0:128], R=[PTb[0]], W=[b_ckvT])
                  cp("act", krT[:, t * 128:(t + 1) * 128], PT[:, 128:256], R=[PTb[0]], W=[b_krT])
              S.barrier()
              ckpt(4, bi)

              Q1 = L1.sub()
              xn, b_xn = Q1.bf(1024, "xn")
              xlT_f, b_xlT = Q1.bf(8 * 256, "xlT/mixT")
              xlT = xlT_f.rearrange("p (k n) -> p k n", k=8)
              mixT, b_mixT = xlT, b_xlT
              stp = Q1.f32(16, "st")
              stp3 = Q1.f32(16, "st3")
              cqn, b_cqn = Q1.bf(256, "cqn")
              cqnT_f, b_cqnT = Q1.bf(2 * 256, "cqnT")
              cqnT = cqnT_f.rearrange("p (k n) -> p k n", k=2)
              qnTs = [Q1.bf(256, f"qnT{i}") for i in range(2)]
              qpT_f, b_qpT = Q1.bf(8 * 256, "qpT/onT")
              qpT = qpT_f.rearrange("p (h n) -> p h n", h=8)
              onT, b_onT = qpT, b_qpT
              rt = (Q1.f32(512, "tc"), Q1.f32(512, "ts"))
              qr, b_qr = Q1.bf(512, "qr")
              qrT_f, b_qrT = Q1.bf(4 * 256, "qrT")
              qrT = qrT_f.rearrange("p (g n) -> p g n", g=4)
              pts = [Q1.bf(512, f"pT{i}") for i in range(NPT)]
              rr, b_rr = Q1.f32(8, "rr")
              on_f, b_on = Q1.bf(2 * 1024, "on")
              on = on_f.rearrange("p (j f) -> p j f", j=2)
              szz, b_szz = Q1.f32(512, "sz")
              mix_f, b_mix = G_c.bitcast(BF16), Buf("mix")
              mix = mix_f.rearrange("p (j f) -> p j f", j=2)
              tmpo, b_tmpo = szz, b_szz
              ybuf = [(h[:, i, :], Buf(f"y{i}")) for i in range(2)]
              b_outd = [Buf(f"outd{bi}{i}") for i in range(2)]
              sc1 = 1.0 / math.sqrt(192.0)
              ptc = 0
              yc = 0
              for s in range(1, 9):
                  tiles = [2 * s, 2 * s + 1]
                  for j, t in enumerate(tiles):
                      make_xlT(h[:, t, :], b_h[t], 1, bi, xn, b_xn, stp, xlT[:, :, j * 128:(j + 1) * 128], b_xlT, j)
                  for j, t in enumerate(tiles):
                      pq, b_pq = PS[5]
                      proj(pq, b_pq, xlT[:, :, j * 128:(j + 1) * 128], b_xlT, w1, b_w1, 0, 256)
                      rstd, _, b_st3 = rstd_of(stp3, pq[:, 0:256], b_pq, 256, 1e-6)
                      stt("dve", cqn, pq[:, 0:256], rstd, small[:, QN:QN + 256], ALU.mult, ALU.mult,
                          R=[b_pq, b_st3, b_small], W=[b_cqn])
                      for kc in range(2):
                          tr(PT[:, kc * 128:(kc + 1) * 128], cqn[:, kc * 128:(kc + 1) * 128], identb, R=[b_cqn, b_identb], W=[PTb[0]])
                      cp("act", cqnT[:, :, j * 128:(j + 1) * 128], PT[:, 0:256].rearrange("p (k n) -> p k n", k=2),
                         R=[PTb[0]], W=[b_cqnT])
                  for j, t in enumerate(tiles):
                      pq, b_pq = PS[5]
                      for kc in range(2):
                          mm(pq, cqnT[:, kc, j * 128:(j + 1) * 128], wqr[:, kc, :, :], kc == 0, kc == 1, R=[b_cqnT, b_wqr], W=[b_pq])
                      rope(rt, pq, b_pq, 8, t, qr, b_qr)
                      for g in range(4):
                          tr(PT[:, 512 + g * 128:512 + (g + 1) * 128], qr[:, g * 128:(g + 1) * 128], identb,
                             R=[b_qr, b_identb], W=[PTb[1]])
                      cp("act", qrT[:, :, j * 128:(j + 1) * 128], PT[:, 512:1024].rearrange("p (g n) -> p g n", g=4),
                         R=[PTb[1]], W=[b_qrT])
                  for hh in range(8):
                      pq, b_pq = PS[5 + hh % 2]
                      qnT, b_qnT = qnTs[hh % 2]
                      for kc in range(2):
                          mm(pq[:, 0:256], wqn[:, kc, hh, :], cqnT[:, kc, :], kc == 0, kc == 1, R=[b_wqn, b_cqnT], W=[b_pq])
                      cp("dve", qnT, pq[:, 0:256], R=[b_pq], W=[b_qnT])
                      mm(pq[:, 256:512], WkT[:, hh, :], qnT, True, True, R=[b_WkT, b_qnT], W=[b_pq])
                      cp("dve", qpT[:, hh, :], pq[:, 256:512], R=[b_pq], W=[b_qpT])
                  items = [(hh, cpi) for hh in range(8) for cpi in range(NT // 2)]

                  def qk1(it, idx):
                      hh, cpi = it
                      hp = hh % 2
                      psc, b_psc = PS[idx % 3]
                      for cc in range(2):
                          c = 2 * cpi + cc
                          mm(psc[:, cc * 256:(cc + 1) * 256], ckvT[:, c * 128:(c + 1) * 128], qpT[:, hh, :], True, False,
                             R=[b_ckvT, b_qpT], W=[b_psc])
                          mm(psc[:, cc * 256:(cc + 1) * 256], krT[hp * 64:(hp + 1) * 64, c * 128:(c + 1) * 128],
                             qrT[hp * 64:(hp + 1) * 64, hh // 2, :], False, True, R=[b_krT, b_qrT], W=[b_psc])

                  def pv1(it, pT, b_pT):
                      hh, cpi = it
                      po, b_po = PS[3 + hh % 2]
                      for cc in range(2):
                          c = 2 * cpi + cc
                          for j in range(2):
                              mm(po[:, j * 132:j * 132 + 129], pT[:, cc * 256 + j * 128:cc * 256 + (j + 1) * 128],
                                 VS[:, c, 0:129], c == 0 and j == 0, c == NT - 1, R=[b_pT, b_VS], W=[b_po], skip=True)

                  qk1(items[0], 0)
                  for idx, it in enumerate(items):
                      hh, cpi = it
                      psc, b_psc = PS[idx % 3]
                      pT, b_pT = pts[ptc % NPT]
                      ptc += 1
                      act(pT, psc, AF.Exp, R=[b_psc], W=[b_pT], scale=sc1)
                      if idx + 1 < len(items):
                          qk1(items[idx + 1], idx + 1)
                      pv1(it, pT, b_pT)
                      if cpi != NT // 2 - 1:
                          continue
                      po, b_po = PS[3 + hh % 2]
                      for j in range(2):
                          recip(rr[:, j:j + 1], po[:, j * 132 + 128:j * 132 + 129], R=[b_po], W=[b_rr])
                          ts("dve", on[:, j, hh * 128:(hh + 1) * 128], po[:, j * 132:j * 132 + 128], rr[:, j:j + 1], None,
                             ALU.mult, None, R=[b_po, b_rr], W=[b_on])
                  for j, t in enumerate(tiles):
                      for hh in range(8):
                          tr(PT[:, hh * 128:(hh + 1) * 128], on[:, j, hh * 128:(hh + 1) * 128], identb, R=[b_on, b_identb],
                             W=[PTb[hh // 4]])
                      cp("act", onT[:, 0:4, j * 128:(j + 1) * 128], PT[:, 0:512].rearrange("p (h n) -> p h n", h=4),
                         R=[PTb[0]], W=[b_onT])
                      cp("dve", onT[:, 4:8, j * 128:(j + 1) * 128], PT[:, 512:1024].rearrange("p (h n) -> p h n", h=4),
                         R=[PTb[1]], W=[b_onT])
                  for j, t in enumerate(tiles):
                      for n in range(2):
                          pe_, b_pe = PS[5]
                          pz, b_pz = PS[6]
                          for q in range(4):
                              hh = n * 4 + q
                              mm(pe_[:, q * 128:(q + 1) * 128], onT[:, hh, j * 128:(j + 1) * 128], wkv[:, hh, 128:256], True, True,
                                 R=[b_onT, b_wkv], W=[b_pe])
                          proj(pz, b_pz, xlT[:, :, j * 128:(j + 1) * 128], b_xlT, w1, b_w1, 448 + n * 512, 512)
                          act(szz, pz, AF.Silu, R=[b_pz], W=[b_szz])
                          tt("dve", mix[:, j, n * 512:(n + 1) * 512], pe_, szz, ALU.mult, R=[b_pe, b_szz], W=[b_mix])
                  for j, t in enumerate(tiles):
                      for kc in range(8):
                          tr(PT[:, kc * 128:(kc + 1) * 128], mix[:, j, kc * 128:(kc + 1) * 128], identb, R=[b_mix, b_identb],
                             W=[PTb[kc // 4]])
                      cp("act", mixT[:, 0:4, j * 128:(j + 1) * 128], PT[:, 0:512].rearrange("p (k n) -> p k n", k=4),
                         R=[PTb[0]], W=[b_mixT])
                      cp("dve", mixT[:, 4:8, j * 128:(j + 1) * 128], PT[:, 512:1024].rearrange("p (k n) -> p k n", k=4),
                         R=[PTb[1]], W=[b_mixT])
                  for j, t in enumerate(tiles):
                      for n in range(2):
                          po, b_po = PS[5 + n]
                          for kc in range(8):
                              mm(po, mixT[:, kc, j * 128:(j + 1) * 128], wo1[:, kc, n * 512:(n + 1) * 512], kc == 0, kc == 7,
                                 R=[b_mixT, b_wo1], W=[b_po])
                          tt("dve", tmpo, po, G_l[:, n * 512:(n + 1) * 512], ALU.mult, R=[b_po, b_Gl], W=[b_tmpo])
                          tt("pool", h[:, t, n * 512:(n + 1) * 512], tmpo, h[:, t, n * 512:(n + 1) * 512], ALU.add,
                             R=[b_tmpo, b_h[t]], W=[b_h[t]])
                      y, b_y = ybuf[yc % 2]
                      rstd, _, b_stf = rstd_of(stp, h[:, t, :], b_h[t], 1024, 1e-6)
                      stt("dve", y, h[:, t, :], rstd, small[:, FINAL:FINAL + 1024], ALU.mult, ALU.mult,
                          R=[b_h[t], b_stf, b_small], W=[b_y])
                      dma("sp", out_d[bi, (t - 2) * 128:(t - 1) * 128, :], y, R=[b_y], W=[b_outd[yc % 2]])
                      yc += 1
              S.barrier()
        except _Stop:
            pass
        S.emit()
        build_program.last_S = S
    return nc


_NC_CACHE = {}


def _rope_tables():
    seq, gw, dim = 2048, 64, 64
    rows = seq // gw
    row = np.repeat(np.arange(rows), gw).astype(np.float32)
    col = np.tile(np.arange(gw), rows).astype(np.float32)
    half = dim // 2
    inv = (np.float32(10000.0) ** (-np.arange(0, half, 2, dtype=np.float32) / np.float32(half))).astype(np.float32)
    ang_r = row[:, None] * inv[None, :]
    ang_c = col[:, None] * inv[None, :]
    ang = np.concatenate([ang_r, ang_r, ang_c, ang_c], axis=-1).astype(np.float32)
    cos = np.cos(ang).astype(np.float32)
    sin = np.sin(ang).astype(np.float32)
    sgn = np.tile(np.concatenate([-np.ones(16), np.ones(16)]), 2).astype(np.float32)
    cos_all = np.concatenate([np.ones((256, 64), np.float32), cos], 0)
    sin_all = np.concatenate([np.zeros((256, 64), np.float32), sin * sgn[None, :]], 0)
    cos_t = cos_all.reshape(NT, 128, 64).transpose(1, 0, 2).reshape(128, NT * 64)
    sin_t = sin_all.reshape(NT, 128, 64).transpose(1, 0, 2).reshape(128, NT * 64)
    return cos_t, sin_t


def _pack_small(core, c, c_ctx, norm_w, ada_b, a_bs, a_ln_w, a_ln_b, lq1, lk1, lq2, lk2, subln, qn, kvn, final_w):
    sm = np.zeros((128, NS), np.float32)
    rep = lambda v: np.broadcast_to(np.asarray(v, np.float32)[None, :], (128, v.shape[0]))
    sm[:, LNW:LNW + 512] = rep(a_ln_w[0])
    sm[:, LNB:LNB + 512] = rep(a_ln_b[0])
    sm[:, SUBLN:SUBLN + 128] = rep(subln[0])
    sm[:, QN:QN + 256] = rep(qn[0])
    sm[:, KVN:KVN + 128] = rep(kvn[0])
    sm[:, FINAL:FINAL + 1024] = rep(final_w)
    sm[:, LQ1:LQ1 + 64] = rep(lq1[0])
    sm[:, LK1:LK1 + 64] = rep(lk1[0])
    sm[:, LQ2:LQ2 + 64] = rep(lq2[0])
    sm[:, LK2:LK2 + 64] = rep(lk2[0])
    sm[:, ABS:ABS + 8] = a_bs[0].T
    sm[:, NW:NW + 16] = norm_w.reshape(2, 8, 128).transpose(2, 0, 1).reshape(128, 16)
    sm[:, ADAB:ADAB + 48] = ada_b.reshape(2, 24, 128).transpose(2, 0, 1).reshape(128, 48)
    cond = np.stack([c[2 * core], c[2 * core + 1], c_ctx], 0)
    sm[:, COND:COND + 24] = cond.reshape(3, 8, 128).transpose(2, 1, 0).reshape(128, 24)
    cos_t, sin_t = _rope_tables()
    sm[:, COS:COS + NT * 64] = cos_t
    sm[:, SIN:SIN + NT * 64] = sin_t
    sm[:, IDENT:IDENT + 128] = np.eye(128, dtype=np.float32)
    return sm


def kernel(x, c, ctx, c_ctx, norm_w, ada_w, ada_b, even_w_in, a_ws, a_bs, a_ln_w, a_ln_b,
           b_lq1, b_lk1, b_lq2, b_lk2, b_subln_w, even_w_out, odd_w_in, c_q_norm_w, c_wq_b,
           c_kv_norm_w, c_wkv_b, odd_w_out, final_w, _dbg=False, _stop=99, _ncores=8):
    f = lambda a: np.ascontiguousarray(np.asarray(a, dtype=np.float32))
    x, c, ctx, c_ctx = f(x), f(c), f(ctx), f(c_ctx)
    key = (bool(_dbg), _stop)
    if key not in _NC_CACHE:
        _NC_CACHE[key] = build_program(dbg=bool(_dbg), stop=_stop)
    nc = _NC_CACHE[key]
    shared = {
        "ada_w": f(ada_w), "w_in0": f(even_w_in)[0], "w_out0": f(even_w_out)[0], "w_in1": f(odd_w_in)[0],
        "wq_b": f(c_wq_b)[0], "wkv_b": f(c_wkv_b)[0], "w_out1": f(odd_w_out)[0],
        "a_wsT": np.ascontiguousarray(f(a_ws)[0].transpose(2, 0, 1)),
        "adab_g": np.ascontiguousarray(f(ada_b)[:, 2048:3072]),
    }
    in_maps = []
    for core in range(_ncores):
        xin = np.concatenate([ctx[2 * core:2 * core + 2], x[2 * core:2 * core + 2]], axis=1)
        sm = _pack_small(core, c, c_ctx, f(norm_w), f(ada_b), f(a_bs), f(a_ln_w), f(a_ln_b), f(b_lq1), f(b_lk1),
                         f(b_lq2), f(b_lk2), f(b_subln_w), f(c_q_norm_w), f(c_kv_norm_w), f(final_w))
        m = {"xin": np.ascontiguousarray(xin), "small": sm}
        m.update(shared)
        in_maps.append(m)
    res = run_bass_kernel_spmd(nc, in_maps, core_ids=list(range(_ncores)))
    out = np.concatenate([r["out"] for r in res.results], axis=0).astype(np.float32)
    if _dbg:
        return out, np.concatenate([r["dbg_h"] for r in res.results], axis=0)
    return out
```

```python
import math
from contextlib import ExitStack
import numpy as np
import concourse.bass as bass
import concourse.mybir as mybir
from concourse.bass_utils import run_bass_kernel_spmd

F32 = mybir.dt.float32
BF16 = mybir.dt.bfloat16
AF = mybir.ActivationFunctionType
ALU = mybir.AluOpType

import os
XLT_DVE_ONLY = os.environ.get('XLT_DVE_ONLY', '0') == '1'
EPOCH = 3800
NT = 18
D = 1024

LNW, LNB, SUBLN, QN, KVN, FINAL = 0, 512, 1024, 1152, 1408, 1536
LQ1, LK1, LQ2, LK2 = 2560, 2624, 2688, 2752
ABS, NW, ADAB, COND, COS, SIN, IDENT = 2816, 2824, 2840, 2888, 2912, 4064, 5216
NS = 5344


class Buf:
    __slots__ = ("name", "lw", "rd", "dsem", "dval", "excl")

    def __init__(self, name, excl=False):
        self.name = name
        self.excl = excl
        self.lw = None
        self.rd = {}
        self.dsem = None
        self.dval = 0


class Sched:
    def __init__(self, nc, stack):
        self.nc = nc
        self.stack = stack
        self.names = ["pe", "act", "dve", "pool", "sp"]
        self.streams = {n: [] for n in self.names}
        self.sems = {n: [self._sem(f"s_{n}0")] for n in self.names}
        self.cnt = {n: 0 for n in self.names}
        self.seen = {n: {} for n in self.names}
        self.dbufs = []
        self.free_dsems = []

    def _sem(self, name):
        return self.stack.enter_context(self.nc.semaphore(name))

    def _cur(self, eng):
        if self.cnt[eng] >= EPOCH:
            self.sems[eng].append(self._sem(f"s_{eng}{len(self.sems[eng])}"))
            self.cnt[eng] = 0
        return self.sems[eng][-1]

    def _need(self, eng, waits, dep):
        if dep is None:
            return
        sem, val = dep
        if self.seen[eng].get(sem, 0) >= val:
            return
        if waits.get(sem, 0) < val:
            waits[sem] = val

    def op(self, eng, fn, R=(), W=()):
        if any(b.excl for b in R):
            W = list(W) + [b for b in R if b.excl and b not in W]
            R = [b for b in R if not b.excl]
        waits = {}
        own = set(id(s) for s in self.sems[eng])
        for b in R:
            self._need(eng, waits, b.lw)
        for b in W:
            if b.lw is not None and id(b.lw[0]) not in own:
                self._need(eng, waits, b.lw)
            for r in b.rd.items():
                if id(r[0]) in own:
                    continue
                self._need(eng, waits, r)
        sem = self._cur(eng)
        self.cnt[eng] += 1
        val = self.cnt[eng]
        for s, v in waits.items():
            self.seen[eng][s] = v
        self.streams[eng].append((list(waits.items()), fn, sem, 1))
        for b in W:
            b.lw = (sem, val)
            b.rd = {}
        for b in R:
            b.rd[sem] = val

    def dma(self, q, fn, R=(), W=()):
        waits = {}
        for b in R:
            self._need(q, waits, b.lw)
        tgt = W[0]
        if tgt.dsem is None:
            if self.free_dsems:
                tgt.dsem, tgt.dval = self.free_dsems.pop()
            else:
                tgt.dsem = self._sem(f"d{len(self.dbufs)}_{tgt.name}".replace("/", "_"))
            self.dbufs.append(tgt)
        for b in W:
            if b.lw is not None and b.lw[0] is not tgt.dsem:
                self._need(q, waits, b.lw)
            for r in b.rd.items():
                self._need(q, waits, r)
        for s, v in waits.items():
            self.seen[q][s] = v
        tgt.dval += 16
        self.streams[q].append((list(waits.items()), fn, tgt.dsem, 16))
        for b in W:
            b.lw = (tgt.dsem, tgt.dval)
            b.rd = {}
        for b in R:
            b.rd[tgt.dsem] = tgt.dval

    def barrier(self):
        deps = []
        for n in self.names:
            if self.cnt[n] > 0:
                deps.append((self.sems[n][-1], self.cnt[n]))
            for s in self.sems[n][:-1]:
                deps.append((s, EPOCH))
        for b in self.dbufs:
            deps.append((b.dsem, b.dval))
        for n in self.names:
            waits = {}
            for d in deps:
                self._need(n, waits, d)
            for s, v in waits.items():
                self.seen[n][s] = v
            if waits:
                self.streams[n].append((list(waits.items()), None, None, 0))
        for b in self.dbufs:
            if b.dval < 3000:
                self.free_dsems.append((b.dsem, b.dval))
            b.dsem = None
        self.dbufs = []

    def emit(self):
        with self.nc.Block() as block:
            def mk(name):
                def body(e):
                    for waits, fn, sem, inc in self.streams[name]:
                        for s, v in waits:
                            e.wait_ge(s, v)
                        if fn is not None:
                            fn(e).then_inc(sem, inc)
                return body
            block.tensor(mk("pe"))
            block.scalar(mk("act"))
            block.vector(mk("dve"))
            block.gpsimd(mk("pool"))
            block.sync(mk("sp"))


class Ring:
    def __init__(self, slots):
        self.slots, self.i = slots, 0

    def next(self):
        self.i += 1
        return self.slots[self.i % len(self.slots)]


class Region:
    def __init__(self, big, lo, hi):
        self.big, self.lo, self.hi, self.cur = big, lo, hi, lo

    def f32(self, words, name="t"):
        a = self.cur
        self.cur += (words + 7) // 8 * 8
        assert self.cur <= self.hi, f"SBUF region overflow at {name}: {self.cur} > {self.hi}"
        return self.big[:, a:a + words], Buf(name)

    def bf(self, elems, name="t"):
        ap, b = self.f32((elems + 1) // 2, name)
        return ap.bitcast(BF16), b

    def sub(self):
        return Region(self.big, self.cur, self.hi)


class _Stop(Exception):
    pass


def build_program(dbg=False, stop=99):
    nc = bass.Bass("TRN2", target_bir_lowering=False)

    def din(name, shape):
        return nc.dram_tensor(name, shape, F32, kind="ExternalInput").ap()

    xin = din("xin", [2, NT * 128, D])
    small_d = din("small", [128, NS])
    ada_w = din("ada_w", [2, D, 3 * D])
    w_in0 = din("w_in0", [D, 3584])
    w_out0 = din("w_out0", [D, D])
    w_in1 = din("w_in1", [D, 1472])
    wq_b = din("wq_b", [256, 1536])
    wkv_b = din("wkv_b", [128, 2048])
    w_out1 = din("w_out1", [D, D])
    a_wsT = din("a_wsT", [128, 8, 128])
    out_d = nc.dram_tensor("out", [2, 2048, D], F32, kind="ExternalOutput").ap()
    adab_g = din("adab_g", [2, D])
    if dbg:
        dbg_h = nc.dram_tensor("dbg_h", [2, NT * 128, D], F32, kind="ExternalOutput").ap()

    with ExitStack() as st:
        S = Sched(nc, st)
        BIGW = 53200
        big = st.enter_context(nc.sbuf_tensor("big", [128, BIGW], F32))[:, :]
        PS = []
        for i in range(7):
            PS.append((st.enter_context(nc.psum_tensor(f"ps{i}", [128, 512], F32))[:, :], Buf(f"ps{i}", excl=True)))
        PT = st.enter_context(nc.psum_tensor("pt", [128, 1024], BF16))[:, :]
        b_PT = Buf("pt", excl=True)
        PTb = [b_PT, b_PT]

        def mm(out, lhsT, rhs, start, stop, R, W, skip=False):
            S.op("pe", lambda e: e.matmul(out, lhsT=lhsT, rhs=rhs, start=start, stop=stop, skip_group_check=skip), R=R, W=W)

        def tr(out, in_, ident, R, W):
            S.op("pe", lambda e: e.transpose(out, in_, ident), R=R, W=W)

        def act(out, in_, func, R, W, scale=1.0, bias=0.0):
            S.op("act", lambda e: e.activation(out=out, in_=in_, func=func, bias=bias, scale=scale), R=R, W=W)

        def tt(eng, out, in0, in1, op, R, W):
            S.op(eng, lambda e: e.tensor_tensor(out=out, in0=in0, in1=in1, op=op), R=R, W=W)

        def ts(eng, out, in0, s1, s2, op0, op1, R, W):
            if s2 is None:
                S.op(eng, lambda e: e.tensor_scalar(out=out, in0=in0, scalar1=s1, scalar2=None, op0=op0), R=R, W=W)
            else:
                S.op(eng, lambda e: e.tensor_scalar(out=out, in0=in0, scalar1=s1, scalar2=s2, op0=op0, op1=op1), R=R, W=W)

        def stt(eng, out, in0, scalar, in1, op0, op1, R, W):
            S.op(eng, lambda e: e.scalar_tensor_tensor(out=out, in0=in0, scalar=scalar, in1=in1, op0=op0, op1=op1), R=R, W=W)

        def cp(eng, out, in_, R, W):
            if eng == "act":
                S.op("act", lambda e: e.copy(out=out, in_=in_), R=R, W=W)
            else:
                S.op(eng, lambda e: e.tensor_copy(out=out, in_=in_), R=R, W=W)

        def recip(out, in_, R, W):
            S.op("dve", lambda e: e.reciprocal(out=out, in_=in_), R=R, W=W)

        def dma(q, out, in_, R, W):
            S.dma(q, lambda e: e.dma_start(out=out, in_=in_), R=R, W=W)

        top = Region(big, 0, BIGW)
        small, b_small = top.f32(NS, "small")
        h_all, _ = top.f32(NT * D, "h")
        h = h_all.rearrange("p (t f) -> p t f", t=NT)
        b_h = [Buf(f"h{t}") for t in range(NT)]
        G_l, b_Gl = top.f32(D, "G_l")
        G_c, b_Gc = top.f32(D, "G_c")
        identb, b_identb = top.bf(128, "identb")
        wsT_f, b_wsT = top.bf(8 * 128, "wsT")
        wsT = wsT_f.rearrange("p (g q) -> p g q", g=8)
        biasT, b_biasT = top.f32(512, "biasT")
        mod_f, b_mod = top.f32(2 * 72, "mod")
        mod = mod_f.rearrange("p (l f r) -> p l f r", l=2, r=3)
        acol_f, b_acol = top.f32(2 * 3 * 8, "acol")
        acol = acol_f.rearrange("p (l r f) -> p l r f", l=2, r=3)
        scT_f, b_scT = top.bf(24, "scT")
        scT = scT_f.rearrange("p (k r) -> p k r", r=3)
        lamt, b_lam = top.f32(8, "lam")
        subln8, b_subln8 = top.f32(128, "subln8")
        stats, b_stats = top.f32(32, "stats")
        condrep_f, b_condrep = top.bf(8 * 128, "condrep")
        condrep = condrep_f.rearrange("p (k n) -> p k n", k=8)
        brow, b_brow = top.bf(1024, "brow")
        ones1, b_ones1 = top.bf(128, "ones1")
        wKV_f, b_wKV = Region(big, BIGW - 4096, BIGW).bf(8 * 1024, "wKV")
        wKV = wKV_f.rearrange("p (k n) -> p k n", k=8)
        phase0 = top.cur

        dma("sp", small, small_d[:, :], R=[], W=[b_small])
        dma("pool", identb, small_d[:, IDENT:IDENT + 128], R=[], W=[b_identb])
        dma("pool", wsT, a_wsT[:, :, :], R=[], W=[b_wsT])
        S.op("dve", lambda e: e.memset(ones1, 1.0), R=[], W=[b_ones1])

        act(scT, small[:, COND:COND + 24].rearrange("p (k r) -> p k r", r=3), AF.Silu, R=[b_small], W=[b_scT])
        setup = Region(big, phase0, BIGW)
        wada = []
        for i in range(2):
            ap, b = setup.bf(8 * 512, f"wada{i}")
            wada.append((ap.rearrange("p (k n) -> p k n", k=8), b))
        pcs = 0
        for l in range(2):
            pm, b_pm = PS[l]
            for j in range(6):
                wt, b_wt = wada[pcs % 2]
                pcs += 1
                dma("pool", wt, ada_w[l, :, j * 512:(j + 1) * 512].rearrange("(k p) n -> p k n", p=128), R=[], W=[b_wt])
                for fl in range(4):
                    fc = j * 4 + fl
                    for kc in range(8):
                        mm(pm[:, fc * 3:fc * 3 + 3], wt[:, kc, fl * 128:(fl + 1) * 128], scT[:, kc, :],
                           kc == 0, kc == 7, R=[b_wt, b_scT], W=[b_pm])
            tt("dve", mod[:, l, :, :], pm[:, 0:72].rearrange("p (f r) -> p f r", r=3),
               small[:, ADAB + l * 24:ADAB + (l + 1) * 24].unsqueeze(2).to_broadcast([128, 24, 3]), ALU.add,
               R=[b_pm, b_small], W=[b_mod])
            for r in range(3):
                stt("dve", acol[:, l, r, :], mod[:, l, 8:16, r], 1.0, small[:, NW + l * 8:NW + (l + 1) * 8],
                    ALU.add, ALU.mult, R=[b_mod, b_small], W=[b_acol])
        lt, b_lt = setup.f32(128, "lamtmp")
        tt("dve", lt[:, 0:64], small[:, LQ1:LQ1 + 64], small[:, LK1:LK1 + 64], ALU.mult, R=[b_small], W=[b_lt])
        tt("dve", lt[:, 64:128], small[:, LQ2:LQ2 + 64], small[:, LK2:LK2 + 64], ALU.mult, R=[b_small], W=[b_lt])
        S.op("dve", lambda e: e.reduce_sum(out=lamt[:, 0:2], in_=lt.rearrange("p (a b) -> p a b", a=2),
                                           axis=mybir.AxisListType.X), R=[b_lt], W=[b_lam])
        act(lamt[:, 2:4], lamt[:, 0:2], AF.Exp, R=[b_lam], W=[b_lam])
        tt("dve", lamt[:, 4:5], lamt[:, 3:4], lamt[:, 2:3], ALU.subtract, R=[b_lam], W=[b_lam])
        ts("dve", lamt[:, 5:6], lamt[:, 4:5], -0.2, None, ALU.add, None, R=[b_lam], W=[b_lam])
        neglam = lamt[:, 5:6]
        ts("dve", subln8, small[:, SUBLN:SUBLN + 128], 0.8, None, ALU.mult, None, R=[b_small], W=[b_subln8])
        cp("dve", biasT.rearrange("p (g d) -> p g d", g=8), small[:, ABS:ABS + 8].unsqueeze(2).to_broadcast([128, 8, 64]),
           R=[b_small], W=[b_biasT])

        def build_G(l, conds):
            load_w("pool", wKV, b_wKV, ada_w[l], 2048, 1024)
            dma("pool", brow[0:1, :], adab_g[l:l + 1, :], R=[], W=[b_brow])
            pg, b_pg = PS[6]
            for (r, G, b_G) in conds:
                cp("dve", condrep, scT[:, :, r:r + 1].to_broadcast([128, 8, 128]), R=[b_scT], W=[b_condrep])
                for n in range(2):
                    for kc in range(8):
                        mm(pg, condrep[:, kc, :], wKV[:, kc, n * 512:(n + 1) * 512], kc == 0, False, R=[b_condrep, b_wKV], W=[b_pg])
                    mm(pg, ones1[0:1, :], brow[0:1, n * 512:(n + 1) * 512], False, True, R=[b_ones1, b_brow], W=[b_pg])
                    cp("dve", G[:, n * 512:(n + 1) * 512], pg, R=[b_pg], W=[b_G])

        def rstd_of(reg_stats, src, b_src, n, eps, name="rs"):
            st_ap, b_st = reg_stats.next() if isinstance(reg_stats, Ring) else reg_stats
            nch = (n + 511) // 512
            w = n // nch
            for c in range(nch):
                S.op("dve", (lambda c: lambda e: e.bn_stats(out=st_ap[:, c * 6:(c + 1) * 6], in_=src[:, c * w:(c + 1) * w]))(c),
                     R=[b_src], W=[b_st])
            S.op("dve", lambda e: e.bn_aggr(out=st_ap[:, 12:14], in_=st_ap[:, 0:6 * nch].rearrange("p (c s) -> p c s", s=6)),
                 R=[b_st], W=[b_st])
            stt("dve", st_ap[:, 14:15], st_ap[:, 12:13], st_ap[:, 12:13], st_ap[:, 13:14], ALU.mult, ALU.add, R=[b_st], W=[b_st])
            act(st_ap[:, 15:16], st_ap[:, 14:15], AF.Ln, R=[b_st], W=[b_st], bias=eps)
            act(st_ap[:, 15:16], st_ap[:, 15:16], AF.Exp, R=[b_st], W=[b_st], scale=-0.5)
            return st_ap[:, 15:16], st_ap[:, 12:13], b_st

        def ckpt(k, bi):
            if stop == k:
                S.barrier()
                if dbg:
                    b_dbg = Buf("dbgstop")
                    for t in range(NT):
                        dma("sp", dbg_h[bi, t * 128:(t + 1) * 128, :], h[:, t, :], R=[b_h[t]], W=[b_dbg])
                    S.barrier()
                raise _Stop()

        def make_xlT(src, b_src, l, r, xn, b_xn, st_pair, xlT_dst, b_xlT, alt):
            rstd, _, b_st = rstd_of(st_pair, src, b_src, 1024, 1e-6)
            act(xn, src, AF.Identity, R=[b_src, b_st], W=[b_xn], scale=rstd)
            ckpt(0.57, 0)
            for half in range(2):
                bp = PTb[half]
                if half == 1:
                    ckpt(0.596, 0)
                for q in range(4):
                    fc = half * 4 + q
                    tr(PT[:, fc * 128:(fc + 1) * 128], xn[:, fc * 128:(fc + 1) * 128], identb, R=[b_xn, b_identb], W=[bp])
                if half == 1:
                    ckpt(0.597, 0)
                ckpt(0.58, 0)
                for q in range(4):
                    fc = half * 4 + q
                    sc = acol[:, l, r, fc:fc + 1]
                    bi = mod[:, l, fc, r:r + 1]
                    if q == 1:
                        ckpt(0.59, 0)
                    if q == 2:
                        ckpt(0.595, 0)
                    if (fc + alt) % 2 == 0 or XLT_DVE_ONLY:
                        ts("dve", xlT_dst[:, fc, :], PT[:, fc * 128:(fc + 1) * 128], sc, bi, ALU.mult, ALU.add,
                           R=[bp, b_acol, b_mod], W=[b_xlT])
                    else:
                        act(xlT_dst[:, fc, :], PT[:, fc * 128:(fc + 1) * 128], AF.Identity, R=[bp, b_acol, b_mod], W=[b_xlT],
                            scale=sc, bias=bi)

        def proj(ps, b_ps, xlT_tile, b_xlT, W, b_W, c0, n, nk=8):
            for kc in range(nk):
                mm(ps[:, 0:n], xlT_tile[:, kc, :], W[:, kc, c0:c0 + n], kc == 0, kc == nk - 1, R=[b_xlT, b_W], W=[b_ps])

        def rope(reg_t, src, b_src, ng, t, dst, b_dst, eng2="pool"):
            (tc_, b_tc), (ts_, b_ts) = reg_t
            n = ng * 64
            cosb = small[:, COS + t * 64:COS + (t + 1) * 64]
            sinb = small[:, SIN + t * 64:SIN + (t + 1) * 64]
            sv = src.rearrange("p (g s t d) -> p g s t d", g=ng, s=2, t=2)
            tv = ts_[:, 0:n].rearrange("p (g s t d) -> p g s t d", g=ng, s=2, t=2)
            sn = sinb.rearrange("p (s t d) -> p s t d", s=2, t=2)
            tt("dve", tc_[:, 0:n].rearrange("p (g d) -> p g d", g=ng), src.rearrange("p (g d) -> p g d", g=ng),
               cosb.unsqueeze(1).to_broadcast([128, ng, 64]), ALU.mult, R=[b_src, b_small], W=[b_tc])
            for hf in range(2):
                tt("dve", tv[:, :, :, hf, :], sv[:, :, :, 1 - hf, :],
                   sn[:, :, hf, :].unsqueeze(1).to_broadcast([128, ng, 2, 16]), ALU.mult, R=[b_src, b_small], W=[b_ts])
            tt(eng2, dst, tc_[:, 0:n], ts_[:, 0:n], ALU.add, R=[b_tc, b_ts], W=[b_dst])

        def load_w(q, dst, b_dst, src2d, c0, n):
            dma(q, dst, src2d[:, c0:c0 + n].rearrange("(k p) n -> p k n", p=128), R=[], W=[b_dst])

        S.barrier()

        def ckpt(k, bi):
            if stop == k:
                S.barrier()
                if dbg:
                    b_dbg = Buf("dbgstop")
                    for t in range(NT):
                        dma("sp", dbg_h[bi, t * 128:(t + 1) * 128, :], h[:, t, :], R=[b_h[t]], W=[b_dbg])
                    S.barrier()
                raise _Stop()

        try:
          ckpt(0, 0)
          for bi in range(2):
              L0 = Region(big, phase0, BIGW)
              wQZ_f, b_wQZ = L0.bf(8 * 1024, "wQZ")
              wQZ = wQZ_f.rearrange("p (k n) -> p k n", k=8)
              woB_f, b_woB = L0.bf(4 * 1024, "woB")
              woB = woB_f.rearrange("p (k n) -> p k n", k=4)
              build_G(0, [(bi, G_l, b_Gl), (2, G_c, b_Gc)])
              ckpt(0.5, bi)

              A = Region(big, L0.cur, BIGW - 4096)
              wA_f, b_wA = A.bf(8 * 1536, "wA")
              wA = wA_f.rearrange("p (k n) -> p k n", k=8)
              woA_f, b_woA = A.bf(4 * 1024, "woA")
              woA = woA_f.rearrange("p (k n) -> p k n", k=4)
              load_w("pool", wA, b_wA, w_in0, 0, 1536)
              dma("pool", woA, w_out0[0:512, :].rearrange("(k p) n -> p k n", p=128), R=[], W=[b_woA])
              xs = [A.f32(1024, f"xs{i}") for i in range(2)]
              xn, b_xn = A.bf(1024, "xn")
              xlTs = []
              for i in range(2):
                  ap, b = A.bf(8 * 128, f"xlT{i}")
                  xlTs.append((ap.rearrange("p (k n) -> p k n", k=8), b))
              stp = Ring([A.f32(16, f"st{i}") for i in range(4)])
              stp2 = A.f32(16, "st2")
              gu, b_gu = A.f32(512, "gu")
              gv, b_gv = A.f32(512, "gv")
              sz, b_sz = A.f32(512, "sz")
              xh, b_xh = A.f32(512, "xh")
              t1, b_t1 = xh, b_xh
              vn, b_vn = A.bf(512, "vn")
              mixa, b_mixa = A.bf(512, "mixa")
              mixT_f, b_mixT = vn, b_vn
              mixT = mixT_f.rearrange("p (k n) -> p k n", k=4)
              tmpo, b_tmpo = xh, b_xh
              load_w("pool", wKV, b_wKV, w_in0, 2048, 1024)
              for t in range(NT):
                  r = 2 if t < 2 else bi
                  G, b_G = (G_c, b_Gc) if t < 2 else (G_l, b_Gl)
                  x_t, b_x = xs[t % 2]
                  xlT, b_xlT = xlTs[t % 2]
                  dma("sp", x_t, xin[bi, t * 128:(t + 1) * 128, :], R=[], W=[b_x])
                  ckpt(0.55, bi)
                  make_xlT(x_t, b_x, 0, r, xn, b_xn, stp, xlT, b_xlT, t)
                  ckpt(0.6, bi)
                  for i in range(3):
                      proj(PS[i][0], PS[i][1], xlT, b_xlT, wA, b_wA, i * 512, 512)
                  ckpt(0.65, bi)
                  act(gu, PS[0][0], AF.Gelu, R=[PS[0][1]], W=[b_gu])
                  act(gv, PS[1][0], AF.Gelu, R=[PS[1][1]], W=[b_gv])
                  act(sz, PS[2][0], AF.Silu, R=[PS[2][1]], W=[b_sz])
                  st2, b_st2 = stp2
                  S.op("dve", (lambda st2, gv: lambda e: e.bn_stats(out=st2[:, 0:6], in_=gv))(st2, gv), R=[b_gv], W=[b_st2])
                  S.op("dve", (lambda st2: lambda e: e.bn_aggr(out=st2[:, 12:14], in_=st2[:, 0:6]))(st2), R=[b_st2], W=[b_st2])
                  act(st2[:, 15:16], st2[:, 13:14], AF.Ln, R=[b_st2], W=[b_st2], bias=1e-5)
                  act(st2[:, 15:16], st2[:, 15:16], AF.Exp, R=[b_st2], W=[b_st2], scale=-0.5)
                  ts("dve", xh, gv, st2[:, 12:13], st2[:, 15:16], ALU.subtract, ALU.mult, R=[b_gv, b_st2], W=[b_xh])
                  tt("pool", xh, xh, small[:, LNW:LNW + 512], ALU.mult, R=[b_xh, b_small], W=[b_xh])
                  tt("pool", vn, xh, small[:, LNB:LNB + 512], ALU.add, R=[b_xh, b_small], W=[b_vn])
                  tt("pool", gu, gu, sz, ALU.mult, R=[b_gu, b_sz], W=[b_gu])
                  psg, b_psg = PS[3]
                  for g in range(8):
                      mm(psg[:, g * 64:(g + 1) * 64], wsT[:, g, :], vn[:, g * 64:(g + 1) * 64], True, True,
                         R=[b_wsT, b_vn], W=[b_psg])
                  tt("dve", t1, psg, biasT, ALU.add, R=[b_psg, b_biasT], W=[b_t1])
                  tt("dve", mixa, t1, gu, ALU.mult, R=[b_t1, b_gu], W=[b_mixa])
                  for q in range(4):
                      tr(PT[:, q * 128:(q + 1) * 128], mixa[:, q * 128:(q + 1) * 128], identb, R=[b_mixa, b_identb], W=[PTb[0]])
                  cp("act", mixT, PT[:, 0:512].rearrange("p (k n) -> p k n", k=4), R=[PTb[0]], W=[b_mixT])
                  for n in range(2):
                      po, b_po = PS[4 + n]
                      for kc in range(4):
                          mm(po, mixT[:, kc, :], woA[:, kc, n * 512:(n + 1) * 512], kc == 0, kc == 3, R=[b_mixT, b_woA], W=[b_po])
                      tt("dve", tmpo, po, G[:, n * 512:(n + 1) * 512], ALU.mult, R=[b_po, b_G], W=[b_tmpo])
                      tt("pool", h[:, t, n * 512:(n + 1) * 512], tmpo, x_t[:, n * 512:(n + 1) * 512], ALU.add,
                         R=[b_tmpo, b_x], W=[b_h[t]])
                  ckpt(0.7, bi)
              S.barrier()
              ckpt(1, bi)

              KV = L0.sub()
              KT_f, b_KT = KV.bf(4 * 2304, "KT")
              KT = KT_f.rearrange("p (h n) -> p h n", h=4)
              V_f, b_V = KV.bf(NT * 4 * 132, "V")
              V = V_f.rearrange("p (t h d) -> p t h d", t=NT, h=4)
              K2 = Region(big, KV.cur, BIGW - 4096)
              xs = [K2.f32(1024, f"xs{i}") for i in range(2)]
              xn, b_xn = K2.bf(1024, "xn")
              xlTs = []
              for i in range(2):
                  ap, b = K2.bf(8 * 128, f"xlT{i}")
                  xlTs.append((ap.rearrange("p (k n) -> p k n", k=8), b))
              stp = Ring([K2.f32(16, f"st{i}") for i in range(4)])
              rt = (K2.f32(512, "tc"), K2.f32(512, "ts"))
              kr, b_kr = K2.bf(512, "kr")
              load_w("pool", wQZ[:, :, 0:512], b_wQZ, w_in0, 1536, 512)
              load_w("pool", wQZ[:, :, 512:1024], b_wQZ, w_in0, 3072, 512)
              dma("pool", woB, w_out0[512:1024, :].rearrange("(k p) n -> p k n", p=128), R=[], W=[b_woB])
              S.op("dve", (lambda V: lambda e: e.memset(V[:, :, :, 128:129], 1.0))(V), R=[], W=[b_V])
              for t in range(NT):
                  r = 2 if t < 2 else bi
                  x_t, b_x = xs[t % 2]
                  xlT, b_xlT = xlTs[t % 2]
                  dma("sp", x_t, xin[bi, t * 128:(t + 1) * 128, :], R=[], W=[b_x])
                  make_xlT(x_t, b_x, 0, r, xn, b_xn, stp, xlT, b_xlT, t)
                  proj(PS[0][0], PS[0][1], xlT, b_xlT, wKV, b_wKV, 0, 512)
                  proj(PS[1][0], PS[1][1], xlT, b_xlT, wKV, b_wKV, 512, 512)
                  rope(rt, PS[0][0], PS[0][1], 8, t, kr, b_kr)
                  for hh in range(4):
                      tr(PT[:, hh * 128:(hh + 1) * 128], kr[:, hh * 128:(hh + 1) * 128], identb, R=[b_kr, b_identb], W=[PTb[0]])
                  cp("act", KT[:, :, t * 128:(t + 1) * 128], PT[:, 0:512].rearrange("p (h n) -> p h n", h=4), R=[PTb[0]], W=[b_KT])
                  cp("act", V[:, t, :, 0:128], PS[1][0].rearrange("p (h d) -> p h d", h=4), R=[PS[1][1]], W=[b_V])
              S.barrier()
              ckpt(2, bi)

              Bp = KV.sub()
              xs = [Bp.f32(1024, f"xs{i}") for i in range(2)]
              xn, b_xn = Bp.bf(1024, "xn")
              xlT_f, b_xlT = Bp.bf(8 * 256, "xlT")
              xlT = xlT_f.rearrange("p (k n) -> p k n", k=8)
              stp = Ring([Bp.f32(16, f"st{i}") for i in range(4)])
              stp3 = Ring([Bp.f32(16, f"st3{i}") for i in range(4)])
              rt = (Bp.f32(512, "tc"), Bp.f32(512, "ts"))
              qr, b_qr = Bp.bf(512, "qr")
              QT_f, b_QT = Bp.bf(4 * 256, "QT")
              QT = QT_f.rearrange("p (h n) -> p h n", h=4)
              NPT = 3
              pts = [Bp.bf(512, f"pT{i}") for i in range(NPT)]
              oh, b_oh = Bp.f32(128, "oh")
              rr, b_rr = Bp.f32(8, "rr")
              omix_f, b_omix = Bp.f32(2 * 512, "omix")
              omix = omix_f.rearrange("p (j f) -> p j f", j=2)
              sbz, b_sbz = Bp.f32(512, "sbz")
              mixb_f, b_mixb = Bp.bf(2 * 512, "mixb")
              mixb = mixb_f.rearrange("p (j f) -> p j f", j=2)
              mixT_f, b_mixT = Bp.bf(4 * 256, "mixT")
              mixT = mixT_f.rearrange("p (k n) -> p k n", k=4)
              tmpo, b_tmpo = sbz, b_sbz
              ptc = 0
              for s in range(9):
                  tiles = [2 * s, 2 * s + 1]
                  r = 2 if s == 0 else bi
                  G, b_G = (G_c, b_Gc) if s == 0 else (G_l, b_Gl)
                  nch = 2 if s == 0 else NT
                  for j, t in enumerate(tiles):
                      x_t, b_x = xs[j]
                      dma("sp", x_t, xin[bi, t * 128:(t + 1) * 128, :], R=[], W=[b_x])
                      make_xlT(x_t, b_x, 0, r, xn, b_xn, stp, xlT[:, :, j * 128:(j + 1) * 128], b_xlT, j)
                  for j, t in enumerate(tiles):
                      proj(PS[5][0], PS[5][1], xlT[:, :, j * 128:(j + 1) * 128], b_xlT, wQZ, b_wQZ, 0, 512)
                      rope(rt, PS[5][0], PS[5][1], 8, t, qr, b_qr)
                      for hh in range(4):
                          tr(PT[:, hh * 128:(hh + 1) * 128], qr[:, hh * 128:(hh + 1) * 128], identb, R=[b_qr, b_identb], W=[PTb[0]])
                      cp("act", QT[:, :, j * 128:(j + 1) * 128], PT[:, 0:512].rearrange("p (h n) -> p h n", h=4), R=[PTb[0]], W=[b_QT])
                  items = [(hh, m, cpi) for hh in range(4) for m in range(2) for cpi in range(nch // 2)]

                  def qk0(it, idx):
                      hh, m, cpi = it
                      psc, b_psc = PS[idx % 3]
                      for cc in range(2):
                          c = 2 * cpi + cc
                          mm(psc[:, cc * 256:(cc + 1) * 256], KT[m * 64:(m + 1) * 64, hh, c * 128:(c + 1) * 128],
                             QT[m * 64:(m + 1) * 64, hh, :], True, True, R=[b_KT, b_QT], W=[b_psc])

                  def pv0(it, idx, pT, b_pT):
                      hh, m, cpi = it
                      po, b_po = PS[3 + m]
                      for cc in range(2):
                          c = 2 * cpi + cc
                          for j in range(2):
                              mm(po[:, j * 132:j * 132 + 129], pT[:, cc * 256 + j * 128:cc * 256 + (j + 1) * 128],
                                 V[:, c, hh, 0:129], c == 0 and j == 0, c == nch - 1, R=[b_pT, b_V], W=[b_po], skip=True)

                  qk0(items[0], 0)
                  for idx, it in enumerate(items):
                      hh, m, cpi = it
                      psc, b_psc = PS[idx % 3]
                      pT, b_pT = pts[ptc % NPT]
                      ptc += 1
                      act(pT, psc, AF.Exp, R=[b_psc], W=[b_pT], scale=0.125)
                      if idx + 1 < len(items):
                          qk0(items[idx + 1], idx + 1)
                      pv0(it, idx, pT, b_pT)
                      if not (m == 1 and cpi == nch // 2 - 1):
                          continue
                      p0, b_p0 = PS[3]
                      p1, b_p1 = PS[4]
                      for j in range(2):
                          recip(rr[:, 0:1], p0[:, j * 132 + 128:j * 132 + 129], R=[b_p0], W=[b_rr])
                          recip(rr[:, 1:2], p1[:, j * 132 + 128:j * 132 + 129], R=[b_p1], W=[b_rr])
                          tt("dve", rr[:, 2:3], rr[:, 1:2], neglam, ALU.mult, R=[b_rr, b_lam], W=[b_rr])
                          ts("dve", oh, p0[:, j * 132:j * 132 + 128], rr[:, 0:1], None, ALU.mult, None, R=[b_p0, b_rr], W=[b_oh])
                          stt("dve", oh, p1[:, j * 132:j * 132 + 128], rr[:, 2:3], oh, ALU.mult, ALU.add, R=[b_p1, b_rr, b_oh], W=[b_oh])
                          rstd, _, b_st3 = rstd_of(stp3, oh, b_oh, 128, 1e-5)
                          stt("dve", omix[:, j, hh * 128:(hh + 1) * 128], oh, rstd, subln8, ALU.mult, ALU.mult,
                              R=[b_oh, b_st3, b_subln8], W=[b_omix])
                  for j, t in enumerate(tiles):
                      proj(PS[5][0], PS[5][1], xlT[:, :, j * 128:(j + 1) * 128], b_xlT, wQZ, b_wQZ, 512, 512)
                      act(sbz, PS[5][0], AF.Silu, R=[PS[5][1]], W=[b_sbz])
                      tt("dve", mixb[:, j, :], omix[:, j, :], sbz, ALU.mult, R=[b_omix, b_sbz], W=[b_mixb])
                      for q in range(4):
                          tr(PT[:, 512 + q * 128:512 + (q + 1) * 128], mixb[:, j, q * 128:(q + 1) * 128], identb,
                             R=[b_mixb, b_identb], W=[PTb[1]])
                      cp("act", mixT[:, :, j * 128:(j + 1) * 128], PT[:, 512:1024].rearrange("p (k n) -> p k n", k=4),
                         R=[PTb[1]], W=[b_mixT])
                  for j, t in enumerate(tiles):
                      for n in range(2):
                          po, b_po = PS[5 + n]
                          for kc in range(4):
                              mm(po, mixT[:, kc, j * 128:(j + 1) * 128], woB[:, kc, n * 512:(n + 1) * 512], kc == 0, kc == 3,
                                 R=[b_mixT, b_woB], W=[b_po])
                          tt("dve", tmpo, po, G[:, n * 512:(n + 1) * 512], ALU.mult, R=[b_po, b_G], W=[b_tmpo])
                          tt("pool", h[:, t, n * 512:(n + 1) * 512], tmpo, h[:, t, n * 512:(n + 1) * 512], ALU.add,
                             R=[b_tmpo, b_h[t]], W=[b_h[t]])
              S.barrier()
              ckpt(3, bi)
              if dbg:
                  b_dbg = Buf(f"dbg{bi}")
                  for t in range(NT):
                      dma("sp", dbg_h[bi, t * 128:(t + 1) * 128, :], h[:, t, :], R=[b_h[t]], W=[b_dbg])
                  S.barrier()

              L1 = Region(big, phase0, BIGW)
              build_G(1, [(bi, G_l, b_Gl)])
              w1_f, b_w1 = L1.bf(8 * 1472, "w_in1")
              w1 = w1_f.rearrange("p (k n) -> p k n", k=8)
              wqn_f, b_wqn = L1.bf(2 * 8 * 128, "wqn")
              wqn = wqn_f.rearrange("p (k h d) -> p k h d", k=2, h=8)
              wqr_f, b_wqr = L1.bf(2 * 8 * 64, "wqr")
              wqr = wqr_f.rearrange("p (k h d) -> p k h d", k=2, h=8)
              wkv_f, b_wkv = L1.bf(2048, "wkv")
              wkv = wkv_f.rearrange("p (h d) -> p h d", h=8)
              WkT_f, b_WkT = L1.bf(8 * 128, "WkT")
              WkT = WkT_f.rearrange("p (h c) -> p h c", h=8)
              wo1_f, b_wo1 = L1.bf(8 * 1024, "wo1")
              wo1 = wo1_f.rearrange("p (k n) -> p k n", k=8)
              ckvT, b_ckvT = L1.bf(2304, "ckvT")
              krT, b_krT = L1.bf(2304, "krT")
              VS_f, b_VS = L1.bf(NT * 132, "VS")
              VS = VS_f.rearrange("p (t d) -> p t d", t=NT)
              load_w("pool", w1, b_w1, w_in1, 0, 1472)
              wq3 = wq_b.rearrange("(k p) (h d) -> p k h d", p=128, h=8)
              for kc in range(2):
                  dma("pool", wqn[:, kc, :, :], wq3[:, kc, :, 0:128], R=[], W=[b_wqn])
                  dma("pool", wqr[:, kc, :, :], wq3[:, kc, :, 128:192], R=[], W=[b_wqr])
              dma("pool", wkv, wkv_b.rearrange("p (h d) -> p h d", h=8), R=[], W=[b_wkv])
              dma("pool", wo1, w_out1[:, :].rearrange("(k p) n -> p k n", p=128), R=[], W=[b_wo1])
              for hh in range(8):
                  tr(PT[:, hh * 128:(hh + 1) * 128], wkv[:, hh, 0:128], identb, R=[b_wkv, b_identb], W=[PTb[hh // 4]])
              cp("dve", WkT[:, 0:4, :], PT[:, 0:512].rearrange("p (h c) -> p h c", h=4), R=[PTb[0]], W=[b_WkT])
              cp("dve", WkT[:, 4:8, :], PT[:, 512:1024].rearrange("p (h c) -> p h c", h=4), R=[PTb[1]], W=[b_WkT])

              ckpt(3.5, bi)
              S.barrier()
              K1 = L1.sub()
              xn, b_xn = K1.bf(1024, "xn")
              xlTs = []
              for i in range(2):
                  ap, b = K1.bf(8 * 128, f"xlT{i}")
                  xlTs.append((ap.rearrange("p (k n) -> p k n", k=8), b))
              stp = Ring([K1.f32(16, f"st{i}") for i in range(4)])
              stp3 = Ring([K1.f32(16, f"st3{i}") for i in range(4)])
              rt = (K1.f32(64, "tc"), K1.f32(64, "ts"))
              krd, b_krd = K1.bf(128, "krd")
              S.op("dve", (lambda VS: lambda e: e.memset(VS[:, :, 128:129], 1.0))(VS), R=[], W=[b_VS])
              for t in range(NT):
                  r = 2 if t < 2 else bi
                  xlT, b_xlT = xlTs[t % 2]
                  make_xlT(h[:, t, :], b_h[t], 1, r, xn, b_xn, stp, xlT, b_xlT, t)
                  pk, b_pk = PS[t % 2]
                  if os.environ.get("NOPROJ", "0") == "1":
                      continue
                  if os.environ.get("USE_WO1", "0") == "1":
                      proj(pk, b_pk, xlT, b_xlT, wo1, b_wo1, 256, 192)
                  else:
                      proj(pk, b_pk, xlT, b_xlT, w1, b_w1, 256, int(os.environ.get("KVN_N", "192")))
                  import os as _os
                  _v = int(_os.environ.get("KV1V", "9"))
                  if _v < 1:
                      continue
                  rstd, _, b_st3 = rstd_of(stp3, pk[:, 0:128], b_pk, 128, 1e-6)
                  stt("dve", VS[:, t, 0:128], pk[:, 0:128], rstd, small[:, KVN:KVN + 128], ALU.mult, ALU.mult,
                      R=[b_pk, b_st3, b_small], W=[b_VS])
                  if _v < 2:
                      continue
                  rope(rt, pk[:, 128:192], b_pk, 1, t, krd[:, 0:64], b_krd, eng2="dve")
                  cp("dve", krd[:, 64:128], krd[:, 0:64], R=[b_krd], W=[b_krd])
                  if _v < 3:
                      continue
                  tr(PT[:, 0:128], VS[:, t, 0:128], identb, R=[b_VS, b_identb], W=[PTb[0]])
                  tr(PT[:, 128:256], krd, identb, R=[b_krd, b_identb], W=[PTb[0]])
                  cp("act", ckvT[:, t * 128:(t + 1) * 128], PT[:, 0:128], R=[PTb[0]], W=[b_ckvT])
                  cp("act", krT[:, t * 128:(t + 1) * 128], PT[:, 128:256], R=[PTb[0]], W=[b_krT])
              S.barrier()
              ckpt(4, bi)

              Q1 = L1.sub()
              xn, b_xn = Q1.bf(1024, "xn")
              xlT_f, b_xlT = Q1.bf(8 * 256, "xlT/mixT")
              xlT = xlT_f.rearrange("p (k n) -> p k n", k=8)
              mixT, b_mixT = xlT, b_xlT
              stp = Ring([Q1.f32(16, f"st{i}") for i in range(4)])
              stp3 = Ring([Q1.f32(16, f"st3{i}") for i in range(4)])
              cqn, b_cqn = Q1.bf(256, "cqn")
              cqnT_f, b_cqnT = Q1.bf(2 * 256, "cqnT")
              cqnT = cqnT_f.rearrange("p (k n) -> p k n", k=2)
              qnTs = [Q1.bf(256, f"qnT{i}") for i in range(2)]
              qpT_f, b_qpT = Q1.bf(8 * 256, "qpT/onT")
              qpT = qpT_f.rearrange("p (h n) -> p h n", h=8)
              onT, b_onT = qpT, b_qpT
              rt = (Q1.f32(512, "tc"), Q1.f32(512, "ts"))
              qr, b_qr = Q1.bf(512, "qr")
              qrT_f, b_qrT = Q1.bf(4 * 256, "qrT")
              qrT = qrT_f.rearrange("p (g n) -> p g n", g=4)
              pts = [Q1.bf(512, f"pT{i}") for i in range(NPT)]
              rr, b_rr = Q1.f32(8, "rr")
              on_f, b_on = Q1.bf(2 * 1024, "on")
              on = on_f.rearrange("p (j f) -> p j f", j=2)
              szz, b_szz = Q1.f32(512, "sz")
              mix_f, b_mix = G_c.bitcast(BF16), Buf("mix")
              mix = mix_f.rearrange("p (j f) -> p j f", j=2)
              tmpo, b_tmpo = szz, b_szz
              ybuf = [(h[:, i, :], Buf(f"y{i}")) for i in range(2)]
              b_outd = [Buf(f"outd{bi}{i}") for i in range(2)]
              sc1 = 1.0 / math.sqrt(192.0)
              ptc = 0
              yc = 0
              for s in range(1, 9):
                  tiles = [2 * s, 2 * s + 1]
                  for j, t in enumerate(tiles):
                      make_xlT(h[:, t, :], b_h[t], 1, bi, xn, b_xn, stp, xlT[:, :, j * 128:(j + 1) * 128], b_xlT, j)
                  for j, t in enumerate(tiles):
                      pq, b_pq = PS[5]
                      proj(pq, b_pq, xlT[:, :, j * 128:(j + 1) * 128], b_xlT, w1, b_w1, 0, 256)
                      rstd, _, b_st3 = rstd_of(stp3, pq[:, 0:256], b_pq, 256, 1e-6)
                      stt("dve", cqn, pq[:, 0:256], rstd, small[:, QN:QN + 256], ALU.mult, ALU.mult,
                          R=[b_pq, b_st3, b_small], W=[b_cqn])
                      for kc in range(2):
                          tr(PT[:, kc * 128:(kc + 1) * 128], cqn[:, kc * 128:(kc + 1) * 128], identb, R=[b_cqn, b_identb], W=[PTb[0]])
                      cp("act", cqnT[:, :, j * 128:(j + 1) * 128], PT[:, 0:256].rearrange("p (k n) -> p k n", k=2),
                         R=[PTb[0]], W=[b_cqnT])
                  for j, t in enumerate(tiles):
                      pq, b_pq = PS[5]
                      for kc in range(2):
                          mm(pq, cqnT[:, kc, j * 128:(j + 1) * 128], wqr[:, kc, :, :], kc == 0, kc == 1, R=[b_cqnT, b_wqr], W=[b_pq])
                      rope(rt, pq, b_pq, 8, t, qr, b_qr)
                      for g in range(4):
                          tr(PT[:, 512 + g * 128:512 + (g + 1) * 128], qr[:, g * 128:(g + 1) * 128], identb,
                             R=[b_qr, b_identb], W=[PTb[1]])
                      cp("act", qrT[:, :, j * 128:(j + 1) * 128], PT[:, 512:1024].rearrange("p (g n) -> p g n", g=4),
                         R=[PTb[1]], W=[b_qrT])
                  for hh in range(8):
                      pq, b_pq = PS[5 + hh % 2]
                      qnT, b_qnT = qnTs[hh % 2]
                      for kc in range(2):
                          mm(pq[:, 0:256], wqn[:, kc, hh, :], cqnT[:, kc, :], kc == 0, kc == 1, R=[b_wqn, b_cqnT], W=[b_pq])
                      cp("dve", qnT, pq[:, 0:256], R=[b_pq], W=[b_qnT])
                      mm(pq[:, 256:512], WkT[:, hh, :], qnT, True, True, R=[b_WkT, b_qnT], W=[b_pq])
                      cp("dve", qpT[:, hh, :], pq[:, 256:512], R=[b_pq], W=[b_qpT])
                  items = [(hh, cpi) for hh in range(8) for cpi in range(NT // 2)]

                  def qk1(it, idx):
                      hh, cpi = it
                      hp = hh % 2
                      psc, b_psc = PS[idx % 3]
                      for cc in range(2):
                          c = 2 * cpi + cc
                          mm(psc[:, cc * 256:(cc + 1) * 256], ckvT[:, c * 128:(c + 1) * 128], qpT[:, hh, :], True, False,
                             R=[b_ckvT, b_qpT], W=[b_psc])
                          mm(psc[:, cc * 256:(cc + 1) * 256], krT[hp * 64:(hp + 1) * 64, c * 128:(c + 1) * 128],
                             qrT[hp * 64:(hp + 1) * 64, hh // 2, :], False, True, R=[b_krT, b_qrT], W=[b_psc])

                  def pv1(it, pT, b_pT):
                      hh, cpi = it
                      po, b_po = PS[3 + hh % 2]
                      for cc in range(2):
                          c = 2 * cpi + cc
                          for j in range(2):
                              mm(po[:, j * 132:j * 132 + 129], pT[:, cc * 256 + j * 128:cc * 256 + (j + 1) * 128],
                                 VS[:, c, 0:129], c == 0 and j == 0, c == NT - 1, R=[b_pT, b_VS], W=[b_po], skip=True)

                  qk1(items[0], 0)
                  for idx, it in enumerate(items):
                      hh, cpi = it
                      psc, b_psc = PS[idx % 3]
                      pT, b_pT = pts[ptc % NPT]
                      ptc += 1
                      act(pT, psc, AF.Exp, R=[b_psc], W=[b_pT], scale=sc1)
                      if idx + 1 < len(items):
                          qk1(items[idx + 1], idx + 1)
                      pv1(it, pT, b_pT)
                      if cpi != NT // 2 - 1:
                          continue
                      po, b_po = PS[3 + hh % 2]
                      for j in range(2):
                          recip(rr[:, j:j + 1], po[:, j * 132 + 128:j * 132 + 129], R=[b_po], W=[b_rr])
                          ts("dve", on[:, j, hh * 128:(hh + 1) * 128], po[:, j * 132:j * 132 + 128], rr[:, j:j + 1], None,
                             ALU.mult, None, R=[b_po, b_rr], W=[b_on])
                  for j, t in enumerate(tiles):
                      for hh in range(8):
                          tr(PT[:, hh * 128:(hh + 1) * 128], on[:, j, hh * 128:(hh + 1) * 128], identb, R=[b_on, b_identb],
                             W=[PTb[hh // 4]])
                      cp("act", onT[:, 0:4, j * 128:(j + 1) * 128], PT[:, 0:512].rearrange("p (h n) -> p h n", h=4),
                         R=[PTb[0]], W=[b_onT])
                      cp("dve", onT[:, 4:8, j * 128:(j + 1) * 128], PT[:, 512:1024].rearrange("p (h n) -> p h n", h=4),
                         R=[PTb[1]], W=[b_onT])
                  for j, t in enumerate(tiles):
                      for n in range(2):
                          pe_, b_pe = PS[5]
                          pz, b_pz = PS[6]
                          for q in range(4):
                              hh = n * 4 + q
                              mm(pe_[:, q * 128:(q + 1) * 128], onT[:, hh, j * 128:(j + 1) * 128], wkv[:, hh, 128:256], True, True,
                                 R=[b_onT, b_wkv], W=[b_pe])
                          proj(pz, b_pz, xlT[:, :, j * 128:(j + 1) * 128], b_xlT, w1, b_w1, 448 + n * 512, 512)
                          act(szz, pz, AF.Silu, R=[b_pz], W=[b_szz])
                          tt("dve", mix[:, j, n * 512:(n + 1) * 512], pe_, szz, ALU.mult, R=[b_pe, b_szz], W=[b_mix])
                  for j, t in enumerate(tiles):
                      for kc in range(8):
                          tr(PT[:, kc * 128:(kc + 1) * 128], mix[:, j, kc * 128:(kc + 1) * 128], identb, R=[b_mix, b_identb],
                             W=[PTb[kc // 4]])
                      cp("act", mixT[:, 0:4, j * 128:(j + 1) * 128], PT[:, 0:512].rearrange("p (k n) -> p k n", k=4),
                         R=[PTb[0]], W=[b_mixT])
                      cp("dve", mixT[:, 4:8, j * 128:(j + 1) * 128], PT[:, 512:1024].rearrange("p (k n) -> p k n", k=4),
                         R=[PTb[1]], W=[b_mixT])
                  for j, t in enumerate(tiles):
                      for n in range(2):
                          po, b_po = PS[5 + n]
                          for kc in range(8):
                              mm(po, mixT[:, kc, j * 128:(j + 1) * 128], wo1[:, kc, n * 512:(n + 1) * 512], kc == 0, kc == 7,
                                 R=[b_mixT, b_wo1], W=[b_po])
                          tt("dve", tmpo, po, G_l[:, n * 512:(n + 1) * 512], ALU.mult, R=[b_po, b_Gl], W=[b_tmpo])
                          tt("pool", h[:, t, n * 512:(n + 1) * 512], tmpo, h[:, t, n * 512:(n + 1) * 512], ALU.add,
                             R=[b_tmpo, b_h[t]], W=[b_h[t]])
                      y, b_y = ybuf[yc % 2]
                      rstd, _, b_stf = rstd_of(stp, h[:, t, :], b_h[t], 1024, 1e-6)
                      stt("dve", y, h[:, t, :], rstd, small[:, FINAL:FINAL + 1024], ALU.mult, ALU.mult,
                          R=[b_h[t], b_stf, b_small], W=[b_y])
                      dma("sp", out_d[bi, (t - 2) * 128:(t - 1) * 128, :], y, R=[b_y], W=[b_outd[yc % 2]])
                      yc += 1
              S.barrier()
        except _Stop:
            pass
        S.emit()
        build_program.last_S = S
    return nc


_NC_CACHE = {}


def _rope_tables():
    seq, gw, dim = 2048, 64, 64
    rows = seq // gw
    row = np.repeat(np.arange(rows), gw).astype(np.float32)
    col = np.tile(np.arange(gw), rows).astype(np.float32)
    half = dim // 2
    inv = (np.float32(10000.0) ** (-np.arange(0, half, 2, dtype=np.float32) / np.float32(half))).astype(np.float32)
    ang_r = row[:, None] * inv[None, :]
    ang_c = col[:, None] * inv[None, :]
    ang = np.concatenate([ang_r, ang_r, ang_c, ang_c], axis=-1).astype(np.float32)
    cos = np.cos(ang).astype(np.float32)
    sin = np.sin(ang).astype(np.float32)
    sgn = np.tile(np.concatenate([-np.ones(16), np.ones(16)]), 2).astype(np.float32)
    cos_all = np.concatenate([np.ones((256, 64), np.float32), cos], 0)
    sin_all = np.concatenate([np.zeros((256, 64), np.float32), sin * sgn[None, :]], 0)
    cos_t = cos_all.reshape(NT, 128, 64).transpose(1, 0, 2).reshape(128, NT * 64)
    sin_t = sin_all.reshape(NT, 128, 64).transpose(1, 0, 2).reshape(128, NT * 64)
    return cos_t, sin_t


def _pack_small(core, c, c_ctx, norm_w, ada_b, a_bs, a_ln_w, a_ln_b, lq1, lk1, lq2, lk2, subln, qn, kvn, final_w):
    sm = np.zeros((128, NS), np.float32)
    rep = lambda v: np.broadcast_to(np.asarray(v, np.float32)[None, :], (128, v.shape[0]))
    sm[:, LNW:LNW + 512] = rep(a_ln_w[0])
    sm[:, LNB:LNB + 512] = rep(a_ln_b[0])
    sm[:, SUBLN:SUBLN + 128] = rep(subln[0])
    sm[:, QN:QN + 256] = rep(qn[0])
    sm[:, KVN:KVN + 128] = rep(kvn[0])
    sm[:, FINAL:FINAL + 1024] = rep(final_w)
    sm[:, LQ1:LQ1 + 64] = rep(lq1[0])
    sm[:, LK1:LK1 + 64] = rep(lk1[0])
    sm[:, LQ2:LQ2 + 64] = rep(lq2[0])
    sm[:, LK2:LK2 + 64] = rep(lk2[0])
    sm[:, ABS:ABS + 8] = a_bs[0].T
    sm[:, NW:NW + 16] = norm_w.reshape(2, 8, 128).transpose(2, 0, 1).reshape(128, 16)
    sm[:, ADAB:ADAB + 48] = ada_b.reshape(2, 24, 128).transpose(2, 0, 1).reshape(128, 48)
    cond = np.stack([c[2 * core], c[2 * core + 1], c_ctx], 0)
    sm[:, COND:COND + 24] = cond.reshape(3, 8, 128).transpose(2, 1, 0).reshape(128, 24)
    cos_t, sin_t = _rope_tables()
    sm[:, COS:COS + NT * 64] = cos_t
    sm[:, SIN:SIN + NT * 64] = sin_t
    sm[:, IDENT:IDENT + 128] = np.eye(128, dtype=np.float32)
    return sm


def kernel(x, c, ctx, c_ctx, norm_w, ada_w, ada_b, even_w_in, a_ws, a_bs, a_ln_w, a_ln_b,
           b_lq1, b_lk1, b_lq2, b_lk2, b_subln_w, even_w_out, odd_w_in, c_q_norm_w, c_wq_b,
           c_kv_norm_w, c_wkv_b, odd_w_out, final_w, _dbg=False, _stop=99, _ncores=8):
    f = lambda a: np.ascontiguousarray(np.asarray(a, dtype=np.float32))
    x, c, ctx, c_ctx = f(x), f(c), f(ctx), f(c_ctx)
    key = (bool(_dbg), _stop)
    if key not in _NC_CACHE:
        _NC_CACHE[key] = build_program(dbg=bool(_dbg), stop=_stop)
    nc = _NC_CACHE[key]
    shared = {
        "ada_w": f(ada_w), "w_in0": f(even_w_in)[0], "w_out0": f(even_w_out)[0], "w_in1": f(odd_w_in)[0],
        "wq_b": f(c_wq_b)[0], "wkv_b": f(c_wkv_b)[0], "w_out1": f(odd_w_out)[0],
        "a_wsT": np.ascontiguousarray(f(a_ws)[0].transpose(2, 0, 1)),
        "adab_g": np.ascontiguousarray(f(ada_b)[:, 2048:3072]),
    }
    in_maps = []
    for core in range(_ncores):
        xin = np.concatenate([ctx[2 * core:2 * core + 2], x[2 * core:2 * core + 2]], axis=1)
        sm = _pack_small(core, c, c_ctx, f(norm_w), f(ada_b), f(a_bs), f(a_ln_w), f(a_ln_b), f(b_lq1), f(b_lk1),
                         f(b_lq2), f(b_lk2), f(b_subln_w), f(c_q_norm_w), f(c_kv_norm_w), f(final_w))
        m = {"xin": np.ascontiguousarray(xin), "small": sm}
        m.update(shared)
        in_maps.append(m)
    res = run_bass_kernel_spmd(nc, in_maps, core_ids=list(range(_ncores)))
    out = np.concatenate([r["out"] for r in res.results], axis=0).astype(np.float32)
    if _dbg:
        return out, np.concatenate([r["dbg_h"] for r in res.results], axis=0)
    return out
```

```python
import math
from contextlib import ExitStack
import numpy as np
import concourse.bass as bass
import concourse.mybir as mybir
from concourse.bass_utils import run_bass_kernel_spmd

F32 = mybir.dt.float32
BF16 = mybir.dt.bfloat16
AF = mybir.ActivationFunctionType
ALU = mybir.AluOpType

import os
XLT_DVE_ONLY = os.environ.get('XLT_DVE_ONLY', '0') == '1'
EPOCH = 3800
NT = 18
D = 1024

LNW, LNB, SUBLN, QN, KVN, FINAL = 0, 512, 1024, 1152, 1408, 1536
LQ1, LK1, LQ2, LK2 = 2560, 2624, 2688, 2752
ABS, NW, ADAB, COND, COS, SIN, IDENT = 2816, 2824, 2840, 2888, 2912, 4064, 5216
NS = 5344


class Buf:
    __slots__ = ("name", "lw", "rd", "dsem", "dval", "excl", "dq")

    def __init__(self, name, excl=False):
        self.name = name
        self.excl = excl
        self.lw = None
        self.rd = {}
        self.dsem = None
        self.dq = None
        self.dval = 0


class Sched:
    def __init__(self, nc, stack):
        self.nc = nc
        self.stack = stack
        self.names = ["pe", "act", "dve", "pool", "sp"]
        self.streams = {n: [] for n in self.names}
        self.sems = {n: [self._sem(f"s_{n}0")] for n in self.names}
        self.cnt = {n: 0 for n in self.names}
        self.seen = {n: {} for n in self.names}
        self.dbufs = []
        self.free_dsems = {}

    def _sem(self, name):
        return self.stack.enter_context(self.nc.semaphore(name))

    def _cur(self, eng):
        if self.cnt[eng] >= EPOCH:
            self.sems[eng].append(self._sem(f"s_{eng}{len(self.sems[eng])}"))
            self.cnt[eng] = 0
        return self.sems[eng][-1]

    def _need(self, eng, waits, dep):
        if dep is None:
            return
        sem, val = dep
        if self.seen[eng].get(sem, 0) >= val:
            return
        if waits.get(sem, 0) < val:
            waits[sem] = val

    def op(self, eng, fn, R=(), W=()):
        if any(b.excl for b in R):
            W = list(W) + [b for b in R if b.excl and b not in W]
            R = [b for b in R if not b.excl]
        waits = {}
        own = set(id(s) for s in self.sems[eng])
        for b in R:
            self._need(eng, waits, b.lw)
        for b in W:
            if b.lw is not None and id(b.lw[0]) not in own:
                self._need(eng, waits, b.lw)
            for r in b.rd.items():
                if id(r[0]) in own:
                    continue
                self._need(eng, waits, r)
        sem = self._cur(eng)
        self.cnt[eng] += 1
        val = self.cnt[eng]
        for s, v in waits.items():
            self.seen[eng][s] = v
        self.streams[eng].append((list(waits.items()), fn, sem, 1))
        for b in W:
            b.lw = (sem, val)
            b.rd = {}
        for b in R:
            b.rd[sem] = val

    def dma(self, q, fn, R=(), W=()):
        waits = {}
        for b in R:
            self._need(q, waits, b.lw)
        tgt = W[0]
        if tgt.dsem is None:
            tgt.dq = q
            if self.free_dsems.get(q):
                tgt.dsem, tgt.dval = self.free_dsems[q].pop()
            else:
                tgt.dsem = self._sem(f"d{len(self.dbufs)}_{tgt.name}".replace("/", "_"))
            self.dbufs.append(tgt)
        for b in W:
            if b.lw is not None and b.lw[0] is not tgt.dsem:
                self._need(q, waits, b.lw)
            for r in b.rd.items():
                self._need(q, waits, r)
        for s, v in waits.items():
            self.seen[q][s] = v
        tgt.dval += 16
        self.streams[q].append((list(waits.items()), fn, tgt.dsem, 16))
        for b in W:
            b.lw = (tgt.dsem, tgt.dval)
            b.rd = {}
        for b in R:
            b.rd[tgt.dsem] = tgt.dval

    def barrier(self):
        deps = []
        for n in self.names:
            if self.cnt[n] > 0:
                deps.append((self.sems[n][-1], self.cnt[n]))
            for s in self.sems[n][:-1]:
                deps.append((s, EPOCH))
        for b in self.dbufs:
            deps.append((b.dsem, b.dval))
        for n in self.names:
            waits = {}
            for d in deps:
                self._need(n, waits, d)
            for s, v in waits.items():
                self.seen[n][s] = v
            if waits:
                self.streams[n].append((list(waits.items()), None, None, 0))
        for b in self.dbufs:
            if b.dval < 3000:
                self.free_dsems.setdefault(b.dq, []).append((b.dsem, b.dval))
            b.dsem = None
        self.dbufs = []

    def emit(self):
        with self.nc.Block() as block:
            def mk(name):
                def body(e):
                    for waits, fn, sem, inc in self.streams[name]:
                        for s, v in waits:
                            e.wait_ge(s, v)
                        if fn is not None:
                            fn(e).then_inc(sem, inc)
                return body
            block.tensor(mk("pe"))
            block.scalar(mk("act"))
            block.vector(mk("dve"))
            block.gpsimd(mk("pool"))
            block.sync(mk("sp"))


class Ring:
    def __init__(self, slots):
        self.slots, self.i = slots, 0

    def next(self):
        self.i += 1
        return self.slots[self.i % len(self.slots)]


class Region:
    def __init__(self, big, lo, hi):
        self.big, self.lo, self.hi, self.cur = big, lo, hi, lo

    def f32(self, words, name="t"):
        a = self.cur
        self.cur += (words + 7) // 8 * 8
        assert self.cur <= self.hi, f"SBUF region overflow at {name}: {self.cur} > {self.hi}"
        return self.big[:, a:a + words], Buf(name)

    def bf(self, elems, name="t"):
        ap, b = self.f32((elems + 1) // 2, name)
        return ap.bitcast(BF16), b

    def sub(self):
        return Region(self.big, self.cur, self.hi)


class _Stop(Exception):
    pass


def build_program(dbg=False, stop=99):
    nc = bass.Bass("TRN2", target_bir_lowering=False)

    def din(name, shape):
        return nc.dram_tensor(name, shape, F32, kind="ExternalInput").ap()

    xin = din("xin", [2, NT * 128, D])
    small_d = din("small", [128, NS])
    ada_w = din("ada_w", [2, D, 3 * D])
    w_in0 = din("w_in0", [D, 3584])
    w_out0 = din("w_out0", [D, D])
    w_in1 = din("w_in1", [D, 1472])
    wq_b = din("wq_b", [256, 1536])
    wkv_b = din("wkv_b", [128, 2048])
    w_out1 = din("w_out1", [D, D])
    a_wsT = din("a_wsT", [128, 8, 128])
    out_d = nc.dram_tensor("out", [2, 2048, D], F32, kind="ExternalOutput").ap()
    adab_g = din("adab_g", [2, D])
    if dbg:
        dbg_h = nc.dram_tensor("dbg_h", [2, NT * 128, D], F32, kind="ExternalOutput").ap()

    with ExitStack() as st:
        S = Sched(nc, st)
        BIGW = 53200
        big = st.enter_context(nc.sbuf_tensor("big", [128, BIGW], F32))[:, :]
        PS = []
        for i in range(7):
            PS.append((st.enter_context(nc.psum_tensor(f"ps{i}", [128, 512], F32))[:, :], Buf(f"ps{i}", excl=True)))
        PT = st.enter_context(nc.psum_tensor("pt", [128, 1024], BF16))[:, :]
        b_PT = Buf("pt", excl=True)
        PTb = [b_PT, b_PT]

        def mm(out, lhsT, rhs, start, stop, R, W, skip=False):
            S.op("pe", lambda e: e.matmul(out, lhsT=lhsT, rhs=rhs, start=start, stop=stop, skip_group_check=skip), R=R, W=W)

        def tr(out, in_, ident, R, W):
            S.op("pe", lambda e: e.transpose(out, in_, ident), R=R, W=W)

        def act(out, in_, func, R, W, scale=1.0, bias=0.0):
            S.op("act", lambda e: e.activation(out=out, in_=in_, func=func, bias=bias, scale=scale), R=R, W=W)

        def tt(eng, out, in0, in1, op, R, W):
            S.op(eng, lambda e: e.tensor_tensor(out=out, in0=in0, in1=in1, op=op), R=R, W=W)

        def ts(eng, out, in0, s1, s2, op0, op1, R, W):
            if s2 is None:
                S.op(eng, lambda e: e.tensor_scalar(out=out, in0=in0, scalar1=s1, scalar2=None, op0=op0), R=R, W=W)
            else:
                S.op(eng, lambda e: e.tensor_scalar(out=out, in0=in0, scalar1=s1, scalar2=s2, op0=op0, op1=op1), R=R, W=W)

        def stt(eng, out, in0, scalar, in1, op0, op1, R, W):
            S.op(eng, lambda e: e.scalar_tensor_tensor(out=out, in0=in0, scalar=scalar, in1=in1, op0=op0, op1=op1), R=R, W=W)

        def cp(eng, out, in_, R, W):
            if eng == "act":
                S.op("act", lambda e: e.copy(out=out, in_=in_), R=R, W=W)
            else:
                S.op(eng, lambda e: e.tensor_copy(out=out, in_=in_), R=R, W=W)

        def recip(out, in_, R, W):
            S.op("dve", lambda e: e.reciprocal(out=out, in_=in_), R=R, W=W)

        def dma(q, out, in_, R, W):
            S.dma(q, lambda e: e.dma_start(out=out, in_=in_), R=R, W=W)

        top = Region(big, 0, BIGW)
        small, b_small = top.f32(NS, "small")
        h_all, _ = top.f32(NT * D, "h")
        h = h_all.rearrange("p (t f) -> p t f", t=NT)
        b_h = [Buf(f"h{t}") for t in range(NT)]
        G_l, b_Gl = top.f32(D, "G_l")
        G_c, b_Gc = top.f32(D, "G_c")
        identb, b_identb = top.bf(128, "identb")
        wsT_f, b_wsT = top.bf(8 * 128, "wsT")
        wsT = wsT_f.rearrange("p (g q) -> p g q", g=8)
        biasT, b_biasT = top.f32(512, "biasT")
        mod_f, b_mod = top.f32(2 * 72, "mod")
        mod = mod_f.rearrange("p (l f r) -> p l f r", l=2, r=3)
        acol_f, b_acol = top.f32(2 * 3 * 8, "acol")
        acol = acol_f.rearrange("p (l r f) -> p l r f", l=2, r=3)
        scT_f, b_scT = top.bf(24, "scT")
        scT = scT_f.rearrange("p (k r) -> p k r", r=3)
        lamt, b_lam = top.f32(8, "lam")
        subln8, b_subln8 = top.f32(128, "subln8")
        stats, b_stats = top.f32(32, "stats")
        condrep_f, b_condrep = top.bf(8 * 128, "condrep")
        condrep = condrep_f.rearrange("p (k n) -> p k n", k=8)
        brow, b_brow = top.bf(1024, "brow")
        ones1, b_ones1 = top.bf(128, "ones1")
        wKV_f, b_wKV = Region(big, BIGW - 4096, BIGW).bf(8 * 1024, "wKV")
        wKV = wKV_f.rearrange("p (k n) -> p k n", k=8)
        phase0 = top.cur

        dma("sp", small, small_d[:, :], R=[], W=[b_small])
        dma("pool", identb, small_d[:, IDENT:IDENT + 128], R=[], W=[b_identb])
        dma("pool", wsT, a_wsT[:, :, :], R=[], W=[b_wsT])
        S.op("dve", lambda e: e.memset(ones1, 1.0), R=[], W=[b_ones1])

        act(scT, small[:, COND:COND + 24].rearrange("p (k r) -> p k r", r=3), AF.Silu, R=[b_small], W=[b_scT])
        setup = Region(big, phase0, BIGW)
        wada = []
        for i in range(2):
            ap, b = setup.bf(8 * 512, f"wada{i}")
            wada.append((ap.rearrange("p (k n) -> p k n", k=8), b))
        pcs = 0
        for l in range(2):
            pm, b_pm = PS[l]
            for j in range(6):
                wt, b_wt = wada[pcs % 2]
                pcs += 1
                dma("pool", wt, ada_w[l, :, j * 512:(j + 1) * 512].rearrange("(k p) n -> p k n", p=128), R=[], W=[b_wt])
                for fl in range(4):
                    fc = j * 4 + fl
                    for kc in range(8):
                        mm(pm[:, fc * 3:fc * 3 + 3], wt[:, kc, fl * 128:(fl + 1) * 128], scT[:, kc, :],
                           kc == 0, kc == 7, R=[b_wt, b_scT], W=[b_pm])
            tt("dve", mod[:, l, :, :], pm[:, 0:72].rearrange("p (f r) -> p f r", r=3),
               small[:, ADAB + l * 24:ADAB + (l + 1) * 24].unsqueeze(2).to_broadcast([128, 24, 3]), ALU.add,
               R=[b_pm, b_small], W=[b_mod])
            for r in range(3):
                stt("dve", acol[:, l, r, :], mod[:, l, 8:16, r], 1.0, small[:, NW + l * 8:NW + (l + 1) * 8],
                    ALU.add, ALU.mult, R=[b_mod, b_small], W=[b_acol])
        lt, b_lt = setup.f32(128, "lamtmp")
        tt("dve", lt[:, 0:64], small[:, LQ1:LQ1 + 64], small[:, LK1:LK1 + 64], ALU.mult, R=[b_small], W=[b_lt])
        tt("dve", lt[:, 64:128], small[:, LQ2:LQ2 + 64], small[:, LK2:LK2 + 64], ALU.mult, R=[b_small], W=[b_lt])
        S.op("dve", lambda e: e.reduce_sum(out=lamt[:, 0:2], in_=lt.rearrange("p (a b) -> p a b", a=2),
                                           axis=mybir.AxisListType.X), R=[b_lt], W=[b_lam])
        act(lamt[:, 2:4], lamt[:, 0:2], AF.Exp, R=[b_lam], W=[b_lam])
        tt("dve", lamt[:, 4:5], lamt[:, 3:4], lamt[:, 2:3], ALU.subtract, R=[b_lam], W=[b_lam])
        ts("dve", lamt[:, 5:6], lamt[:, 4:5], -0.2, None, ALU.add, None, R=[b_lam], W=[b_lam])
        neglam = lamt[:, 5:6]
        ts("dve", subln8, small[:, SUBLN:SUBLN + 128], 0.8, None, ALU.mult, None, R=[b_small], W=[b_subln8])
        cp("dve", biasT.rearrange("p (g d) -> p g d", g=8), small[:, ABS:ABS + 8].unsqueeze(2).to_broadcast([128, 8, 64]),
           R=[b_small], W=[b_biasT])

        def build_G(l, conds):
            load_w("pool", wKV, b_wKV, ada_w[l], 2048, 1024)
            dma("pool", brow[0:1, :], adab_g[l:l + 1, :], R=[], W=[b_brow])
            pg, b_pg = PS[6]
            for (r, G, b_G) in conds:
                cp("dve", condrep, scT[:, :, r:r + 1].to_broadcast([128, 8, 128]), R=[b_scT], W=[b_condrep])
                for n in range(2):
                    for kc in range(8):
                        mm(pg, condrep[:, kc, :], wKV[:, kc, n * 512:(n + 1) * 512], kc == 0, False, R=[b_condrep, b_wKV], W=[b_pg])
                    mm(pg, ones1[0:1, :], brow[0:1, n * 512:(n + 1) * 512], False, True, R=[b_ones1, b_brow], W=[b_pg])
                    cp("dve", G[:, n * 512:(n + 1) * 512], pg, R=[b_pg], W=[b_G])

        def rstd_of(reg_stats, src, b_src, n, eps, name="rs"):
            st_ap, b_st = reg_stats.next() if isinstance(reg_stats, Ring) else reg_stats
            nch = (n + 511) // 512
            w = n // nch
            for c in range(nch):
                S.op("dve", (lambda c: lambda e: e.bn_stats(out=st_ap[:, c * 6:(c + 1) * 6], in_=src[:, c * w:(c + 1) * w]))(c),
                     R=[b_src], W=[b_st])
            S.op("dve", lambda e: e.bn_aggr(out=st_ap[:, 12:14], in_=st_ap[:, 0:6 * nch].rearrange("p (c s) -> p c s", s=6)),
                 R=[b_st], W=[b_st])
            stt("dve", st_ap[:, 14:15], st_ap[:, 12:13], st_ap[:, 12:13], st_ap[:, 13:14], ALU.mult, ALU.add, R=[b_st], W=[b_st])
            act(st_ap[:, 15:16], st_ap[:, 14:15], AF.Ln, R=[b_st], W=[b_st], bias=eps)
            act(st_ap[:, 15:16], st_ap[:, 15:16], AF.Exp, R=[b_st], W=[b_st], scale=-0.5)
            return st_ap[:, 15:16], st_ap[:, 12:13], b_st

        def ckpt(k, bi):
            if stop == k:
                S.barrier()
                if dbg:
                    b_dbg = Buf("dbgstop")
                    for t in range(NT):
                        dma("sp", dbg_h[bi, t * 128:(t + 1) * 128, :], h[:, t, :], R=[b_h[t]], W=[b_dbg])
                    S.barrier()
                raise _Stop()

        def make_xlT(src, b_src, l, r, xn, b_xn, st_pair, xlT_dst, b_xlT, alt):
            rstd, _, b_st = rstd_of(st_pair, src, b_src, 1024, 1e-6)
            act(xn, src, AF.Identity, R=[b_src, b_st], W=[b_xn], scale=rstd)
            ckpt(0.57, 0)
            for half in range(2):
                bp = PTb[half]
                if half == 1:
                    ckpt(0.596, 0)
                for q in range(4):
                    fc = half * 4 + q
                    tr(PT[:, fc * 128:(fc + 1) * 128], xn[:, fc * 128:(fc + 1) * 128], identb, R=[b_xn, b_identb], W=[bp])
                if half == 1:
                    ckpt(0.597, 0)
                ckpt(0.58, 0)
                for q in range(4):
                    fc = half * 4 + q
                    sc = acol[:, l, r, fc:fc + 1]
                    bi = mod[:, l, fc, r:r + 1]
                    if q == 1:
                        ckpt(0.59, 0)
                    if q == 2:
                        ckpt(0.595, 0)
                    if (fc + alt) % 2 == 0 or XLT_DVE_ONLY:
                        ts("dve", xlT_dst[:, fc, :], PT[:, fc * 128:(fc + 1) * 128], sc, bi, ALU.mult, ALU.add,
                           R=[bp, b_acol, b_mod], W=[b_xlT])
                    else:
                        act(xlT_dst[:, fc, :], PT[:, fc * 128:(fc + 1) * 128], AF.Identity, R=[bp, b_acol, b_mod], W=[b_xlT],
                            scale=sc, bias=bi)

        def proj(ps, b_ps, xlT_tile, b_xlT, W, b_W, c0, n, nk=8):
            for kc in range(nk):
                mm(ps[:, 0:n], xlT_tile[:, kc, :], W[:, kc, c0:c0 + n], kc == 0, kc == nk - 1, R=[b_xlT, b_W], W=[b_ps])

        def rope(reg_t, src, b_src, ng, t, dst, b_dst, eng2="pool"):
            (tc_, b_tc), (ts_, b_ts) = reg_t
            n = ng * 64
            cosb = small[:, COS + t * 64:COS + (t + 1) * 64]
            sinb = small[:, SIN + t * 64:SIN + (t + 1) * 64]
            sv = src.rearrange("p (g s t d) -> p g s t d", g=ng, s=2, t=2)
            tv = ts_[:, 0:n].rearrange("p (g s t d) -> p g s t d", g=ng, s=2, t=2)
            sn = sinb.rearrange("p (s t d) -> p s t d", s=2, t=2)
            tt("dve", tc_[:, 0:n].rearrange("p (g d) -> p g d", g=ng), src.rearrange("p (g d) -> p g d", g=ng),
               cosb.unsqueeze(1).to_broadcast([128, ng, 64]), ALU.mult, R=[b_src, b_small], W=[b_tc])
            for hf in range(2):
                tt("dve", tv[:, :, :, hf, :], sv[:, :, :, 1 - hf, :],
                   sn[:, :, hf, :].unsqueeze(1).to_broadcast([128, ng, 2, 16]), ALU.mult, R=[b_src, b_small], W=[b_ts])
            tt(eng2, dst, tc_[:, 0:n], ts_[:, 0:n], ALU.add, R=[b_tc, b_ts], W=[b_dst])

        def load_w(q, dst, b_dst, src2d, c0, n):
            dma(q, dst, src2d[:, c0:c0 + n].rearrange("(k p) n -> p k n", p=128), R=[], W=[b_dst])

        S.barrier()

        def ckpt(k, bi):
            if stop == k:
                S.barrier()
                if dbg:
                    b_dbg = Buf("dbgstop")
                    for t in range(NT):
                        dma("sp", dbg_h[bi, t * 128:(t + 1) * 128, :], h[:, t, :], R=[b_h[t]], W=[b_dbg])
                    S.barrier()
                raise _Stop()

        try:
          ckpt(0, 0)
          for bi in range(2):
              L0 = Region(big, phase0, BIGW)
              wQZ_f, b_wQZ = L0.bf(8 * 1024, "wQZ")
              wQZ = wQZ_f.rearrange("p (k n) -> p k n", k=8)
              woB_f, b_woB = L0.bf(4 * 1024, "woB")
              woB = woB_f.rearrange("p (k n) -> p k n", k=4)
              build_G(0, [(bi, G_l, b_Gl), (2, G_c, b_Gc)])
              ckpt(0.5, bi)

              A = Region(big, L0.cur, BIGW - 4096)
              wA_f, b_wA = A.bf(8 * 1536, "wA")
              wA = wA_f.rearrange("p (k n) -> p k n", k=8)
              woA_f, b_woA = A.bf(4 * 1024, "woA")
              woA = woA_f.rearrange("p (k n) -> p k n", k=4)
              load_w("pool", wA, b_wA, w_in0, 0, 1536)
              dma("pool", woA, w_out0[0:512, :].rearrange("(k p) n -> p k n", p=128), R=[], W=[b_woA])
              xs = [A.f32(1024, f"xs{i}") for i in range(2)]
              xn, b_xn = A.bf(1024, "xn")
              xlTs = []
              for i in range(2):
                  ap, b = A.bf(8 * 128, f"xlT{i}")
                  xlTs.append((ap.rearrange("p (k n) -> p k n", k=8), b))
              stp = Ring([A.f32(16, f"st{i}") for i in range(4)])
              stp2 = A.f32(16, "st2")
              gu, b_gu = A.f32(512, "gu")
              gv, b_gv = A.f32(512, "gv")
              sz, b_sz = A.f32(512, "sz")
              xh, b_xh = A.f32(512, "xh")
              t1, b_t1 = xh, b_xh
              vn, b_vn = A.bf(512, "vn")
              mixa, b_mixa = A.bf(512, "mixa")
              mixT_f, b_mixT = vn, b_vn
              mixT = mixT_f.rearrange("p (k n) -> p k n", k=4)
              tmpo, b_tmpo = xh, b_xh
              load_w("pool", wKV, b_wKV, w_in0, 2048, 1024)
              for t in range(NT):
                  r = 2 if t < 2 else bi
                  G, b_G = (G_c, b_Gc) if t < 2 else (G_l, b_Gl)
                  x_t, b_x = xs[t % 2]
                  xlT, b_xlT = xlTs[t % 2]
                  dma("sp", x_t, xin[bi, t * 128:(t + 1) * 128, :], R=[], W=[b_x])
                  ckpt(0.55, bi)
                  make_xlT(x_t, b_x, 0, r, xn, b_xn, stp, xlT, b_xlT, t)
                  ckpt(0.6, bi)
                  for i in range(3):
                      proj(PS[i][0], PS[i][1], xlT, b_xlT, wA, b_wA, i * 512, 512)
                  ckpt(0.65, bi)
                  act(gu, PS[0][0], AF.Gelu, R=[PS[0][1]], W=[b_gu])
                  act(gv, PS[1][0], AF.Gelu, R=[PS[1][1]], W=[b_gv])
                  act(sz, PS[2][0], AF.Silu, R=[PS[2][1]], W=[b_sz])
                  st2, b_st2 = stp2
                  S.op("dve", (lambda st2, gv: lambda e: e.bn_stats(out=st2[:, 0:6], in_=gv))(st2, gv), R=[b_gv], W=[b_st2])
                  S.op("dve", (lambda st2: lambda e: e.bn_aggr(out=st2[:, 12:14], in_=st2[:, 0:6]))(st2), R=[b_st2], W=[b_st2])
                  act(st2[:, 15:16], st2[:, 13:14], AF.Ln, R=[b_st2], W=[b_st2], bias=1e-5)
                  act(st2[:, 15:16], st2[:, 15:16], AF.Exp, R=[b_st2], W=[b_st2], scale=-0.5)
                  ts("dve", xh, gv, st2[:, 12:13], st2[:, 15:16], ALU.subtract, ALU.mult, R=[b_gv, b_st2], W=[b_xh])
                  tt("pool", xh, xh, small[:, LNW:LNW + 512], ALU.mult, R=[b_xh, b_small], W=[b_xh])
                  tt("pool", vn, xh, small[:, LNB:LNB + 512], ALU.add, R=[b_xh, b_small], W=[b_vn])
                  tt("pool", gu, gu, sz, ALU.mult, R=[b_gu, b_sz], W=[b_gu])
                  psg, b_psg = PS[3]
                  for g in range(8):
                      mm(psg[:, g * 64:(g + 1) * 64], wsT[:, g, :], vn[:, g * 64:(g + 1) * 64], True, True,
                         R=[b_wsT, b_vn], W=[b_psg])
                  tt("dve", t1, psg, biasT, ALU.add, R=[b_psg, b_biasT], W=[b_t1])
                  tt("dve", mixa, t1, gu, ALU.mult, R=[b_t1, b_gu], W=[b_mixa])
                  for q in range(4):
                      tr(PT[:, q * 128:(q + 1) * 128], mixa[:, q * 128:(q + 1) * 128], identb, R=[b_mixa, b_identb], W=[PTb[0]])
                  cp("act", mixT, PT[:, 0:512].rearrange("p (k n) -> p k n", k=4), R=[PTb[0]], W=[b_mixT])
                  for n in range(2):
                      po, b_po = PS[4 + n]
                      for kc in range(4):
                          mm(po, mixT[:, kc, :], woA[:, kc, n * 512:(n + 1) * 512], kc == 0, kc == 3, R=[b_mixT, b_woA], W=[b_po])
                      tt("dve", tmpo, po, G[:, n * 512:(n + 1) * 512], ALU.mult, R=[b_po, b_G], W=[b_tmpo])
                      tt("pool", h[:, t, n * 512:(n + 1) * 512], tmpo, x_t[:, n * 512:(n + 1) * 512], ALU.add,
                         R=[b_tmpo, b_x], W=[b_h[t]])
                  ckpt(0.7, bi)
              S.barrier()
              ckpt(1, bi)

              KV = L0.sub()
              KT_f, b_KT = KV.bf(4 * 2304, "KT")
              KT = KT_f.rearrange("p (h n) -> p h n", h=4)
              V_f, b_V = KV.bf(NT * 4 * 132, "V")
              V = V_f.rearrange("p (t h d) -> p t h d", t=NT, h=4)
              K2 = Region(big, KV.cur, BIGW - 4096)
              xs = [K2.f32(1024, f"xs{i}") for i in range(2)]
              xn, b_xn = K2.bf(1024, "xn")
              xlTs = []
              for i in range(2):
                  ap, b = K2.bf(8 * 128, f"xlT{i}")
                  xlTs.append((ap.rearrange("p (k n) -> p k n", k=8), b))
              stp = Ring([K2.f32(16, f"st{i}") for i in range(4)])
              rt = (K2.f32(512, "tc"), K2.f32(512, "ts"))
              kr, b_kr = K2.bf(512, "kr")
              load_w("pool", wQZ[:, :, 0:512], b_wQZ, w_in0, 1536, 512)
              load_w("pool", wQZ[:, :, 512:1024], b_wQZ, w_in0, 3072, 512)
              dma("pool", woB, w_out0[512:1024, :].rearrange("(k p) n -> p k n", p=128), R=[], W=[b_woB])
              S.op("dve", (lambda V: lambda e: e.memset(V[:, :, :, 128:129], 1.0))(V), R=[], W=[b_V])
              for t in range(NT):
                  r = 2 if t < 2 else bi
                  x_t, b_x = xs[t % 2]
                  xlT, b_xlT = xlTs[t % 2]
                  dma("sp", x_t, xin[bi, t * 128:(t + 1) * 128, :], R=[], W=[b_x])
                  make_xlT(x_t, b_x, 0, r, xn, b_xn, stp, xlT, b_xlT, t)
                  proj(PS[0][0], PS[0][1], xlT, b_xlT, wKV, b_wKV, 0, 512)
                  proj(PS[1][0], PS[1][1], xlT, b_xlT, wKV, b_wKV, 512, 512)
                  rope(rt, PS[0][0], PS[0][1], 8, t, kr, b_kr)
                  for hh in range(4):
                      tr(PT[:, hh * 128:(hh + 1) * 128], kr[:, hh * 128:(hh + 1) * 128], identb, R=[b_kr, b_identb], W=[PTb[0]])
                  cp("act", KT[:, :, t * 128:(t + 1) * 128], PT[:, 0:512].rearrange("p (h n) -> p h n", h=4), R=[PTb[0]], W=[b_KT])
                  cp("act", V[:, t, :, 0:128], PS[1][0].rearrange("p (h d) -> p h d", h=4), R=[PS[1][1]], W=[b_V])
              S.barrier()
              ckpt(2, bi)

              Bp = KV.sub()
              xs = [Bp.f32(1024, f"xs{i}") for i in range(2)]
              xn, b_xn = Bp.bf(1024, "xn")
              xlT_f, b_xlT = Bp.bf(8 * 256, "xlT")
              xlT = xlT_f.rearrange("p (k n) -> p k n", k=8)
              stp = Ring([Bp.f32(16, f"st{i}") for i in range(4)])
              stp3 = Ring([Bp.f32(16, f"st3{i}") for i in range(4)])
              rt = (Bp.f32(512, "tc"), Bp.f32(512, "ts"))
              qr, b_qr = Bp.bf(512, "qr")
              QT_f, b_QT = Bp.bf(4 * 256, "QT")
              QT = QT_f.rearrange("p (h n) -> p h n", h=4)
              NPT = 3
              pts = [Bp.bf(512, f"pT{i}") for i in range(NPT)]
              oh, b_oh = Bp.f32(128, "oh")
              rr, b_rr = Bp.f32(8, "rr")
              omix_f, b_omix = Bp.f32(2 * 512, "omix")
              omix = omix_f.rearrange("p (j f) -> p j f", j=2)
              sbz, b_sbz = Bp.f32(512, "sbz")
              mixb_f, b_mixb = Bp.bf(2 * 512, "mixb")
              mixb = mixb_f.rearrange("p (j f) -> p j f", j=2)
              mixT_f, b_mixT = Bp.bf(4 * 256, "mixT")
              mixT = mixT_f.rearrange("p (k n) -> p k n", k=4)
              tmpo, b_tmpo = sbz, b_sbz
              ptc = 0
              for s in range(9):
                  tiles = [2 * s, 2 * s + 1]
                  r = 2 if s == 0 else bi
                  G, b_G = (G_c, b_Gc) if s == 0 else (G_l, b_Gl)
                  nch = 2 if s == 0 else NT
                  for j, t in enumerate(tiles):
                      x_t, b_x = xs[j]
                      dma("sp", x_t, xin[bi, t * 128:(t + 1) * 128, :], R=[], W=[b_x])
                      make_xlT(x_t, b_x, 0, r, xn, b_xn, stp, xlT[:, :, j * 128:(j + 1) * 128], b_xlT, j)
                  for j, t in enumerate(tiles):
                      proj(PS[5][0], PS[5][1], xlT[:, :, j * 128:(j + 1) * 128], b_xlT, wQZ, b_wQZ, 0, 512)
                      rope(rt, PS[5][0], PS[5][1], 8, t, qr, b_qr)
                      for hh in range(4):
                          tr(PT[:, hh * 128:(hh + 1) * 128], qr[:, hh * 128:(hh + 1) * 128], identb, R=[b_qr, b_identb], W=[PTb[0]])
                      cp("act", QT[:, :, j * 128:(j + 1) * 128], PT[:, 0:512].rearrange("p (h n) -> p h n", h=4), R=[PTb[0]], W=[b_QT])
                  items = [(hh, m, cpi) for hh in range(4) for m in range(2) for cpi in range(nch // 2)]

                  def qk0(it, idx):
                      hh, m, cpi = it
                      psc, b_psc = PS[idx % 3]
                      for cc in range(2):
                          c = 2 * cpi + cc
                          mm(psc[:, cc * 256:(cc + 1) * 256], KT[m * 64:(m + 1) * 64, hh, c * 128:(c + 1) * 128],
                             QT[m * 64:(m + 1) * 64, hh, :], True, True, R=[b_KT, b_QT], W=[b_psc])

                  def pv0(it, idx, pT, b_pT):
                      hh, m, cpi = it
                      po, b_po = PS[3 + m]
                      for cc in range(2):
                          c = 2 * cpi + cc
                          for j in range(2):
                              mm(po[:, j * 132:j * 132 + 129], pT[:, cc * 256 + j * 128:cc * 256 + (j + 1) * 128],
                                 V[:, c, hh, 0:129], c == 0 and j == 0, c == nch - 1, R=[b_pT, b_V], W=[b_po], skip=True)

                  for k0 in range(min(2, len(items))):
                      qk0(items[k0], k0)
                  for idx, it in enumerate(items):
                      hh, m, cpi = it
                      psc, b_psc = PS[idx % 3]
                      pT, b_pT = pts[ptc % NPT]
                      ptc += 1
                      act(pT, psc, AF.Exp, R=[b_psc], W=[b_pT], scale=0.125)
                      if idx + 2 < len(items):
                          qk0(items[idx + 2], idx + 2)
                      pv0(it, idx, pT, b_pT)
                      if not (m == 1 and cpi == nch // 2 - 1):
                          continue
                      p0, b_p0 = PS[3]
                      p1, b_p1 = PS[4]
                      for j in range(2):
                          recip(rr[:, 0:1], p0[:, j * 132 + 128:j * 132 + 129], R=[b_p0], W=[b_rr])
                          recip(rr[:, 1:2], p1[:, j * 132 + 128:j * 132 + 129], R=[b_p1], W=[b_rr])
                          tt("dve", rr[:, 2:3], rr[:, 1:2], neglam, ALU.mult, R=[b_rr, b_lam], W=[b_rr])
                          ts("dve", oh, p0[:, j * 132:j * 132 + 128], rr[:, 0:1], None, ALU.mult, None, R=[b_p0, b_rr], W=[b_oh])
                          stt("dve", oh, p1[:, j * 132:j * 132 + 128], rr[:, 2:3], oh, ALU.mult, ALU.add, R=[b_p1, b_rr, b_oh], W=[b_oh])
                          rstd, _, b_st3 = rstd_of(stp3, oh, b_oh, 128, 1e-5)
                          stt("dve", omix[:, j, hh * 128:(hh + 1) * 128], oh, rstd, subln8, ALU.mult, ALU.mult,
                              R=[b_oh, b_st3, b_subln8], W=[b_omix])
                  for j, t in enumerate(tiles):
                      proj(PS[5][0], PS[5][1], xlT[:, :, j * 128:(j + 1) * 128], b_xlT, wQZ, b_wQZ, 512, 512)
                      act(sbz, PS[5][0], AF.Silu, R=[PS[5][1]], W=[b_sbz])
                      tt("dve", mixb[:, j, :], omix[:, j, :], sbz, ALU.mult, R=[b_omix, b_sbz], W=[b_mixb])
                      for q in range(4):
                          tr(PT[:, 512 + q * 128:512 + (q + 1) * 128], mixb[:, j, q * 128:(q + 1) * 128], identb,
                             R=[b_mixb, b_identb], W=[PTb[1]])
                      cp("act", mixT[:, :, j * 128:(j + 1) * 128], PT[:, 512:1024].rearrange("p (k n) -> p k n", k=4),
                         R=[PTb[1]], W=[b_mixT])
                  for j, t in enumerate(tiles):
                      for n in range(2):
                          po, b_po = PS[5 + n]
                          for kc in range(4):
                              mm(po, mixT[:, kc, j * 128:(j + 1) * 128], woB[:, kc, n * 512:(n + 1) * 512], kc == 0, kc == 3,
                                 R=[b_mixT, b_woB], W=[b_po])
                          tt("dve", tmpo, po, G[:, n * 512:(n + 1) * 512], ALU.mult, R=[b_po, b_G], W=[b_tmpo])
                          tt("pool", h[:, t, n * 512:(n + 1) * 512], tmpo, h[:, t, n * 512:(n + 1) * 512], ALU.add,
                             R=[b_tmpo, b_h[t]], W=[b_h[t]])
              S.barrier()
              ckpt(3, bi)
              if dbg:
                  b_dbg = Buf(f"dbg{bi}")
                  for t in range(NT):
                      dma("sp", dbg_h[bi, t * 128:(t + 1) * 128, :], h[:, t, :], R=[b_h[t]], W=[b_dbg])
                  S.barrier()

              L1 = Region(big, phase0, BIGW)
              build_G(1, [(bi, G_l, b_Gl)])
              w1_f, b_w1 = L1.bf(8 * 1472, "w_in1")
              w1 = w1_f.rearrange("p (k n) -> p k n", k=8)
              wqn_f, b_wqn = L1.bf(2 * 8 * 128, "wqn")
              wqn = wqn_f.rearrange("p (k h d) -> p k h d", k=2, h=8)
              wqr_f, b_wqr = L1.bf(2 * 8 * 64, "wqr")
              wqr = wqr_f.rearrange("p (k h d) -> p k h d", k=2, h=8)
              wkv_f, b_wkv = L1.bf(2048, "wkv")
              wkv = wkv_f.rearrange("p (h d) -> p h d", h=8)
              WkT_f, b_WkT = L1.bf(8 * 128, "WkT")
              WkT = WkT_f.rearrange("p (h c) -> p h c", h=8)
              wo1_f, b_wo1 = L1.bf(8 * 1024, "wo1")
              wo1 = wo1_f.rearrange("p (k n) -> p k n", k=8)
              ckvT, b_ckvT = L1.bf(2304, "ckvT")
              krT, b_krT = L1.bf(2304, "krT")
              VS_f, b_VS = L1.bf(NT * 132, "VS")
              VS = VS_f.rearrange("p (t d) -> p t d", t=NT)
              load_w("pool", w1, b_w1, w_in1, 0, 1472)
              wq3 = wq_b.rearrange("(k p) (h d) -> p k h d", p=128, h=8)
              for kc in range(2):
                  dma("pool", wqn[:, kc, :, :], wq3[:, kc, :, 0:128], R=[], W=[b_wqn])
                  dma("pool", wqr[:, kc, :, :], wq3[:, kc, :, 128:192], R=[], W=[b_wqr])
              dma("pool", wkv, wkv_b.rearrange("p (h d) -> p h d", h=8), R=[], W=[b_wkv])
              dma("pool", wo1, w_out1[:, :].rearrange("(k p) n -> p k n", p=128), R=[], W=[b_wo1])
              for hh in range(8):
                  tr(PT[:, hh * 128:(hh + 1) * 128], wkv[:, hh, 0:128], identb, R=[b_wkv, b_identb], W=[PTb[hh // 4]])
              cp("dve", WkT[:, 0:4, :], PT[:, 0:512].rearrange("p (h c) -> p h c", h=4), R=[PTb[0]], W=[b_WkT])
              cp("dve", WkT[:, 4:8, :], PT[:, 512:1024].rearrange("p (h c) -> p h c", h=4), R=[PTb[1]], W=[b_WkT])

              ckpt(3.5, bi)
              S.barrier()
              K1 = L1.sub()
              xn, b_xn = K1.bf(1024, "xn")
              xlTs = []
              for i in range(2):
                  ap, b = K1.bf(8 * 128, f"xlT{i}")
                  xlTs.append((ap.rearrange("p (k n) -> p k n", k=8), b))
              stp = Ring([K1.f32(16, f"st{i}") for i in range(4)])
              stp3 = Ring([K1.f32(16, f"st3{i}") for i in range(4)])
              rt = (K1.f32(64, "tc"), K1.f32(64, "ts"))
              krd, b_krd = K1.bf(128, "krd")
              S.op("dve", (lambda VS: lambda e: e.memset(VS[:, :, 128:129], 1.0))(VS), R=[], W=[b_VS])
              for t in range(NT):
                  r = 2 if t < 2 else bi
                  xlT, b_xlT = xlTs[t % 2]
                  make_xlT(h[:, t, :], b_h[t], 1, r, xn, b_xn, stp, xlT, b_xlT, t)
                  pk, b_pk = PS[t % 2]
                  if os.environ.get("NOPROJ", "0") == "1":
                      continue
                  if os.environ.get("USE_WO1", "0") == "1":
                      proj(pk, b_pk, xlT, b_xlT, wo1, b_wo1, 256, 192)
                  else:
                      proj(pk, b_pk, xlT, b_xlT, w1, b_w1, 256, int(os.environ.get("KVN_N", "192")))
                  import os as _os
                  _v = int(_os.environ.get("KV1V", "9"))
                  if _v < 1:
                      continue
                  rstd, _, b_st3 = rstd_of(stp3, pk[:, 0:128], b_pk, 128, 1e-6)
                  stt("dve", VS[:, t, 0:128], pk[:, 0:128], rstd, small[:, KVN:KVN + 128], ALU.mult, ALU.mult,
                      R=[b_pk, b_st3, b_small], W=[b_VS])
                  if _v < 2:
                      continue
                  rope(rt, pk[:, 128:192], b_pk, 1, t, krd[:, 0:64], b_krd, eng2="dve")
                  cp("dve", krd[:, 64:128], krd[:, 0:64], R=[b_krd], W=[b_krd])
                  if _v < 3:
                      continue
                  tr(PT[:, 0:128], VS[:, t, 0:128], identb, R=[b_VS, b_identb], W=[PTb[0]])
                  tr(PT[:, 128:256], krd, identb, R=[b_krd, b_identb], W=[PTb[0]])
                  cp("act", ckvT[:, t * 128:(t + 1) * 128], PT[:, 0:128], R=[PTb[0]], W=[b_ckvT])
                  cp("act", krT[:, t * 128:(t + 1) * 128], PT[:, 128:256], R=[PTb[0]], W=[b_krT])
              S.barrier()
              ckpt(4, bi)

              Q1 = L1.sub()
              xn, b_xn = Q1.bf(1024, "xn")
              xlT_f, b_xlT = Q1.bf(8 * 256, "xlT/mixT")
              xlT = xlT_f.rearrange("p (k n) -> p k n", k=8)
              mixT, b_mixT = xlT, b_xlT
              stp = Ring([Q1.f32(16, f"st{i}") for i in range(4)])
              stp3 = Ring([Q1.f32(16, f"st3{i}") for i in range(4)])
              cqn, b_cqn = Q1.bf(256, "cqn")
              cqnT_f, b_cqnT = Q1.bf(2 * 256, "cqnT")
              cqnT = cqnT_f.rearrange("p (k n) -> p k n", k=2)
              qnTs = [Q1.bf(256, f"qnT{i}") for i in range(2)]
              qpT_f, b_qpT = Q1.bf(8 * 256, "qpT/onT")
              qpT = qpT_f.rearrange("p (h n) -> p h n", h=8)
              onT, b_onT = qpT, b_qpT
              rt = (Q1.f32(512, "tc"), Q1.f32(512, "ts"))
              qr, b_qr = Q1.bf(512, "qr")
              qrT_f, b_qrT = Q1.bf(4 * 256, "qrT")
              qrT = qrT_f.rearrange("p (g n) -> p g n", g=4)
              pts = [Q1.bf(512, f"pT{i}") for i in range(NPT)]
              rr, b_rr = Q1.f32(8, "rr")
              on_f, b_on = Q1.bf(2 * 1024, "on")
              on = on_f.rearrange("p (j f) -> p j f", j=2)
              szz, b_szz = Q1.f32(512, "sz")
              mix_f, b_mix = G_c.bitcast(BF16), Buf("mix")
              mix = mix_f.rearrange("p (j f) -> p j f", j=2)
              tmpo, b_tmpo = szz, b_szz
              ybuf = [(h[:, i, :], Buf(f"y{i}")) for i in range(2)]
              b_outd = [Buf(f"outd{bi}{i}") for i in range(2)]
              sc1 = 1.0 / math.sqrt(192.0)
              ptc = 0
              yc = 0
              for s in range(1, 9):
                  tiles = [2 * s, 2 * s + 1]
                  for j, t in enumerate(tiles):
                      make_xlT(h[:, t, :], b_h[t], 1, bi, xn, b_xn, stp, xlT[:, :, j * 128:(j + 1) * 128], b_xlT, j)
                  for j, t in enumerate(tiles):
                      pq, b_pq = PS[5]
                      proj(pq, b_pq, xlT[:, :, j * 128:(j + 1) * 128], b_xlT, w1, b_w1, 0, 256)
                      rstd, _, b_st3 = rstd_of(stp3, pq[:, 0:256], b_pq, 256, 1e-6)
                      stt("dve", cqn, pq[:, 0:256], rstd, small[:, QN:QN + 256], ALU.mult, ALU.mult,
                          R=[b_pq, b_st3, b_small], W=[b_cqn])
                      for kc in range(2):
                          tr(PT[:, kc * 128:(kc + 1) * 128], cqn[:, kc * 128:(kc + 1) * 128], identb, R=[b_cqn, b_identb], W=[PTb[0]])
                      cp("act", cqnT[:, :, j * 128:(j + 1) * 128], PT[:, 0:256].rearrange("p (k n) -> p k n", k=2),
                         R=[PTb[0]], W=[b_cqnT])
                  for j, t in enumerate(tiles):
                      pq, b_pq = PS[5]
                      for kc in range(2):
                          mm(pq, cqnT[:, kc, j * 128:(j + 1) * 128], wqr[:, kc, :, :], kc == 0, kc == 1, R=[b_cqnT, b_wqr], W=[b_pq])
                      rope(rt, pq, b_pq, 8, t, qr, b_qr)
                      for g in range(4):
                          tr(PT[:, 512 + g * 128:512 + (g + 1) * 128], qr[:, g * 128:(g + 1) * 128], identb,
                             R=[b_qr, b_identb], W=[PTb[1]])
                      cp("act", qrT[:, :, j * 128:(j + 1) * 128], PT[:, 512:1024].rearrange("p (g n) -> p g n", g=4),
                         R=[PTb[1]], W=[b_qrT])
                  for hh in range(8):
                      pq, b_pq = PS[5 + hh % 2]
                      qnT, b_qnT = qnTs[hh % 2]
                      for kc in range(2):
                          mm(pq[:, 0:256], wqn[:, kc, hh, :], cqnT[:, kc, :], kc == 0, kc == 1, R=[b_wqn, b_cqnT], W=[b_pq])
                      cp("dve", qnT, pq[:, 0:256], R=[b_pq], W=[b_qnT])
                      mm(pq[:, 256:512], WkT[:, hh, :], qnT, True, True, R=[b_WkT, b_qnT], W=[b_pq])
                      cp("dve", qpT[:, hh, :], pq[:, 256:512], R=[b_pq], W=[b_qpT])
                  items = [(hh, cpi) for hh in range(8) for cpi in range(NT // 2)]

                  def qk1(it, idx):
                      hh, cpi = it
                      hp = hh % 2
                      psc, b_psc = PS[idx % 3]
                      for cc in range(2):
                          c = 2 * cpi + cc
                          mm(psc[:, cc * 256:(cc + 1) * 256], ckvT[:, c * 128:(c + 1) * 128], qpT[:, hh, :], True, False,
                             R=[b_ckvT, b_qpT], W=[b_psc])
                          mm(psc[:, cc * 256:(cc + 1) * 256], krT[hp * 64:(hp + 1) * 64, c * 128:(c + 1) * 128],
                             qrT[hp * 64:(hp + 1) * 64, hh // 2, :], False, True, R=[b_krT, b_qrT], W=[b_psc])

                  def pv1(it, pT, b_pT):
                      hh, cpi = it
                      po, b_po = PS[3 + hh % 2]
                      for cc in range(2):
                          c = 2 * cpi + cc
                          for j in range(2):
                              mm(po[:, j * 132:j * 132 + 129], pT[:, cc * 256 + j * 128:cc * 256 + (j + 1) * 128],
                                 VS[:, c, 0:129], c == 0 and j == 0, c == NT - 1, R=[b_pT, b_VS], W=[b_po], skip=True)

                  for k0 in range(2):
                      qk1(items[k0], k0)
                  for idx, it in enumerate(items):
                      hh, cpi = it
                      psc, b_psc = PS[idx % 3]
                      pT, b_pT = pts[ptc % NPT]
                      ptc += 1
                      act(pT, psc, AF.Exp, R=[b_psc], W=[b_pT], scale=sc1)
                      if idx + 2 < len(items):
                          qk1(items[idx + 2], idx + 2)
                      pv1(it, pT, b_pT)
                      if cpi != NT // 2 - 1:
                          continue
                      po, b_po = PS[3 + hh % 2]
                      for j in range(2):
                          recip(rr[:, j:j + 1], po[:, j * 132 + 128:j * 132 + 129], R=[b_po], W=[b_rr])
                          ts("dve", on[:, j, hh * 128:(hh + 1) * 128], po[:, j * 132:j * 132 + 128], rr[:, j:j + 1], None,
                             ALU.mult, None, R=[b_po, b_rr], W=[b_on])
                  for j, t in enumerate(tiles):
                      for hh in range(8):
                          tr(PT[:, hh * 128:(hh + 1) * 128], on[:, j, hh * 128:(hh + 1) * 128], identb, R=[b_on, b_identb],
                             W=[PTb[hh // 4]])
                      cp("act", onT[:, 0:4, j * 128:(j + 1) * 128], PT[:, 0:512].rearrange("p (h n) -> p h n", h=4),
                         R=[PTb[0]], W=[b_onT])
                      cp("dve", onT[:, 4:8, j * 128:(j + 1) * 128], PT[:, 512:1024].rearrange("p (h n) -> p h n", h=4),
                         R=[PTb[1]], W=[b_onT])
                  for j, t in enumerate(tiles):
                      for n in range(2):
                          pe_, b_pe = PS[5]
                          pz, b_pz = PS[6]
                          for q in range(4):
                              hh = n * 4 + q
                              mm(pe_[:, q * 128:(q + 1) * 128], onT[:, hh, j * 128:(j + 1) * 128], wkv[:, hh, 128:256], True, True,
                                 R=[b_onT, b_wkv], W=[b_pe])
                          proj(pz, b_pz, xlT[:, :, j * 128:(j + 1) * 128], b_xlT, w1, b_w1, 448 + n * 512, 512)
                          act(szz, pz, AF.Silu, R=[b_pz], W=[b_szz])
                          tt("dve", mix[:, j, n * 512:(n + 1) * 512], pe_, szz, ALU.mult, R=[b_pe, b_szz], W=[b_mix])
                  for j, t in enumerate(tiles):
                      for kc in range(8):
                          tr(PT[:, kc * 128:(kc + 1) * 128], mix[:, j, kc * 128:(kc + 1) * 128], identb, R=[b_mix, b_identb],
                             W=[PTb[kc // 4]])
                      cp("act", mixT[:, 0:4, j * 128:(j + 1) * 128], PT[:, 0:512].rearrange("p (k n) -> p k n", k=4),
                         R=[PTb[0]], W=[b_mixT])
                      cp("dve", mixT[:, 4:8, j * 128:(j + 1) * 128], PT[:, 512:1024].rearrange("p (k n) -> p k n", k=4),
                         R=[PTb[1]], W=[b_mixT])
                  for j, t in enumerate(tiles):
                      for n in range(2):
                          po, b_po = PS[5 + n]
                          for kc in range(8):
                              mm(po, mixT[:, kc, j * 128:(j + 1) * 128], wo1[:, kc, n * 512:(n + 1) * 512], kc == 0, kc == 7,
                                 R=[b_mixT, b_wo1], W=[b_po])
                          tt("dve", tmpo, po, G_l[:, n * 512:(n + 1) * 512], ALU.mult, R=[b_po, b_Gl], W=[b_tmpo])
                          tt("pool", h[:, t, n * 512:(n + 1) * 512], tmpo, h[:, t, n * 512:(n + 1) * 512], ALU.add,
                             R=[b_tmpo, b_h[t]], W=[b_h[t]])
                      y, b_y = ybuf[yc % 2]
                      rstd, _, b_stf = rstd_of(stp, h[:, t, :], b_h[t], 1024, 1e-6)
                      stt("dve", y, h[:, t, :], rstd, small[:, FINAL:FINAL + 1024], ALU.mult, ALU.mult,
                          R=[b_h[t], b_stf, b_small], W=[b_y])
                      dma("sp", out_d[bi, (t - 2) * 128:(t - 1) * 128, :], y, R=[b_y], W=[b_outd[yc % 2]])
                      yc += 1
              S.barrier()
        except _Stop:
            pass
        S.emit()
        build_program.last_S = S
    return nc


_NC_CACHE = {}


def _rope_tables():
    seq, gw, dim = 2048, 64, 64
    rows = seq // gw
    row = np.repeat(np.arange(rows), gw).astype(np.float32)
    col = np.tile(np.arange(gw), rows).astype(np.float32)
    half = dim // 2
    inv = (np.float32(10000.0) ** (-np.arange(0, half, 2, dtype=np.float32) / np.float32(half))).astype(np.float32)
    ang_r = row[:, None] * inv[None, :]
    ang_c = col[:, None] * inv[None, :]
    ang = np.concatenate([ang_r, ang_r, ang_c, ang_c], axis=-1).astype(np.float32)
    cos = np.cos(ang).astype(np.float32)
    sin = np.sin(ang).astype(np.float32)
    sgn = np.tile(np.concatenate([-np.ones(16), np.ones(16)]), 2).astype(np.float32)
    cos_all = np.concatenate([np.ones((256, 64), np.float32), cos], 0)
    sin_all = np.concatenate([np.zeros((256, 64), np.float32), sin * sgn[None, :]], 0)
    cos_t = cos_all.reshape(NT, 128, 64).transpose(1, 0, 2).reshape(128, NT * 64)
    sin_t = sin_all.reshape(NT, 128, 64).transpose(1, 0, 2).reshape(128, NT * 64)
    return cos_t, sin_t


def _pack_small(core, c, c_ctx, norm_w, ada_b, a_bs, a_ln_w, a_ln_b, lq1, lk1, lq2, lk2, subln, qn, kvn, final_w):
    sm = np.zeros((128, NS), np.float32)
    rep = lambda v: np.broadcast_to(np.asarray(v, np.float32)[None, :], (128, v.shape[0]))
    sm[:, LNW:LNW + 512] = rep(a_ln_w[0])
    sm[:, LNB:LNB + 512] = rep(a_ln_b[0])
    sm[:, SUBLN:SUBLN + 128] = rep(subln[0])
    sm[:, QN:QN + 256] = rep(qn[0])
    sm[:, KVN:KVN + 128] = rep(kvn[0])
    sm[:, FINAL:FINAL + 1024] = rep(final_w)
    sm[:, LQ1:LQ1 + 64] = rep(lq1[0])
    sm[:, LK1:LK1 + 64] = rep(lk1[0])
    sm[:, LQ2:LQ2 + 64] = rep(lq2[0])
    sm[:, LK2:LK2 + 64] = rep(lk2[0])
    sm[:, ABS:ABS + 8] = a_bs[0].T
    sm[:, NW:NW + 16] = norm_w.reshape(2, 8, 128).transpose(2, 0, 1).reshape(128, 16)
    sm[:, ADAB:ADAB + 48] = ada_b.reshape(2, 24, 128).transpose(2, 0, 1).reshape(128, 48)
    cond = np.stack([c[2 * core], c[2 * core + 1], c_ctx], 0)
    sm[:, COND:COND + 24] = cond.reshape(3, 8, 128).transpose(2, 1, 0).reshape(128, 24)
    cos_t, sin_t = _rope_tables()
    sm[:, COS:COS + NT * 64] = cos_t
    sm[:, SIN:SIN + NT * 64] = sin_t
    sm[:, IDENT:IDENT + 128] = np.eye(128, dtype=np.float32)
    return sm


def kernel(x, c, ctx, c_ctx, norm_w, ada_w, ada_b, even_w_in, a_ws, a_bs, a_ln_w, a_ln_b,
           b_lq1, b_lk1, b_lq2, b_lk2, b_subln_w, even_w_out, odd_w_in, c_q_norm_w, c_wq_b,
           c_kv_norm_w, c_wkv_b, odd_w_out, final_w, _dbg=False, _stop=99, _ncores=8):
    f = lambda a: np.ascontiguousarray(np.asarray(a, dtype=np.float32))
    x, c, ctx, c_ctx = f(x), f(c), f(ctx), f(c_ctx)
    key = (bool(_dbg), _stop)
    if key not in _NC_CACHE:
        _NC_CACHE[key] = build_program(dbg=bool(_dbg), stop=_stop)
    nc = _NC_CACHE[key]
    shared = {
        "ada_w": f(ada_w), "w_in0": f(even_w_in)[0], "w_out0": f(even_w_out)[0], "w_in1": f(odd_w_in)[0],
        "wq_b": f(c_wq_b)[0], "wkv_b": f(c_wkv_b)[0], "w_out1": f(odd_w_out)[0],
        "a_wsT": np.ascontiguousarray(f(a_ws)[0].transpose(2, 0, 1)),
        "adab_g": np.ascontiguousarray(f(ada_b)[:, 2048:3072]),
    }
    in_maps = []
    for core in range(_ncores):
        xin = np.concatenate([ctx[2 * core:2 * core + 2], x[2 * core:2 * core + 2]], axis=1)
        sm = _pack_small(core, c, c_ctx, f(norm_w), f(ada_b), f(a_bs), f(a_ln_w), f(a_ln_b), f(b_lq1), f(b_lk1),
                         f(b_lq2), f(b_lk2), f(b_subln_w), f(c_q_norm_w), f(c_kv_norm_w), f(final_w))
        m = {"xin": np.ascontiguousarray(xin), "small": sm}
        m.update(shared)
        in_maps.append(m)
    res = run_bass_kernel_spmd(nc, in_maps, core_ids=list(range(_ncores)))
    out = np.concatenate([r["out"] for r in res.results], axis=0).astype(np.float32)
    if _dbg:
        return out, np.concatenate([r["dbg_h"] for r in res.results], axis=0)
    return out
```

```python
import math
from contextlib import ExitStack
import numpy as np
import concourse.bass as bass
import concourse.mybir as mybir
from concourse.bass_utils import run_bass_kernel_spmd

F32 = mybir.dt.float32
BF16 = mybir.dt.bfloat16
AF = mybir.ActivationFunctionType
ALU = mybir.AluOpType

import os
XLT_DVE_ONLY = os.environ.get('XLT_DVE_ONLY', '0') == '1'
EPOCH = 3800
NT = 18
D = 1024

LNW, LNB, SUBLN, QN, KVN, FINAL = 0, 512, 1024, 1152, 1408, 1536
LQ1, LK1, LQ2, LK2 = 2560, 2624, 2688, 2752
ABS, NW, ADAB, COND, COS, SIN, IDENT = 2816, 2824, 2840, 2888, 2912, 4064, 5216
NS = 5344


class Buf:
    __slots__ = ("name", "lw", "rd", "dsem", "dval", "excl", "dq")

    def __init__(self, name, excl=False):
        self.name = name
        self.excl = excl
        self.lw = None
        self.rd = {}
        self.dsem = None
        self.dq = None
        self.dval = 0


class Sched:
    def __init__(self, nc, stack):
        self.nc = nc
        self.stack = stack
        self.names = ["pe", "act", "dve", "pool", "sp"]
        self.streams = {n: [] for n in self.names}
        self.sems = {n: [self._sem(f"s_{n}0")] for n in self.names}
        self.cnt = {n: 0 for n in self.names}
        self.seen = {n: {} for n in self.names}
        self.dbufs = []
        self.free_dsems = {}

    def _sem(self, name):
        return self.stack.enter_context(self.nc.semaphore(name))

    def _cur(self, eng):
        if self.cnt[eng] >= EPOCH:
            self.sems[eng].append(self._sem(f"s_{eng}{len(self.sems[eng])}"))
            self.cnt[eng] = 0
        return self.sems[eng][-1]

    def _need(self, eng, waits, dep):
        if dep is None:
            return
        sem, val = dep
        if self.seen[eng].get(sem, 0) >= val:
            return
        if waits.get(sem, 0) < val:
            waits[sem] = val

    def op(self, eng, fn, R=(), W=()):
        if any(b.excl for b in R):
            W = list(W) + [b for b in R if b.excl and b not in W]
            R = [b for b in R if not b.excl]
        waits = {}
        own = set(id(s) for s in self.sems[eng])
        for b in R:
            self._need(eng, waits, b.lw)
        for b in W:
            if b.lw is not None and id(b.lw[0]) not in own:
                self._need(eng, waits, b.lw)
            for r in b.rd.items():
                if id(r[0]) in own:
                    continue
                self._need(eng, waits, r)
        sem = self._cur(eng)
        self.cnt[eng] += 1
        val = self.cnt[eng]
        for s, v in waits.items():
            self.seen[eng][s] = v
        self.streams[eng].append((list(waits.items()), fn, sem, 1))
        for b in W:
            b.lw = (sem, val)
            b.rd = {}
        for b in R:
            b.rd[sem] = val

    def dma(self, q, fn, R=(), W=()):
        waits = {}
        for b in R:
            self._need(q, waits, b.lw)
        tgt = W[0]
        if tgt.dsem is None:
            tgt.dq = q
            if self.free_dsems.get(q):
                tgt.dsem, tgt.dval = self.free_dsems[q].pop()
            else:
                tgt.dsem = self._sem(f"d{len(self.dbufs)}_{tgt.name}".replace("/", "_"))
            self.dbufs.append(tgt)
        for b in W:
            if b.lw is not None and b.lw[0] is not tgt.dsem:
                self._need(q, waits, b.lw)
            for r in b.rd.items():
                self._need(q, waits, r)
        for s, v in waits.items():
            self.seen[q][s] = v
        tgt.dval += 16
        self.streams[q].append((list(waits.items()), fn, tgt.dsem, 16))
        for b in W:
            b.lw = (tgt.dsem, tgt.dval)
            b.rd = {}
        for b in R:
            b.rd[tgt.dsem] = tgt.dval

    def barrier(self):
        deps = []
        for n in self.names:
            if self.cnt[n] > 0:
                deps.append((self.sems[n][-1], self.cnt[n]))
            for s in self.sems[n][:-1]:
                deps.append((s, EPOCH))
        for b in self.dbufs:
            deps.append((b.dsem, b.dval))
        for n in self.names:
            waits = {}
            for d in deps:
                self._need(n, waits, d)
            for s, v in waits.items():
                self.seen[n][s] = v
            if waits:
                self.streams[n].append((list(waits.items()), None, None, 0))
        for b in self.dbufs:
            if b.dval < 3000:
                self.free_dsems.setdefault(b.dq, []).append((b.dsem, b.dval))
            b.dsem = None
        self.dbufs = []

    def emit(self):
        with self.nc.Block() as block:
            def mk(name):
                def body(e):
                    for waits, fn, sem, inc in self.streams[name]:
                        for s, v in waits:
                            e.wait_ge(s, v)
                        if fn is not None:
                            fn(e).then_inc(sem, inc)
                return body
            block.tensor(mk("pe"))
            block.scalar(mk("act"))
            block.vector(mk("dve"))
            block.gpsimd(mk("pool"))
            block.sync(mk("sp"))


class Ring:
    def __init__(self, slots):
        self.slots, self.i = slots, 0

    def next(self):
        self.i += 1
        return self.slots[self.i % len(self.slots)]


class Region:
    def __init__(self, big, lo, hi):
        self.big, self.lo, self.hi, self.cur = big, lo, hi, lo

    def f32(self, words, name="t"):
        a = self.cur
        self.cur += (words + 7) // 8 * 8
        assert self.cur <= self.hi, f"SBUF region overflow at {name}: {self.cur} > {self.hi}"
        return self.big[:, a:a + words], Buf(name)

    def bf(self, elems, name="t"):
        ap, b = self.f32((elems + 1) // 2, name)
        return ap.bitcast(BF16), b

    def sub(self):
        return Region(self.big, self.cur, self.hi)


class _Stop(Exception):
    pass


def build_program(dbg=False, stop=99):
    nc = bass.Bass("TRN2", target_bir_lowering=False)

    def din(name, shape):
        return nc.dram_tensor(name, shape, F32, kind="ExternalInput").ap()

    xin = din("xin", [2, NT * 128, D])
    small_d = din("small", [128, NS])
    ada_w = din("ada_w", [2, D, 3 * D])
    w_in0 = din("w_in0", [D, 3584])
    w_out0 = din("w_out0", [D, D])
    w_in1 = din("w_in1", [D, 1472])
    wq_b = din("wq_b", [256, 1536])
    wkv_b = din("wkv_b", [128, 2048])
    w_out1 = din("w_out1", [D, D])
    a_wsT = din("a_wsT", [128, 8, 128])
    out_d = nc.dram_tensor("out", [2, 2048, D], F32, kind="ExternalOutput").ap()
    adab_g = din("adab_g", [2, D])
    if dbg:
        dbg_h = nc.dram_tensor("dbg_h", [2, NT * 128, D], F32, kind="ExternalOutput").ap()

    with ExitStack() as st:
        S = Sched(nc, st)
        BIGW = 53200
        big = st.enter_context(nc.sbuf_tensor("big", [128, BIGW], F32))[:, :]
        PS = []
        for i in range(7):
            PS.append((st.enter_context(nc.psum_tensor(f"ps{i}", [128, 512], F32))[:, :], Buf(f"ps{i}", excl=True)))
        PT = st.enter_context(nc.psum_tensor("pt", [128, 1024], BF16))[:, :]
        b_PT = Buf("pt", excl=True)
        PTb = [b_PT, b_PT]

        def mm(out, lhsT, rhs, start, stop, R, W, skip=False):
            S.op("pe", lambda e: e.matmul(out, lhsT=lhsT, rhs=rhs, start=start, stop=stop, skip_group_check=skip), R=R, W=W)

        def tr(out, in_, ident, R, W):
            S.op("pe", lambda e: e.transpose(out, in_, ident), R=R, W=W)

        def act(out, in_, func, R, W, scale=1.0, bias=0.0):
            S.op("act", lambda e: e.activation(out=out, in_=in_, func=func, bias=bias, scale=scale), R=R, W=W)

        def tt(eng, out, in0, in1, op, R, W):
            S.op(eng, lambda e: e.tensor_tensor(out=out, in0=in0, in1=in1, op=op), R=R, W=W)

        def ts(eng, out, in0, s1, s2, op0, op1, R, W):
            if s2 is None:
                S.op(eng, lambda e: e.tensor_scalar(out=out, in0=in0, scalar1=s1, scalar2=None, op0=op0), R=R, W=W)
            else:
                S.op(eng, lambda e: e.tensor_scalar(out=out, in0=in0, scalar1=s1, scalar2=s2, op0=op0, op1=op1), R=R, W=W)

        def stt(eng, out, in0, scalar, in1, op0, op1, R, W):
            S.op(eng, lambda e: e.scalar_tensor_tensor(out=out, in0=in0, scalar=scalar, in1=in1, op0=op0, op1=op1), R=R, W=W)

        def cp(eng, out, in_, R, W):
            if eng == "act":
                S.op("act", lambda e: e.copy(out=out, in_=in_), R=R, W=W)
            else:
                S.op(eng, lambda e: e.tensor_copy(out=out, in_=in_), R=R, W=W)

        def recip(out, in_, R, W):
            S.op("dve", lambda e: e.reciprocal(out=out, in_=in_), R=R, W=W)

        def dma(q, out, in_, R, W):
            S.dma(q, lambda e: e.dma_start(out=out, in_=in_), R=R, W=W)

        top = Region(big, 0, BIGW)
        small, b_small = top.f32(NS, "small")
        h_all, _ = top.f32(NT * D, "h")
        h = h_all.rearrange("p (t f) -> p t f", t=NT)
        b_h = [Buf(f"h{t}") for t in range(NT)]
        G_l, b_Gl = top.f32(D, "G_l")
        G_c, b_Gc = top.f32(D, "G_c")
        identb, b_identb = top.bf(128, "identb")
        wsT_f, b_wsT = top.bf(8 * 128, "wsT")
        wsT = wsT_f.rearrange("p (g q) -> p g q", g=8)
        biasT, b_biasT = top.f32(512, "biasT")
        mod_f, b_mod = top.f32(2 * 72, "mod")
        mod = mod_f.rearrange("p (l f r) -> p l f r", l=2, r=3)
        acol_f, b_acol = top.f32(2 * 3 * 8, "acol")
        acol = acol_f.rearrange("p (l r f) -> p l r f", l=2, r=3)
        scT_f, b_scT = top.bf(24, "scT")
        scT = scT_f.rearrange("p (k r) -> p k r", r=3)
        lamt, b_lam = top.f32(8, "lam")
        subln8, b_subln8 = top.f32(128, "subln8")
        stats, b_stats = top.f32(32, "stats")
        condrep_f, b_condrep = top.bf(8 * 128, "condrep")
        condrep = condrep_f.rearrange("p (k n) -> p k n", k=8)
        brow, b_brow = top.bf(1024, "brow")
        ones1, b_ones1 = top.bf(128, "ones1")
        wKV_f, b_wKV = Region(big, BIGW - 4096, BIGW).bf(8 * 1024, "wKV")
        wKV = wKV_f.rearrange("p (k n) -> p k n", k=8)
        phase0 = top.cur

        dma("sp", small, small_d[:, :], R=[], W=[b_small])
        dma("pool", identb, small_d[:, IDENT:IDENT + 128], R=[], W=[b_identb])
        dma("pool", wsT, a_wsT[:, :, :], R=[], W=[b_wsT])
        S.op("dve", lambda e: e.memset(ones1, 1.0), R=[], W=[b_ones1])

        act(scT, small[:, COND:COND + 24].rearrange("p (k r) -> p k r", r=3), AF.Silu, R=[b_small], W=[b_scT])
        setup = Region(big, phase0, BIGW)
        wada = []
        for i in range(2):
            ap, b = setup.bf(8 * 512, f"wada{i}")
            wada.append((ap.rearrange("p (k n) -> p k n", k=8), b))
        pcs = 0
        for l in range(2):
            pm, b_pm = PS[l]
            for j in range(6):
                wt, b_wt = wada[pcs % 2]
                pcs += 1
                dma("pool", wt, ada_w[l, :, j * 512:(j + 1) * 512].rearrange("(k p) n -> p k n", p=128), R=[], W=[b_wt])
                for fl in range(4):
                    fc = j * 4 + fl
                    for kc in range(8):
                        mm(pm[:, fc * 3:fc * 3 + 3], wt[:, kc, fl * 128:(fl + 1) * 128], scT[:, kc, :],
                           kc == 0, kc == 7, R=[b_wt, b_scT], W=[b_pm])
            tt("dve", mod[:, l, :, :], pm[:, 0:72].rearrange("p (f r) -> p f r", r=3),
               small[:, ADAB + l * 24:ADAB + (l + 1) * 24].unsqueeze(2).to_broadcast([128, 24, 3]), ALU.add,
               R=[b_pm, b_small], W=[b_mod])
            for r in range(3):
                stt("dve", acol[:, l, r, :], mod[:, l, 8:16, r], 1.0, small[:, NW + l * 8:NW + (l + 1) * 8],
                    ALU.add, ALU.mult, R=[b_mod, b_small], W=[b_acol])
        lt, b_lt = setup.f32(128, "lamtmp")
        tt("dve", lt[:, 0:64], small[:, LQ1:LQ1 + 64], small[:, LK1:LK1 + 64], ALU.mult, R=[b_small], W=[b_lt])
        tt("dve", lt[:, 64:128], small[:, LQ2:LQ2 + 64], small[:, LK2:LK2 + 64], ALU.mult, R=[b_small], W=[b_lt])
        S.op("dve", lambda e: e.reduce_sum(out=lamt[:, 0:2], in_=lt.rearrange("p (a b) -> p a b", a=2),
                                           axis=mybir.AxisListType.X), R=[b_lt], W=[b_lam])
        act(lamt[:, 2:4], lamt[:, 0:2], AF.Exp, R=[b_lam], W=[b_lam])
        tt("dve", lamt[:, 4:5], lamt[:, 3:4], lamt[:, 2:3], ALU.subtract, R=[b_lam], W=[b_lam])
        ts("dve", lamt[:, 5:6], lamt[:, 4:5], -0.2, None, ALU.add, None, R=[b_lam], W=[b_lam])
        neglam = lamt[:, 5:6]
        ts("dve", subln8, small[:, SUBLN:SUBLN + 128], 0.8, None, ALU.mult, None, R=[b_small], W=[b_subln8])
        cp("dve", biasT.rearrange("p (g d) -> p g d", g=8), small[:, ABS:ABS + 8].unsqueeze(2).to_broadcast([128, 8, 64]),
           R=[b_small], W=[b_biasT])

        def build_G(l, conds):
            load_w("pool", wKV, b_wKV, ada_w[l], 2048, 1024)
            dma("pool", brow[0:1, :], adab_g[l:l + 1, :], R=[], W=[b_brow])
            pg, b_pg = PS[6]
            for (r, G, b_G) in conds:
                cp("dve", condrep, scT[:, :, r:r + 1].to_broadcast([128, 8, 128]), R=[b_scT], W=[b_condrep])
                for n in range(2):
                    for kc in range(8):
                        mm(pg, condrep[:, kc, :], wKV[:, kc, n * 512:(n + 1) * 512], kc == 0, False, R=[b_condrep, b_wKV], W=[b_pg])
                    mm(pg, ones1[0:1, :], brow[0:1, n * 512:(n + 1) * 512], False, True, R=[b_ones1, b_brow], W=[b_pg])
                    cp("dve", G[:, n * 512:(n + 1) * 512], pg, R=[b_pg], W=[b_G])

        def rstd_of(reg_stats, src, b_src, n, eps, name="rs"):
            st_ap, b_st = reg_stats.next() if isinstance(reg_stats, Ring) else reg_stats
            nch = (n + 511) // 512
            w = n // nch
            for c in range(nch):
                S.op("dve", (lambda c: lambda e: e.bn_stats(out=st_ap[:, c * 6:(c + 1) * 6], in_=src[:, c * w:(c + 1) * w]))(c),
                     R=[b_src], W=[b_st])
            S.op("dve", lambda e: e.bn_aggr(out=st_ap[:, 12:14], in_=st_ap[:, 0:6 * nch].rearrange("p (c s) -> p c s", s=6)),
                 R=[b_st], W=[b_st])
            stt("dve", st_ap[:, 14:15], st_ap[:, 12:13], st_ap[:, 12:13], st_ap[:, 13:14], ALU.mult, ALU.add, R=[b_st], W=[b_st])
            act(st_ap[:, 15:16], st_ap[:, 14:15], AF.Ln, R=[b_st], W=[b_st], bias=eps)
            act(st_ap[:, 15:16], st_ap[:, 15:16], AF.Exp, R=[b_st], W=[b_st], scale=-0.5)
            return st_ap[:, 15:16], st_ap[:, 12:13], b_st

        def ckpt(k, bi):
            if stop == k:
                S.barrier()
                if dbg:
                    b_dbg = Buf("dbgstop")
                    for t in range(NT):
                        dma("sp", dbg_h[bi, t * 128:(t + 1) * 128, :], h[:, t, :], R=[b_h[t]], W=[b_dbg])
                    S.barrier()
                raise _Stop()

        PT2 = PS[6][0].bitcast(BF16)

        def make_xlT(src, b_src, l, r, xn, b_xn, st_pair, xlT_dst, b_xlT, alt, two_banks=False):
            rstd, _, b_st = rstd_of(st_pair, src, b_src, 1024, 1e-6)
            act(xn, src, AF.Identity, R=[b_src, b_st], W=[b_xn], scale=rstd)
            ckpt(0.57, 0)
            if two_banks:
                for half in range(2):
                    pt_ap, bp = (PT, b_PT) if half == 0 else (PT2, PS[6][1])
                    for q in range(4):
                        fc = half * 4 + q
                        tr(pt_ap[:, q * 128:(q + 1) * 128], xn[:, fc * 128:(fc + 1) * 128], identb, R=[b_xn, b_identb], W=[bp])
                for q in range(4):
                    for half in range(2):
                        pt_ap, bp = (PT, b_PT) if half == 0 else (PT2, PS[6][1])
                        fc = half * 4 + q
                        sc = acol[:, l, r, fc:fc + 1]
                        bi_ = mod[:, l, fc, r:r + 1]
                        if half == 0:
                            ts("dve", xlT_dst[:, fc, :], pt_ap[:, q * 128:(q + 1) * 128], sc, bi_, ALU.mult, ALU.add,
                               R=[bp, b_acol, b_mod], W=[b_xlT])
                        else:
                            act(xlT_dst[:, fc, :], pt_ap[:, q * 128:(q + 1) * 128], AF.Identity, R=[bp, b_acol, b_mod], W=[b_xlT],
                                scale=sc, bias=bi_)
                return
            for half in range(2):
                bp = PTb[half]
                if half == 1:
                    ckpt(0.596, 0)
                for q in range(4):
                    fc = half * 4 + q
                    tr(PT[:, fc * 128:(fc + 1) * 128], xn[:, fc * 128:(fc + 1) * 128], identb, R=[b_xn, b_identb], W=[bp])
                if half == 1:
                    ckpt(0.597, 0)
                ckpt(0.58, 0)
                for q in range(4):
                    fc = half * 4 + q
                    sc = acol[:, l, r, fc:fc + 1]
                    bi = mod[:, l, fc, r:r + 1]
                    if q == 1:
                        ckpt(0.59, 0)
                    if q == 2:
                        ckpt(0.595, 0)
                    if (fc + alt) % 2 == 0 or XLT_DVE_ONLY:
                        ts("dve", xlT_dst[:, fc, :], PT[:, fc * 128:(fc + 1) * 128], sc, bi, ALU.mult, ALU.add,
                           R=[bp, b_acol, b_mod], W=[b_xlT])
                    else:
                        act(xlT_dst[:, fc, :], PT[:, fc * 128:(fc + 1) * 128], AF.Identity, R=[bp, b_acol, b_mod], W=[b_xlT],
                            scale=sc, bias=bi)

        def proj(ps, b_ps, xlT_tile, b_xlT, W, b_W, c0, n, nk=8):
            for kc in range(nk):
                mm(ps[:, 0:n], xlT_tile[:, kc, :], W[:, kc, c0:c0 + n], kc == 0, kc == nk - 1, R=[b_xlT, b_W], W=[b_ps])

        def rope(reg_t, src, b_src, ng, t, dst, b_dst, eng2="pool"):
            (tc_, b_tc), (ts_, b_ts) = reg_t
            n = ng * 64
            cosb = small[:, COS + t * 64:COS + (t + 1) * 64]
            sinb = small[:, SIN + t * 64:SIN + (t + 1) * 64]
            sv = src.rearrange("p (g s t d) -> p g s t d", g=ng, s=2, t=2)
            tv = ts_[:, 0:n].rearrange("p (g s t d) -> p g s t d", g=ng, s=2, t=2)
            sn = sinb.rearrange("p (s t d) -> p s t d", s=2, t=2)
            tt("dve", tc_[:, 0:n].rearrange("p (g d) -> p g d", g=ng), src.rearrange("p (g d) -> p g d", g=ng),
               cosb.unsqueeze(1).to_broadcast([128, ng, 64]), ALU.mult, R=[b_src, b_small], W=[b_tc])
            for hf in range(2):
                tt("dve", tv[:, :, :, hf, :], sv[:, :, :, 1 - hf, :],
                   sn[:, :, hf, :].unsqueeze(1).to_broadcast([128, ng, 2, 16]), ALU.mult, R=[b_src, b_small], W=[b_ts])
            tt(eng2, dst, tc_[:, 0:n], ts_[:, 0:n], ALU.add, R=[b_tc, b_ts], W=[b_dst])

        def load_w(q, dst, b_dst, src2d, c0, n):
            dma(q, dst, src2d[:, c0:c0 + n].rearrange("(k p) n -> p k n", p=128), R=[], W=[b_dst])

        S.barrier()

        def ckpt(k, bi):
            if stop == k:
                S.barrier()
                if dbg:
                    b_dbg = Buf("dbgstop")
                    for t in range(NT):
                        dma("sp", dbg_h[bi, t * 128:(t + 1) * 128, :], h[:, t, :], R=[b_h[t]], W=[b_dbg])
                    S.barrier()
                raise _Stop()

        try:
          ckpt(0, 0)
          for bi in range(2):
              L0 = Region(big, phase0, BIGW)
              wQZ_f, b_wQZ = L0.bf(8 * 1024, "wQZ")
              wQZ = wQZ_f.rearrange("p (k n) -> p k n", k=8)
              woB_f, b_woB = L0.bf(4 * 1024, "woB")
              woB = woB_f.rearrange("p (k n) -> p k n", k=4)
              build_G(0, [(bi, G_l, b_Gl), (2, G_c, b_Gc)])
              ckpt(0.5, bi)

              A = Region(big, L0.cur, BIGW - 4096)
              wA_f, b_wA = A.bf(8 * 1536, "wA")
              wA = wA_f.rearrange("p (k n) -> p k n", k=8)
              woA_f, b_woA = A.bf(4 * 1024, "woA")
              woA = woA_f.rearrange("p (k n) -> p k n", k=4)
              load_w("pool", wA, b_wA, w_in0, 0, 1536)
              dma("pool", woA, w_out0[0:512, :].rearrange("(k p) n -> p k n", p=128), R=[], W=[b_woA])
              xs = [A.f32(1024, f"xs{i}") for i in range(2)]
              xn, b_xn = A.bf(1024, "xn")
              xlTs = []
              for i in range(2):
                  ap, b = A.bf(8 * 128, f"xlT{i}")
                  xlTs.append((ap.rearrange("p (k n) -> p k n", k=8), b))
              stp = Ring([A.f32(16, f"st{i}") for i in range(4)])
              stp2 = A.f32(16, "st2")
              gu, b_gu = A.f32(512, "gu")
              gv, b_gv = A.f32(512, "gv")
              sz, b_sz = A.f32(512, "sz")
              xh, b_xh = A.f32(512, "xh")
              t1, b_t1 = xh, b_xh
              vn, b_vn = A.bf(512, "vn")
              mixa, b_mixa = A.bf(512, "mixa")
              mixT_f, b_mixT = vn, b_vn
              mixT = mixT_f.rearrange("p (k n) -> p k n", k=4)
              tmpo, b_tmpo = xh, b_xh
              load_w("pool", wKV, b_wKV, w_in0, 2048, 1024)
              for t in range(NT):
                  r = 2 if t < 2 else bi
                  G, b_G = (G_c, b_Gc) if t < 2 else (G_l, b_Gl)
                  x_t, b_x = xs[t % 2]
                  xlT, b_xlT = xlTs[t % 2]
                  dma("sp", x_t, xin[bi, t * 128:(t + 1) * 128, :], R=[], W=[b_x])
                  ckpt(0.55, bi)
                  make_xlT(x_t, b_x, 0, r, xn, b_xn, stp, xlT, b_xlT, t, two_banks=True)
                  ckpt(0.6, bi)
                  for i in range(3):
                      proj(PS[i][0], PS[i][1], xlT, b_xlT, wA, b_wA, i * 512, 512)
                  ckpt(0.65, bi)
                  act(gu, PS[0][0], AF.Gelu, R=[PS[0][1]], W=[b_gu])
                  act(gv, PS[1][0], AF.Gelu, R=[PS[1][1]], W=[b_gv])
                  act(sz, PS[2][0], AF.Silu, R=[PS[2][1]], W=[b_sz])
                  st2, b_st2 = stp2
                  S.op("dve", (lambda st2, gv: lambda e: e.bn_stats(out=st2[:, 0:6], in_=gv))(st2, gv), R=[b_gv], W=[b_st2])
                  S.op("dve", (lambda st2: lambda e: e.bn_aggr(out=st2[:, 12:14], in_=st2[:, 0:6]))(st2), R=[b_st2], W=[b_st2])
                  act(st2[:, 15:16], st2[:, 13:14], AF.Ln, R=[b_st2], W=[b_st2], bias=1e-5)
                  act(st2[:, 15:16], st2[:, 15:16], AF.Exp, R=[b_st2], W=[b_st2], scale=-0.5)
                  ts("dve", xh, gv, st2[:, 12:13], st2[:, 15:16], ALU.subtract, ALU.mult, R=[b_gv, b_st2], W=[b_xh])
                  tt("pool", xh, xh, small[:, LNW:LNW + 512], ALU.mult, R=[b_xh, b_small], W=[b_xh])
                  tt("pool", vn, xh, small[:, LNB:LNB + 512], ALU.add, R=[b_xh, b_small], W=[b_vn])
                  tt("pool", gu, gu, sz, ALU.mult, R=[b_gu, b_sz], W=[b_gu])
                  psg, b_psg = PS[3]
                  for g in range(8):
                      mm(psg[:, g * 64:(g + 1) * 64], wsT[:, g, :], vn[:, g * 64:(g + 1) * 64], True, True,
                         R=[b_wsT, b_vn], W=[b_psg])
                  tt("dve", t1, psg, biasT, ALU.add, R=[b_psg, b_biasT], W=[b_t1])
                  tt("dve", mixa, t1, gu, ALU.mult, R=[b_t1, b_gu], W=[b_mixa])
                  for q in range(4):
                      tr(PT[:, q * 128:(q + 1) * 128], mixa[:, q * 128:(q + 1) * 128], identb, R=[b_mixa, b_identb], W=[PTb[0]])
                  cp("act", mixT, PT[:, 0:512].rearrange("p (k n) -> p k n", k=4), R=[PTb[0]], W=[b_mixT])
                  for n in range(2):
                      po, b_po = PS[4 + n]
                      for kc in range(4):
                          mm(po, mixT[:, kc, :], woA[:, kc, n * 512:(n + 1) * 512], kc == 0, kc == 3, R=[b_mixT, b_woA], W=[b_po])
                      tt("dve", tmpo, po, G[:, n * 512:(n + 1) * 512], ALU.mult, R=[b_po, b_G], W=[b_tmpo])
                      tt("pool", h[:, t, n * 512:(n + 1) * 512], tmpo, x_t[:, n * 512:(n + 1) * 512], ALU.add,
                         R=[b_tmpo, b_x], W=[b_h[t]])
                  ckpt(0.7, bi)
              S.barrier()
              ckpt(1, bi)

              KV = L0.sub()
              KT_f, b_KT = KV.bf(4 * 2304, "KT")
              KT = KT_f.rearrange("p (h n) -> p h n", h=4)
              V_f, b_V = KV.bf(NT * 4 * 132, "V")
              V = V_f.rearrange("p (t h d) -> p t h d", t=NT, h=4)
              K2 = Region(big, KV.cur, BIGW - 4096)
              xs = [K2.f32(1024, f"xs{i}") for i in range(2)]
              xn, b_xn = K2.bf(1024, "xn")
              xlTs = []
              for i in range(2):
                  ap, b = K2.bf(8 * 128, f"xlT{i}")
                  xlTs.append((ap.rearrange("p (k n) -> p k n", k=8), b))
              stp = Ring([K2.f32(16, f"st{i}") for i in range(4)])
              rt = (K2.f32(512, "tc"), K2.f32(512, "ts"))
              kr, b_kr = K2.bf(512, "kr")
              load_w("pool", wQZ[:, :, 0:512], b_wQZ, w_in0, 1536, 512)
              load_w("pool", wQZ[:, :, 512:1024], b_wQZ, w_in0, 3072, 512)
              dma("pool", woB, w_out0[512:1024, :].rearrange("(k p) n -> p k n", p=128), R=[], W=[b_woB])
              S.op("dve", (lambda V: lambda e: e.memset(V[:, :, :, 128:129], 1.0))(V), R=[], W=[b_V])
              for t in range(NT):
                  r = 2 if t < 2 else bi
                  x_t, b_x = xs[t % 2]
                  xlT, b_xlT = xlTs[t % 2]
                  dma("sp", x_t, xin[bi, t * 128:(t + 1) * 128, :], R=[], W=[b_x])
                  make_xlT(x_t, b_x, 0, r, xn, b_xn, stp, xlT, b_xlT, t, two_banks=True)
                  proj(PS[0][0], PS[0][1], xlT, b_xlT, wKV, b_wKV, 0, 512)
                  proj(PS[1][0], PS[1][1], xlT, b_xlT, wKV, b_wKV, 512, 512)
                  rope(rt, PS[0][0], PS[0][1], 8, t, kr, b_kr)
                  for hh in range(4):
                      tr(PT[:, hh * 128:(hh + 1) * 128], kr[:, hh * 128:(hh + 1) * 128], identb, R=[b_kr, b_identb], W=[PTb[0]])
                  cp("act", KT[:, :, t * 128:(t + 1) * 128], PT[:, 0:512].rearrange("p (h n) -> p h n", h=4), R=[PTb[0]], W=[b_KT])
                  cp("act", V[:, t, :, 0:128], PS[1][0].rearrange("p (h d) -> p h d", h=4), R=[PS[1][1]], W=[b_V])
              S.barrier()
              ckpt(2, bi)

              Bp = KV.sub()
              xs = [Bp.f32(1024, f"xs{i}") for i in range(2)]
              xn, b_xn = Bp.bf(1024, "xn")
              xlT_f, b_xlT = Bp.bf(8 * 256, "xlT")
              xlT = xlT_f.rearrange("p (k n) -> p k n", k=8)
              stp = Ring([Bp.f32(16, f"st{i}") for i in range(4)])
              stp3 = Ring([Bp.f32(16, f"st3{i}") for i in range(4)])
              rt = (Bp.f32(512, "tc"), Bp.f32(512, "ts"))
              qr, b_qr = Bp.bf(512, "qr")
              QT_f, b_QT = Bp.bf(4 * 256, "QT")
              QT = QT_f.rearrange("p (h n) -> p h n", h=4)
              NPT = 3
              pts = [Bp.bf(512, f"pT{i}") for i in range(NPT)]
              oh, b_oh = Bp.f32(128, "oh")
              rr, b_rr = Bp.f32(8, "rr")
              omix_f, b_omix = Bp.f32(2 * 512, "omix")
              omix = omix_f.rearrange("p (j f) -> p j f", j=2)
              sbz, b_sbz = Bp.f32(512, "sbz")
              mixb_f, b_mixb = Bp.bf(2 * 512, "mixb")
              mixb = mixb_f.rearrange("p (j f) -> p j f", j=2)
              mixT_f, b_mixT = Bp.bf(4 * 256, "mixT")
              mixT = mixT_f.rearrange("p (k n) -> p k n", k=4)
              tmpo, b_tmpo = sbz, b_sbz
              ptc = 0
              for s in range(9):
                  tiles = [2 * s, 2 * s + 1]
                  r = 2 if s == 0 else bi
                  G, b_G = (G_c, b_Gc) if s == 0 else (G_l, b_Gl)
                  nch = 2 if s == 0 else NT
                  for j, t in enumerate(tiles):
                      x_t, b_x = xs[j]
                      dma("sp", x_t, xin[bi, t * 128:(t + 1) * 128, :], R=[], W=[b_x])
                      make_xlT(x_t, b_x, 0, r, xn, b_xn, stp, xlT[:, :, j * 128:(j + 1) * 128], b_xlT, j)
                  for j, t in enumerate(tiles):
                      proj(PS[5][0], PS[5][1], xlT[:, :, j * 128:(j + 1) * 128], b_xlT, wQZ, b_wQZ, 0, 512)
                      rope(rt, PS[5][0], PS[5][1], 8, t, qr, b_qr)
                      for hh in range(4):
                          tr(PT[:, hh * 128:(hh + 1) * 128], qr[:, hh * 128:(hh + 1) * 128], identb, R=[b_qr, b_identb], W=[PTb[0]])
                      cp("act", QT[:, :, j * 128:(j + 1) * 128], PT[:, 0:512].rearrange("p (h n) -> p h n", h=4), R=[PTb[0]], W=[b_QT])
                  items = [(hh, m, cpi) for hh in range(4) for m in range(2) for cpi in range(nch // 2)]

                  def qk0(it, idx):
                      hh, m, cpi = it
                      psc, b_psc = PS[idx % 3]
                      for cc in range(2):
                          c = 2 * cpi + cc
                          mm(psc[:, cc * 256:(cc + 1) * 256], KT[m * 64:(m + 1) * 64, hh, c * 128:(c + 1) * 128],
                             QT[m * 64:(m + 1) * 64, hh, :], True, True, R=[b_KT, b_QT], W=[b_psc])

                  def pv0(it, idx, pT, b_pT):
                      hh, m, cpi = it
                      po, b_po = PS[3 + m]
                      for cc in range(2):
                          c = 2 * cpi + cc
                          for j in range(2):
                              mm(po[:, j * 132:j * 132 + 129], pT[:, cc * 256 + j * 128:cc * 256 + (j + 1) * 128],
                                 V[:, c, hh, 0:129], c == 0 and j == 0, c == nch - 1, R=[b_pT, b_V], W=[b_po], skip=True)

                  for k0 in range(min(2, len(items))):
                      qk0(items[k0], k0)
                  for idx, it in enumerate(items):
                      hh, m, cpi = it
                      psc, b_psc = PS[idx % 3]
                      pT, b_pT = pts[ptc % NPT]
                      ptc += 1
                      act(pT, psc, AF.Exp, R=[b_psc], W=[b_pT], scale=0.125)
                      if idx + 2 < len(items):
                          qk0(items[idx + 2], idx + 2)
                      pv0(it, idx, pT, b_pT)
                      if not (m == 1 and cpi == nch // 2 - 1):
                          continue
                      p0, b_p0 = PS[3]
                      p1, b_p1 = PS[4]
                      for j in range(2):
                          recip(rr[:, 0:1], p0[:, j * 132 + 128:j * 132 + 129], R=[b_p0], W=[b_rr])
                          recip(rr[:, 1:2], p1[:, j * 132 + 128:j * 132 + 129], R=[b_p1], W=[b_rr])
                          tt("dve", rr[:, 2:3], rr[:, 1:2], neglam, ALU.mult, R=[b_rr, b_lam], W=[b_rr])
                          ts("dve", oh, p0[:, j * 132:j * 132 + 128], rr[:, 0:1], None, ALU.mult, None, R=[b_p0, b_rr], W=[b_oh])
                          stt("dve", oh, p1[:, j * 132:j * 132 + 128], rr[:, 2:3], oh, ALU.mult, ALU.add, R=[b_p1, b_rr, b_oh], W=[b_oh])
                          rstd, _, b_st3 = rstd_of(stp3, oh, b_oh, 128, 1e-5)
                          stt("dve", omix[:, j, hh * 128:(hh + 1) * 128], oh, rstd, subln8, ALU.mult, ALU.mult,
                              R=[b_oh, b_st3, b_subln8], W=[b_omix])
                  for j, t in enumerate(tiles):
                      proj(PS[5][0], PS[5][1], xlT[:, :, j * 128:(j + 1) * 128], b_xlT, wQZ, b_wQZ, 512, 512)
                      act(sbz, PS[5][0], AF.Silu, R=[PS[5][1]], W=[b_sbz])
                      tt("dve", mixb[:, j, :], omix[:, j, :], sbz, ALU.mult, R=[b_omix, b_sbz], W=[b_mixb])
                      for q in range(4):
                          tr(PT[:, 512 + q * 128:512 + (q + 1) * 128], mixb[:, j, q * 128:(q + 1) * 128], identb,
                             R=[b_mixb, b_identb], W=[PTb[1]])
                      cp("act", mixT[:, :, j * 128:(j + 1) * 128], PT[:, 512:1024].rearrange("p (k n) -> p k n", k=4),
                         R=[PTb[1]], W=[b_mixT])
                  for j, t in enumerate(tiles):
                      for n in range(2):
                          po, b_po = PS[5 + n]
                          for kc in range(4):
                              mm(po, mixT[:, kc, j * 128:(j + 1) * 128], woB[:, kc, n * 512:(n + 1) * 512], kc == 0, kc == 3,
                                 R=[b_mixT, b_woB], W=[b_po])
                          tt("dve", tmpo, po, G[:, n * 512:(n + 1) * 512], ALU.mult, R=[b_po, b_G], W=[b_tmpo])
                          tt("pool", h[:, t, n * 512:(n + 1) * 512], tmpo, h[:, t, n * 512:(n + 1) * 512], ALU.add,
                             R=[b_tmpo, b_h[t]], W=[b_h[t]])
              S.barrier()
              ckpt(3, bi)
              if dbg:
                  b_dbg = Buf(f"dbg{bi}")
                  for t in range(NT):
                      dma("sp", dbg_h[bi, t * 128:(t + 1) * 128, :], h[:, t, :], R=[b_h[t]], W=[b_dbg])
                  S.barrier()

              L1 = Region(big, phase0, BIGW)
              build_G(1, [(bi, G_l, b_Gl)])
              w1_f, b_w1 = L1.bf(8 * 1472, "w_in1")
              w1 = w1_f.rearrange("p (k n) -> p k n", k=8)
              wqn_f, b_wqn = L1.bf(2 * 8 * 128, "wqn")
              wqn = wqn_f.rearrange("p (k h d) -> p k h d", k=2, h=8)
              wqr_f, b_wqr = L1.bf(2 * 8 * 64, "wqr")
              wqr = wqr_f.rearrange("p (k h d) -> p k h d", k=2, h=8)
              wkv_f, b_wkv = L1.bf(2048, "wkv")
              wkv = wkv_f.rearrange("p (h d) -> p h d", h=8)
              WkT_f, b_WkT = L1.bf(8 * 128, "WkT")
              WkT = WkT_f.rearrange("p (h c) -> p h c", h=8)
              wo1_f, b_wo1 = L1.bf(8 * 1024, "wo1")
              wo1 = wo1_f.rearrange("p (k n) -> p k n", k=8)
              ckvT, b_ckvT = L1.bf(2304, "ckvT")
              krT, b_krT = L1.bf(2304, "krT")
              VS_f, b_VS = L1.bf(NT * 132, "VS")
              VS = VS_f.rearrange("p (t d) -> p t d", t=NT)
              load_w("pool", w1, b_w1, w_in1, 0, 1472)
              wq3 = wq_b.rearrange("(k p) (h d) -> p k h d", p=128, h=8)
              for kc in range(2):
                  dma("pool", wqn[:, kc, :, :], wq3[:, kc, :, 0:128], R=[], W=[b_wqn])
                  dma("pool", wqr[:, kc, :, :], wq3[:, kc, :, 128:192], R=[], W=[b_wqr])
              dma("pool", wkv, wkv_b.rearrange("p (h d) -> p h d", h=8), R=[], W=[b_wkv])
              dma("pool", wo1, w_out1[:, :].rearrange("(k p) n -> p k n", p=128), R=[], W=[b_wo1])
              for hh in range(8):
                  tr(PT[:, hh * 128:(hh + 1) * 128], wkv[:, hh, 0:128], identb, R=[b_wkv, b_identb], W=[PTb[hh // 4]])
              cp("dve", WkT[:, 0:4, :], PT[:, 0:512].rearrange("p (h c) -> p h c", h=4), R=[PTb[0]], W=[b_WkT])
              cp("dve", WkT[:, 4:8, :], PT[:, 512:1024].rearrange("p (h c) -> p h c", h=4), R=[PTb[1]], W=[b_WkT])

              ckpt(3.5, bi)
              S.barrier()
              K1 = L1.sub()
              xn, b_xn = K1.bf(1024, "xn")
              xlTs = []
              for i in range(2):
                  ap, b = K1.bf(8 * 128, f"xlT{i}")
                  xlTs.append((ap.rearrange("p (k n) -> p k n", k=8), b))
              stp = Ring([K1.f32(16, f"st{i}") for i in range(4)])
              stp3 = Ring([K1.f32(16, f"st3{i}") for i in range(4)])
              rt = (K1.f32(64, "tc"), K1.f32(64, "ts"))
              krd, b_krd = K1.bf(128, "krd")
              S.op("dve", (lambda VS: lambda e: e.memset(VS[:, :, 128:129], 1.0))(VS), R=[], W=[b_VS])
              for t in range(NT):
                  r = 2 if t < 2 else bi
                  xlT, b_xlT = xlTs[t % 2]
                  make_xlT(h[:, t, :], b_h[t], 1, r, xn, b_xn, stp, xlT, b_xlT, t, two_banks=True)
                  pk, b_pk = PS[t % 2]
                  if os.environ.get("NOPROJ", "0") == "1":
                      continue
                  if os.environ.get("USE_WO1", "0") == "1":
                      proj(pk, b_pk, xlT, b_xlT, wo1, b_wo1, 256, 192)
                  else:
                      proj(pk, b_pk, xlT, b_xlT, w1, b_w1, 256, int(os.environ.get("KVN_N", "192")))
                  import os as _os
                  _v = int(_os.environ.get("KV1V", "9"))
                  if _v < 1:
                      continue
                  rstd, _, b_st3 = rstd_of(stp3, pk[:, 0:128], b_pk, 128, 1e-6)
                  stt("dve", VS[:, t, 0:128], pk[:, 0:128], rstd, small[:, KVN:KVN + 128], ALU.mult, ALU.mult,
                      R=[b_pk, b_st3, b_small], W=[b_VS])
                  if _v < 2:
                      continue
                  rope(rt, pk[:, 128:192], b_pk, 1, t, krd[:, 0:64], b_krd, eng2="dve")
                  cp("dve", krd[:, 64:128], krd[:, 0:64], R=[b_krd], W=[b_krd])
                  if _v < 3:
                      continue
                  tr(PT[:, 0:128], VS[:, t, 0:128], identb, R=[b_VS, b_identb], W=[PTb[0]])
                  tr(PT[:, 128:256], krd, identb, R=[b_krd, b_identb], W=[PTb[0]])
                  cp("act", ckvT[:, t * 128:(t + 1) * 128], PT[:, 0:128], R=[PTb[0]], W=[b_ckvT])
                  cp("act", krT[:, t * 128:(t + 1) * 128], PT[:, 128:256], R=[PTb[0]], W=[b_krT])
              S.barrier()
              ckpt(4, bi)

              Q1 = L1.sub()
              xn, b_xn = Q1.bf(1024, "xn")
              xlT_f, b_xlT = Q1.bf(8 * 256, "xlT/mixT")
              xlT = xlT_f.rearrange("p (k n) -> p k n", k=8)
              mixT, b_mixT = xlT, b_xlT
              stp = Ring([Q1.f32(16, f"st{i}") for i in range(4)])
              stp3 = Ring([Q1.f32(16, f"st3{i}") for i in range(4)])
              cqn, b_cqn = Q1.bf(256, "cqn")
              cqnT_f, b_cqnT = Q1.bf(2 * 256, "cqnT")
              cqnT = cqnT_f.rearrange("p (k n) -> p k n", k=2)
              qnTs = [Q1.bf(256, f"qnT{i}") for i in range(2)]
              qpT_f, b_qpT = Q1.bf(8 * 256, "qpT/onT")
              qpT = qpT_f.rearrange("p (h n) -> p h n", h=8)
              onT, b_onT = qpT, b_qpT
              rt = (Q1.f32(512, "tc"), Q1.f32(512, "ts"))
              qr, b_qr = Q1.bf(512, "qr")
              qrT_f, b_qrT = Q1.bf(4 * 256, "qrT")
              qrT = qrT_f.rearrange("p (g n) -> p g n", g=4)
              pts = [Q1.bf(512, f"pT{i}") for i in range(NPT)]
              rr, b_rr = Q1.f32(8, "rr")
              on_f, b_on = Q1.bf(2 * 1024, "on")
              on = on_f.rearrange("p (j f) -> p j f", j=2)
              szz, b_szz = Q1.f32(512, "sz")
              mix_f, b_mix = G_c.bitcast(BF16), Buf("mix")
              mix = mix_f.rearrange("p (j f) -> p j f", j=2)
              tmpo, b_tmpo = szz, b_szz
              ybuf = [(h[:, i, :], Buf(f"y{i}")) for i in range(2)]
              b_outd = [Buf(f"outd{bi}{i}") for i in range(2)]
              sc1 = 1.0 / math.sqrt(192.0)
              ptc = 0
              yc = 0
              for s in range(1, 9):
                  tiles = [2 * s, 2 * s + 1]
                  for j, t in enumerate(tiles):
                      make_xlT(h[:, t, :], b_h[t], 1, bi, xn, b_xn, stp, xlT[:, :, j * 128:(j + 1) * 128], b_xlT, j)
                  for j, t in enumerate(tiles):
                      pq, b_pq = PS[5]
                      proj(pq, b_pq, xlT[:, :, j * 128:(j + 1) * 128], b_xlT, w1, b_w1, 0, 256)
                      rstd, _, b_st3 = rstd_of(stp3, pq[:, 0:256], b_pq, 256, 1e-6)
                      stt("dve", cqn, pq[:, 0:256], rstd, small[:, QN:QN + 256], ALU.mult, ALU.mult,
                          R=[b_pq, b_st3, b_small], W=[b_cqn])
                      for kc in range(2):
                          tr(PT[:, kc * 128:(kc + 1) * 128], cqn[:, kc * 128:(kc + 1) * 128], identb, R=[b_cqn, b_identb], W=[PTb[0]])
                      cp("act", cqnT[:, :, j * 128:(j + 1) * 128], PT[:, 0:256].rearrange("p (k n) -> p k n", k=2),
                         R=[PTb[0]], W=[b_cqnT])
                  for j, t in enumerate(tiles):
                      pq, b_pq = PS[5]
                      for kc in range(2):
                          mm(pq, cqnT[:, kc, j * 128:(j + 1) * 128], wqr[:, kc, :, :], kc == 0, kc == 1, R=[b_cqnT, b_wqr], W=[b_pq])
                      rope(rt, pq, b_pq, 8, t, qr, b_qr)
                      for g in range(4):
                          tr(PT[:, 512 + g * 128:512 + (g + 1) * 128], qr[:, g * 128:(g + 1) * 128], identb,
                             R=[b_qr, b_identb], W=[PTb[1]])
                      cp("act", qrT[:, :, j * 128:(j + 1) * 128], PT[:, 512:1024].rearrange("p (g n) -> p g n", g=4),
                         R=[PTb[1]], W=[b_qrT])
                  for hh in range(8):
                      pq, b_pq = PS[5 + hh % 2]
                      qnT, b_qnT = qnTs[hh % 2]
                      for kc in range(2):
                          mm(pq[:, 0:256], wqn[:, kc, hh, :], cqnT[:, kc, :], kc == 0, kc == 1, R=[b_wqn, b_cqnT], W=[b_pq])
                      cp("dve", qnT, pq[:, 0:256], R=[b_pq], W=[b_qnT])
                      mm(pq[:, 256:512], WkT[:, hh, :], qnT, True, True, R=[b_WkT, b_qnT], W=[b_pq])
                      cp("dve", qpT[:, hh, :], pq[:, 256:512], R=[b_pq], W=[b_qpT])
                  items = [(hh, cpi) for hh in range(8) for cpi in range(NT // 2)]

                  def qk1(it, idx):
                      hh, cpi = it
                      hp = hh % 2
                      psc, b_psc = PS[idx % 3]
                      for cc in range(2):
                          c = 2 * cpi + cc
                          mm(psc[:, cc * 256:(cc + 1) * 256], ckvT[:, c * 128:(c + 1) * 128], qpT[:, hh, :], True, False,
                             R=[b_ckvT, b_qpT], W=[b_psc])
                          mm(psc[:, cc * 256:(cc + 1) * 256], krT[hp * 64:(hp + 1) * 64, c * 128:(c + 1) * 128],
                             qrT[hp * 64:(hp + 1) * 64, hh // 2, :], False, True, R=[b_krT, b_qrT], W=[b_psc])

                  def pv1(it, pT, b_pT):
                      hh, cpi = it
                      po, b_po = PS[3 + hh % 2]
                      for cc in range(2):
                          c = 2 * cpi + cc
                          for j in range(2):
                              mm(po[:, j * 132:j * 132 + 129], pT[:, cc * 256 + j * 128:cc * 256 + (j + 1) * 128],
                                 VS[:, c, 0:129], c == 0 and j == 0, c == NT - 1, R=[b_pT, b_VS], W=[b_po], skip=True)

                  for k0 in range(2):
                      qk1(items[k0], k0)
                  for idx, it in enumerate(items):
                      hh, cpi = it
                      psc, b_psc = PS[idx % 3]
                      pT, b_pT = pts[ptc % NPT]
                      ptc += 1
                      act(pT, psc, AF.Exp, R=[b_psc], W=[b_pT], scale=sc1)
                      if idx + 2 < len(items):
                          qk1(items[idx + 2], idx + 2)
                      pv1(it, pT, b_pT)
                      if cpi != NT // 2 - 1:
                          continue
                      po, b_po = PS[3 + hh % 2]
                      for j in range(2):
                          recip(rr[:, j:j + 1], po[:, j * 132 + 128:j * 132 + 129], R=[b_po], W=[b_rr])
                          ts("dve", on[:, j, hh * 128:(hh + 1) * 128], po[:, j * 132:j * 132 + 128], rr[:, j:j + 1], None,
                             ALU.mult, None, R=[b_po, b_rr], W=[b_on])
                  for j, t in enumerate(tiles):
                      for hh in range(8):
                          tr(PT[:, hh * 128:(hh + 1) * 128], on[:, j, hh * 128:(hh + 1) * 128], identb, R=[b_on, b_identb],
                             W=[PTb[hh // 4]])
                      cp("act", onT[:, 0:4, j * 128:(j + 1) * 128], PT[:, 0:512].rearrange("p (h n) -> p h n", h=4),
                         R=[PTb[0]], W=[b_onT])
                      cp("dve", onT[:, 4:8, j * 128:(j + 1) * 128], PT[:, 512:1024].rearrange("p (h n) -> p h n", h=4),
                         R=[PTb[1]], W=[b_onT])
                  for j, t in enumerate(tiles):
                      for n in range(2):
                          pe_, b_pe = PS[5]
                          pz, b_pz = PS[6]
                          for q in range(4):
                              hh = n * 4 + q
                              mm(pe_[:, q * 128:(q + 1) * 128], onT[:, hh, j * 128:(j + 1) * 128], wkv[:, hh, 128:256], True, True,
                                 R=[b_onT, b_wkv], W=[b_pe])
                          proj(pz, b_pz, xlT[:, :, j * 128:(j + 1) * 128], b_xlT, w1, b_w1, 448 + n * 512, 512)
                          act(szz, pz, AF.Silu, R=[b_pz], W=[b_szz])
                          tt("dve", mix[:, j, n * 512:(n + 1) * 512], pe_, szz, ALU.mult, R=[b_pe, b_szz], W=[b_mix])
                  for j, t in enumerate(tiles):
                      for kc in range(8):
                          tr(PT[:, kc * 128:(kc + 1) * 128], mix[:, j, kc * 128:(kc + 1) * 128], identb, R=[b_mix, b_identb],
                             W=[PTb[kc // 4]])
                      cp("act", mixT[:, 0:4, j * 128:(j + 1) * 128], PT[:, 0:512].rearrange("p (k n) -> p k n", k=4),
                         R=[PTb[0]], W=[b_mixT])
                      cp("dve", mixT[:, 4:8, j * 128:(j + 1) * 128], PT[:, 512:1024].rearrange("p (k n) -> p k n", k=4),
                         R=[PTb[1]], W=[b_mixT])
                  for j, t in enumerate(tiles):
                      for n in range(2):
                          po, b_po = PS[5 + n]
                          for kc in range(8):
                              mm(po, mixT[:, kc, j * 128:(j + 1) * 128], wo1[:, kc, n * 512:(n + 1) * 512], kc == 0, kc == 7,
                                 R=[b_mixT, b_wo1], W=[b_po])
                          tt("dve", tmpo, po, G_l[:, n * 512:(n + 1) * 512], ALU.mult, R=[b_po, b_Gl], W=[b_tmpo])
                          tt("pool", h[:, t, n * 512:(n + 1) * 512], tmpo, h[:, t, n * 512:(n + 1) * 512], ALU.add,
                             R=[b_tmpo, b_h[t]], W=[b_h[t]])
                      y, b_y = ybuf[yc % 2]
                      rstd, _, b_stf = rstd_of(stp, h[:, t, :], b_h[t], 1024, 1e-6)
                      stt("dve", y, h[:, t, :], rstd, small[:, FINAL:FINAL + 1024], ALU.mult, ALU.mult,
                          R=[b_h[t], b_stf, b_small], W=[b_y])
                      dma("sp", out_d[bi, (t - 2) * 128:(t - 1) * 128, :], y, R=[b_y], W=[b_outd[yc % 2]])
                      yc += 1
              S.barrier()
        except _Stop:
            pass
        S.emit()
        build_program.last_S = S
    return nc


_NC_CACHE = {}


def _rope_tables():
    seq, gw, dim = 2048, 64, 64
    rows = seq // gw
    row = np.repeat(np.arange(rows), gw).astype(np.float32)
    col = np.tile(np.arange(gw), rows).astype(np.float32)
    half = dim // 2
    inv = (np.float32(10000.0) ** (-np.arange(0, half, 2, dtype=np.float32) / np.float32(half))).astype(np.float32)
    ang_r = row[:, None] * inv[None, :]
    ang_c = col[:, None] * inv[None, :]
    ang = np.concatenate([ang_r, ang_r, ang_c, ang_c], axis=-1).astype(np.float32)
    cos = np.cos(ang).astype(np.float32)
    sin = np.sin(ang).astype(np.float32)
    sgn = np.tile(np.concatenate([-np.ones(16), np.ones(16)]), 2).astype(np.float32)
    cos_all = np.concatenate([np.ones((256, 64), np.float32), cos], 0)
    sin_all = np.concatenate([np.zeros((256, 64), np.float32), sin * sgn[None, :]], 0)
    cos_t = cos_all.reshape(NT, 128, 64).transpose(1, 0, 2).reshape(128, NT * 64)
    sin_t = sin_all.reshape(NT, 128, 64).transpose(1, 0, 2).reshape(128, NT * 64)
    return cos_t, sin_t


def _pack_small(core, c, c_ctx, norm_w, ada_b, a_bs, a_ln_w, a_ln_b, lq1, lk1, lq2, lk2, subln, qn, kvn, final_w):
    sm = np.zeros((128, NS), np.float32)
    rep = lambda v: np.broadcast_to(np.asarray(v, np.float32)[None, :], (128, v.shape[0]))
    sm[:, LNW:LNW + 512] = rep(a_ln_w[0])
    sm[:, LNB:LNB + 512] = rep(a_ln_b[0])
    sm[:, SUBLN:SUBLN + 128] = rep(subln[0])
    sm[:, QN:QN + 256] = rep(qn[0])
    sm[:, KVN:KVN + 128] = rep(kvn[0])
    sm[:, FINAL:FINAL + 1024] = rep(final_w)
    sm[:, LQ1:LQ1 + 64] = rep(lq1[0])
    sm[:, LK1:LK1 + 64] = rep(lk1[0])
    sm[:, LQ2:LQ2 + 64] = rep(lq2[0])
    sm[:, LK2:LK2 + 64] = rep(lk2[0])
    sm[:, ABS:ABS + 8] = a_bs[0].T
    sm[:, NW:NW + 16] = norm_w.reshape(2, 8, 128).transpose(2, 0, 1).reshape(128, 16)
    sm[:, ADAB:ADAB + 48] = ada_b.reshape(2, 24, 128).transpose(2, 0, 1).reshape(128, 48)
    cond = np.stack([c[2 * core], c[2 * core + 1], c_ctx], 0)
    sm[:, COND:COND + 24] = cond.reshape(3, 8, 128).transpose(2, 1, 0).reshape(128, 24)
    cos_t, sin_t = _rope_tables()
    sm[:, COS:COS + NT * 64] = cos_t
    sm[:, SIN:SIN + NT * 64] = sin_t
    sm[:, IDENT:IDENT + 128] = np.eye(128, dtype=np.float32)
    return sm


def kernel(x, c, ctx, c_ctx, norm_w, ada_w, ada_b, even_w_in, a_ws, a_bs, a_ln_w, a_ln_b,
           b_lq1, b_lk1, b_lq2, b_lk2, b_subln_w, even_w_out, odd_w_in, c_q_norm_w, c_wq_b,
           c_kv_norm_w, c_wkv_b, odd_w_out, final_w, _dbg=False, _stop=99, _ncores=8):
    f = lambda a: np.ascontiguousarray(np.asarray(a, dtype=np.float32))
    x, c, ctx, c_ctx = f(x), f(c), f(ctx), f(c_ctx)
    key = (bool(_dbg), _stop)
    if key not in _NC_CACHE:
        _NC_CACHE[key] = build_program(dbg=bool(_dbg), stop=_stop)
    nc = _NC_CACHE[key]
    shared = {
        "ada_w": f(ada_w), "w_in0": f(even_w_in)[0], "w_out0": f(even_w_out)[0], "w_in1": f(odd_w_in)[0],
        "wq_b": f(c_wq_b)[0], "wkv_b": f(c_wkv_b)[0], "w_out1": f(odd_w_out)[0],
        "a_wsT": np.ascontiguousarray(f(a_ws)[0].transpose(2, 0, 1)),
        "adab_g": np.ascontiguousarray(f(ada_b)[:, 2048:3072]),
    }
    in_maps = []
    for core in range(_ncores):
        xin = np.concatenate([ctx[2 * core:2 * core + 2], x[2 * core:2 * core + 2]], axis=1)
        sm = _pack_small(core, c, c_ctx, f(norm_w), f(ada_b), f(a_bs), f(a_ln_w), f(a_ln_b), f(b_lq1), f(b_lk1),
                         f(b_lq2), f(b_lk2), f(b_subln_w), f(c_q_norm_w), f(c_kv_norm_w), f(final_w))
        m = {"xin": np.ascontiguousarray(xin), "small": sm}
        m.update(shared)
        in_maps.append(m)
    res = run_bass_kernel_spmd(nc, in_maps, core_ids=list(range(_ncores)))
    out = np.concatenate([r["out"] for r in res.results], axis=0).astype(np.float32)
    if _dbg:
        return out, np.concatenate([r["dbg_h"] for r in res.results], axis=0)
    return out
```

```python
import math
from contextlib import ExitStack
import numpy as np
import concourse.bass as bass
import concourse.mybir as mybir
from concourse.bass_utils import run_bass_kernel_spmd

F32 = mybir.dt.float32
BF16 = mybir.dt.bfloat16
AF = mybir.ActivationFunctionType
ALU = mybir.AluOpType

import os
XLT_DVE_ONLY = os.environ.get('XLT_DVE_ONLY', '0') == '1'
EPOCH = 3800
NT = 18
D = 1024

LNW, LNB, SUBLN, QN, KVN, FINAL = 0, 512, 1024, 1152, 1408, 1536
LQ1, LK1, LQ2, LK2 = 2560, 2624, 2688, 2752
ABS, NW, ADAB, COND, COS, SIN, IDENT = 2816, 2824, 2840, 2888, 2912, 4064, 5216
NS = 5344


class Buf:
    __slots__ = ("name", "lw", "rd", "dsem", "dval", "excl", "dq")

    def __init__(self, name, excl=False):
        self.name = name
        self.excl = excl
        self.lw = None
        self.rd = {}
        self.dsem = None
        self.dq = None
        self.dval = 0


class Sched:
    def __init__(self, nc, stack):
        self.nc = nc
        self.stack = stack
        self.names = ["pe", "act", "dve", "pool", "sp"]
        self.streams = {n: [] for n in self.names}
        self.sems = {n: [self._sem(f"s_{n}0")] for n in self.names}
        self.cnt = {n: 0 for n in self.names}
        self.seen = {n: {} for n in self.names}
        self.dbufs = []
        self.free_dsems = {}

    def _sem(self, name):
        return self.stack.enter_context(self.nc.semaphore(name))

    def _cur(self, eng):
        if self.cnt[eng] >= EPOCH:
            self.sems[eng].append(self._sem(f"s_{eng}{len(self.sems[eng])}"))
            self.cnt[eng] = 0
        return self.sems[eng][-1]

    def _need(self, eng, waits, dep):
        if dep is None:
            return
        sem, val = dep
        if self.seen[eng].get(sem, 0) >= val:
            return
        if waits.get(sem, 0) < val:
            waits[sem] = val

    def op(self, eng, fn, R=(), W=()):
        if any(b.excl for b in R):
            W = list(W) + [b for b in R if b.excl and b not in W]
            R = [b for b in R if not b.excl]
        waits = {}
        own = set(id(s) for s in self.sems[eng])
        for b in R:
            self._need(eng, waits, b.lw)
        for b in W:
            if b.lw is not None and id(b.lw[0]) not in own:
                self._need(eng, waits, b.lw)
            for r in b.rd.items():
                if id(r[0]) in own:
                    continue
                self._need(eng, waits, r)
        sem = self._cur(eng)
        self.cnt[eng] += 1
        val = self.cnt[eng]
        for s, v in waits.items():
            self.seen[eng][s] = v
        self.streams[eng].append((list(waits.items()), fn, sem, 1))
        for b in W:
            b.lw = (sem, val)
            b.rd = {}
        for b in R:
            b.rd[sem] = val

    def dma(self, q, fn, R=(), W=()):
        waits = {}
        for b in R:
            self._need(q, waits, b.lw)
        tgt = W[0]
        if tgt.dsem is None:
            tgt.dq = q
            if self.free_dsems.get(q):
                tgt.dsem, tgt.dval = self.free_dsems[q].pop()
            else:
                tgt.dsem = self._sem(f"d{len(self.dbufs)}_{tgt.name}".replace("/", "_"))
            self.dbufs.append(tgt)
        for b in W:
            if b.lw is not None and b.lw[0] is not tgt.dsem:
                self._need(q, waits, b.lw)
            for r in b.rd.items():
                self._need(q, waits, r)
        for s, v in waits.items():
            self.seen[q][s] = v
        tgt.dval += 16
        self.streams[q].append((list(waits.items()), fn, tgt.dsem, 16))
        for b in W:
            b.lw = (tgt.dsem, tgt.dval)
            b.rd = {}
        for b in R:
            b.rd[tgt.dsem] = tgt.dval

    def barrier(self):
        deps = []
        for n in self.names:
            if self.cnt[n] > 0:
                deps.append((self.sems[n][-1], self.cnt[n]))
            for s in self.sems[n][:-1]:
                deps.append((s, EPOCH))
        for b in self.dbufs:
            deps.append((b.dsem, b.dval))
        for n in self.names:
            waits = {}
            for d in deps:
                self._need(n, waits, d)
            for s, v in waits.items():
                self.seen[n][s] = v
            if waits:
                self.streams[n].append((list(waits.items()), None, None, 0))
        for b in self.dbufs:
            if b.dval < 3000:
                self.free_dsems.setdefault(b.dq, []).append((b.dsem, b.dval))
            b.dsem = None
        self.dbufs = []

    def emit(self):
        with self.nc.Block() as block:
            def mk(name):
                def body(e):
                    for waits, fn, sem, inc in self.streams[name]:
                        for s, v in waits:
                            e.wait_ge(s, v)
                        if fn is not None:
                            fn(e).then_inc(sem, inc)
                return body
            block.tensor(mk("pe"))
            block.scalar(mk("act"))
            block.vector(mk("dve"))
            block.gpsimd(mk("pool"))
            block.sync(mk("sp"))


class Ring:
    def __init__(self, slots):
        self.slots, self.i = slots, 0

    def next(self):
        self.i += 1
        return self.slots[self.i % len(self.slots)]


class Region:
    def __init__(self, big, lo, hi):
        self.big, self.lo, self.hi, self.cur = big, lo, hi, lo

    def f32(self, words, name="t"):
        a = self.cur
        self.cur += (words + 7) // 8 * 8
        assert self.cur <= self.hi, f"SBUF region overflow at {name}: {self.cur} > {self.hi}"
        return self.big[:, a:a + words], Buf(name)

    def bf(self, elems, name="t"):
        ap, b = self.f32((elems + 1) // 2, name)
        return ap.bitcast(BF16), b

    def sub(self):
        return Region(self.big, self.cur, self.hi)


class _Stop(Exception):
    pass


def build_program(dbg=False, stop=99):
    nc = bass.Bass("TRN2", target_bir_lowering=False)

    def din(name, shape):
        return nc.dram_tensor(name, shape, F32, kind="ExternalInput").ap()

    xin = din("xin", [2, NT * 128, D])
    small_d = din("small", [128, NS])
    ada_w = din("ada_w", [2, D, 3 * D])
    w_in0 = din("w_in0", [D, 3584])
    w_out0 = din("w_out0", [D, D])
    w_in1 = din("w_in1", [D, 1472])
    wq_b = din("wq_b", [256, 1536])
    wkv_b = din("wkv_b", [128, 2048])
    w_out1 = din("w_out1", [D, D])
    a_wsT = din("a_wsT", [128, 8, 128])
    out_d = nc.dram_tensor("out", [2, 2048, D], F32, kind="ExternalOutput").ap()
    adab_g = din("adab_g", [2, D])
    if dbg:
        dbg_h = nc.dram_tensor("dbg_h", [2, NT * 128, D], F32, kind="ExternalOutput").ap()

    with ExitStack() as st:
        S = Sched(nc, st)
        BIGW = 53200
        big = st.enter_context(nc.sbuf_tensor("big", [128, BIGW], F32))[:, :]
        PS = []
        for i in range(7):
            PS.append((st.enter_context(nc.psum_tensor(f"ps{i}", [128, 512], F32))[:, :], Buf(f"ps{i}", excl=True)))
        PT = st.enter_context(nc.psum_tensor("pt", [128, 1024], BF16))[:, :]
        b_PT = Buf("pt", excl=True)
        PTb = [b_PT, b_PT]

        def mm(out, lhsT, rhs, start, stop, R, W, skip=False):
            S.op("pe", lambda e: e.matmul(out, lhsT=lhsT, rhs=rhs, start=start, stop=stop, skip_group_check=skip), R=R, W=W)

        def tr(out, in_, ident, R, W):
            S.op("pe", lambda e: e.transpose(out, in_, ident), R=R, W=W)

        def act(out, in_, func, R, W, scale=1.0, bias=0.0):
            S.op("act", lambda e: e.activation(out=out, in_=in_, func=func, bias=bias, scale=scale), R=R, W=W)

        def tt(eng, out, in0, in1, op, R, W):
            S.op(eng, lambda e: e.tensor_tensor(out=out, in0=in0, in1=in1, op=op), R=R, W=W)

        def ts(eng, out, in0, s1, s2, op0, op1, R, W):
            if s2 is None:
                S.op(eng, lambda e: e.tensor_scalar(out=out, in0=in0, scalar1=s1, scalar2=None, op0=op0), R=R, W=W)
            else:
                S.op(eng, lambda e: e.tensor_scalar(out=out, in0=in0, scalar1=s1, scalar2=s2, op0=op0, op1=op1), R=R, W=W)

        def stt(eng, out, in0, scalar, in1, op0, op1, R, W):
            S.op(eng, lambda e: e.scalar_tensor_tensor(out=out, in0=in0, scalar=scalar, in1=in1, op0=op0, op1=op1), R=R, W=W)

        def cp(eng, out, in_, R, W):
            if eng == "act":
                S.op("act", lambda e: e.copy(out=out, in_=in_), R=R, W=W)
            else:
                S.op(eng, lambda e: e.tensor_copy(out=out, in_=in_), R=R, W=W)

        def recip(out, in_, R, W):
            S.op("dve", lambda e: e.reciprocal(out=out, in_=in_), R=R, W=W)

        def dma(q, out, in_, R, W):
            S.dma(q, lambda e: e.dma_start(out=out, in_=in_), R=R, W=W)

        top = Region(big, 0, BIGW)
        small, b_small = top.f32(NS, "small")
        h_all, _ = top.f32(NT * D, "h")
        h = h_all.rearrange("p (t f) -> p t f", t=NT)
        b_h = [Buf(f"h{t}") for t in range(NT)]
        G_l, b_Gl = top.f32(D, "G_l")
        G_c, b_Gc = top.f32(D, "G_c")
        identb, b_identb = top.bf(128, "identb")
        wsT_f, b_wsT = top.bf(8 * 128, "wsT")
        wsT = wsT_f.rearrange("p (g q) -> p g q", g=8)
        biasT, b_biasT = top.f32(512, "biasT")
        mod_f, b_mod = top.f32(2 * 72, "mod")
        mod = mod_f.rearrange("p (l f r) -> p l f r", l=2, r=3)
        acol_f, b_acol = top.f32(2 * 3 * 8, "acol")
        acol = acol_f.rearrange("p (l r f) -> p l r f", l=2, r=3)
        scT_f, b_scT = top.bf(24, "scT")
        scT = scT_f.rearrange("p (k r) -> p k r", r=3)
        lamt, b_lam = top.f32(8, "lam")
        subln8, b_subln8 = top.f32(128, "subln8")
        stats, b_stats = top.f32(32, "stats")
        condrep_f, b_condrep = top.bf(8 * 128, "condrep")
        condrep = condrep_f.rearrange("p (k n) -> p k n", k=8)
        brow, b_brow = top.bf(1024, "brow")
        ones1, b_ones1 = top.bf(128, "ones1")
        wKV_f, b_wKV = Region(big, BIGW - 4096, BIGW).bf(8 * 1024, "wKV")
        wKV = wKV_f.rearrange("p (k n) -> p k n", k=8)
        phase0 = top.cur

        dma("sp", small, small_d[:, :], R=[], W=[b_small])
        dma("pool", identb, small_d[:, IDENT:IDENT + 128], R=[], W=[b_identb])
        dma("pool", wsT, a_wsT[:, :, :], R=[], W=[b_wsT])
        S.op("dve", lambda e: e.memset(ones1, 1.0), R=[], W=[b_ones1])

        act(scT, small[:, COND:COND + 24].rearrange("p (k r) -> p k r", r=3), AF.Silu, R=[b_small], W=[b_scT])
        setup = Region(big, phase0, BIGW)
        wada = []
        for i in range(2):
            ap, b = setup.bf(8 * 512, f"wada{i}")
            wada.append((ap.rearrange("p (k n) -> p k n", k=8), b))
        pcs = 0
        for l in range(2):
            pm, b_pm = PS[l]
            for j in range(6):
                wt, b_wt = wada[pcs % 2]
                pcs += 1
                dma("pool", wt, ada_w[l, :, j * 512:(j + 1) * 512].rearrange("(k p) n -> p k n", p=128), R=[], W=[b_wt])
                for fl in range(4):
                    fc = j * 4 + fl
                    for kc in range(8):
                        mm(pm[:, fc * 3:fc * 3 + 3], wt[:, kc, fl * 128:(fl + 1) * 128], scT[:, kc, :],
                           kc == 0, kc == 7, R=[b_wt, b_scT], W=[b_pm])
            tt("dve", mod[:, l, :, :], pm[:, 0:72].rearrange("p (f r) -> p f r", r=3),
               small[:, ADAB + l * 24:ADAB + (l + 1) * 24].unsqueeze(2).to_broadcast([128, 24, 3]), ALU.add,
               R=[b_pm, b_small], W=[b_mod])
            for r in range(3):
                stt("dve", acol[:, l, r, :], mod[:, l, 8:16, r], 1.0, small[:, NW + l * 8:NW + (l + 1) * 8],
                    ALU.add, ALU.mult, R=[b_mod, b_small], W=[b_acol])
        lt, b_lt = setup.f32(128, "lamtmp")
        tt("dve", lt[:, 0:64], small[:, LQ1:LQ1 + 64], small[:, LK1:LK1 + 64], ALU.mult, R=[b_small], W=[b_lt])
        tt("dve", lt[:, 64:128], small[:, LQ2:LQ2 + 64], small[:, LK2:LK2 + 64], ALU.mult, R=[b_small], W=[b_lt])
        S.op("dve", lambda e: e.reduce_sum(out=lamt[:, 0:2], in_=lt.rearrange("p (a b) -> p a b", a=2),
                                           axis=mybir.AxisListType.X), R=[b_lt], W=[b_lam])
        act(lamt[:, 2:4], lamt[:, 0:2], AF.Exp, R=[b_lam], W=[b_lam])
        tt("dve", lamt[:, 4:5], lamt[:, 3:4], lamt[:, 2:3], ALU.subtract, R=[b_lam], W=[b_lam])
        ts("dve", lamt[:, 5:6], lamt[:, 4:5], -0.2, None, ALU.add, None, R=[b_lam], W=[b_lam])
        neglam = lamt[:, 5:6]
        ts("dve", subln8, small[:, SUBLN:SUBLN + 128], 0.8, None, ALU.mult, None, R=[b_small], W=[b_subln8])
        cp("dve", biasT.rearrange("p (g d) -> p g d", g=8), small[:, ABS:ABS + 8].unsqueeze(2).to_broadcast([128, 8, 64]),
           R=[b_small], W=[b_biasT])

        def build_G(l, conds):
            load_w("pool", wKV, b_wKV, ada_w[l], 2048, 1024)
            dma("pool", brow[0:1, :], adab_g[l:l + 1, :], R=[], W=[b_brow])
            pg, b_pg = PS[6]
            for (r, G, b_G) in conds:
                cp("dve", condrep, scT[:, :, r:r + 1].to_broadcast([128, 8, 128]), R=[b_scT], W=[b_condrep])
                for n in range(2):
                    for kc in range(8):
                        mm(pg, condrep[:, kc, :], wKV[:, kc, n * 512:(n + 1) * 512], kc == 0, False, R=[b_condrep, b_wKV], W=[b_pg])
                    mm(pg, ones1[0:1, :], brow[0:1, n * 512:(n + 1) * 512], False, True, R=[b_ones1, b_brow], W=[b_pg])
                    cp("dve", G[:, n * 512:(n + 1) * 512], pg, R=[b_pg], W=[b_G])

        def rstd_of(reg_stats, src, b_src, n, eps, name="rs"):
            st_ap, b_st = reg_stats.next() if isinstance(reg_stats, Ring) else reg_stats
            nch = (n + 511) // 512
            w = n // nch
            for c in range(nch):
                S.op("dve", (lambda c: lambda e: e.bn_stats(out=st_ap[:, c * 6:(c + 1) * 6], in_=src[:, c * w:(c + 1) * w]))(c),
                     R=[b_src], W=[b_st])
            S.op("dve", lambda e: e.bn_aggr(out=st_ap[:, 12:14], in_=st_ap[:, 0:6 * nch].rearrange("p (c s) -> p c s", s=6)),
                 R=[b_st], W=[b_st])
            stt("dve", st_ap[:, 14:15], st_ap[:, 12:13], st_ap[:, 12:13], st_ap[:, 13:14], ALU.mult, ALU.add, R=[b_st], W=[b_st])
            act(st_ap[:, 15:16], st_ap[:, 14:15], AF.Ln, R=[b_st], W=[b_st], bias=eps)
            act(st_ap[:, 15:16], st_ap[:, 15:16], AF.Exp, R=[b_st], W=[b_st], scale=-0.5)
            return st_ap[:, 15:16], st_ap[:, 12:13], b_st

        def ckpt(k, bi):
            if stop == k:
                S.barrier()
                if dbg:
                    b_dbg = Buf("dbgstop")
                    for t in range(NT):
                        dma("sp", dbg_h[bi, t * 128:(t + 1) * 128, :], h[:, t, :], R=[b_h[t]], W=[b_dbg])
                    S.barrier()
                raise _Stop()

        PT2 = PS[6][0].bitcast(BF16)

        def prep_xn(src, b_src, xn, b_xn, st_pair):
            rstd, _, b_st = rstd_of(st_pair, src, b_src, 1024, 1e-6)
            act(xn, src, AF.Identity, R=[b_src, b_st], W=[b_xn], scale=rstd)

        def make_xlT(src, b_src, l, r, xn, b_xn, st_pair, xlT_dst, b_xlT, alt, two_banks=False, pre=False):
            if not pre:
                prep_xn(src, b_src, xn, b_xn, st_pair)
            ckpt(0.57, 0)
            if two_banks:
                for half in range(2):
                    pt_ap, bp = (PT, b_PT) if half == 0 else (PT2, PS[6][1])
                    for q in range(4):
                        fc = half * 4 + q
                        tr(pt_ap[:, q * 128:(q + 1) * 128], xn[:, fc * 128:(fc + 1) * 128], identb, R=[b_xn, b_identb], W=[bp])
                for q in range(4):
                    for half in range(2):
                        pt_ap, bp = (PT, b_PT) if half == 0 else (PT2, PS[6][1])
                        fc = half * 4 + q
                        sc = acol[:, l, r, fc:fc + 1]
                        bi_ = mod[:, l, fc, r:r + 1]
                        if half == 0:
                            ts("dve", xlT_dst[:, fc, :], pt_ap[:, q * 128:(q + 1) * 128], sc, bi_, ALU.mult, ALU.add,
                               R=[bp, b_acol, b_mod], W=[b_xlT])
                        else:
                            act(xlT_dst[:, fc, :], pt_ap[:, q * 128:(q + 1) * 128], AF.Identity, R=[bp, b_acol, b_mod], W=[b_xlT],
                                scale=sc, bias=bi_)
                return
            for half in range(2):
                bp = PTb[half]
                if half == 1:
                    ckpt(0.596, 0)
                for q in range(4):
                    fc = half * 4 + q
                    tr(PT[:, fc * 128:(fc + 1) * 128], xn[:, fc * 128:(fc + 1) * 128], identb, R=[b_xn, b_identb], W=[bp])
                if half == 1:
                    ckpt(0.597, 0)
                ckpt(0.58, 0)
                for q in range(4):
                    fc = half * 4 + q
                    sc = acol[:, l, r, fc:fc + 1]
                    bi = mod[:, l, fc, r:r + 1]
                    if q == 1:
                        ckpt(0.59, 0)
                    if q == 2:
                        ckpt(0.595, 0)
                    if (fc + alt) % 2 == 0 or XLT_DVE_ONLY:
                        ts("dve", xlT_dst[:, fc, :], PT[:, fc * 128:(fc + 1) * 128], sc, bi, ALU.mult, ALU.add,
                           R=[bp, b_acol, b_mod], W=[b_xlT])
                    else:
                        act(xlT_dst[:, fc, :], PT[:, fc * 128:(fc + 1) * 128], AF.Identity, R=[bp, b_acol, b_mod], W=[b_xlT],
                            scale=sc, bias=bi)

        def proj(ps, b_ps, xlT_tile, b_xlT, W, b_W, c0, n, nk=8):
            for kc in range(nk):
                mm(ps[:, 0:n], xlT_tile[:, kc, :], W[:, kc, c0:c0 + n], kc == 0, kc == nk - 1, R=[b_xlT, b_W], W=[b_ps])

        def rope(reg_t, src, b_src, ng, t, dst, b_dst, eng2="pool"):
            (tc_, b_tc), (ts_, b_ts) = reg_t
            n = ng * 64
            cosb = small[:, COS + t * 64:COS + (t + 1) * 64]
            sinb = small[:, SIN + t * 64:SIN + (t + 1) * 64]
            sv = src.rearrange("p (g s t d) -> p g s t d", g=ng, s=2, t=2)
            tv = ts_[:, 0:n].rearrange("p (g s t d) -> p g s t d", g=ng, s=2, t=2)
            sn = sinb.rearrange("p (s t d) -> p s t d", s=2, t=2)
            tt("dve", tc_[:, 0:n].rearrange("p (g d) -> p g d", g=ng), src.rearrange("p (g d) -> p g d", g=ng),
               cosb.unsqueeze(1).to_broadcast([128, ng, 64]), ALU.mult, R=[b_src, b_small], W=[b_tc])
            for hf in range(2):
                tt("dve", tv[:, :, :, hf, :], sv[:, :, :, 1 - hf, :],
                   sn[:, :, hf, :].unsqueeze(1).to_broadcast([128, ng, 2, 16]), ALU.mult, R=[b_src, b_small], W=[b_ts])
            tt(eng2, dst, tc_[:, 0:n], ts_[:, 0:n], ALU.add, R=[b_tc, b_ts], W=[b_dst])

        def load_w(q, dst, b_dst, src2d, c0, n):
            dma(q, dst, src2d[:, c0:c0 + n].rearrange("(k p) n -> p k n", p=128), R=[], W=[b_dst])

        S.barrier()

        def ckpt(k, bi):
            if stop == k:
                S.barrier()
                if dbg:
                    b_dbg = Buf("dbgstop")
                    for t in range(NT):
                        dma("sp", dbg_h[bi, t * 128:(t + 1) * 128, :], h[:, t, :], R=[b_h[t]], W=[b_dbg])
                    S.barrier()
                raise _Stop()

        try:
          ckpt(0, 0)
          for bi in range(2):
              L0 = Region(big, phase0, BIGW)
              wQZ_f, b_wQZ = L0.bf(8 * 1024, "wQZ")
              wQZ = wQZ_f.rearrange("p (k n) -> p k n", k=8)
              woB_f, b_woB = L0.bf(4 * 1024, "woB")
              woB = woB_f.rearrange("p (k n) -> p k n", k=4)
              build_G(0, [(bi, G_l, b_Gl), (2, G_c, b_Gc)])
              ckpt(0.5, bi)

              A = Region(big, L0.cur, BIGW - 4096)
              wA_f, b_wA = A.bf(8 * 1536, "wA")
              wA = wA_f.rearrange("p (k n) -> p k n", k=8)
              woA_f, b_woA = A.bf(4 * 1024, "woA")
              woA = woA_f.rearrange("p (k n) -> p k n", k=4)
              load_w("pool", wA, b_wA, w_in0, 0, 1536)
              dma("pool", woA, w_out0[0:512, :].rearrange("(k p) n -> p k n", p=128), R=[], W=[b_woA])
              xs = [A.f32(1024, f"xs{i}") for i in range(2)]
              xn, b_xn = A.bf(1024, "xn")
              xlTs = []
              for i in range(2):
                  ap, b = A.bf(8 * 128, f"xlT{i}")
                  xlTs.append((ap.rearrange("p (k n) -> p k n", k=8), b))
              stp = Ring([A.f32(16, f"st{i}") for i in range(4)])
              stp2 = A.f32(16, "st2")
              gu, b_gu = A.f32(512, "gu")
              gv, b_gv = A.f32(512, "gv")
              sz, b_sz = A.f32(512, "sz")
              xh, b_xh = A.f32(512, "xh")
              t1, b_t1 = xh, b_xh
              vn, b_vn = A.bf(512, "vn")
              mixa, b_mixa = A.bf(512, "mixa")
              mixT_f, b_mixT = vn, b_vn
              mixT = mixT_f.rearrange("p (k n) -> p k n", k=4)
              tmpo, b_tmpo = xh, b_xh
              load_w("pool", wKV, b_wKV, w_in0, 2048, 1024)
              for t in range(NT):
                  r = 2 if t < 2 else bi
                  G, b_G = (G_c, b_Gc) if t < 2 else (G_l, b_Gl)
                  x_t, b_x = xs[t % 2]
                  xlT, b_xlT = xlTs[t % 2]
                  if t == 0:
                      dma("sp", x_t, xin[bi, t * 128:(t + 1) * 128, :], R=[], W=[b_x])
                      prep_xn(x_t, b_x, xn, b_xn, stp)
                  make_xlT(x_t, b_x, 0, r, xn, b_xn, stp, xlT, b_xlT, t, two_banks=True, pre=True)
                  for i in range(3):
                      proj(PS[i][0], PS[i][1], xlT, b_xlT, wA, b_wA, i * 512, 512)
                  if t + 1 < NT:
                      x_nx, b_xnx = xs[(t + 1) % 2]
                      dma("sp", x_nx, xin[bi, (t + 1) * 128:(t + 2) * 128, :], R=[], W=[b_xnx])
                      prep_xn(x_nx, b_xnx, xn, b_xn, stp)
                  act(gu, PS[0][0], AF.Gelu, R=[PS[0][1]], W=[b_gu])
                  act(gv, PS[1][0], AF.Gelu, R=[PS[1][1]], W=[b_gv])
                  act(sz, PS[2][0], AF.Silu, R=[PS[2][1]], W=[b_sz])
                  st2, b_st2 = stp2
                  S.op("dve", (lambda st2, gv: lambda e: e.bn_stats(out=st2[:, 0:6], in_=gv))(st2, gv), R=[b_gv], W=[b_st2])
                  S.op("dve", (lambda st2: lambda e: e.bn_aggr(out=st2[:, 12:14], in_=st2[:, 0:6]))(st2), R=[b_st2], W=[b_st2])
                  act(st2[:, 15:16], st2[:, 13:14], AF.Ln, R=[b_st2], W=[b_st2], bias=1e-5)
                  act(st2[:, 15:16], st2[:, 15:16], AF.Exp, R=[b_st2], W=[b_st2], scale=-0.5)
                  ts("dve", xh, gv, st2[:, 12:13], st2[:, 15:16], ALU.subtract, ALU.mult, R=[b_gv, b_st2], W=[b_xh])
                  tt("pool", xh, xh, small[:, LNW:LNW + 512], ALU.mult, R=[b_xh, b_small], W=[b_xh])
                  tt("pool", vn, xh, small[:, LNB:LNB + 512], ALU.add, R=[b_xh, b_small], W=[b_vn])
                  tt("pool", gu, gu, sz, ALU.mult, R=[b_gu, b_sz], W=[b_gu])
                  psg, b_psg = PS[3]
                  for g in range(8):
                      mm(psg[:, g * 64:(g + 1) * 64], wsT[:, g, :], vn[:, g * 64:(g + 1) * 64], True, True,
                         R=[b_wsT, b_vn], W=[b_psg])
                  tt("dve", t1, psg, biasT, ALU.add, R=[b_psg, b_biasT], W=[b_t1])
                  tt("dve", mixa, t1, gu, ALU.mult, R=[b_t1, b_gu], W=[b_mixa])
                  for q in range(4):
                      tr(PT[:, q * 128:(q + 1) * 128], mixa[:, q * 128:(q + 1) * 128], identb, R=[b_mixa, b_identb], W=[PTb[0]])
                  cp("act", mixT, PT[:, 0:512].rearrange("p (k n) -> p k n", k=4), R=[PTb[0]], W=[b_mixT])
                  for n in range(2):
                      po, b_po = PS[4 + n]
                      for kc in range(4):
                          mm(po, mixT[:, kc, :], woA[:, kc, n * 512:(n + 1) * 512], kc == 0, kc == 3, R=[b_mixT, b_woA], W=[b_po])
                      tt("dve", tmpo, po, G[:, n * 512:(n + 1) * 512], ALU.mult, R=[b_po, b_G], W=[b_tmpo])
                      tt("pool", h[:, t, n * 512:(n + 1) * 512], tmpo, x_t[:, n * 512:(n + 1) * 512], ALU.add,
                         R=[b_tmpo, b_x], W=[b_h[t]])
                  ckpt(0.7, bi)
              S.barrier()
              ckpt(1, bi)

              KV = L0.sub()
              KT_f, b_KT = KV.bf(4 * 2304, "KT")
              KT = KT_f.rearrange("p (h n) -> p h n", h=4)
              V_f, b_V = KV.bf(NT * 4 * 132, "V")
              V = V_f.rearrange("p (t h d) -> p t h d", t=NT, h=4)
              K2 = Region(big, KV.cur, BIGW - 4096)
              xs = [K2.f32(1024, f"xs{i}") for i in range(2)]
              xn, b_xn = K2.bf(1024, "xn")
              xlTs = []
              for i in range(2):
                  ap, b = K2.bf(8 * 128, f"xlT{i}")
                  xlTs.append((ap.rearrange("p (k n) -> p k n", k=8), b))
              stp = Ring([K2.f32(16, f"st{i}") for i in range(4)])
              rt = (K2.f32(512, "tc"), K2.f32(512, "ts"))
              kr, b_kr = K2.bf(512, "kr")
              load_w("pool", wQZ[:, :, 0:512], b_wQZ, w_in0, 1536, 512)
              load_w("pool", wQZ[:, :, 512:1024], b_wQZ, w_in0, 3072, 512)
              dma("pool", woB, w_out0[512:1024, :].rearrange("(k p) n -> p k n", p=128), R=[], W=[b_woB])
              S.op("dve", (lambda V: lambda e: e.memset(V[:, :, :, 128:129], 1.0))(V), R=[], W=[b_V])
              for t in range(NT):
                  r = 2 if t < 2 else bi
                  x_t, b_x = xs[t % 2]
                  xlT, b_xlT = xlTs[t % 2]
                  if t == 0:
                      dma("sp", x_t, xin[bi, t * 128:(t + 1) * 128, :], R=[], W=[b_x])
                      prep_xn(x_t, b_x, xn, b_xn, stp)
                  make_xlT(x_t, b_x, 0, r, xn, b_xn, stp, xlT, b_xlT, t, two_banks=True, pre=True)
                  proj(PS[0][0], PS[0][1], xlT, b_xlT, wKV, b_wKV, 0, 512)
                  proj(PS[1][0], PS[1][1], xlT, b_xlT, wKV, b_wKV, 512, 512)
                  if t + 1 < NT:
                      x_nx, b_xnx = xs[(t + 1) % 2]
                      dma("sp", x_nx, xin[bi, (t + 1) * 128:(t + 2) * 128, :], R=[], W=[b_xnx])
                      prep_xn(x_nx, b_xnx, xn, b_xn, stp)
                  rope(rt, PS[0][0], PS[0][1], 8, t, kr, b_kr)
                  for hh in range(4):
                      tr(PT[:, hh * 128:(hh + 1) * 128], kr[:, hh * 128:(hh + 1) * 128], identb, R=[b_kr, b_identb], W=[PTb[0]])
                  cp("act", KT[:, :, t * 128:(t + 1) * 128], PT[:, 0:512].rearrange("p (h n) -> p h n", h=4), R=[PTb[0]], W=[b_KT])
                  cp("act", V[:, t, :, 0:128], PS[1][0].rearrange("p (h d) -> p h d", h=4), R=[PS[1][1]], W=[b_V])
              S.barrier()
              ckpt(2, bi)

              Bp = KV.sub()
              xs = [Bp.f32(1024, f"xs{i}") for i in range(2)]
              xn, b_xn = Bp.bf(1024, "xn")
              xlT_f, b_xlT = Bp.bf(8 * 256, "xlT")
              xlT = xlT_f.rearrange("p (k n) -> p k n", k=8)
              stp = Ring([Bp.f32(16, f"st{i}") for i in range(4)])
              stp3 = Ring([Bp.f32(16, f"st3{i}") for i in range(4)])
              rt = (Bp.f32(512, "tc"), Bp.f32(512, "ts"))
              qr, b_qr = Bp.bf(512, "qr")
              QT_f, b_QT = Bp.bf(4 * 256, "QT")
              QT = QT_f.rearrange("p (h n) -> p h n", h=4)
              NPT = 3
              pts = [Bp.bf(512, f"pT{i}") for i in range(NPT)]
              oh, b_oh = Bp.f32(128, "oh")
              rr, b_rr = Bp.f32(8, "rr")
              omix_f, b_omix = Bp.f32(2 * 512, "omix")
              omix = omix_f.rearrange("p (j f) -> p j f", j=2)
              sbz, b_sbz = Bp.f32(512, "sbz")
              mixb_f, b_mixb = Bp.bf(2 * 512, "mixb")
              mixb = mixb_f.rearrange("p (j f) -> p j f", j=2)
              mixT_f, b_mixT = Bp.bf(4 * 256, "mixT")
              mixT = mixT_f.rearrange("p (k n) -> p k n", k=4)
              tmpo, b_tmpo = sbz, b_sbz
              ptc = 0
              for s in range(9):
                  tiles = [2 * s, 2 * s + 1]
                  r = 2 if s == 0 else bi
                  G, b_G = (G_c, b_Gc) if s == 0 else (G_l, b_Gl)
                  nch = 2 if s == 0 else NT
                  for j, t in enumerate(tiles):
                      x_t, b_x = xs[j]
                      dma("sp", x_t, xin[bi, t * 128:(t + 1) * 128, :], R=[], W=[b_x])
                      make_xlT(x_t, b_x, 0, r, xn, b_xn, stp, xlT[:, :, j * 128:(j + 1) * 128], b_xlT, j)
                  for j, t in enumerate(tiles):
                      proj(PS[5][0], PS[5][1], xlT[:, :, j * 128:(j + 1) * 128], b_xlT, wQZ, b_wQZ, 0, 512)
                      rope(rt, PS[5][0], PS[5][1], 8, t, qr, b_qr)
                      for hh in range(4):
                          tr(PT[:, hh * 128:(hh + 1) * 128], qr[:, hh * 128:(hh + 1) * 128], identb, R=[b_qr, b_identb], W=[PTb[0]])
                      cp("act", QT[:, :, j * 128:(j + 1) * 128], PT[:, 0:512].rearrange("p (h n) -> p h n", h=4), R=[PTb[0]], W=[b_QT])
                  items = [(hh, m, cpi) for hh in range(4) for m in range(2) for cpi in range(nch // 2)]

                  def qk0(it, idx):
                      hh, m, cpi = it
                      psc, b_psc = PS[idx % 3]
                      for cc in range(2):
                          c = 2 * cpi + cc
                          mm(psc[:, cc * 256:(cc + 1) * 256], KT[m * 64:(m + 1) * 64, hh, c * 128:(c + 1) * 128],
                             QT[m * 64:(m + 1) * 64, hh, :], True, True, R=[b_KT, b_QT], W=[b_psc])

                  def pv0(it, idx, pT, b_pT):
                      hh, m, cpi = it
                      po, b_po = PS[3 + m]
                      for cc in range(2):
                          c = 2 * cpi + cc
                          for j in range(2):
                              mm(po[:, j * 132:j * 132 + 129], pT[:, cc * 256 + j * 128:cc * 256 + (j + 1) * 128],
                                 V[:, c, hh, 0:129], c == 0 and j == 0, c == nch - 1, R=[b_pT, b_V], W=[b_po], skip=True)

                  for k0 in range(min(2, len(items))):
                      qk0(items[k0], k0)
                  for idx, it in enumerate(items):
                      hh, m, cpi = it
                      psc, b_psc = PS[idx % 3]
                      pT, b_pT = pts[ptc % NPT]
                      ptc += 1
                      act(pT, psc, AF.Exp, R=[b_psc], W=[b_pT], scale=0.125)
                      if idx + 2 < len(items):
                          qk0(items[idx + 2], idx + 2)
                      pv0(it, idx, pT, b_pT)
                      if not (m == 1 and cpi == nch // 2 - 1):
                          continue
                      p0, b_p0 = PS[3]
                      p1, b_p1 = PS[4]
                      for j in range(2):
                          recip(rr[:, 0:1], p0[:, j * 132 + 128:j * 132 + 129], R=[b_p0], W=[b_rr])
                          recip(rr[:, 1:2], p1[:, j * 132 + 128:j * 132 + 129], R=[b_p1], W=[b_rr])
                          tt("dve", rr[:, 2:3], rr[:, 1:2], neglam, ALU.mult, R=[b_rr, b_lam], W=[b_rr])
                          ts("dve", oh, p0[:, j * 132:j * 132 + 128], rr[:, 0:1], None, ALU.mult, None, R=[b_p0, b_rr], W=[b_oh])
                          stt("dve", oh, p1[:, j * 132:j * 132 + 128], rr[:, 2:3], oh, ALU.mult, ALU.add, R=[b_p1, b_rr, b_oh], W=[b_oh])
                          rstd, _, b_st3 = rstd_of(stp3, oh, b_oh, 128, 1e-5)
                          stt("dve", omix[:, j, hh * 128:(hh + 1) * 128], oh, rstd, subln8, ALU.mult, ALU.mult,
                              R=[b_oh, b_st3, b_subln8], W=[b_omix])
                  for j, t in enumerate(tiles):
                      proj(PS[5][0], PS[5][1], xlT[:, :, j * 128:(j + 1) * 128], b_xlT, wQZ, b_wQZ, 512, 512)
                      act(sbz, PS[5][0], AF.Silu, R=[PS[5][1]], W=[b_sbz])
                      tt("dve", mixb[:, j, :], omix[:, j, :], sbz, ALU.mult, R=[b_omix, b_sbz], W=[b_mixb])
                      for q in range(4):
                          tr(PT[:, 512 + q * 128:512 + (q + 1) * 128], mixb[:, j, q * 128:(q + 1) * 128], identb,
                             R=[b_mixb, b_identb], W=[PTb[1]])
                      cp("act", mixT[:, :, j * 128:(j + 1) * 128], PT[:, 512:1024].rearrange("p (k n) -> p k n", k=4),
                         R=[PTb[1]], W=[b_mixT])
                  for j, t in enumerate(tiles):
                      for n in range(2):
                          po, b_po = PS[5 + n]
                          for kc in range(4):
                              mm(po, mixT[:, kc, j * 128:(j + 1) * 128], woB[:, kc, n * 512:(n + 1) * 512], kc == 0, kc == 3,
                                 R=[b_mixT, b_woB], W=[b_po])
                          tt("dve", tmpo, po, G[:, n * 512:(n + 1) * 512], ALU.mult, R=[b_po, b_G], W=[b_tmpo])
                          tt("pool", h[:, t, n * 512:(n + 1) * 512], tmpo, h[:, t, n * 512:(n + 1) * 512], ALU.add,
                             R=[b_tmpo, b_h[t]], W=[b_h[t]])
              S.barrier()
              ckpt(3, bi)
              if dbg:
                  b_dbg = Buf(f"dbg{bi}")
                  for t in range(NT):
                      dma("sp", dbg_h[bi, t * 128:(t + 1) * 128, :], h[:, t, :], R=[b_h[t]], W=[b_dbg])
                  S.barrier()

              L1 = Region(big, phase0, BIGW)
              build_G(1, [(bi, G_l, b_Gl)])
              w1_f, b_w1 = L1.bf(8 * 1472, "w_in1")
              w1 = w1_f.rearrange("p (k n) -> p k n", k=8)
              wqn_f, b_wqn = L1.bf(2 * 8 * 128, "wqn")
              wqn = wqn_f.rearrange("p (k h d) -> p k h d", k=2, h=8)
              wqr_f, b_wqr = L1.bf(2 * 8 * 64, "wqr")
              wqr = wqr_f.rearrange("p (k h d) -> p k h d", k=2, h=8)
              wkv_f, b_wkv = L1.bf(2048, "wkv")
              wkv = wkv_f.rearrange("p (h d) -> p h d", h=8)
              WkT_f, b_WkT = L1.bf(8 * 128, "WkT")
              WkT = WkT_f.rearrange("p (h c) -> p h c", h=8)
              wo1_f, b_wo1 = L1.bf(8 * 1024, "wo1")
              wo1 = wo1_f.rearrange("p (k n) -> p k n", k=8)
              ckvT, b_ckvT = L1.bf(2304, "ckvT")
              krT, b_krT = L1.bf(2304, "krT")
              VS_f, b_VS = L1.bf(NT * 132, "VS")
              VS = VS_f.rearrange("p (t d) -> p t d", t=NT)
              load_w("pool", w1, b_w1, w_in1, 0, 1472)
              wq3 = wq_b.rearrange("(k p) (h d) -> p k h d", p=128, h=8)
              for kc in range(2):
                  dma("pool", wqn[:, kc, :, :], wq3[:, kc, :, 0:128], R=[], W=[b_wqn])
                  dma("pool", wqr[:, kc, :, :], wq3[:, kc, :, 128:192], R=[], W=[b_wqr])
              dma("pool", wkv, wkv_b.rearrange("p (h d) -> p h d", h=8), R=[], W=[b_wkv])
              dma("pool", wo1, w_out1[:, :].rearrange("(k p) n -> p k n", p=128), R=[], W=[b_wo1])
              for hh in range(8):
                  tr(PT[:, hh * 128:(hh + 1) * 128], wkv[:, hh, 0:128], identb, R=[b_wkv, b_identb], W=[PTb[hh // 4]])
              cp("dve", WkT[:, 0:4, :], PT[:, 0:512].rearrange("p (h c) -> p h c", h=4), R=[PTb[0]], W=[b_WkT])
              cp("dve", WkT[:, 4:8, :], PT[:, 512:1024].rearrange("p (h c) -> p h c", h=4), R=[PTb[1]], W=[b_WkT])

              ckpt(3.5, bi)
              S.barrier()
              K1 = L1.sub()
              xn, b_xn = K1.bf(1024, "xn")
              xlTs = []
              for i in range(2):
                  ap, b = K1.bf(8 * 128, f"xlT{i}")
                  xlTs.append((ap.rearrange("p (k n) -> p k n", k=8), b))
              stp = Ring([K1.f32(16, f"st{i}") for i in range(4)])
              stp3 = Ring([K1.f32(16, f"st3{i}") for i in range(4)])
              rt = (K1.f32(64, "tc"), K1.f32(64, "ts"))
              krd, b_krd = K1.bf(128, "krd")
              S.op("dve", (lambda VS: lambda e: e.memset(VS[:, :, 128:129], 1.0))(VS), R=[], W=[b_VS])
              for t in range(NT):
                  r = 2 if t < 2 else bi
                  xlT, b_xlT = xlTs[t % 2]
                  make_xlT(h[:, t, :], b_h[t], 1, r, xn, b_xn, stp, xlT, b_xlT, t, two_banks=True)
                  pk, b_pk = PS[t % 2]
                  if os.environ.get("NOPROJ", "0") == "1":
                      continue
                  if os.environ.get("USE_WO1", "0") == "1":
                      proj(pk, b_pk, xlT, b_xlT, wo1, b_wo1, 256, 192)
                  else:
                      proj(pk, b_pk, xlT, b_xlT, w1, b_w1, 256, int(os.environ.get("KVN_N", "192")))
                  import os as _os
                  _v = int(_os.environ.get("KV1V", "9"))
                  if _v < 1:
                      continue
                  rstd, _, b_st3 = rstd_of(stp3, pk[:, 0:128], b_pk, 128, 1e-6)
                  stt("dve", VS[:, t, 0:128], pk[:, 0:128], rstd, small[:, KVN:KVN + 128], ALU.mult, ALU.mult,
                      R=[b_pk, b_st3, b_small], W=[b_VS])
                  if _v < 2:
                      continue
                  rope(rt, pk[:, 128:192], b_pk, 1, t, krd[:, 0:64], b_krd, eng2="dve")
                  cp("dve", krd[:, 64:128], krd[:, 0:64], R=[b_krd], W=[b_krd])
                  if _v < 3:
                      continue
                  tr(PT[:, 0:128], VS[:, t, 0:128], identb, R=[b_VS, b_identb], W=[PTb[0]])
                  tr(PT[:, 128:256], krd, identb, R=[b_krd, b_identb], W=[PTb[0]])
                  cp("act", ckvT[:, t * 128:(t + 1) * 128], PT[:, 0:128], R=[PTb[0]], W=[b_ckvT])
                  cp("act", krT[:, t * 128:(t + 1) * 128], PT[:, 128:256], R=[PTb[0]], W=[b_krT])
              S.barrier()
              ckpt(4, bi)

              Q1 = L1.sub()
              xn, b_xn = Q1.bf(1024, "xn")
              xlT_f, b_xlT = Q1.bf(8 * 256, "xlT/mixT")
              xlT = xlT_f.rearrange("p (k n) -> p k n", k=8)
              mixT, b_mixT = xlT, b_xlT
              stp = Ring([Q1.f32(16, f"st{i}") for i in range(4)])
              stp3 = Ring([Q1.f32(16, f"st3{i}") for i in range(4)])
              cqn, b_cqn = Q1.bf(256, "cqn")
              cqnT_f, b_cqnT = Q1.bf(2 * 256, "cqnT")
              cqnT = cqnT_f.rearrange("p (k n) -> p k n", k=2)
              qnTs = [Q1.bf(256, f"qnT{i}") for i in range(2)]
              qpT_f, b_qpT = Q1.bf(8 * 256, "qpT/onT")
              qpT = qpT_f.rearrange("p (h n) -> p h n", h=8)
              onT, b_onT = qpT, b_qpT
              rt = (Q1.f32(512, "tc"), Q1.f32(512, "ts"))
              qr, b_qr = Q1.bf(512, "qr")
              qrT_f, b_qrT = Q1.bf(4 * 256, "qrT")
              qrT = qrT_f.rearrange("p (g n) -> p g n", g=4)
              pts = [Q1.bf(512, f"pT{i}") for i in range(NPT)]
              rr, b_rr = Q1.f32(8, "rr")
              on_f, b_on = Q1.bf(2 * 1024, "on")
              on = on_f.rearrange("p (j f) -> p j f", j=2)
              szz, b_szz = Q1.f32(512, "sz")
              mix_f, b_mix = G_c.bitcast(BF16), Buf("mix")
              mix = mix_f.rearrange("p (j f) -> p j f", j=2)
              tmpo, b_tmpo = szz, b_szz
              ybuf = [(h[:, i, :], Buf(f"y{i}")) for i in range(2)]
              b_outd = [Buf(f"outd{bi}{i}") for i in range(2)]
              sc1 = 1.0 / math.sqrt(192.0)
              ptc = 0
              yc = 0
              for s in range(1, 9):
                  tiles = [2 * s, 2 * s + 1]
                  for j, t in enumerate(tiles):
                      make_xlT(h[:, t, :], b_h[t], 1, bi, xn, b_xn, stp, xlT[:, :, j * 128:(j + 1) * 128], b_xlT, j)
                  for j, t in enumerate(tiles):
                      pq, b_pq = PS[5]
                      proj(pq, b_pq, xlT[:, :, j * 128:(j + 1) * 128], b_xlT, w1, b_w1, 0, 256)
                      rstd, _, b_st3 = rstd_of(stp3, pq[:, 0:256], b_pq, 256, 1e-6)
                      stt("dve", cqn, pq[:, 0:256], rstd, small[:, QN:QN + 256], ALU.mult, ALU.mult,
                          R=[b_pq, b_st3, b_small], W=[b_cqn])
                      for kc in range(2):
                          tr(PT[:, kc * 128:(kc + 1) * 128], cqn[:, kc * 128:(kc + 1) * 128], identb, R=[b_cqn, b_identb], W=[PTb[0]])
                      cp("act", cqnT[:, :, j * 128:(j + 1) * 128], PT[:, 0:256].rearrange("p (k n) -> p k n", k=2),
                         R=[PTb[0]], W=[b_cqnT])
                  for j, t in enumerate(tiles):
                      pq, b_pq = PS[5]
                      for kc in range(2):
                          mm(pq, cqnT[:, kc, j * 128:(j + 1) * 128], wqr[:, kc, :, :], kc == 0, kc == 1, R=[b_cqnT, b_wqr], W=[b_pq])
                      rope(rt, pq, b_pq, 8, t, qr, b_qr)
                      for g in range(4):
                          tr(PT[:, 512 + g * 128:512 + (g + 1) * 128], qr[:, g * 128:(g + 1) * 128], identb,
                             R=[b_qr, b_identb], W=[PTb[1]])
                      cp("act", qrT[:, :, j * 128:(j + 1) * 128], PT[:, 512:1024].rearrange("p (g n) -> p g n", g=4),
                         R=[PTb[1]], W=[b_qrT])
                  for hh in range(8):
                      pq, b_pq = PS[5 + hh % 2]
                      qnT, b_qnT = qnTs[hh % 2]
                      for kc in range(2):
                          mm(pq[:, 0:256], wqn[:, kc, hh, :], cqnT[:, kc, :], kc == 0, kc == 1, R=[b_wqn, b_cqnT], W=[b_pq])
                      cp("dve", qnT, pq[:, 0:256], R=[b_pq], W=[b_qnT])
                      mm(pq[:, 256:512], WkT[:, hh, :], qnT, True, True, R=[b_WkT, b_qnT], W=[b_pq])
                      cp("dve", qpT[:, hh, :], pq[:, 256:512], R=[b_pq], W=[b_qpT])
                  items = [(hh, cpi) for hh in range(8) for cpi in range(NT // 2)]

                  def qk1(it, idx):
                      hh, cpi = it
                      hp = hh % 2
                      psc, b_psc = PS[idx % 3]
                      for cc in range(2):
                          c = 2 * cpi + cc
                          mm(psc[:, cc * 256:(cc + 1) * 256], ckvT[:, c * 128:(c + 1) * 128], qpT[:, hh, :], True, False,
                             R=[b_ckvT, b_qpT], W=[b_psc])
                          mm(psc[:, cc * 256:(cc + 1) * 256], krT[hp * 64:(hp + 1) * 64, c * 128:(c + 1) * 128],
                             qrT[hp * 64:(hp + 1) * 64, hh // 2, :], False, True, R=[b_krT, b_qrT], W=[b_psc])

                  def pv1(it, pT, b_pT):
                      hh, cpi = it
                      po, b_po = PS[3 + hh % 2]
                      for cc in range(2):
                          c = 2 * cpi + cc
                          for j in range(2):
                              mm(po[:, j * 132:j * 132 + 129], pT[:, cc * 256 + j * 128:cc * 256 + (j + 1) * 128],
                                 VS[:, c, 0:129], c == 0 and j == 0, c == NT - 1, R=[b_pT, b_VS], W=[b_po], skip=True)

                  for k0 in range(2):
                      qk1(items[k0], k0)
                  for idx, it in enumerate(items):
                      hh, cpi = it
                      psc, b_psc = PS[idx % 3]
                      pT, b_pT = pts[ptc % NPT]
                      ptc += 1
                      act(pT, psc, AF.Exp, R=[b_psc], W=[b_pT], scale=sc1)
                      if idx + 2 < len(items):
                          qk1(items[idx + 2], idx + 2)
                      pv1(it, pT, b_pT)
                      if cpi != NT // 2 - 1:
                          continue
                      po, b_po = PS[3 + hh % 2]
                      for j in range(2):
                          recip(rr[:, j:j + 1], po[:, j * 132 + 128:j * 132 + 129], R=[b_po], W=[b_rr])
                          ts("dve", on[:, j, hh * 128:(hh + 1) * 128], po[:, j * 132:j * 132 + 128], rr[:, j:j + 1], None,
                             ALU.mult, None, R=[b_po, b_rr], W=[b_on])
                  for j, t in enumerate(tiles):
                      for hh in range(8):
                          tr(PT[:, hh * 128:(hh + 1) * 128], on[:, j, hh * 128:(hh + 1) * 128], identb, R=[b_on, b_identb],
                             W=[PTb[hh // 4]])
                      cp("act", onT[:, 0:4, j * 128:(j + 1) * 128], PT[:, 0:512].rearrange("p (h n) -> p h n", h=4),
                         R=[PTb[0]], W=[b_onT])
                      cp("dve", onT[:, 4:8, j * 128:(j + 1) * 128], PT[:, 512:1024].rearrange("p (h n) -> p h n", h=4),
                         R=[PTb[1]], W=[b_onT])
                  for j, t in enumerate(tiles):
                      for n in range(2):
                          pe_, b_pe = PS[5]
                          pz, b_pz = PS[6]
                          for q in range(4):
                              hh = n * 4 + q
                              mm(pe_[:, q * 128:(q + 1) * 128], onT[:, hh, j * 128:(j + 1) * 128], wkv[:, hh, 128:256], True, True,
                                 R=[b_onT, b_wkv], W=[b_pe])
                          proj(pz, b_pz, xlT[:, :, j * 128:(j + 1) * 128], b_xlT, w1, b_w1, 448 + n * 512, 512)
                          act(szz, pz, AF.Silu, R=[b_pz], W=[b_szz])
                          tt("dve", mix[:, j, n * 512:(n + 1) * 512], pe_, szz, ALU.mult, R=[b_pe, b_szz], W=[b_mix])
                  for j, t in enumerate(tiles):
                      for kc in range(8):
                          tr(PT[:, kc * 128:(kc + 1) * 128], mix[:, j, kc * 128:(kc + 1) * 128], identb, R=[b_mix, b_identb],
                             W=[PTb[kc // 4]])
                      cp("act", mixT[:, 0:4, j * 128:(j + 1) * 128], PT[:, 0:512].rearrange("p (k n) -> p k n", k=4),
                         R=[PTb[0]], W=[b_mixT])
                      cp("dve", mixT[:, 4:8, j * 128:(j + 1) * 128], PT[:, 512:1024].rearrange("p (k n) -> p k n", k=4),
                         R=[PTb[1]], W=[b_mixT])
                  for j, t in enumerate(tiles):
                      for n in range(2):
                          po, b_po = PS[5 + n]
                          for kc in range(8):
                              mm(po, mixT[:, kc, j * 128:(j + 1) * 128], wo1[:, kc, n * 512:(n + 1) * 512], kc == 0, kc == 7,
                                 R=[b_mixT, b_wo1], W=[b_po])
                          tt("dve", tmpo, po, G_l[:, n * 512:(n + 1) * 512], ALU.mult, R=[b_po, b_Gl], W=[b_tmpo])
                          tt("pool", h[:, t, n * 512:(n + 1) * 512], tmpo, h[:, t, n * 512:(n + 1) * 512], ALU.add,
                             R=[b_tmpo, b_h[t]], W=[b_h[t]])
                      y, b_y = ybuf[yc % 2]
                      rstd, _, b_stf = rstd_of(stp, h[:, t, :], b_h[t], 1024, 1e-6)
                      stt("dve", y, h[:, t, :], rstd, small[:, FINAL:FINAL + 1024], ALU.mult, ALU.mult,
                          R=[b_h[t], b_stf, b_small], W=[b_y])
                      dma("sp", out_d[bi, (t - 2) * 128:(t - 1) * 128, :], y, R=[b_y], W=[b_outd[yc % 2]])
                      yc += 1
              S.barrier()
        except _Stop:
            pass
        S.emit()
        build_program.last_S = S
    return nc


_NC_CACHE = {}


def _rope_tables():
    seq, gw, dim = 2048, 64, 64
    rows = seq // gw
    row = np.repeat(np.arange(rows), gw).astype(np.float32)
    col = np.tile(np.arange(gw), rows).astype(np.float32)
    half = dim // 2
    inv = (np.float32(10000.0) ** (-np.arange(0, half, 2, dtype=np.float32) / np.float32(half))).astype(np.float32)
    ang_r = row[:, None] * inv[None, :]
    ang_c = col[:, None] * inv[None, :]
    ang = np.concatenate([ang_r, ang_r, ang_c, ang_c], axis=-1).astype(np.float32)
    cos = np.cos(ang).astype(np.float32)
    sin = np.sin(ang).astype(np.float32)
    sgn = np.tile(np.concatenate([-np.ones(16), np.ones(16)]), 2).astype(np.float32)
    cos_all = np.concatenate([np.ones((256, 64), np.float32), cos], 0)
    sin_all = np.concatenate([np.zeros((256, 64), np.float32), sin * sgn[None, :]], 0)
    cos_t = cos_all.reshape(NT, 128, 64).transpose(1, 0, 2).reshape(128, NT * 64)
    sin_t = sin_all.reshape(NT, 128, 64).transpose(1, 0, 2).reshape(128, NT * 64)
    return cos_t, sin_t


def _pack_small(core, c, c_ctx, norm_w, ada_b, a_bs, a_ln_w, a_ln_b, lq1, lk1, lq2, lk2, subln, qn, kvn, final_w):
    sm = np.zeros((128, NS), np.float32)
    rep = lambda v: np.broadcast_to(np.asarray(v, np.float32)[None, :], (128, v.shape[0]))
    sm[:, LNW:LNW + 512] = rep(a_ln_w[0])
    sm[:, LNB:LNB + 512] = rep(a_ln_b[0])
    sm[:, SUBLN:SUBLN + 128] = rep(subln[0])
    sm[:, QN:QN + 256] = rep(qn[0])
    sm[:, KVN:KVN + 128] = rep(kvn[0])
    sm[:, FINAL:FINAL + 1024] = rep(final_w)
    sm[:, LQ1:LQ1 + 64] = rep(lq1[0])
    sm[:, LK1:LK1 + 64] = rep(lk1[0])
    sm[:, LQ2:LQ2 + 64] = rep(lq2[0])
    sm[:, LK2:LK2 + 64] = rep(lk2[0])
    sm[:, ABS:ABS + 8] = a_bs[0].T
    sm[:, NW:NW + 16] = norm_w.reshape(2, 8, 128).transpose(2, 0, 1).reshape(128, 16)
    sm[:, ADAB:ADAB + 48] = ada_b.reshape(2, 24, 128).transpose(2, 0, 1).reshape(128, 48)
    cond = np.stack([c[2 * core], c[2 * core + 1], c_ctx], 0)
    sm[:, COND:COND + 24] = cond.reshape(3, 8, 128).transpose(2, 1, 0).reshape(128, 24)
    cos_t, sin_t = _rope_tables()
    sm[:, COS:COS + NT * 64] = cos_t
    sm[:, SIN:SIN + NT * 64] = sin_t
    sm[:, IDENT:IDENT + 128] = np.eye(128, dtype=np.float32)
    return sm


def kernel(x, c, ctx, c_ctx, norm_w, ada_w, ada_b, even_w_in, a_ws, a_bs, a_ln_w, a_ln_b,
           b_lq1, b_lk1, b_lq2, b_lk2, b_subln_w, even_w_out, odd_w_in, c_q_norm_w, c_wq_b,
           c_kv_norm_w, c_wkv_b, odd_w_out, final_w, _dbg=False, _stop=99, _ncores=8):
    f = lambda a: np.ascontiguousarray(np.asarray(a, dtype=np.float32))
    x, c, ctx, c_ctx = f(x), f(c), f(ctx), f(c_ctx)
    key = (bool(_dbg), _stop)
    if key not in _NC_CACHE:
        _NC_CACHE[key] = build_program(dbg=bool(_dbg), stop=_stop)
    nc = _NC_CACHE[key]
    shared = {
        "ada_w": f(ada_w), "w_in0": f(even_w_in)[0], "w_out0": f(even_w_out)[0], "w_in1": f(odd_w_in)[0],
        "wq_b": f(c_wq_b)[0], "wkv_b": f(c_wkv_b)[0], "w_out1": f(odd_w_out)[0],
        "a_wsT": np.ascontiguousarray(f(a_ws)[0].transpose(2, 0, 1)),
        "adab_g": np.ascontiguousarray(f(ada_b)[:, 2048:3072]),
    }
    in_maps = []
    for core in range(_ncores):
        xin = np.concatenate([ctx[2 * core:2 * core + 2], x[2 * core:2 * core + 2]], axis=1)
        sm = _pack_small(core, c, c_ctx, f(norm_w), f(ada_b), f(a_bs), f(a_ln_w), f(a_ln_b), f(b_lq1), f(b_lk1),
                         f(b_lq2), f(b_lk2), f(b_subln_w), f(c_q_norm_w), f(c_kv_norm_w), f(final_w))
        m = {"xin": np.ascontiguousarray(xin), "small": sm}
        m.update(shared)
        in_maps.append(m)
    res = run_bass_kernel_spmd(nc, in_maps, core_ids=list(range(_ncores)))
    out = np.concatenate([r["out"] for r in res.results], axis=0).astype(np.float32)
    if _dbg:
        return out, np.concatenate([r["dbg_h"] for r in res.results], axis=0)
    return out
```

```python
import math
from contextlib import ExitStack
import numpy as np
import concourse.bass as bass
import concourse.mybir as mybir
from concourse.bass_utils import run_bass_kernel_spmd

F32 = mybir.dt.float32
BF16 = mybir.dt.bfloat16
AF = mybir.ActivationFunctionType
ALU = mybir.AluOpType

import os
XLT_DVE_ONLY = os.environ.get('XLT_DVE_ONLY', '0') == '1'
EPOCH = 3800
NT = 18
D = 1024

LNW, LNB, SUBLN, QN, KVN, FINAL = 0, 512, 1024, 1152, 1408, 1536
LQ1, LK1, LQ2, LK2 = 2560, 2624, 2688, 2752
ABS, NW, ADAB, COND, COS, SIN, IDENT = 2816, 2824, 2840, 2888, 2912, 4064, 5216
NS = 5344


class Buf:
    __slots__ = ("name", "lw", "rd", "dsem", "dval", "excl", "dq")

    def __init__(self, name, excl=False):
        self.name = name
        self.excl = excl
        self.lw = None
        self.rd = {}
        self.dsem = None
        self.dq = None
        self.dval = 0


class Sched:
    def __init__(self, nc, stack):
        self.nc = nc
        self.stack = stack
        self.names = ["pe", "act", "dve", "pool", "sp"]
        self.streams = {n: [] for n in self.names}
        self.sems = {n: [self._sem(f"s_{n}0")] for n in self.names}
        self.cnt = {n: 0 for n in self.names}
        self.seen = {n: {} for n in self.names}
        self.dbufs = []
        self.free_dsems = {}

    def _sem(self, name):
        return self.stack.enter_context(self.nc.semaphore(name))

    def _cur(self, eng):
        if self.cnt[eng] >= EPOCH:
            self.sems[eng].append(self._sem(f"s_{eng}{len(self.sems[eng])}"))
            self.cnt[eng] = 0
        return self.sems[eng][-1]

    def _need(self, eng, waits, dep):
        if dep is None:
            return
        sem, val = dep
        if self.seen[eng].get(sem, 0) >= val:
            return
        if waits.get(sem, 0) < val:
            waits[sem] = val

    def op(self, eng, fn, R=(), W=()):
        if any(b.excl for b in R):
            W = list(W) + [b for b in R if b.excl and b not in W]
            R = [b for b in R if not b.excl]
        waits = {}
        own = set(id(s) for s in self.sems[eng])
        for b in R:
            self._need(eng, waits, b.lw)
        for b in W:
            if b.lw is not None and id(b.lw[0]) not in own:
                self._need(eng, waits, b.lw)
            for r in b.rd.items():
                if id(r[0]) in own:
                    continue
                self._need(eng, waits, r)
        sem = self._cur(eng)
        self.cnt[eng] += 1
        val = self.cnt[eng]
        for s, v in waits.items():
            self.seen[eng][s] = v
        self.streams[eng].append((list(waits.items()), fn, sem, 1))
        for b in W:
            b.lw = (sem, val)
            b.rd = {}
        for b in R:
            b.rd[sem] = val

    def dma(self, q, fn, R=(), W=()):
        waits = {}
        for b in R:
            self._need(q, waits, b.lw)
        tgt = W[0]
        if tgt.dsem is None:
            tgt.dq = q
            if self.free_dsems.get(q):
                tgt.dsem, tgt.dval = self.free_dsems[q].pop()
            else:
                tgt.dsem = self._sem(f"d{len(self.dbufs)}_{tgt.name}".replace("/", "_"))
            self.dbufs.append(tgt)
        for b in W:
            if b.lw is not None and b.lw[0] is not tgt.dsem:
                self._need(q, waits, b.lw)
            for r in b.rd.items():
                self._need(q, waits, r)
        for s, v in waits.items():
            self.seen[q][s] = v
        tgt.dval += 16
        self.streams[q].append((list(waits.items()), fn, tgt.dsem, 16))
        for b in W:
            b.lw = (tgt.dsem, tgt.dval)
            b.rd = {}
        for b in R:
            b.rd[tgt.dsem] = tgt.dval

    def barrier(self):
        deps = []
        for n in self.names:
            if self.cnt[n] > 0:
                deps.append((self.sems[n][-1], self.cnt[n]))
            for s in self.sems[n][:-1]:
                deps.append((s, EPOCH))
        for b in self.dbufs:
            deps.append((b.dsem, b.dval))
        for n in self.names:
            waits = {}
            for d in deps:
                self._need(n, waits, d)
            for s, v in waits.items():
                self.seen[n][s] = v
            if waits:
                self.streams[n].append((list(waits.items()), None, None, 0))
        for b in self.dbufs:
            if b.dval < 3000:
                self.free_dsems.setdefault(b.dq, []).append((b.dsem, b.dval))
            b.dsem = None
        self.dbufs = []

    def emit(self):
        with self.nc.Block() as block:
            def mk(name):
                def body(e):
                    for waits, fn, sem, inc in self.streams[name]:
                        for s, v in waits:
                            e.wait_ge(s, v)
                        if fn is not None:
                            fn(e).then_inc(sem, inc)
                return body
            block.tensor(mk("pe"))
            block.scalar(mk("act"))
            block.vector(mk("dve"))
            block.gpsimd(mk("pool"))
            block.sync(mk("sp"))


class Ring:
    def __init__(self, slots):
        self.slots, self.i = slots, 0

    def next(self):
        self.i += 1
        return self.slots[self.i % len(self.slots)]


class Region:
    def __init__(self, big, lo, hi):
        self.big, self.lo, self.hi, self.cur = big, lo, hi, lo

    def f32(self, words, name="t"):
        a = self.cur
        self.cur += (words + 7) // 8 * 8
        assert self.cur <= self.hi, f"SBUF region overflow at {name}: {self.cur} > {self.hi}"
        return self.big[:, a:a + words], Buf(name)

    def bf(self, elems, name="t"):
        ap, b = self.f32((elems + 1) // 2, name)
        return ap.bitcast(BF16), b

    def sub(self):
        return Region(self.big, self.cur, self.hi)


class _Stop(Exception):
    pass


def build_program(dbg=False, stop=99):
    nc = bass.Bass("TRN2", target_bir_lowering=False)

    def din(name, shape):
        return nc.dram_tensor(name, shape, F32, kind="ExternalInput").ap()

    xin = din("xin", [2, NT * 128, D])
    small_d = din("small", [128, NS])
    ada_w = din("ada_w", [2, D, 3 * D])
    w_in0 = din("w_in0", [D, 3584])
    w_out0 = din("w_out0", [D, D])
    w_in1 = din("w_in1", [D, 1472])
    wq_b = din("wq_b", [256, 1536])
    wkv_b = din("wkv_b", [128, 2048])
    w_out1 = din("w_out1", [D, D])
    a_wsT = din("a_wsT", [128, 8, 128])
    out_d = nc.dram_tensor("out", [2, 2048, D], F32, kind="ExternalOutput").ap()
    adab_g = din("adab_g", [2, D])
    if dbg:
        dbg_h = nc.dram_tensor("dbg_h", [2, NT * 128, D], F32, kind="ExternalOutput").ap()

    with ExitStack() as st:
        S = Sched(nc, st)
        BIGW = 53200
        big = st.enter_context(nc.sbuf_tensor("big", [128, BIGW], F32))[:, :]
        PS = []
        for i in range(7):
            PS.append((st.enter_context(nc.psum_tensor(f"ps{i}", [128, 512], F32))[:, :], Buf(f"ps{i}", excl=True)))
        PT = st.enter_context(nc.psum_tensor("pt", [128, 1024], BF16))[:, :]
        b_PT = Buf("pt", excl=True)
        PTb = [b_PT, b_PT]

        def mm(out, lhsT, rhs, start, stop, R, W, skip=False):
            S.op("pe", lambda e: e.matmul(out, lhsT=lhsT, rhs=rhs, start=start, stop=stop, skip_group_check=skip), R=R, W=W)

        def tr(out, in_, ident, R, W):
            S.op("pe", lambda e: e.transpose(out, in_, ident), R=R, W=W)

        def act(out, in_, func, R, W, scale=1.0, bias=0.0):
            S.op("act", lambda e: e.activation(out=out, in_=in_, func=func, bias=bias, scale=scale), R=R, W=W)

        def tt(eng, out, in0, in1, op, R, W):
            S.op(eng, lambda e: e.tensor_tensor(out=out, in0=in0, in1=in1, op=op), R=R, W=W)

        def ts(eng, out, in0, s1, s2, op0, op1, R, W):
            if s2 is None:
                S.op(eng, lambda e: e.tensor_scalar(out=out, in0=in0, scalar1=s1, scalar2=None, op0=op0), R=R, W=W)
            else:
                S.op(eng, lambda e: e.tensor_scalar(out=out, in0=in0, scalar1=s1, scalar2=s2, op0=op0, op1=op1), R=R, W=W)

        def stt(eng, out, in0, scalar, in1, op0, op1, R, W):
            S.op(eng, lambda e: e.scalar_tensor_tensor(out=out, in0=in0, scalar=scalar, in1=in1, op0=op0, op1=op1), R=R, W=W)

        def cp(eng, out, in_, R, W):
            if eng == "act":
                S.op("act", lambda e: e.copy(out=out, in_=in_), R=R, W=W)
            else:
                S.op(eng, lambda e: e.tensor_copy(out=out, in_=in_), R=R, W=W)

        def recip(out, in_, R, W):
            S.op("dve", lambda e: e.reciprocal(out=out, in_=in_), R=R, W=W)

        def dma(q, out, in_, R, W):
            S.dma(q, lambda e: e.dma_start(out=out, in_=in_), R=R, W=W)

        top = Region(big, 0, BIGW)
        small, b_small = top.f32(NS, "small")
        h_all, _ = top.f32(NT * D, "h")
        h = h_all.rearrange("p (t f) -> p t f", t=NT)
        b_h = [Buf(f"h{t}") for t in range(NT)]
        G_l, b_Gl = top.f32(D, "G_l")
        G_c, b_Gc = top.f32(D, "G_c")
        identb, b_identb = top.bf(128, "identb")
        wsT_f, b_wsT = top.bf(8 * 128, "wsT")
        wsT = wsT_f.rearrange("p (g q) -> p g q", g=8)
        biasT, b_biasT = top.f32(512, "biasT")
        mod_f, b_mod = top.f32(2 * 72, "mod")
        mod = mod_f.rearrange("p (l f r) -> p l f r", l=2, r=3)
        acol_f, b_acol = top.f32(2 * 3 * 8, "acol")
        acol = acol_f.rearrange("p (l r f) -> p l r f", l=2, r=3)
        scT_f, b_scT = top.bf(24, "scT")
        scT = scT_f.rearrange("p (k r) -> p k r", r=3)
        lamt, b_lam = top.f32(8, "lam")
        subln8, b_subln8 = top.f32(128, "subln8")
        stats, b_stats = top.f32(32, "stats")
        condrep_f, b_condrep = top.bf(8 * 128, "condrep")
        condrep = condrep_f.rearrange("p (k n) -> p k n", k=8)
        brow, b_brow = top.bf(1024, "brow")
        ones1, b_ones1 = top.bf(128, "ones1")
        wKV_f, b_wKV = Region(big, BIGW - 4096, BIGW).bf(8 * 1024, "wKV")
        wKV = wKV_f.rearrange("p (k n) -> p k n", k=8)
        phase0 = top.cur

        dma("sp", small, small_d[:, :], R=[], W=[b_small])
        dma("pool", identb, small_d[:, IDENT:IDENT + 128], R=[], W=[b_identb])
        dma("pool", wsT, a_wsT[:, :, :], R=[], W=[b_wsT])
        S.op("dve", lambda e: e.memset(ones1, 1.0), R=[], W=[b_ones1])

        act(scT, small[:, COND:COND + 24].rearrange("p (k r) -> p k r", r=3), AF.Silu, R=[b_small], W=[b_scT])
        setup = Region(big, phase0, BIGW)
        wada = []
        for i in range(2):
            ap, b = setup.bf(8 * 512, f"wada{i}")
            wada.append((ap.rearrange("p (k n) -> p k n", k=8), b))
        pcs = 0
        for l in range(2):
            pm, b_pm = PS[l]
            for j in range(6):
                wt, b_wt = wada[pcs % 2]
                pcs += 1
                dma("pool", wt, ada_w[l, :, j * 512:(j + 1) * 512].rearrange("(k p) n -> p k n", p=128), R=[], W=[b_wt])
                for fl in range(4):
                    fc = j * 4 + fl
                    for kc in range(8):
                        mm(pm[:, fc * 3:fc * 3 + 3], wt[:, kc, fl * 128:(fl + 1) * 128], scT[:, kc, :],
                           kc == 0, kc == 7, R=[b_wt, b_scT], W=[b_pm])
            tt("dve", mod[:, l, :, :], pm[:, 0:72].rearrange("p (f r) -> p f r", r=3),
               small[:, ADAB + l * 24:ADAB + (l + 1) * 24].unsqueeze(2).to_broadcast([128, 24, 3]), ALU.add,
               R=[b_pm, b_small], W=[b_mod])
            for r in range(3):
                stt("dve", acol[:, l, r, :], mod[:, l, 8:16, r], 1.0, small[:, NW + l * 8:NW + (l + 1) * 8],
                    ALU.add, ALU.mult, R=[b_mod, b_small], W=[b_acol])
        lt, b_lt = setup.f32(128, "lamtmp")
        tt("dve", lt[:, 0:64], small[:, LQ1:LQ1 + 64], small[:, LK1:LK1 + 64], ALU.mult, R=[b_small], W=[b_lt])
        tt("dve", lt[:, 64:128], small[:, LQ2:LQ2 + 64], small[:, LK2:LK2 + 64], ALU.mult, R=[b_small], W=[b_lt])
        S.op("dve", lambda e: e.reduce_sum(out=lamt[:, 0:2], in_=lt.rearrange("p (a b) -> p a b", a=2),
                                           axis=mybir.AxisListType.X), R=[b_lt], W=[b_lam])
        act(lamt[:, 2:4], lamt[:, 0:2], AF.Exp, R=[b_lam], W=[b_lam])
        tt("dve", lamt[:, 4:5], lamt[:, 3:4], lamt[:, 2:3], ALU.subtract, R=[b_lam], W=[b_lam])
        ts("dve", lamt[:, 5:6], lamt[:, 4:5], -0.2, None, ALU.add, None, R=[b_lam], W=[b_lam])
        neglam = lamt[:, 5:6]
        ts("dve", subln8, small[:, SUBLN:SUBLN + 128], 0.8, None, ALU.mult, None, R=[b_small], W=[b_subln8])
        cp("dve", biasT.rearrange("p (g d) -> p g d", g=8), small[:, ABS:ABS + 8].unsqueeze(2).to_broadcast([128, 8, 64]),
           R=[b_small], W=[b_biasT])

        def build_G(l, conds):
            load_w("pool", wKV, b_wKV, ada_w[l], 2048, 1024)
            dma("pool", brow[0:1, :], adab_g[l:l + 1, :], R=[], W=[b_brow])
            pg, b_pg = PS[6]
            for (r, G, b_G) in conds:
                cp("dve", condrep, scT[:, :, r:r + 1].to_broadcast([128, 8, 128]), R=[b_scT], W=[b_condrep])
                for n in range(2):
                    for kc in range(8):
                        mm(pg, condrep[:, kc, :], wKV[:, kc, n * 512:(n + 1) * 512], kc == 0, False, R=[b_condrep, b_wKV], W=[b_pg])
                    mm(pg, ones1[0:1, :], brow[0:1, n * 512:(n + 1) * 512], False, True, R=[b_ones1, b_brow], W=[b_pg])
                    cp("dve", G[:, n * 512:(n + 1) * 512], pg, R=[b_pg], W=[b_G])

        def rstd_of(reg_stats, src, b_src, n, eps, name="rs"):
            st_ap, b_st = reg_stats.next() if isinstance(reg_stats, Ring) else reg_stats
            nch = (n + 511) // 512
            w = n // nch
            for c in range(nch):
                S.op("dve", (lambda c: lambda e: e.bn_stats(out=st_ap[:, c * 6:(c + 1) * 6], in_=src[:, c * w:(c + 1) * w]))(c),
                     R=[b_src], W=[b_st])
            S.op("dve", lambda e: e.bn_aggr(out=st_ap[:, 12:14], in_=st_ap[:, 0:6 * nch].rearrange("p (c s) -> p c s", s=6)),
                 R=[b_st], W=[b_st])
            stt("dve", st_ap[:, 14:15], st_ap[:, 12:13], st_ap[:, 12:13], st_ap[:, 13:14], ALU.mult, ALU.add, R=[b_st], W=[b_st])
            act(st_ap[:, 15:16], st_ap[:, 14:15], AF.Ln, R=[b_st], W=[b_st], bias=eps)
            act(st_ap[:, 15:16], st_ap[:, 15:16], AF.Exp, R=[b_st], W=[b_st], scale=-0.5)
            return st_ap[:, 15:16], st_ap[:, 12:13], b_st

        def ckpt(k, bi):
            if stop == k:
                S.barrier()
                if dbg:
                    b_dbg = Buf("dbgstop")
                    for t in range(NT):
                        dma("sp", dbg_h[bi, t * 128:(t + 1) * 128, :], h[:, t, :], R=[b_h[t]], W=[b_dbg])
                    S.barrier()
                raise _Stop()

        PT2 = PS[6][0].bitcast(BF16)

        def prep_xn(src, b_src, xn, b_xn, st_pair):
            rstd, _, b_st = rstd_of(st_pair, src, b_src, 1024, 1e-6)
            act(xn, src, AF.Identity, R=[b_src, b_st], W=[b_xn], scale=rstd)

        def make_xlT(src, b_src, l, r, xn, b_xn, st_pair, xlT_dst, b_xlT, alt, two_banks=False, pre=False):
            if not pre:
                prep_xn(src, b_src, xn, b_xn, st_pair)
            ckpt(0.57, 0)
            if two_banks:
                for half in range(2):
                    pt_ap, bp = (PT, b_PT) if half == 0 else (PT2, PS[6][1])
                    for q in range(4):
                        fc = half * 4 + q
                        tr(pt_ap[:, q * 128:(q + 1) * 128], xn[:, fc * 128:(fc + 1) * 128], identb, R=[b_xn, b_identb], W=[bp])
                for q in range(4):
                    for half in range(2):
                        pt_ap, bp = (PT, b_PT) if half == 0 else (PT2, PS[6][1])
                        fc = half * 4 + q
                        sc = acol[:, l, r, fc:fc + 1]
                        bi_ = mod[:, l, fc, r:r + 1]
                        if half == 0:
                            ts("dve", xlT_dst[:, fc, :], pt_ap[:, q * 128:(q + 1) * 128], sc, bi_, ALU.mult, ALU.add,
                               R=[bp, b_acol, b_mod], W=[b_xlT])
                        else:
                            act(xlT_dst[:, fc, :], pt_ap[:, q * 128:(q + 1) * 128], AF.Identity, R=[bp, b_acol, b_mod], W=[b_xlT],
                                scale=sc, bias=bi_)
                return
            for half in range(2):
                bp = PTb[half]
                if half == 1:
                    ckpt(0.596, 0)
                for q in range(4):
                    fc = half * 4 + q
                    tr(PT[:, fc * 128:(fc + 1) * 128], xn[:, fc * 128:(fc + 1) * 128], identb, R=[b_xn, b_identb], W=[bp])
                if half == 1:
                    ckpt(0.597, 0)
                ckpt(0.58, 0)
                for q in range(4):
                    fc = half * 4 + q
                    sc = acol[:, l, r, fc:fc + 1]
                    bi = mod[:, l, fc, r:r + 1]
                    if q == 1:
                        ckpt(0.59, 0)
                    if q == 2:
                        ckpt(0.595, 0)
                    if (fc + alt) % 2 == 0 or XLT_DVE_ONLY:
                        ts("dve", xlT_dst[:, fc, :], PT[:, fc * 128:(fc + 1) * 128], sc, bi, ALU.mult, ALU.add,
                           R=[bp, b_acol, b_mod], W=[b_xlT])
                    else:
                        act(xlT_dst[:, fc, :], PT[:, fc * 128:(fc + 1) * 128], AF.Identity, R=[bp, b_acol, b_mod], W=[b_xlT],
                            scale=sc, bias=bi)

        def proj(ps, b_ps, xlT_tile, b_xlT, W, b_W, c0, n, nk=8):
            for kc in range(nk):
                mm(ps[:, 0:n], xlT_tile[:, kc, :], W[:, kc, c0:c0 + n], kc == 0, kc == nk - 1, R=[b_xlT, b_W], W=[b_ps])

        def rope(reg_t, src, b_src, ng, t, dst, b_dst, eng2="pool"):
            (tc_, b_tc), (ts_, b_ts) = reg_t
            n = ng * 64
            cosb = small[:, COS + t * 64:COS + (t + 1) * 64]
            sinb = small[:, SIN + t * 64:SIN + (t + 1) * 64]
            sv = src.rearrange("p (g s t d) -> p g s t d", g=ng, s=2, t=2)
            tv = ts_[:, 0:n].rearrange("p (g s t d) -> p g s t d", g=ng, s=2, t=2)
            sn = sinb.rearrange("p (s t d) -> p s t d", s=2, t=2)
            tt("dve", tc_[:, 0:n].rearrange("p (g d) -> p g d", g=ng), src.rearrange("p (g d) -> p g d", g=ng),
               cosb.unsqueeze(1).to_broadcast([128, ng, 64]), ALU.mult, R=[b_src, b_small], W=[b_tc])
            for hf in range(2):
                tt("dve", tv[:, :, :, hf, :], sv[:, :, :, 1 - hf, :],
                   sn[:, :, hf, :].unsqueeze(1).to_broadcast([128, ng, 2, 16]), ALU.mult, R=[b_src, b_small], W=[b_ts])
            tt(eng2, dst, tc_[:, 0:n], ts_[:, 0:n], ALU.add, R=[b_tc, b_ts], W=[b_dst])

        def load_w(q, dst, b_dst, src2d, c0, n):
            dma(q, dst, src2d[:, c0:c0 + n].rearrange("(k p) n -> p k n", p=128), R=[], W=[b_dst])

        S.barrier()

        def ckpt(k, bi):
            if stop == k:
                S.barrier()
                if dbg:
                    b_dbg = Buf("dbgstop")
                    for t in range(NT):
                        dma("sp", dbg_h[bi, t * 128:(t + 1) * 128, :], h[:, t, :], R=[b_h[t]], W=[b_dbg])
                    S.barrier()
                raise _Stop()

        try:
          ckpt(0, 0)
          for bi in range(2):
              L0 = Region(big, phase0, BIGW)
              wQZ_f, b_wQZ = L0.bf(8 * 1024, "wQZ")
              wQZ = wQZ_f.rearrange("p (k n) -> p k n", k=8)
              woB_f, b_woB = L0.bf(4 * 1024, "woB")
              woB = woB_f.rearrange("p (k n) -> p k n", k=4)
              build_G(0, [(bi, G_l, b_Gl), (2, G_c, b_Gc)])
              ckpt(0.5, bi)

              A = Region(big, L0.cur, BIGW - 4096)
              wA_f, b_wA = A.bf(8 * 1536, "wA")
              wA = wA_f.rearrange("p (k n) -> p k n", k=8)
              woA_f, b_woA = A.bf(4 * 1024, "woA")
              woA = woA_f.rearrange("p (k n) -> p k n", k=4)
              load_w("pool", wA, b_wA, w_in0, 0, 1536)
              dma("pool", woA, w_out0[0:512, :].rearrange("(k p) n -> p k n", p=128), R=[], W=[b_woA])
              xs = [A.f32(1024, f"xs{i}") for i in range(2)]
              xn, b_xn = A.bf(1024, "xn")
              xlTs = []
              for i in range(2):
                  ap, b = A.bf(8 * 128, f"xlT{i}")
                  xlTs.append((ap.rearrange("p (k n) -> p k n", k=8), b))
              stp = Ring([A.f32(16, f"st{i}") for i in range(4)])
              stp2 = A.f32(16, "st2")
              gu, b_gu = A.f32(512, "gu")
              gv, b_gv = A.f32(512, "gv")
              sz, b_sz = A.f32(512, "sz")
              xh, b_xh = A.f32(512, "xh")
              t1, b_t1 = xh, b_xh
              vn, b_vn = A.bf(512, "vn")
              mixa, b_mixa = A.bf(512, "mixa")
              mixT_f, b_mixT = vn, b_vn
              mixT = mixT_f.rearrange("p (k n) -> p k n", k=4)
              tmpo, b_tmpo = xh, b_xh
              load_w("pool", wKV, b_wKV, w_in0, 2048, 1024)
              for t in range(NT):
                  r = 2 if t < 2 else bi
                  G, b_G = (G_c, b_Gc) if t < 2 else (G_l, b_Gl)
                  x_t, b_x = xs[t % 2]
                  xlT, b_xlT = xlTs[t % 2]
                  if t == 0:
                      dma("sp", x_t, xin[bi, t * 128:(t + 1) * 128, :], R=[], W=[b_x])
                      prep_xn(x_t, b_x, xn, b_xn, stp)
                  make_xlT(x_t, b_x, 0, r, xn, b_xn, stp, xlT, b_xlT, t, two_banks=True, pre=True)
                  for i in range(3):
                      proj(PS[i][0], PS[i][1], xlT, b_xlT, wA, b_wA, i * 512, 512)
                  if t + 1 < NT:
                      x_nx, b_xnx = xs[(t + 1) % 2]
                      dma("sp", x_nx, xin[bi, (t + 1) * 128:(t + 2) * 128, :], R=[], W=[b_xnx])
                      prep_xn(x_nx, b_xnx, xn, b_xn, stp)
                  act(gu, PS[0][0], AF.Gelu, R=[PS[0][1]], W=[b_gu])
                  act(gv, PS[1][0], AF.Gelu, R=[PS[1][1]], W=[b_gv])
                  act(sz, PS[2][0], AF.Silu, R=[PS[2][1]], W=[b_sz])
                  st2, b_st2 = stp2
                  S.op("dve", (lambda st2, gv: lambda e: e.bn_stats(out=st2[:, 0:6], in_=gv))(st2, gv), R=[b_gv], W=[b_st2])
                  S.op("dve", (lambda st2: lambda e: e.bn_aggr(out=st2[:, 12:14], in_=st2[:, 0:6]))(st2), R=[b_st2], W=[b_st2])
                  act(st2[:, 15:16], st2[:, 13:14], AF.Ln, R=[b_st2], W=[b_st2], bias=1e-5)
                  act(st2[:, 15:16], st2[:, 15:16], AF.Exp, R=[b_st2], W=[b_st2], scale=-0.5)
                  ts("dve", xh, gv, st2[:, 12:13], st2[:, 15:16], ALU.subtract, ALU.mult, R=[b_gv, b_st2], W=[b_xh])
                  tt("pool", xh, xh, small[:, LNW:LNW + 512], ALU.mult, R=[b_xh, b_small], W=[b_xh])
                  tt("pool", vn, xh, small[:, LNB:LNB + 512], ALU.add, R=[b_xh, b_small], W=[b_vn])
                  tt("pool", gu, gu, sz, ALU.mult, R=[b_gu, b_sz], W=[b_gu])
                  psg, b_psg = PS[3]
                  for g in range(8):
                      mm(psg[:, g * 64:(g + 1) * 64], wsT[:, g, :], vn[:, g * 64:(g + 1) * 64], True, True,
                         R=[b_wsT, b_vn], W=[b_psg])
                  tt("dve", t1, psg, biasT, ALU.add, R=[b_psg, b_biasT], W=[b_t1])
                  tt("dve", mixa, t1, gu, ALU.mult, R=[b_t1, b_gu], W=[b_mixa])
                  for q in range(4):
                      tr(PT[:, q * 128:(q + 1) * 128], mixa[:, q * 128:(q + 1) * 128], identb, R=[b_mixa, b_identb], W=[PTb[0]])
                  cp("act", mixT, PT[:, 0:512].rearrange("p (k n) -> p k n", k=4), R=[PTb[0]], W=[b_mixT])
                  for n in range(2):
                      po, b_po = PS[4 + n]
                      for kc in range(4):
                          mm(po, mixT[:, kc, :], woA[:, kc, n * 512:(n + 1) * 512], kc == 0, kc == 3, R=[b_mixT, b_woA], W=[b_po])
                      tt("dve", tmpo, po, G[:, n * 512:(n + 1) * 512], ALU.mult, R=[b_po, b_G], W=[b_tmpo])
                      tt("pool", h[:, t, n * 512:(n + 1) * 512], tmpo, x_t[:, n * 512:(n + 1) * 512], ALU.add,
                         R=[b_tmpo, b_x], W=[b_h[t]])
                  ckpt(0.7, bi)
              S.barrier()
              ckpt(1, bi)

              KV = L0.sub()
              KT_f, b_KT = KV.bf(4 * 2304, "KT")
              KT = KT_f.rearrange("p (h n) -> p h n", h=4)
              V_f, b_V = KV.bf(NT * 4 * 132, "V")
              V = V_f.rearrange("p (t h d) -> p t h d", t=NT, h=4)
              K2 = Region(big, KV.cur, BIGW - 4096)
              xs = [K2.f32(1024, f"xs{i}") for i in range(2)]
              xn, b_xn = K2.bf(1024, "xn")
              xlTs = []
              for i in range(2):
                  ap, b = K2.bf(8 * 128, f"xlT{i}")
                  xlTs.append((ap.rearrange("p (k n) -> p k n", k=8), b))
              stp = Ring([K2.f32(16, f"st{i}") for i in range(4)])
              rt = (K2.f32(512, "tc"), K2.f32(512, "ts"))
              kr, b_kr = K2.bf(512, "kr")
              load_w("pool", wQZ[:, :, 0:512], b_wQZ, w_in0, 1536, 512)
              load_w("pool", wQZ[:, :, 512:1024], b_wQZ, w_in0, 3072, 512)
              dma("pool", woB, w_out0[512:1024, :].rearrange("(k p) n -> p k n", p=128), R=[], W=[b_woB])
              S.op("dve", (lambda V: lambda e: e.memset(V[:, :, :, 128:129], 1.0))(V), R=[], W=[b_V])
              for t in range(NT):
                  r = 2 if t < 2 else bi
                  x_t, b_x = xs[t % 2]
                  xlT, b_xlT = xlTs[t % 2]
                  if t == 0:
                      dma("sp", x_t, xin[bi, t * 128:(t + 1) * 128, :], R=[], W=[b_x])
                      prep_xn(x_t, b_x, xn, b_xn, stp)
                  make_xlT(x_t, b_x, 0, r, xn, b_xn, stp, xlT, b_xlT, t, two_banks=True, pre=True)
                  proj(PS[0][0], PS[0][1], xlT, b_xlT, wKV, b_wKV, 0, 512)
                  proj(PS[1][0], PS[1][1], xlT, b_xlT, wKV, b_wKV, 512, 512)
                  if t + 1 < NT:
                      x_nx, b_xnx = xs[(t + 1) % 2]
                      dma("sp", x_nx, xin[bi, (t + 1) * 128:(t + 2) * 128, :], R=[], W=[b_xnx])
                      prep_xn(x_nx, b_xnx, xn, b_xn, stp)
                  rope(rt, PS[0][0], PS[0][1], 8, t, kr, b_kr)
                  for hh in range(4):
                      tr(PT[:, hh * 128:(hh + 1) * 128], kr[:, hh * 128:(hh + 1) * 128], identb, R=[b_kr, b_identb], W=[PTb[0]])
                  cp("act", KT[:, :, t * 128:(t + 1) * 128], PT[:, 0:512].rearrange("p (h n) -> p h n", h=4), R=[PTb[0]], W=[b_KT])
                  cp("act", V[:, t, :, 0:128], PS[1][0].rearrange("p (h d) -> p h d", h=4), R=[PS[1][1]], W=[b_V])
              S.barrier()
              ckpt(2, bi)

              Bp = KV.sub()
              xs = [Bp.f32(1024, f"xs{i}") for i in range(2)]
              xn, b_xn = Bp.bf(1024, "xn")
              xlT_f, b_xlT = Bp.bf(8 * 256, "xlT")
              xlT = xlT_f.rearrange("p (k n) -> p k n", k=8)
              stp = Ring([Bp.f32(16, f"st{i}") for i in range(4)])
              stp3 = Ring([Bp.f32(16, f"st3{i}") for i in range(4)])
              rt = (Bp.f32(512, "tc"), Bp.f32(512, "ts"))
              qr, b_qr = Bp.bf(512, "qr")
              QT_f, b_QT = Bp.bf(4 * 256, "QT")
              QT = QT_f.rearrange("p (h n) -> p h n", h=4)
              NPT = 3
              pts = [Bp.bf(512, f"pT{i}") for i in range(NPT)]
              oh, b_oh = Bp.f32(128, "oh")
              rr, b_rr = Bp.f32(8, "rr")
              omix_f, b_omix = Bp.f32(2 * 512, "omix")
              omix = omix_f.rearrange("p (j f) -> p j f", j=2)
              sbz, b_sbz = Bp.f32(512, "sbz")
              mixb_f, b_mixb = Bp.bf(2 * 512, "mixb")
              mixb = mixb_f.rearrange("p (j f) -> p j f", j=2)
              mixT_f, b_mixT = Bp.bf(4 * 256, "mixT")
              mixT = mixT_f.rearrange("p (k n) -> p k n", k=4)
              tmpo, b_tmpo = sbz, b_sbz
              ptc = 0
              for s in range(9):
                  tiles = [2 * s, 2 * s + 1]
                  r = 2 if s == 0 else bi
                  G, b_G = (G_c, b_Gc) if s == 0 else (G_l, b_Gl)
                  nch = 2 if s == 0 else NT
                  for j, t in enumerate(tiles):
                      x_t, b_x = xs[j]
                      dma("sp", x_t, xin[bi, t * 128:(t + 1) * 128, :], R=[], W=[b_x])
                      make_xlT(x_t, b_x, 0, r, xn, b_xn, stp, xlT[:, :, j * 128:(j + 1) * 128], b_xlT, j)
                  for j, t in enumerate(tiles):
                      proj(PS[5][0], PS[5][1], xlT[:, :, j * 128:(j + 1) * 128], b_xlT, wQZ, b_wQZ, 0, 512)
                      rope(rt, PS[5][0], PS[5][1], 8, t, qr, b_qr)
                      for hh in range(4):
                          tr(PT[:, hh * 128:(hh + 1) * 128], qr[:, hh * 128:(hh + 1) * 128], identb, R=[b_qr, b_identb], W=[PTb[0]])
                      cp("act", QT[:, :, j * 128:(j + 1) * 128], PT[:, 0:512].rearrange("p (h n) -> p h n", h=4), R=[PTb[0]], W=[b_QT])
                  items = [(hh, m, cpi) for hh in range(4) for m in range(2) for cpi in range(nch // 2)]

                  def qk0(it, idx):
                      hh, m, cpi = it
                      psc, b_psc = PS[idx % 3]
                      for cc in range(2):
                          c = 2 * cpi + cc
                          mm(psc[:, cc * 256:(cc + 1) * 256], KT[m * 64:(m + 1) * 64, hh, c * 128:(c + 1) * 128],
                             QT[m * 64:(m + 1) * 64, hh, :], True, True, R=[b_KT, b_QT], W=[b_psc])

                  def pv0(it, idx, pT, b_pT):
                      hh, m, cpi = it
                      po, b_po = PS[3 + m]
                      for cc in range(2):
                          c = 2 * cpi + cc
                          for j in range(2):
                              mm(po[:, j * 132:j * 132 + 129], pT[:, cc * 256 + j * 128:cc * 256 + (j + 1) * 128],
                                 V[:, c, hh, 0:129], c == 0 and j == 0, c == nch - 1, R=[b_pT, b_V], W=[b_po], skip=True)

                  for k0 in range(min(2, len(items))):
                      qk0(items[k0], k0)
                  for idx, it in enumerate(items):
                      hh, m, cpi = it
                      psc, b_psc = PS[idx % 3]
                      pT, b_pT = pts[ptc % NPT]
                      ptc += 1
                      act(pT, psc, AF.Exp, R=[b_psc], W=[b_pT], scale=0.125)
                      if idx + 2 < len(items):
                          qk0(items[idx + 2], idx + 2)
                      pv0(it, idx, pT, b_pT)
                      if not (m == 1 and cpi == nch // 2 - 1):
                          continue
                      p0, b_p0 = PS[3]
                      p1, b_p1 = PS[4]
                      for j in range(2):
                          recip(rr[:, 0:1], p0[:, j * 132 + 128:j * 132 + 129], R=[b_p0], W=[b_rr])
                          recip(rr[:, 1:2], p1[:, j * 132 + 128:j * 132 + 129], R=[b_p1], W=[b_rr])
                          tt("dve", rr[:, 2:3], rr[:, 1:2], neglam, ALU.mult, R=[b_rr, b_lam], W=[b_rr])
                          ts("dve", oh, p0[:, j * 132:j * 132 + 128], rr[:, 0:1], None, ALU.mult, None, R=[b_p0, b_rr], W=[b_oh])
                          stt("dve", oh, p1[:, j * 132:j * 132 + 128], rr[:, 2:3], oh, ALU.mult, ALU.add, R=[b_p1, b_rr, b_oh], W=[b_oh])
                          rstd, _, b_st3 = rstd_of(stp3, oh, b_oh, 128, 1e-5)
                          stt("dve", omix[:, j, hh * 128:(hh + 1) * 128], oh, rstd, subln8, ALU.mult, ALU.mult,
                              R=[b_oh, b_st3, b_subln8], W=[b_omix])
                  for j, t in enumerate(tiles):
                      proj(PS[5][0], PS[5][1], xlT[:, :, j * 128:(j + 1) * 128], b_xlT, wQZ, b_wQZ, 512, 512)
                      act(sbz, PS[5][0], AF.Silu, R=[PS[5][1]], W=[b_sbz])
                      tt("dve", mixb[:, j, :], omix[:, j, :], sbz, ALU.mult, R=[b_omix, b_sbz], W=[b_mixb])
                      for q in range(4):
                          tr(PT[:, 512 + q * 128:512 + (q + 1) * 128], mixb[:, j, q * 128:(q + 1) * 128], identb,
                             R=[b_mixb, b_identb], W=[PTb[1]])
                      cp("act", mixT[:, :, j * 128:(j + 1) * 128], PT[:, 512:1024].rearrange("p (k n) -> p k n", k=4),
                         R=[PTb[1]], W=[b_mixT])
                  for j, t in enumerate(tiles):
                      for n in range(2):
                          po, b_po = PS[5 + n]
                          for kc in range(4):
                              mm(po, mixT[:, kc, j * 128:(j + 1) * 128], woB[:, kc, n * 512:(n + 1) * 512], kc == 0, kc == 3,
                                 R=[b_mixT, b_woB], W=[b_po])
                          tt("dve", tmpo, po, G[:, n * 512:(n + 1) * 512], ALU.mult, R=[b_po, b_G], W=[b_tmpo])
                          tt("pool", h[:, t, n * 512:(n + 1) * 512], tmpo, h[:, t, n * 512:(n + 1) * 512], ALU.add,
                             R=[b_tmpo, b_h[t]], W=[b_h[t]])
              S.barrier()
              ckpt(3, bi)
              if dbg:
                  b_dbg = Buf(f"dbg{bi}")
                  for t in range(NT):
                      dma("sp", dbg_h[bi, t * 128:(t + 1) * 128, :], h[:, t, :], R=[b_h[t]], W=[b_dbg])
                  S.barrier()

              L1 = Region(big, phase0, BIGW)
              build_G(1, [(bi, G_l, b_Gl)])
              w1_f, b_w1 = L1.bf(8 * 1472, "w_in1")
              w1 = w1_f.rearrange("p (k n) -> p k n", k=8)
              wqn_f, b_wqn = L1.bf(2 * 8 * 128, "wqn")
              wqn = wqn_f.rearrange("p (k h d) -> p k h d", k=2, h=8)
              wqr_f, b_wqr = L1.bf(2 * 8 * 64, "wqr")
              wqr = wqr_f.rearrange("p (k h d) -> p k h d", k=2, h=8)
              wkv_f, b_wkv = L1.bf(2048, "wkv")
              wkv = wkv_f.rearrange("p (h d) -> p h d", h=8)
              WkT_f, b_WkT = L1.bf(8 * 128, "WkT")
              WkT = WkT_f.rearrange("p (h c) -> p h c", h=8)
              wo1_f, b_wo1 = L1.bf(8 * 1024, "wo1")
              wo1 = wo1_f.rearrange("p (k n) -> p k n", k=8)
              ckvT, b_ckvT = L1.bf(2304, "ckvT")
              krT, b_krT = L1.bf(2304, "krT")
              VS_f, b_VS = L1.bf(NT * 132, "VS")
              VS = VS_f.rearrange("p (t d) -> p t d", t=NT)
              load_w("pool", w1, b_w1, w_in1, 0, 1472)
              wq3 = wq_b.rearrange("(k p) (h d) -> p k h d", p=128, h=8)
              for kc in range(2):
                  dma("pool", wqn[:, kc, :, :], wq3[:, kc, :, 0:128], R=[], W=[b_wqn])
                  dma("pool", wqr[:, kc, :, :], wq3[:, kc, :, 128:192], R=[], W=[b_wqr])
              dma("pool", wkv, wkv_b.rearrange("p (h d) -> p h d", h=8), R=[], W=[b_wkv])
              dma("pool", wo1, w_out1[:, :].rearrange("(k p) n -> p k n", p=128), R=[], W=[b_wo1])
              for hh in range(8):
                  tr(PT[:, hh * 128:(hh + 1) * 128], wkv[:, hh, 0:128], identb, R=[b_wkv, b_identb], W=[PTb[hh // 4]])
              cp("dve", WkT[:, 0:4, :], PT[:, 0:512].rearrange("p (h c) -> p h c", h=4), R=[PTb[0]], W=[b_WkT])
              cp("dve", WkT[:, 4:8, :], PT[:, 512:1024].rearrange("p (h c) -> p h c", h=4), R=[PTb[1]], W=[b_WkT])

              ckpt(3.5, bi)
              S.barrier()
              K1 = L1.sub()
              xn, b_xn = K1.bf(1024, "xn")
              xlTs = []
              for i in range(2):
                  ap, b = K1.bf(8 * 128, f"xlT{i}")
                  xlTs.append((ap.rearrange("p (k n) -> p k n", k=8), b))
              stp = Ring([K1.f32(16, f"st{i}") for i in range(4)])
              stp3 = Ring([K1.f32(16, f"st3{i}") for i in range(4)])
              rt = (K1.f32(64, "tc"), K1.f32(64, "ts"))
              krd, b_krd = K1.bf(128, "krd")
              S.op("dve", (lambda VS: lambda e: e.memset(VS[:, :, 128:129], 1.0))(VS), R=[], W=[b_VS])
              for t in range(NT):
                  r = 2 if t < 2 else bi
                  xlT, b_xlT = xlTs[t % 2]
                  if t == 0:
                      prep_xn(h[:, 0, :], b_h[0], xn, b_xn, stp)
                  make_xlT(h[:, t, :], b_h[t], 1, r, xn, b_xn, stp, xlT, b_xlT, t, two_banks=True, pre=True)
                  pk, b_pk = PS[t % 2]
                  if os.environ.get("NOPROJ", "0") == "1":
                      continue
                  if os.environ.get("USE_WO1", "0") == "1":
                      proj(pk, b_pk, xlT, b_xlT, wo1, b_wo1, 256, 192)
                  else:
                      proj(pk, b_pk, xlT, b_xlT, w1, b_w1, 256, int(os.environ.get("KVN_N", "192")))
                  if t + 1 < NT:
                      prep_xn(h[:, t + 1, :], b_h[t + 1], xn, b_xn, stp)
                  import os as _os
                  _v = int(_os.environ.get("KV1V", "9"))
                  if _v < 1:
                      continue
                  rstd, _, b_st3 = rstd_of(stp3, pk[:, 0:128], b_pk, 128, 1e-6)
                  stt("dve", VS[:, t, 0:128], pk[:, 0:128], rstd, small[:, KVN:KVN + 128], ALU.mult, ALU.mult,
                      R=[b_pk, b_st3, b_small], W=[b_VS])
                  if _v < 2:
                      continue
                  rope(rt, pk[:, 128:192], b_pk, 1, t, krd[:, 0:64], b_krd, eng2="dve")
                  cp("dve", krd[:, 64:128], krd[:, 0:64], R=[b_krd], W=[b_krd])
                  if _v < 3:
                      continue
                  tr(PT[:, 0:128], VS[:, t, 0:128], identb, R=[b_VS, b_identb], W=[PTb[0]])
                  tr(PT[:, 128:256], krd, identb, R=[b_krd, b_identb], W=[PTb[0]])
                  cp("act", ckvT[:, t * 128:(t + 1) * 128], PT[:, 0:128], R=[PTb[0]], W=[b_ckvT])
                  cp("act", krT[:, t * 128:(t + 1) * 128], PT[:, 128:256], R=[PTb[0]], W=[b_krT])
              S.barrier()
              ckpt(4, bi)

              Q1 = L1.sub()
              xn, b_xn = Q1.bf(1024, "xn")
              xlT_f, b_xlT = Q1.bf(8 * 256, "xlT/mixT")
              xlT = xlT_f.rearrange("p (k n) -> p k n", k=8)
              mixT, b_mixT = xlT, b_xlT
              stp = Ring([Q1.f32(16, f"st{i}") for i in range(4)])
              stp3 = Ring([Q1.f32(16, f"st3{i}") for i in range(4)])
              cqn, b_cqn = Q1.bf(256, "cqn")
              cqnT_f, b_cqnT = Q1.bf(2 * 256, "cqnT")
              cqnT = cqnT_f.rearrange("p (k n) -> p k n", k=2)
              qnTs = [Q1.bf(256, f"qnT{i}") for i in range(2)]
              qpT_f, b_qpT = Q1.bf(8 * 256, "qpT/onT")
              qpT = qpT_f.rearrange("p (h n) -> p h n", h=8)
              onT, b_onT = qpT, b_qpT
              rt = (Q1.f32(512, "tc"), Q1.f32(512, "ts"))
              qr, b_qr = Q1.bf(512, "qr")
              qrT_f, b_qrT = Q1.bf(4 * 256, "qrT")
              qrT = qrT_f.rearrange("p (g n) -> p g n", g=4)
              pts = [Q1.bf(512, f"pT{i}") for i in range(NPT)]
              rr, b_rr = Q1.f32(8, "rr")
              on_f, b_on = Q1.bf(2 * 1024, "on")
              on = on_f.rearrange("p (j f) -> p j f", j=2)
              szz, b_szz = Q1.f32(512, "sz")
              mix_f, b_mix = G_c.bitcast(BF16), Buf("mix")
              mix = mix_f.rearrange("p (j f) -> p j f", j=2)
              tmpo, b_tmpo = szz, b_szz
              ybuf = [(h[:, i, :], Buf(f"y{i}")) for i in range(2)]
              b_outd = [Buf(f"outd{bi}{i}") for i in range(2)]
              sc1 = 1.0 / math.sqrt(192.0)
              ptc = 0
              yc = 0
              for s in range(1, 9):
                  tiles = [2 * s, 2 * s + 1]
                  for j, t in enumerate(tiles):
                      make_xlT(h[:, t, :], b_h[t], 1, bi, xn, b_xn, stp, xlT[:, :, j * 128:(j + 1) * 128], b_xlT, j)
                  for j, t in enumerate(tiles):
                      pq, b_pq = PS[5]
                      proj(pq, b_pq, xlT[:, :, j * 128:(j + 1) * 128], b_xlT, w1, b_w1, 0, 256)
                      rstd, _, b_st3 = rstd_of(stp3, pq[:, 0:256], b_pq, 256, 1e-6)
                      stt("dve", cqn, pq[:, 0:256], rstd, small[:, QN:QN + 256], ALU.mult, ALU.mult,
                          R=[b_pq, b_st3, b_small], W=[b_cqn])
                      for kc in range(2):
                          tr(PT[:, kc * 128:(kc + 1) * 128], cqn[:, kc * 128:(kc + 1) * 128], identb, R=[b_cqn, b_identb], W=[PTb[0]])
                      cp("act", cqnT[:, :, j * 128:(j + 1) * 128], PT[:, 0:256].rearrange("p (k n) -> p k n", k=2),
                         R=[PTb[0]], W=[b_cqnT])
                  for j, t in enumerate(tiles):
                      pq, b_pq = PS[5]
                      for kc in range(2):
                          mm(pq, cqnT[:, kc, j * 128:(j + 1) * 128], wqr[:, kc, :, :], kc == 0, kc == 1, R=[b_cqnT, b_wqr], W=[b_pq])
                      rope(rt, pq, b_pq, 8, t, qr, b_qr)
                      for g in range(4):
                          tr(PT[:, 512 + g * 128:512 + (g + 1) * 128], qr[:, g * 128:(g + 1) * 128], identb,
                             R=[b_qr, b_identb], W=[PTb[1]])
                      cp("act", qrT[:, :, j * 128:(j + 1) * 128], PT[:, 512:1024].rearrange("p (g n) -> p g n", g=4),
                         R=[PTb[1]], W=[b_qrT])
                  for hh in range(8):
                      pq, b_pq = PS[5 + hh % 2]
                      qnT, b_qnT = qnTs[hh % 2]
                      for kc in range(2):
                          mm(pq[:, 0:256], wqn[:, kc, hh, :], cqnT[:, kc, :], kc == 0, kc == 1, R=[b_wqn, b_cqnT], W=[b_pq])
                      cp("dve", qnT, pq[:, 0:256], R=[b_pq], W=[b_qnT])
                      mm(pq[:, 256:512], WkT[:, hh, :], qnT, True, True, R=[b_WkT, b_qnT], W=[b_pq])
                      cp("dve", qpT[:, hh, :], pq[:, 256:512], R=[b_pq], W=[b_qpT])
                  items = [(hh, cpi) for hh in range(8) for cpi in range(NT // 2)]

                  def qk1(it, idx):
                      hh, cpi = it
                      hp = hh % 2
                      psc, b_psc = PS[idx % 3]
                      for cc in range(2):
                          c = 2 * cpi + cc
                          mm(psc[:, cc * 256:(cc + 1) * 256], ckvT[:, c * 128:(c + 1) * 128], qpT[:, hh, :], True, False,
                             R=[b_ckvT, b_qpT], W=[b_psc])
                          mm(psc[:, cc * 256:(cc + 1) * 256], krT[hp * 64:(hp + 1) * 64, c * 128:(c + 1) * 128],
                             qrT[hp * 64:(hp + 1) * 64, hh // 2, :], False, True, R=[b_krT, b_qrT], W=[b_psc])

                  def pv1(it, pT, b_pT):
                      hh, cpi = it
                      po, b_po = PS[3 + hh % 2]
                      for cc in range(2):
                          c = 2 * cpi + cc
                          for j in range(2):
                              mm(po[:, j * 132:j * 132 + 129], pT[:, cc * 256 + j * 128:cc * 256 + (j + 1) * 128],
                                 VS[:, c, 0:129], c == 0 and j == 0, c == NT - 1, R=[b_pT, b_VS], W=[b_po], skip=True)

                  for k0 in range(2):
                      qk1(items[k0], k0)
                  for idx, it in enumerate(items):
                      hh, cpi = it
                      psc, b_psc = PS[idx % 3]
                      pT, b_pT = pts[ptc % NPT]
                      ptc += 1
                      act(pT, psc, AF.Exp, R=[b_psc], W=[b_pT], scale=sc1)
                      if idx + 2 < len(items):
                          qk1(items[idx + 2], idx + 2)
                      pv1(it, pT, b_pT)
                      if cpi != NT // 2 - 1:
                          continue
                      po, b_po = PS[3 + hh % 2]
                      for j in range(2):
                          recip(rr[:, j:j + 1], po[:, j * 132 + 128:j * 132 + 129], R=[b_po], W=[b_rr])
                          ts("dve", on[:, j, hh * 128:(hh + 1) * 128], po[:, j * 132:j * 132 + 128], rr[:, j:j + 1], None,
                             ALU.mult, None, R=[b_po, b_rr], W=[b_on])
                  for j, t in enumerate(tiles):
                      for hh in range(8):
                          tr(PT[:, hh * 128:(hh + 1) * 128], on[:, j, hh * 128:(hh + 1) * 128], identb, R=[b_on, b_identb],
                             W=[PTb[hh // 4]])
                      cp("act", onT[:, 0:4, j * 128:(j + 1) * 128], PT[:, 0:512].rearrange("p (h n) -> p h n", h=4),
                         R=[PTb[0]], W=[b_onT])
                      cp("dve", onT[:, 4:8, j * 128:(j + 1) * 128], PT[:, 512:1024].rearrange("p (h n) -> p h n", h=4),
                         R=[PTb[1]], W=[b_onT])
                  for j, t in enumerate(tiles):
                      for n in range(2):
                          pe_, b_pe = PS[5]
                          pz, b_pz = PS[6]
                          for q in range(4):
                              hh = n * 4 + q
                              mm(pe_[:, q * 128:(q + 1) * 128], onT[:, hh, j * 128:(j + 1) * 128], wkv[:, hh, 128:256], True, True,
                                 R=[b_onT, b_wkv], W=[b_pe])
                          proj(pz, b_pz, xlT[:, :, j * 128:(j + 1) * 128], b_xlT, w1, b_w1, 448 + n * 512, 512)
                          act(szz, pz, AF.Silu, R=[b_pz], W=[b_szz])
                          tt("dve", mix[:, j, n * 512:(n + 1) * 512], pe_, szz, ALU.mult, R=[b_pe, b_szz], W=[b_mix])
                  for j, t in enumerate(tiles):
                      for kc in range(8):
                          tr(PT[:, kc * 128:(kc + 1) * 128], mix[:, j, kc * 128:(kc + 1) * 128], identb, R=[b_mix, b_identb],
                             W=[PTb[kc // 4]])
                      cp("act", mixT[:, 0:4, j * 128:(j + 1) * 128], PT[:, 0:512].rearrange("p (k n) -> p k n", k=4),
                         R=[PTb[0]], W=[b_mixT])
                      cp("dve", mixT[:, 4:8, j * 128:(j + 1) * 128], PT[:, 512:1024].rearrange("p (k n) -> p k n", k=4),
                         R=[PTb[1]], W=[b_mixT])
                  for j, t in enumerate(tiles):
                      for n in range(2):
                          po, b_po = PS[5 + n]
                          for kc in range(8):
                              mm(po, mixT[:, kc, j * 128:(j + 1) * 128], wo1[:, kc, n * 512:(n + 1) * 512], kc == 0, kc == 7,
                                 R=[b_mixT, b_wo1], W=[b_po])
                          tt("dve", tmpo, po, G_l[:, n * 512:(n + 1) * 512], ALU.mult, R=[b_po, b_Gl], W=[b_tmpo])
                          tt("pool", h[:, t, n * 512:(n + 1) * 512], tmpo, h[:, t, n * 512:(n + 1) * 512], ALU.add,
                             R=[b_tmpo, b_h[t]], W=[b_h[t]])
                      y, b_y = ybuf[yc % 2]
                      rstd, _, b_stf = rstd_of(stp, h[:, t, :], b_h[t], 1024, 1e-6)
                      stt("dve", y, h[:, t, :], rstd, small[:, FINAL:FINAL + 1024], ALU.mult, ALU.mult,
                          R=[b_h[t], b_stf, b_small], W=[b_y])
                      dma("sp", out_d[bi, (t - 2) * 128:(t - 1) * 128, :], y, R=[b_y], W=[b_outd[yc % 2]])
                      yc += 1
              S.barrier()
        except _Stop:
            pass
        S.emit()
        build_program.last_S = S
    return nc


_NC_CACHE = {}


def _rope_tables():
    seq, gw, dim = 2048, 64, 64
    rows = seq // gw
    row = np.repeat(np.arange(rows), gw).astype(np.float32)
    col = np.tile(np.arange(gw), rows).astype(np.float32)
    half = dim // 2
    inv = (np.float32(10000.0) ** (-np.arange(0, half, 2, dtype=np.float32) / np.float32(half))).astype(np.float32)
    ang_r = row[:, None] * inv[None, :]
    ang_c = col[:, None] * inv[None, :]
    ang = np.concatenate([ang_r, ang_r, ang_c, ang_c], axis=-1).astype(np.float32)
    cos = np.cos(ang).astype(np.float32)
    sin = np.sin(ang).astype(np.float32)
    sgn = np.tile(np.concatenate([-np.ones(16), np.ones(16)]), 2).astype(np.float32)
    cos_all = np.concatenate([np.ones((256, 64), np.float32), cos], 0)
    sin_all = np.concatenate([np.zeros((256, 64), np.float32), sin * sgn[None, :]], 0)
    cos_t = cos_all.reshape(NT, 128, 64).transpose(1, 0, 2).reshape(128, NT * 64)
    sin_t = sin_all.reshape(NT, 128, 64).transpose(1, 0, 2).reshape(128, NT * 64)
    return cos_t, sin_t


def _pack_small(core, c, c_ctx, norm_w, ada_b, a_bs, a_ln_w, a_ln_b, lq1, lk1, lq2, lk2, subln, qn, kvn, final_w):
    sm = np.zeros((128, NS), np.float32)
    rep = lambda v: np.broadcast_to(np.asarray(v, np.float32)[None, :], (128, v.shape[0]))
    sm[:, LNW:LNW + 512] = rep(a_ln_w[0])
    sm[:, LNB:LNB + 512] = rep(a_ln_b[0])
    sm[:, SUBLN:SUBLN + 128] = rep(subln[0])
    sm[:, QN:QN + 256] = rep(qn[0])
    sm[:, KVN:KVN + 128] = rep(kvn[0])
    sm[:, FINAL:FINAL + 1024] = rep(final_w)
    sm[:, LQ1:LQ1 + 64] = rep(lq1[0])
    sm[:, LK1:LK1 + 64] = rep(lk1[0])
    sm[:, LQ2:LQ2 + 64] = rep(lq2[0])
    sm[:, LK2:LK2 + 64] = rep(lk2[0])
    sm[:, ABS:ABS + 8] = a_bs[0].T
    sm[:, NW:NW + 16] = norm_w.reshape(2, 8, 128).transpose(2, 0, 1).reshape(128, 16)
    sm[:, ADAB:ADAB + 48] = ada_b.reshape(2, 24, 128).transpose(2, 0, 1).reshape(128, 48)
    cond = np.stack([c[2 * core], c[2 * core + 1], c_ctx], 0)
    sm[:, COND:COND + 24] = cond.reshape(3, 8, 128).transpose(2, 1, 0).reshape(128, 24)
    cos_t, sin_t = _rope_tables()
    sm[:, COS:COS + NT * 64] = cos_t
    sm[:, SIN:SIN + NT * 64] = sin_t
    sm[:, IDENT:IDENT + 128] = np.eye(128, dtype=np.float32)
    return sm


def kernel(x, c, ctx, c_ctx, norm_w, ada_w, ada_b, even_w_in, a_ws, a_bs, a_ln_w, a_ln_b,
           b_lq1, b_lk1, b_lq2, b_lk2, b_subln_w, even_w_out, odd_w_in, c_q_norm_w, c_wq_b,
           c_kv_norm_w, c_wkv_b, odd_w_out, final_w, _dbg=False, _stop=99, _ncores=8):
    f = lambda a: np.ascontiguousarray(np.asarray(a, dtype=np.float32))
    x, c, ctx, c_ctx = f(x), f(c), f(ctx), f(c_ctx)
    key = (bool(_dbg), _stop)
    if key not in _NC_CACHE:
        _NC_CACHE[key] = build_program(dbg=bool(_dbg), stop=_stop)
    nc = _NC_CACHE[key]
    shared = {
        "ada_w": f(ada_w), "w_in0": f(even_w_in)[0], "w_out0": f(even_w_out)[0], "w_in1": f(odd_w_in)[0],
        "wq_b": f(c_wq_b)[0], "wkv_b": f(c_wkv_b)[0], "w_out1": f(odd_w_out)[0],
        "a_wsT": np.ascontiguousarray(f(a_ws)[0].transpose(2, 0, 1)),
        "adab_g": np.ascontiguousarray(f(ada_b)[:, 2048:3072]),
    }
    in_maps = []
    for core in range(_ncores):
        xin = np.concatenate([ctx[2 * core:2 * core + 2], x[2 * core:2 * core + 2]], axis=1)
        sm = _pack_small(core, c, c_ctx, f(norm_w), f(ada_b), f(a_bs), f(a_ln_w), f(a_ln_b), f(b_lq1), f(b_lk1),
                         f(b_lq2), f(b_lk2), f(b_subln_w), f(c_q_norm_w), f(c_kv_norm_w), f(final_w))
        m = {"xin": np.ascontiguousarray(xin), "small": sm}
        m.update(shared)
        in_maps.append(m)
    res = run_bass_kernel_spmd(nc, in_maps, core_ids=list(range(_ncores)))
    out = np.concatenate([r["out"] for r in res.results], axis=0).astype(np.float32)
    if _dbg:
        return out, np.concatenate([r["dbg_h"] for r in res.results], axis=0)
    return out
```
